# Optimizing a Trainium2 kernel written in Bass

```python
import math
import jax, jax.numpy as jnp
from jax import lax
import numpy as np

D_MODEL = 1024
BATCH = 2
SEQ = 8192
DEPTH = 4

ATT_HEADS = 8
KV_HEADS = 2
GQA_GROUP = ATT_HEADS // KV_HEADS
HEAD_DIM = 64
ATT_WIDTH = ATT_HEADS * HEAD_DIM
KV_WIDTH = KV_HEADS * HEAD_DIM
WINDOW = 128
BLOCK = 128
SSM_WIDTH = D_MODEL - ATT_WIDTH
SSM_GROUP = 16
SSM_GROUPS = SSM_WIDTH // SSM_GROUP
SSM_STATE = 64
DT_MIN = 1e-3
DT_MAX = 1e-1
MIX_WIDTH = ATT_WIDTH + SSM_WIDTH
IN_WIDTH = ATT_WIDTH + 2 * KV_WIDTH + SSM_WIDTH
D_FF = 4 * D_MODEL
EPS = 1e-6

kernel_name = "hymba_style_swa_s5_hybrid_encoder"


def rms_norm(x, gain):
    xf = x.astype(jnp.float32)
    y = xf * lax.rsqrt(jnp.mean(xf * xf, axis=-1, keepdims=True) + EPS)
    return (y * gain.astype(jnp.float32)).astype(x.dtype)


def alibi_slopes(n_heads):
    return jnp.exp2(-8.0 * jnp.arange(1, n_heads + 1, dtype=jnp.float32) / n_heads)


def windowed_gqa(q, k, v, q_gain, k_gain, sink):
    bsz, seq = q.shape[0], q.shape[1]
    nb = seq // BLOCK
    q = rms_norm(q, q_gain)
    k = rms_norm(k, k_gain)
    qb = q.reshape(bsz, nb, BLOCK, KV_HEADS, GQA_GROUP, HEAD_DIM)

    def band(t):
        tp = jnp.pad(t, ((0, 0), (BLOCK, BLOCK), (0, 0), (0, 0)))
        tb = tp.reshape(bsz, nb + 2, BLOCK, KV_HEADS, HEAD_DIM)
        return jnp.concatenate([tb[:, :-2], tb[:, 1:-1], tb[:, 2:]], axis=2)

    kb, vb = band(k), band(v)
    scores = jnp.einsum('bnqkgd,bnckd->bnkgqc', qb, kb,
                        preferred_element_type=jnp.float32) / math.sqrt(HEAD_DIM)
    q_idx = jnp.arange(BLOCK)[:, None]
    c_idx = jnp.arange(3 * BLOCK)[None, :]
    dist = jnp.abs(q_idx - c_idx + BLOCK)
    key_pos = (jnp.arange(nb)[:, None] - 1) * BLOCK + jnp.arange(3 * BLOCK)[None, :]
    valid = (dist <= WINDOW)[None] & ((key_pos >= 0) & (key_pos < seq))[:, None, :]
    slopes = alibi_slopes(ATT_HEADS).reshape(KV_HEADS, GQA_GROUP)
    bias = -slopes[:, :, None, None] * dist.astype(jnp.float32)
    neg = jnp.finfo(jnp.float32).min
    scores = jnp.where(valid[None, :, None, None], scores + bias, neg)
    sk = sink.astype(jnp.float32).reshape(1, 1, KV_HEADS, GQA_GROUP, 1, 1)
    m = jnp.maximum(jnp.max(scores, axis=-1, keepdims=True), sk)
    p = jnp.exp(scores - m)
    denom = jnp.sum(p, axis=-1, keepdims=True) + jnp.exp(sk - m)
    out = jnp.einsum('bnkgqc,bnckd->bnqkgd', (p / denom).astype(v.dtype), vb)
    return out.reshape(bsz, seq, ATT_WIDTH)


def complex_diag_scan(a_re, a_im, b_re, b_im, reverse):
    ar = jnp.broadcast_to(a_re, b_re.shape)
    ai = jnp.broadcast_to(a_im, b_re.shape)

    def combine(e1, e2):
        a1r, a1i, b1r, b1i = e1
        a2r, a2i, b2r, b2i = e2
        return (a1r * a2r - a1i * a2i,
                a1r * a2i + a1i * a2r,
                a2r * b1r - a2i * b1i + b2r,
                a2r * b1i + a2i * b1r + b2i)

    _, _, xr, xi = lax.associative_scan(combine, (ar, ai, b_re, b_im), reverse=reverse, axis=1)
    return xr, xi


def s5_mixer(u, lam_re, lam_im, log_dt, b_re, b_im, c_re, c_im, d_skip, w_glu):
    bsz, seq = u.shape[0], u.shape[1]
    uf = u.astype(jnp.float32).reshape(bsz, seq, SSM_GROUPS, SSM_GROUP)
    y = d_skip.astype(jnp.float32).reshape(SSM_GROUPS, SSM_GROUP) * uf
    br = b_re.astype(jnp.float32)
    bi = b_im.astype(jnp.float32)
    for direction, reverse in enumerate((False, True)):
        lr = lam_re[direction].astype(jnp.float32)
        li = lam_im[direction].astype(jnp.float32)
        dt = jnp.exp(log_dt[direction].astype(jnp.float32))[:, None]
        mag = jnp.exp(lr * dt)
        abr = mag * jnp.cos(li * dt)
        abi = mag * jnp.sin(li * dt)
        den = lr * lr + li * li
        zr = ((abr - 1.0) * lr + abi * li) / den
        zi = (abi * lr - (abr - 1.0) * li) / den
        bbr = zr[..., None] * br - zi[..., None] * bi
        bbi = zr[..., None] * bi + zi[..., None] * br
        bur = jnp.einsum('bsgh,gph->bsgp', uf, bbr)
        bui = jnp.einsum('bsgh,gph->bsgp', uf, bbi)
        xr, xi = complex_diag_scan(abr, abi, bur, bui, reverse)
        y = (y + jnp.einsum('bsgp,ghp->bsgh', xr, c_re[direction].astype(jnp.float32))
             - jnp.einsum('bsgp,ghp->bsgh', xi, c_im[direction].astype(jnp.float32)))
    y = jax.nn.gelu(y).reshape(bsz, seq, SSM_WIDTH).astype(u.dtype)
    g_val, g_gate = jnp.split(y @ w_glu, 2, axis=-1)
    return g_val * jax.nn.sigmoid(g_gate)


def setup_inputs(seed: int = 0) -> dict:
    key = jax.random.key(seed)
    ks = jax.random.split(key, 20)
    nrm = jax.random.normal
    f32 = jnp.float32
    x = nrm(ks[0], (BATCH, SEQ, D_MODEL), f32)
    norm1 = 1.0 + 0.05 * nrm(ks[1], (DEPTH, D_MODEL), f32)
    w_in = nrm(ks[2], (DEPTH, D_MODEL, IN_WIDTH), f32) * D_MODEL ** -0.5
    q_gain = 1.0 + 0.05 * nrm(ks[3], (DEPTH, HEAD_DIM), f32)
    k_gain = 1.0 + 0.05 * nrm(ks[4], (DEPTH, HEAD_DIM), f32)
    sink = 0.5 * nrm(ks[5], (DEPTH, ATT_HEADS), f32)
    lam_re = -0.5 + 0.01 * nrm(ks[6], (DEPTH, 2, SSM_GROUPS, SSM_STATE), f32)
    lam_im = (math.pi * jnp.arange(SSM_STATE, dtype=f32)
              + 0.01 * nrm(ks[7], (DEPTH, 2, SSM_GROUPS, SSM_STATE), f32))
    log_dt = jax.random.uniform(ks[8], (DEPTH, 2, SSM_GROUPS), f32,
                                minval=math.log(DT_MIN), maxval=math.log(DT_MAX))
    b_re = nrm(ks[9], (DEPTH, SSM_GROUPS, SSM_STATE, SSM_GROUP), f32) * (2 * SSM_GROUP) ** -0.5
    b_im = nrm(ks[10], (DEPTH, SSM_GROUPS, SSM_STATE, SSM_GROUP), f32) * (2 * SSM_GROUP) ** -0.5
    c_re = nrm(ks[11], (DEPTH, 2, SSM_GROUPS, SSM_GROUP, SSM_STATE), f32) * SSM_STATE ** -0.5
    c_im = nrm(ks[12], (DEPTH, 2, SSM_GROUPS, SSM_GROUP, SSM_STATE), f32) * SSM_STATE ** -0.5
    d_skip = nrm(ks[13], (DEPTH, SSM_WIDTH), f32)
    w_glu = nrm(ks[14], (DEPTH, SSM_WIDTH, 2 * SSM_WIDTH), f32) * SSM_WIDTH ** -0.5
    w_out = nrm(ks[15], (DEPTH, MIX_WIDTH, D_MODEL), f32) * (0.5 * MIX_WIDTH ** -0.5)
    norm2 = 1.0 + 0.05 * nrm(ks[16], (DEPTH, D_MODEL), f32)
    w_ff1 = nrm(ks[17], (DEPTH, D_MODEL, D_FF), f32) * D_MODEL ** -0.5
    w_ff2 = nrm(ks[18], (DEPTH, D_FF, D_MODEL), f32) * (0.5 * D_FF ** -0.5)
    return {"x": x, "norm1": norm1, "w_in": w_in, "q_gain": q_gain, "k_gain": k_gain,
            "sink": sink, "lam_re": lam_re, "lam_im": lam_im, "log_dt": log_dt,
            "b_re": b_re, "b_im": b_im, "c_re": c_re, "c_im": c_im, "d_skip": d_skip,
            "w_glu": w_glu, "w_out": w_out, "norm2": norm2, "w_ff1": w_ff1, "w_ff2": w_ff2}


def reference(x, norm1, w_in, q_gain, k_gain, sink, lam_re, lam_im, log_dt,
              b_re, b_im, c_re, c_im, d_skip, w_glu, w_out, norm2, w_ff1, w_ff2):
    bsz, seq = x.shape[0], x.shape[1]
    q_end = ATT_WIDTH
    k_end = q_end + KV_WIDTH
    v_end = k_end + KV_WIDTH
    for layer in range(DEPTH):
        h = rms_norm(x, norm1[layer])
        z = h @ w_in[layer]
        q = z[..., :q_end].reshape(bsz, seq, ATT_HEADS, HEAD_DIM)
        k = z[..., q_end:k_end].reshape(bsz, seq, KV_HEADS, HEAD_DIM)
        v = z[..., k_end:v_end].reshape(bsz, seq, KV_HEADS, HEAD_DIM)
        u = z[..., v_end:]
        att = windowed_gqa(q, k, v, q_gain[layer], k_gain[layer], sink[layer])
        ssm = s5_mixer(u, lam_re[layer], lam_im[layer], log_dt[layer], b_re[layer], b_im[layer],
                       c_re[layer], c_im[layer], d_skip[layer], w_glu[layer])
        x = x + jnp.concatenate([att, ssm], axis=-1) @ w_out[layer]
        h = rms_norm(x, norm2[layer])
        x = x + jnp.square(jax.nn.relu(h @ w_ff1[layer])) @ w_ff2[layer]
    return x
```

```python
import math
import os
import numpy as np
from contextlib import ExitStack
import concourse.bass as bass
import concourse.mybir as mybir
from concourse.bass_utils import run_bass_kernel_spmd

F32 = mybir.dt.float32
BF16 = mybir.dt.bfloat16
I32 = mybir.dt.int32
AF = mybir.ActivationFunctionType
ALU = mybir.AluOpType

DEPTH = 4
T = 2048
NTT = 4
EPS = 1e-6
TWO_PI = 2.0 * math.pi


class Buf:
    def __init__(self, name=""):
        self.name = name
        self.w = None
        self.r = []


class EngQ:
    def __init__(self, name, sem):
        self.name = name
        self.sem = sem
        self.count = 0
        self.ops = []
        self.seen = {}


class Prog:
    ENGS = ["sync", "scalar", "gpsimd", "vector", "tensor"]

    def __init__(self, nc, es, ndma=8):
        self.nc = nc
        self.q = {e: EngQ(e, es.enter_context(nc.semaphore("s_" + e))) for e in self.ENGS}
        self.dq = {e: [EngQ(f"d_{e}{k}", es.enter_context(nc.semaphore(f"d_{e}{k}"))) for k in range(ndma)]
                   for e in ["sync", "gpsimd"]}
        self.rr = {e: 0 for e in self.dq}
        self.nops = 0

    def _waits(self, eng, reads, writes):
        q = self.q[eng]
        need = {}

        def add(tok):
            if tok is None:
                return
            s, v = tok
            if eng == "tensor" and s is q:
                return
            if need.get(s, 0) < v:
                need[s] = v
        for b in reads:
            add(b.w)
        for b in writes:
            add(b.w)
            for t in b.r:
                add(t)
        for s, v in need.items():
            if q.seen.get(s, 0) >= v:
                continue
            q.seen[s] = v
            q.ops.append(lambda e, s=s, v=v: e.wait_ge(s.sem, v))

    def _mark(self, tok, reads, writes):
        for b in reads:
            b.r = [t for t in b.r if t[0] is not tok[0]] + [tok]
        for b in writes:
            b.w = tok
            b.r = []

    def op(self, eng, fn, reads=(), writes=()):
        return self.group(eng, [fn], reads, writes)

    def group(self, eng, fns, reads=(), writes=()):
        q = self.q[eng]
        self._waits(eng, reads, writes)
        for fn in fns[:-1]:
            q.ops.append(lambda e, fn=fn: fn(e))
        q.count += 1
        tok = (q, q.count)
        q.ops.append(lambda e, fn=fns[-1], q=q: fn(e).then_inc(q.sem, 1))
        self._mark(tok, reads, writes)
        self.nops += len(fns)
        return tok

    def dma(self, eng, fn, reads=(), writes=()):
        self._waits(eng, reads, writes)
        k = self.rr[eng]
        self.rr[eng] = (k + 1) % len(self.dq[eng])
        d = self.dq[eng][k]
        q_ = self.q[eng]
        if d.count > 0 and q_.seen.get(d, 0) < d.count:
            q_.seen[d] = d.count
            q_.ops.append(lambda e, d=d, v=d.count: e.wait_ge(d.sem, v))
        d.count += 16
        tok = (d, d.count)
        self.q[eng].ops.append(lambda e, fn=fn, d=d: fn(e).then_inc(d.sem, 16))
        self._mark(tok, reads, writes)
        self.nops += 1
        return tok

    def wait_all(self, eng, bufs):
        self._waits(eng, [], bufs)

    def finish(self, eng="sync"):
        q = self.q[eng]
        allq = [x for x in self.q.values() if x is not q] + [d for ds in self.dq.values() for d in ds]
        for s_ in allq:
            if s_.count > 0:
                q.ops.append(lambda e, s_=s_, v=s_.count: e.wait_ge(s_.sem, v))

    def run(self):
        with self.nc.Block() as block:
            for e in self.ENGS:
                ops = self.q[e].ops

                def body(eng, ops=ops):
                    for o in ops:
                        o(eng)
                getattr(block, e)(body)


class _Stop(Exception):
    pass


def build_nc(nlayers=DEPTH, taps=None, stop_after=None):
    nc = bass.Bass("TRN2", target_bir_lowering=False)
    L = DEPTH

    def din(name, shape, dt=F32):
        return nc.dram_tensor(name, list(shape), dt, kind="ExternalInput").ap()

    xT_d = din("xT", [1024, T])
    LW = nlayers
    w_in_d = din("w_in", [LW, 1024, 1280])
    w_glu_d = din("w_glu", [LW, 512, 1024])
    w_out_d = din("w_out", [LW, 1024, 1024])
    w_ff1_d = din("w_ff1", [LW, 1024, 4096])
    w_ff2_d = din("w_ff2", [LW, 4096, 1024])
    g1_d = din("g1", [128, L, 8])
    g2_d = din("g2", [128, L, 8])
    qg_d = din("qg", [128, L])
    kg_d = din("kg", [128, L])
    sink_d = din("sinkr", [128, L, 8])
    lre_d = din("lre", [128, L, 32])
    lim_d = din("lim", [128, L, 32])
    ldt_d = din("ldt", [128, L, 32])
    bre_d = din("bre", [128, L, 256])
    bim_d = din("bim", [128, L, 256])
    cre_d = din("cre", [128, L, 512])
    cim_d = din("cim", [128, L, 512])
    dcol_d = din("dcol", [128, L, 32])
    etab_d = din("etab", [128, 3072])
    cst_d = din("cst", [128, 656])
    sel_d = din("sel", [128, 16])
    out_d = nc.dram_tensor("outT", [1024, T], F32, kind="ExternalOutput").ap()
    taps = list(taps) if taps else []
    dbg_d = nc.dram_tensor("dbg", [max(1, len(taps)), 128, 2048], F32, kind="ExternalOutput").ap() if taps else None

    xsp = nc.dram_tensor("xsp", [1024, T], F32).ap()
    hal_src = nc.dram_tensor("hal_src", [128, 512], F32).ap()
    hal_dst = nc.dram_tensor("hal_dst", [4 * 128, 512], F32).ap()
    st_src = nc.dram_tensor("st_src", [128, 64], F32).ap()
    st_dst = nc.dram_tensor("st_dst", [4 * 128, 64], F32).ap()
    GROUPS = [[0, 1, 2, 3], [4, 5, 6, 7]]

    with ExitStack() as es:
        P = Prog(nc, es)

        def sb(name, shape, dt=F32):
            return es.enter_context(nc.sbuf_tensor(name, list(shape), dt))

        XR = sb("XR", [128, 16384], F32)
        XRb = XR[:, :].bitcast(BF16)
        A1 = sb("A1", [128, 16384], BF16)
        A1f = A1[:, :].bitcast(F32)
        WA = sb("WA", [128, 16384], BF16)
        WAf = WA[:, :].bitcast(F32)
        A3 = sb("A3", [128, 8192], BF16)
        A4 = sb("A4", [128, 8192], BF16)
        MISC = sb("MISC", [128, 4096], BF16)

        xT = XR[:, :].rearrange("p (k t) -> p k t", k=8)
        b_x = [[Buf() for _ in range(NTT)] for _ in range(8)]
        allx = [b for row in b_x for b in row]
        U = XRb[:, 0:8192].rearrange("p (g c) -> p g c", g=32); b_U = Buf()
        PT = XRb[:, 8192:16384].rearrange("p (d g r n) -> p d g r n", d=2, g=32, r=2); b_PT = Buf()
        Qr = XRb[:, 16384:20480].rearrange("p (d g f) -> p d g f", d=2, g=16)
        Qn = XRb[:, 20480:24576].rearrange("p (d g f) -> p d g f", d=2, g=16); b_Q = Buf()
        Tm = XRb[:, 24576:28672].rearrange("p (g f) -> p g f", g=32); b_T = Buf()
        etab = XRb[:, 28672:31744].rearrange("p (k h q) -> p k h q", k=3, h=8); b_et = Buf()
        xr_mixer = [b_U, b_PT, b_Q, b_T, b_et]

        hT = A1[:, :].rearrange("p (k t) -> p k t", k=8); b_h = [Buf() for _ in range(NTT)]
        bre = A1f[:, 0:256].rearrange("p (g h) -> p g h", g=16)
        bim = A1f[:, 256:512].rearrange("p (g h) -> p g h", g=16)
        cre = A1f[:, 512:1024].rearrange("p (d g h) -> p d g h", d=2, g=16)
        cim = A1f[:, 1024:1536].rearrange("p (d g h) -> p d g h", d=2, g=16)
        b_par2 = Buf()
        halg = A1f[:, 1536:3584].rearrange("p (r f) -> p r f", r=4); b_halg = Buf()
        tA = A1f[:, 4096:6144]; tB = A1f[:, 6144:8192]; b_tA = Buf(); b_tB = Buf()
        Yall = A1[:, 0:8192].rearrange("p (g c) -> p g c", g=32); b_Yall = Buf()
        yT2 = A3[:, :].rearrange("p (a j c) -> p a j c", a=4, j=8); b_yT2 = Buf()
        cosT = A1f[:, 4096:4608].rearrange("p (g c) -> p g c", g=2)
        sinT = A1f[:, 4608:5120].rearrange("p (g c) -> p g c", g=2); b_tab = Buf()
        Wr = A1f[:, 5120:5632].rearrange("p (g c) -> p g c", g=2)
        Wi = A1f[:, 5632:6144].rearrange("p (g c) -> p g c", g=2); b_W = Buf()
        Sr = A1f[:, 6144:6656].rearrange("p (g c) -> p g c", g=2)
        Si = A1f[:, 6656:7168].rearrange("p (g c) -> p g c", g=2); b_S = Buf()
        Hb16 = {}
        for d in range(2):
            for ni, n in enumerate("ri"):
                o = 14336 + (d * 2 + ni) * 512
                Hb16[(d, n)] = A1[:, o:o + 512].rearrange("p (g c) -> p g c", g=2)
        b_H = Buf()
        a1_s2 = [b_par2, b_halg, b_tA, b_tB]
        b_Yb = [Buf() for _ in range(8)]

        win = WA[:, 0:10240].rearrange("p (k f) -> p k f", k=8); b_WA = Buf(); b_WB = Buf()
        X_r = WA[:, 0:4096].rearrange("p (d g f) -> p d g f", d=2, g=16)
        X_i = WA[:, 4096:8192].rearrange("p (d g f) -> p d g f", d=2, g=16); b_X = Buf()
        pw = {n: WAf[:, 4096 + i * 288:4096 + (i + 1) * 288].rearrange("p (k c) -> p k c", k=9)
              for i, n in enumerate(["pr", "pi", "mr", "mi"])}
        b_pw = Buf()
        cs_s = WAf[:, 5248:5536]; cs_c = WAf[:, 5536:5824]
        mg_p = WAf[:, 5824:6112]; mg_m = WAf[:, 6112:6400]
        Bb = {"r": WAf[:, 6400:6912].rearrange("p (d g h) -> p d g h", d=2, g=16),
              "i": WAf[:, 6912:7424].rearrange("p (d g h) -> p d g h", d=2, g=16)}
        b_Bb = Buf()
        zt = [WAf[:, 7424 + i * 32:7424 + (i + 1) * 32] for i in range(6)]; b_zt = Buf()
        wa_s2 = [b_X, b_pw, b_Bb, b_zt]
        wglu = WA[:, 0:4096].rearrange("p (k f) -> p k f", k=4)
        wout = WA[:, 4096:12288].rearrange("p (k f) -> p k f", k=8)
        wsl = [(WA[:, s * 8192:s * 8192 + 4096].rearrange("p (k f) -> p k f", k=8),
                WA[:, s * 8192 + 4096:s * 8192 + 8192].rearrange("p (k f) -> p k f", k=4)) for s in range(2)]

        qT = A3[:, :].rearrange("p (j t) -> p j t", j=4); b_q = Buf()
        gluT = A1[:, 8192:16384].rearrange("p (j t) -> p j t", j=4); b_glu = Buf()
        a1_p1 = b_Yb + [b_Yall, b_glu, b_tab, b_W, b_S, b_H]
        uT2 = A4[:, :].rearrange("p (a i c) -> p a i c", a=4, i=8); b_uT2 = Buf()
        attT = A4[:, :].rearrange("p (j t) -> p j t", j=4); b_att = Buf()

        pE = [MISC[:, i * 512:(i + 1) * 512] for i in range(3)]; b_pE = [Buf() for _ in range(3)]
        pM = [MISC[:, 1536 + i * 512:1536 + (i + 1) * 512] for i in range(3)]; b_pM = [Buf() for _ in range(3)]
        rden = MISC[:, 3072:3584].bitcast(F32).rearrange("p (a q) -> p a q", a=2); b_rden = Buf()
        rl = [MISC[:, i * 512:(i + 1) * 512] for i in range(2)]; b_rl = [Buf(), Buf()]
        aT = MISC[:, 1024:3072].rearrange("p (k t) -> p k t", k=4); b_a = [Buf() for _ in range(4)]
        misc_att = b_pE + b_pM + [b_rden]
        misc_ffn = b_rl + b_a

        kT = sb("kT", [128, T + 256], BF16); b_k = Buf()
        vtm = sb("vtm", [128, 18, 128], BF16); b_v = Buf()
        cst = sb("cst_sb", [128, 656]); b_cst = Buf()
        cstb = sb("cstb_sb", [128, 128], BF16)
        onesb = sb("onesb", [128, 128], BF16)
        blk2 = sb("blk2", [128, 128], BF16)
        ones_pn = sb("ones_pn", [128, 2, 64], BF16)
        sel = sb("sel_sb", [128, 16]); b_sel = Buf()
        g1 = sb("g1_sb", [128, L, 8]); g2 = sb("g2_sb", [128, L, 8])
        qg = sb("qg_sb", [128, L]); kg = sb("kg_sb", [128, L])
        sinkr = sb("sink_sb", [128, L * 8]); esink = sb("esink", [128, L, 8])
        b_small = Buf()
        epsb = sb("epsb", [128, 1])
        lre = sb("lre_sb", [128, 32]); lim = sb("lim_sb", [128, 32]); ldt = sb("ldt_sb", [128, 32])
        dcol = sb("dcol_sb", [128, 32]); b_par = Buf()
        lrdt = sb("lrdt", [128, 32]); th = sb("th", [128, 32]); Thu = sb("Thu", [128, 32])
        R8 = sb("R8", [128, 32]); Acr = sb("Acr", [128, 32]); Aci = sb("Aci", [128, 32])
        finr = sb("finr", [128, 32]); fini = sb("fini", [128, 32]); mag2048 = sb("mag2048", [128, 32])
        b_der = Buf()
        tS = [sb(f"tS{i}", [128, 512]) for i in range(3)]; b_tS = [Buf() for _ in range(3)]
        tSI = sb("tSI", [128, 512], I32); b_tSI = Buf()
        Fst = sb("Fst", [128, 64]); b_F = Buf()
        stg = sb("stg", [128, 4, 64]); b_stg = Buf()
        carry = sb("carry", [128, 64]); b_carry = Buf()
        hcr = sb("hcr", [128, 64]); b_hcr = Buf()
        halt = sb("halt", [128, 512]); b_halt = Buf()
        sq = sb("sq", [128, 2, 512], BF16); b_sq = [Buf(), Buf()]
        rstd = sb("rstd", [128, 512]); b_rstd = Buf()
        sig = sb("sig", [128, 512]); b_sig = Buf()
        gatec = sb("gatec", [128, 8]); b_gate = Buf()

        psb = [es.enter_context(nc.psum_tensor(f"ps{i}", [128, 512], F32)) for i in range(8)]
        b_ps = [Buf() for _ in range(8)]
        ps_rr = [0]

        def nps():
            i = ps_rr[0]
            ps_rr[0] = (i + 1) % 8
            return psb[i], b_ps[i]

        def V(fn, r=(), w=()):
            return P.op("vector", fn, r, w)

        def A(fn, r=(), w=()):
            return P.op("scalar", fn, r, w)

        def MM(fns, r=(), w=()):
            return P.group("tensor", fns, r, w)

        def I(method, *a, **k):
            return lambda e: getattr(e, method)(*a, **k)

        def mm(out, lhsT, rhs, start=True, stop=True):
            return I("matmul", out, lhsT=lhsT, rhs=rhs, start=start, stop=stop)

        def gate(old, new):
            V(I("memset", gatec[:, 0:1], 0.0), [], list(old) + list(new) + [b_gate])

        tap_i = [0]

        deferred_taps = []

        def tap(name, ap, bufs, n):
            if name in taps:
                deferred_taps.append((name, ap, bufs, n))

        def emit_taps():
            if not deferred_taps:
                return
            P.finish("vector")
            P.finish("sync")
            for (name, ap, bufs, n) in deferred_taps:
                idx = taps.index(name)
                for c0 in range(0, n, 512):
                    c1 = min(n, c0 + 512)
                    V(I("tensor_copy", tS[2][:, 0:c1 - c0], ap[:, c0:c1]), list(bufs), [b_tS[2]])
                    P.dma("sync", I("dma_start", out=dbg_d[idx, :, c0:c1], in_=tS[2][:, 0:c1 - c0]), reads=[b_tS[2]])

        P.dma("sync", I("dma_start", out=cst[:], in_=cst_d), writes=[b_cst])
        P.dma("sync", I("dma_start", out=sel[:], in_=sel_d), writes=[b_sel])
        for (t_sb, t_d) in [(g1, g1_d), (g2, g2_d), (qg, qg_d), (kg, kg_d)]:
            P.dma("sync", I("dma_start", out=t_sb[:], in_=t_d), writes=[b_small])
        P.dma("sync", I("dma_start", out=sinkr[:], in_=sink_d.rearrange("p l h -> p (l h)")), writes=[b_small])
        for k in range(8):
            P.dma("sync", I("dma_start", out=xT[:, k, :], in_=xT_d[k * 128:(k + 1) * 128, :]),
                  writes=b_x[k])
        ident = cst[:, 0:128]; Mf = cst[:, 128:256]; Mb = cst[:, 256:384]; kvec = cst[:, 384:393]
        cpos = cst[:, 400:656]
        V(I("tensor_copy", cstb[:], ident), [b_cst], [b_cst])
        V(I("memset", onesb[:], 1.0), [], [b_cst])
        V(I("memset", blk2[:], 0.0), [], [b_cst])
        V(I("memset", blk2[0:64, 0:64], 1.0), [], [b_cst])
        V(I("memset", blk2[64:128, 64:128], 1.0), [], [b_cst])
        V(I("memset", epsb[:], EPS), [], [b_cst])
        V(I("tensor_copy", ones_pn[:, 0, :], sel[:, 12:13].to_broadcast([128, 64])), [b_sel], [b_cst])
        V(I("tensor_copy", ones_pn[:, 1, :], sel[:, 13:14].to_broadcast([128, 64])), [b_sel], [b_cst])
        A(I("activation", esink[:, :, :].rearrange("p l h -> p (l h)"), sinkr[:], AF.Exp), [b_small], [b_small])
        identb = cstb
        esrow = sb("esrow", [1, 8, 128], BF16); b_esr = Buf()

        def rms_tile(l, tt, gain):
            ts = slice(tt * 512, (tt + 1) * 512)
            pt, bp = nps()
            for k in range(8):
                s = k % 2
                A(I("activation", sq[:, s, :], xT[:, k, ts], AF.Square), [b_x[k][tt]], [b_sq[s]])
                P.op("tensor", mm(pt[:, :], onesb[:, :], sq[:, s, :], start=(k == 0), stop=(k == 7)), [b_sq[s], b_cst], [bp])
            A(I("activation", rstd[:], pt[:, :], AF.Ln, bias=epsb[:], scale=1.0 / 1024.0), [bp, b_cst], [b_rstd])
            A(I("activation", rstd[:], rstd[:], AF.Exp, scale=-0.5), [b_rstd], [b_rstd])
            for k in range(8):
                V(I("scalar_tensor_tensor", out=hT[:, k, ts], in0=xT[:, k, ts], scalar=gain[:, l, k:k + 1],
                                                        in1=rstd[:], op0=ALU.mult, op1=ALU.mult),
                  [b_x[k][tt], b_rstd, b_small], [b_h[tt]])

        def sincos_turns(tin, n, out_s, out_c, rb, wb):
            a_ = tS[0][:, 0:n]; b_ = tS[1][:, 0:n]; i_ = tSI[:, 0:n]
            for (off, outp) in [(0.0, out_s), (0.25, out_c)]:
                V(I("tensor_scalar_add", a_, tin, off), rb, [b_tS[0]])
                V(I("tensor_copy", i_, a_), [b_tS[0]], [b_tSI])
                V(I("tensor_copy", b_, i_), [b_tSI], [b_tS[1]])
                V(I("tensor_sub", a_, a_, b_), [b_tS[0], b_tS[1]], [b_tS[0]])
                V(I("tensor_scalar", a_, a_, -0.4999999, 0.4999999, ALU.max, ALU.min), [b_tS[0]], [b_tS[0]])
                A(I("activation", outp, a_, AF.Sin, scale=TWO_PI), [b_tS[0]], wb)

        def ck(name):
            if stop_after == name:
                raise _Stop()

        if os.environ.get("HALO_FIRST"):
            V(I("memset", halt[:], 1.0), [], [b_halt])
            P.dma("gpsimd", I("dma_start", out=hal_src, in_=halt[:]), reads=[b_halt], writes=[b_halg])
            P.op("gpsimd", I("collective_compute", "AllGather", ALU.bypass, replica_groups=GROUPS,
                             ins=[hal_src.opt()], outs=[hal_dst.opt()]), [b_halg], [b_halg])
            P.dma("gpsimd", I("dma_start", out=halg, in_=hal_dst.rearrange("(r p) f -> p r f", p=128)),
                  reads=[b_halg], writes=[b_halg])
            if os.environ.get("HALO_FIRST") == "only":
                raise_stop = True

        def layer(l):
            P.dma("gpsimd", I("dma_start", out=win, in_=w_in_d[l].rearrange("(k p) f -> p k f", p=128)),
                  writes=[b_WA, b_WB])
            for (t_sb, t_d) in [(lre, lre_d), (lim, lim_d), (ldt, ldt_d), (dcol, dcol_d)]:
                P.dma("sync", I("dma_start", out=t_sb[:], in_=t_d[:, l]), writes=[b_par])

            for tt in range(NTT):
                ts = slice(tt * 512, (tt + 1) * 512)
                rms_tile(l, tt, g1)
                for k in range(8 if not os.environ.get("NOSPILL") else 0):
                    P.dma("sync", I("dma_start", out=xsp[k * 128:(k + 1) * 128, ts], in_=xT[:, k, ts]),
                          reads=[b_x[k][tt]])
                for oc in range(5):
                    pt, bp = nps()
                    MM([mm(pt[:, :], win[:, k, oc * 128:(oc + 1) * 128], hT[:, k, ts], k == 0, k == 7) for k in range(8)],
                       [b_WA, b_h[tt]], [bp])
                    A(I("activation", sq[:, 0, :], pt[:, :], AF.Square), [bp], [b_sq[0]])
                    p2, bp2 = nps()
                    MM([mm(p2[:, :], blk2[:, :], sq[:, 0, :])], [b_sq[0], b_cst], [bp2])
                    A(I("activation", rstd[:], p2[:, :], AF.Ln, bias=epsb[:], scale=1.0 / 64.0),
                      [bp2, b_cst], [b_rstd])
                    A(I("activation", rstd[:], rstd[:], AF.Exp, scale=-0.5), [b_rstd], [b_rstd])
                    if oc < 4:
                        V(I("scalar_tensor_tensor", out=qT[:, oc, ts], in0=pt[:, :], scalar=qg[:, l:l + 1],
                                                                               in1=rstd[:], op0=ALU.mult, op1=ALU.mult),
                          [bp, b_rstd, b_small], [b_q])
                    else:
                        V(I("scalar_tensor_tensor", out=kT[:, 128 + tt * 512:128 + (tt + 1) * 512], in0=pt[:, :],
                                                                         scalar=kg[:, l:l + 1], in1=rstd[:], op0=ALU.mult, op1=ALU.mult),
                          [bp, b_rstd, b_small], [b_k])
                        if tt == 0:
                            V(I("scalar_tensor_tensor", out=halt[:, 0:128], in0=pt[:, 0:128], scalar=kg[:, l:l + 1],
                                                                      in1=rstd[:, 0:128], op0=ALU.mult, op1=ALU.mult),
                              [bp, b_rstd, b_small], [b_halt])
                        if tt == NTT - 1:
                            V(I("scalar_tensor_tensor", out=halt[:, 128:256], in0=pt[:, 384:512], scalar=kg[:, l:l + 1],
                                                                      in1=rstd[:, 384:512], op0=ALU.mult, op1=ALU.mult),
                              [bp, b_rstd, b_small], [b_halt])
                pt, bp = nps()
                fns = []
                for b4 in range(4):
                    for k in range(8):
                        fns.append(mm(pt[:, b4 * 128:(b4 + 1) * 128], hT[:, k, tt * 512 + b4 * 128: tt * 512 + (b4 + 1) * 128],
                                      win[:, k, 640:768], k == 0, k == 7))
                MM(fns, [b_WA, b_h[tt]], [bp])
                A(I("activation", vtm[:, 1 + tt * 4:1 + (tt + 1) * 4, :].rearrange("p b f -> p (b f)"),
                                                       pt[:, :], AF.Copy), [bp], [b_v])
                if tt == 0:
                    V(I("tensor_copy", halt[:, 256:384], pt[:, 0:128]), [bp], [b_halt])
                if tt == NTT - 1:
                    V(I("tensor_copy", halt[:, 384:512], pt[:, 384:512]), [bp], [b_halt])
                for a_ in range(4):
                    oc = 6 + a_
                    pt, bp = nps()
                    MM([mm(pt[:, :], win[:, k, oc * 128:(oc + 1) * 128], hT[:, k, ts], k == 0, k == 7) for k in range(8)],
                       [b_WA, b_h[tt]], [bp])
                    A(I("activation", uT2[:, a_, :, tt * 64:(tt + 1) * 64],
                                                                  pt[:, :].rearrange("p (c i) -> p i c", i=8), AF.Copy),
                      [bp], [b_uT2])
            if l == 0:
                tap("hT", hT.rearrange("p k t -> p (k t)"), b_h, 2048)
                tap("qT", qT.rearrange("p j t -> p (j t)"), [b_q], 2048)
                tap("kT", kT[:, 128:128 + 2048], [b_k], 2048)
                tap("uT2", uT2.rearrange("p a i c -> p (a i c)"), [b_uT2], 2048)

            ck("S1")
            gate(b_h, a1_s2)
            P.dma("gpsimd", I("dma_start", out=hal_src, in_=halt[:]), reads=[b_halt], writes=[b_halg])
            if not os.environ.get("NOCC"):
                P.op("gpsimd", I("collective_compute", "AllGather", ALU.bypass, replica_groups=GROUPS,
                                 ins=[hal_src.opt()], outs=[hal_dst.opt()]), [b_halg], [b_halg])
            P.dma("gpsimd", I("dma_start", out=halg, in_=hal_dst.rearrange("(r p) f -> p r f", p=128)),
                  reads=[b_halg], writes=[b_halg])

            ck("halo")
            gate([b_WA, b_WB], wa_s2)
            gate(allx, xr_mixer)
            P.dma("gpsimd", I("dma_start", out=etab.rearrange("p k h q -> p (k h q)"), in_=etab_d), writes=[b_et])
            for (t_ap, t_d) in [(bre, bre_d), (bim, bim_d), (cre, cre_d), (cim, cim_d)]:
                P.dma("sync", I("dma_start",
                    out=t_ap.rearrange("p g h -> p (g h)") if len(t_ap.shape) == 3 else t_ap.rearrange("p d g h -> p (d g h)"),
                    in_=t_d[:, l]), writes=[b_par2])
            A(I("activation", lrdt[:], ldt[:], AF.Exp), [b_par], [b_der])
            V(I("tensor_mul", th[:], lim[:], lrdt[:]), [b_par, b_der], [b_der])
            V(I("tensor_mul", lrdt[:], lre[:], lrdt[:]), [b_par, b_der], [b_der])
            V(I("tensor_scalar_mul", th[:], th[:], 1.0 / TWO_PI), [b_der], [b_der])
            kb9 = kvec.unsqueeze(2).to_broadcast([128, 9, 32])
            ph9 = tS[2][:, 0:288]
            V(I("tensor_tensor", ph9.rearrange("p (k c) -> p k c", k=9), th[:].unsqueeze(1).to_broadcast([128, 9, 32]),
                                        kb9, ALU.mult), [b_der, b_cst], [b_tS[2]])
            sincos_turns(ph9, 288, cs_s, cs_c, [b_tS[2]], [b_pw])
            V(I("tensor_tensor", ph9.rearrange("p (k c) -> p k c", k=9), lrdt[:].unsqueeze(1).to_broadcast([128, 9, 32]),
                                        kb9, ALU.mult), [b_der, b_cst, b_pw], [b_tS[2]])
            A(I("activation", mg_p, ph9, AF.Exp), [b_tS[2]], [b_pw])
            A(I("activation", mg_m, ph9, AF.Exp, scale=-1.0), [b_tS[2]], [b_pw])
            fl = lambda t: t.rearrange("p k c -> p (k c)")
            V(I("tensor_mul", fl(pw["pr"]), mg_p, cs_c), [b_pw], [b_pw])
            V(I("tensor_mul", fl(pw["pi"]), mg_p, cs_s), [b_pw], [b_pw])
            V(I("tensor_mul", fl(pw["mr"]), mg_m, cs_c), [b_pw], [b_pw])
            V(I("scalar_tensor_tensor", out=fl(pw["mi"]), in0=mg_m, scalar=-1.0, in1=cs_s,
                                               op0=ALU.mult, op1=ALU.mult), [b_pw], [b_pw])
            ck("S2a")
            ar1, ai1 = pw["pr"][:, 1, :], pw["pi"][:, 1, :]
            zin = [b_zt, b_par, b_pw]
            V(I("tensor_mul", zt[0], lre[:], lre[:]), zin, [b_zt])
            V(I("tensor_mul", zt[1], lim[:], lim[:]), zin, [b_zt])
            V(I("tensor_add", zt[0], zt[0], zt[1]), zin, [b_zt])
            V(I("reciprocal", zt[0], zt[0]), zin, [b_zt])
            V(I("tensor_scalar_add", zt[1], ar1, -1.0), zin, [b_zt])
            V(I("tensor_mul", zt[2], zt[1], lre[:]), zin, [b_zt])
            V(I("tensor_mul", zt[3], ai1, lim[:]), zin, [b_zt])
            V(I("tensor_add", zt[2], zt[2], zt[3]), zin, [b_zt])
            V(I("tensor_mul", zt[2], zt[2], zt[0]), zin, [b_zt])
            V(I("tensor_mul", zt[3], ai1, lre[:]), zin, [b_zt])
            V(I("tensor_mul", zt[4], zt[1], lim[:]), zin, [b_zt])
            V(I("tensor_sub", zt[3], zt[3], zt[4]), zin, [b_zt])
            V(I("tensor_mul", zt[3], zt[3], zt[0]), zin, [b_zt])
            for d in range(2):
                zr = zt[2][:, d * 16:(d + 1) * 16].unsqueeze(2).to_broadcast([128, 16, 16])
                zi = zt[3][:, d * 16:(d + 1) * 16].unsqueeze(2).to_broadcast([128, 16, 16])
                t1 = tA[:, 0:256].rearrange("p (a b) -> p a b", a=16)
                t2 = tB[:, 0:256].rearrange("p (a b) -> p a b", a=16)
                V(I("tensor_tensor", t1, bre, zr, ALU.mult), [b_par2, b_zt], [b_tA])
                V(I("tensor_tensor", t2, bim, zi, ALU.mult), [b_par2, b_zt], [b_tB])
                V(I("tensor_sub", Bb["r"][:, d], t1, t2), [b_tA, b_tB], [b_Bb])
                V(I("tensor_tensor", t1, bim, zr, ALU.mult), [b_par2, b_zt], [b_tA])
                V(I("tensor_tensor", t2, bre, zi, ALU.mult), [b_par2, b_zt], [b_tB])
                V(I("tensor_add", Bb["i"][:, d], t1, t2), [b_tA, b_tB], [b_Bb])

            ck("S2b")

            def cmul_tab(out_r, out_i, tr, ti, Xr_, Xi_, wb, neg_i=False):
                trb = tr.rearrange("p k g -> p g k").unsqueeze(3).to_broadcast([128, 16, 8, 16])
                tib = ti.rearrange("p k g -> p g k").unsqueeze(3).to_broadcast([128, 16, 8, 16])
                Xrb = Xr_.unsqueeze(2).to_broadcast([128, 16, 8, 16])
                Xib = Xi_.unsqueeze(2).to_broadcast([128, 16, 8, 16])
                v4 = lambda t: t.rearrange("p (g k h) -> p g k h", g=16, k=8)
                o4 = lambda t: t.rearrange("p g (k h) -> p g k h", k=8)
                rd = [b_pw, b_Bb, b_par2]
                V(I("tensor_tensor", v4(tA), trb, Xrb, ALU.mult), rd, [b_tA])
                V(I("tensor_tensor", v4(tB), tib, Xib, ALU.mult), rd, [b_tB])
                V(I("tensor_sub", o4(out_r), v4(tA), v4(tB)), [b_tA, b_tB], [wb])
                V(I("tensor_tensor", v4(tA), trb, Xib, ALU.mult), rd, [b_tA])
                V(I("tensor_tensor", v4(tB), tib, Xrb, ALU.mult), rd, [b_tB])
                if neg_i:
                    V(I("scalar_tensor_tensor", out=o4(out_i), in0=v4(tA), scalar=-1.0, in1=v4(tB),
                                                       op0=ALU.mult, op1=ALU.subtract), [b_tA, b_tB], [wb])
                else:
                    V(I("tensor_add", o4(out_i), v4(tA), v4(tB)), [b_tA, b_tB], [wb])

            def tab(name, lo, hi, rev, d):
                t = pw[name][:, lo:hi, d * 16:(d + 1) * 16]
                return t[:, ::-1, :] if rev else t

            for d in range(2):
                rev = (d == 1)
                cmul_tab(Qr[:, d], Qn[:, d], tab("pr", 1, 9, rev, d), tab("pi", 1, 9, rev, d), cre[:, d], cim[:, d], b_Q, neg_i=True)
                cmul_tab(X_r[:, d], X_i[:, d], tab("mr", 1, 9, rev, d), tab("mi", 1, 9, rev, d), Bb["r"][:, d], Bb["i"][:, d], b_X)
            ck("S2c")
            for gq4 in range(4):
                for g2_ in range(2):
                    ps_ = slice(g2_ * 64, g2_ * 64 + 64)
                    ptf, bpf = nps(); ptb, bpb = nps()
                    for (d, pt, bp) in [(0, ptf, bpf), (1, ptb, bpb)]:
                        fns = []
                        for k4 in range(4):
                            gp = gq4 * 4 + k4
                            fns.append(mm(pt[:, k4 * 128:(k4 + 1) * 128], X_r[ps_, d, gp, :], Qr[ps_, d, gp, :], True, False))
                            fns.append(mm(pt[:, k4 * 128:(k4 + 1) * 128], X_i[ps_, d, gp, :], Qn[ps_, d, gp, :], False, True))
                        MM(fns, [b_X, b_Q], [bp])
                    m4 = lambda m: m.unsqueeze(1).to_broadcast([128, 4, 128])
                    v3 = lambda t: t.rearrange("p (a b) -> p a b", a=4)
                    V(I("tensor_tensor", v3(tA[:, 0:512]), v3(ptf[:, :]), m4(Mf), ALU.mult), [bpf, b_cst], [b_tA])
                    V(I("tensor_tensor", v3(tB[:, 0:512]), v3(ptb[:, :]), m4(Mb), ALU.mult), [bpb, b_cst], [b_tB])
                    V(I("tensor_add", tA[:, 0:512], tA[:, 0:512], tB[:, 0:512]), [b_tA, b_tB], [b_tA])
                    for k4 in range(4):
                        g = 2 * (gq4 * 4 + k4) + g2_
                        V(I("scalar_tensor_tensor", out=Tm[:, g, :], in0=ident, scalar=dcol[:, g:g + 1],
                            in1=tA[:, k4 * 128:(k4 + 1) * 128], op0=ALU.mult, op1=ALU.add),
                          [b_tA, b_par, b_cst], [b_T])
            ck("S2d")
            for d in range(2):
                rev = (d == 1)
                cmul_tab(X_r[:, d], X_i[:, d], tab("pr", 0, 8, not rev, d), tab("pi", 0, 8, not rev, d),
                         Bb["r"][:, d], Bb["i"][:, d], b_X)
            for d in range(2):
                for gq4 in range(4):
                    for g2_ in range(2):
                        ps_ = slice(g2_ * 64, g2_ * 64 + 64)
                        pt, bp = nps()
                        fns = []
                        for k4 in range(4):
                            gp = gq4 * 4 + k4
                            for ri, Pm in enumerate([X_r, X_i]):
                                c0 = (k4 * 2 + ri) * 64
                                fns.append(mm(pt[:, c0:c0 + 64], Pm[ps_, d, gp, :], identb[ps_, g2_ * 64:g2_ * 64 + 64]))
                        MM(fns, [b_X, b_cst], [bp])
                        g0 = 2 * gq4 * 4 + g2_
                        V(I("tensor_copy", PT[:, d, g0:g0 + 7:2, :, :].rearrange("p g r n -> p g (r n)"),
                            pt[:, :].rearrange("p (g f) -> p g f", g=4)), [bp], [b_PT])
            ck("S2e")
            V(I("tensor_scalar_mul", Thu[:], th[:], 8.0), [b_der], [b_der])
            V(I("tensor_copy", tSI[:, 0:32], Thu[:]), [b_der], [b_tSI])
            V(I("tensor_copy", tS[1][:, 0:32], tSI[:, 0:32]), [b_tSI], [b_tS[1]])
            V(I("tensor_sub", Thu[:], Thu[:], tS[1][:, 0:32]), [b_tS[1], b_der], [b_der])
            A(I("activation", R8[:], lrdt[:], AF.Exp, scale=8.0), [b_der], [b_der])
            V(I("tensor_scalar_mul", tS[2][:, 0:32], Thu[:], 256.0), [b_der], [b_tS[2]])
            sincos_turns(tS[2][:, 0:32], 32, fini[:], finr[:], [b_tS[2]], [b_der])
            A(I("activation", mag2048[:], lrdt[:], AF.Exp, scale=2048.0), [b_der], [b_der])
            V(I("tensor_mul", Acr[:], mag2048[:], finr[:]), [b_der], [b_der])
            V(I("tensor_mul", Aci[:], mag2048[:], fini[:]), [b_der], [b_der])
            if l == 0:
                tap("Tm", Tm.rearrange("p g f -> p (g f)"), [b_T], 2048)
                tap("Qr", Qr.rearrange("p d g f -> p (d g f)"), [b_Q], 2048)
                tap("PT", PT.rearrange("p d g r n -> p (d g r n)"), [b_PT], 2048)

            ck("S2")
            for (so, kcols, vcols, kdst, vblk) in [(4, slice(128, 256), slice(384, 512), slice(0, 128), 0),
                                                   (8, slice(0, 128), slice(256, 384), slice(T + 128, T + 256), 17)]:
                for (cols, dst_ap, wb) in [(kcols, kT[:, kdst], b_k), (vcols, vtm[:, vblk, :], b_v)]:
                    acc = tS[2][:, 0:128]
                    V(I("tensor_scalar_mul", acc, halg[:, 0, cols], sel[:, so:so + 1]),
                      [b_halg, b_sel], [b_tS[2]])
                    for r in range(1, 4):
                        V(I("scalar_tensor_tensor", out=acc, in0=halg[:, r, cols],
                                                                                  scalar=sel[:, so + r:so + r + 1], in1=acc,
                                                                                  op0=ALU.mult, op1=ALU.add),
                          [b_halg, b_sel], [b_tS[2]])
                    V(I("tensor_copy", dst_ap, acc), [b_tS[2]], [wb])

            ck("halosel")
            for i in range(8):
                for g8 in range(8):
                    P.dma("sync", I("dma_start",
                        out=U[i * 16:(i + 1) * 16, :, :].rearrange("p (a g) c -> p a g c", a=4)[:, :, g8, :],
                        in_=uT2[g8 * 16:(g8 + 1) * 16, :, i, :]), reads=[b_uT2], writes=[b_U])
            if l == 0:
                tap("U", U.rearrange("p g c -> p (g c)"), [b_U], 2048)
            ck("relayout")
            gate(a1_s2, a1_p1)
            gate(wa_s2, [b_WA, b_WB])
            P.dma("gpsimd", I("dma_start", out=wglu, in_=w_glu_d[l].rearrange("(k p) f -> p k f", p=128)), writes=[b_WA])
            P.dma("gpsimd", I("dma_start", out=wout, in_=w_out_d[l].rearrange("(k p) f -> p k f", p=128)), writes=[b_WA, b_WB])

            def ssm_batch(bt, phase):
                for d in range(2):
                    c0 = d * 16 + bt * 2
                    gsl = slice(c0, c0 + 2)
                    phv = tS[2][:, 0:512].rearrange("p (g c) -> p g c", g=2)
                    V(I("tensor_tensor", phv, Thu[:, gsl].unsqueeze(2).to_broadcast([128, 2, 256]),
                                                         cpos.unsqueeze(1).to_broadcast([128, 2, 256]), ALU.mult),
                      [b_der, b_cst], [b_tS[2]])
                    sincos_turns(tS[2][:, 0:512], 512, sinT.rearrange("p g c -> p (g c)"),
                                 cosT.rearrange("p g c -> p (g c)"), [b_tS[2]], [b_tab])
                    zb = []
                    for ri in range(2):
                        pt, bp = nps()
                        fns = []
                        for gq in range(2):
                            gp = bt * 2 + gq
                            for g2_ in range(2):
                                g = gp * 2 + g2_
                                fns.append(mm(pt[g2_ * 64:(g2_ + 1) * 64, gq * 256:(gq + 1) * 256], PT[:, d, g, ri, :], U[:, g, :]))
                        MM(fns, [b_PT, b_U], [bp])
                        zb.append((pt, bp))
                    (pzr, bzr), (pzi, bzi) = zb
                    zr_ = pzr[:, :].rearrange("p (g c) -> p g c", g=2)
                    zi_ = pzi[:, :].rearrange("p (g c) -> p g c", g=2)
                    if d == 1:
                        zr_ = zr_[:, :, ::-1]; zi_ = zi_[:, :, ::-1]
                    a3 = tS[0][:, 0:512].rearrange("p (g c) -> p g c", g=2)
                    b3 = tS[1][:, 0:512].rearrange("p (g c) -> p g c", g=2)
                    V(I("tensor_tensor", a3, zr_, cosT, ALU.mult), [bzr, b_tab], [b_tS[0]])
                    V(I("tensor_tensor", b3, zi_, sinT, ALU.mult), [bzi, b_tab], [b_tS[1]])
                    V(I("tensor_add", Wr, a3, b3), [b_tS[0], b_tS[1]], [b_W])
                    V(I("tensor_tensor", a3, zi_, cosT, ALU.mult), [bzi, b_tab], [b_tS[0]])
                    V(I("tensor_tensor", b3, zr_, sinT, ALU.mult), [bzr, b_tab], [b_tS[1]])
                    V(I("tensor_sub", Wi, a3, b3), [b_tS[0], b_tS[1]], [b_W])
                    for gq in range(2):
                        col = c0 + gq
                        for (Wt, St, ci) in [(Wr, Sr, 0), (Wi, Si, 1)]:
                            cc = ci * 32 + col
                            init = 0.0 if phase == 0 else carry[:, cc:cc + 1]
                            V(I("tensor_tensor_scan",
                                St[:, gq, :], R8[:, col:col + 1].to_broadcast([128, 256]), Wt[:, gq, :], init, ALU.mult, ALU.add),
                              [b_W, b_der, b_carry], [b_S])
                    if phase == 0:
                        fr = finr[:, gsl]; fi = fini[:, gsl]
                        s_r = Sr[:, :, 255]; s_i = Si[:, :, 255]
                        o_r = Fst[:, c0:c0 + 2]; o_i = Fst[:, 32 + c0:32 + c0 + 2]
                        t0 = tS[0][:, 0:2]; t1 = tS[1][:, 0:2]
                        V(I("tensor_mul", t0, s_r, fr), [b_S, b_der], [b_tS[0]])
                        V(I("tensor_mul", t1, s_i, fi), [b_S, b_der], [b_tS[1]])
                        V(I("tensor_sub", o_r, t0, t1), [b_tS[0], b_tS[1]], [b_F])
                        V(I("tensor_mul", t0, s_r, fi), [b_S, b_der], [b_tS[0]])
                        V(I("tensor_mul", t1, s_i, fr), [b_S, b_der], [b_tS[1]])
                        V(I("tensor_add", o_i, t0, t1), [b_tS[0], b_tS[1]], [b_F])
                    else:
                        Hr_ = Hb16[(d, "r")]; Hi_ = Hb16[(d, "i")]
                        if d == 0:
                            o_r = Hr_[:, :, 1:256]; o_i = Hi_[:, :, 1:256]
                            o_r0 = Hr_[:, :, 0]; o_i0 = Hi_[:, :, 0]
                        else:
                            o_r = Hr_[:, :, 254::-1]; o_i = Hi_[:, :, 254::-1]
                            o_r0 = Hr_[:, :, 255]; o_i0 = Hi_[:, :, 255]
                        a3s = a3[:, :, 0:255]; b3s = b3[:, :, 0:255]
                        V(I("tensor_tensor", a3s, Sr[:, :, 0:255], cosT[:, :, 0:255], ALU.mult), [b_S, b_tab], [b_tS[0]])
                        V(I("tensor_tensor", b3s, Si[:, :, 0:255], sinT[:, :, 0:255], ALU.mult), [b_S, b_tab], [b_tS[1]])
                        V(I("tensor_sub", o_r, a3s, b3s), [b_tS[0], b_tS[1]], [b_H])
                        V(I("tensor_tensor", a3s, Sr[:, :, 0:255], sinT[:, :, 0:255], ALU.mult), [b_S, b_tab], [b_tS[0]])
                        V(I("tensor_tensor", b3s, Si[:, :, 0:255], cosT[:, :, 0:255], ALU.mult), [b_S, b_tab], [b_tS[1]])
                        V(I("tensor_add", o_i, a3s, b3s), [b_tS[0], b_tS[1]], [b_H])
                        V(I("tensor_copy", o_r0, carry[:, c0:c0 + 2]), [b_carry], [b_H])
                        V(I("tensor_copy", o_i0, carry[:, 32 + c0:32 + c0 + 2]), [b_carry], [b_H])

            for bt in range(8):
                ssm_batch(bt, 0)
            ck("ssm0")
            P.dma("gpsimd", I("dma_start", out=st_src, in_=Fst[:]), reads=[b_F], writes=[b_stg])
            P.op("gpsimd", I("collective_compute", "AllGather", ALU.bypass, replica_groups=GROUPS,
                                                         ins=[st_src.opt()], outs=[st_dst.opt()]), [b_stg], [b_stg])
            P.dma("gpsimd", I("dma_start", out=stg[:], in_=st_dst.rearrange("(r p) f -> p r f", p=128)),
                  reads=[b_stg], writes=[b_stg])

            ck("stx")
            V(I("tensor_copy", esrow[:, :, :], esink[0:1, l, :].unsqueeze(2).to_broadcast([1, 8, 128])), [b_small], [b_esr])
            gate([b_uT2], [b_att])
            for qb in range(16):
                qs = slice(qb * 128, (qb + 1) * 128)
                for kvh in range(2):
                    hs_ = slice(kvh * 64, kvh * 64 + 64)
                    for kb in range(3):
                        pt, bp = nps()
                        MM([mm(pt[:, :].rearrange("p (j q) -> p j q", j=4), kT[hs_, (qb + kb) * 128:(qb + kb + 1) * 128], qT[hs_, :, qs])],
                           [b_k, b_q], [bp])
                        A(I("activation", pE[kb], pt[:, :], AF.Exp, scale=0.125), [bp], [b_pE[kb]])
                        P.op("gpsimd", I("tensor_tensor", pM[kb], pE[kb],
                                         etab[:, kb, kvh * 4:(kvh + 1) * 4, :].rearrange("p j q -> p (j q)"), ALU.mult),
                             [b_pE[kb], b_et], [b_pM[kb]])
                    pn, bpn = nps(); pd, bpd = nps()
                    fn_n = []; fn_d = []
                    for j in range(4):
                        po = slice((j % 2) * 64, (j % 2) * 64 + 64)
                        cs2 = slice((j // 2) * 128, (j // 2) * 128 + 128)
                        for kb in range(3):
                            if kb == 0 and qb == 0:
                                ol = ones_pn[:, 0, :]
                            elif kb == 2 and qb == 15:
                                ol = ones_pn[:, 1, :]
                            else:
                                ol = onesb[:, 0:64]
                            fn_n.append(mm(pn[po, cs2], vtm[:, qb + kb, hs_], pM[kb][:, j * 128:(j + 1) * 128], kb == 0, kb == 2))
                            if kb == 0:
                                fn_d.append(mm(pd[po, cs2], onesb[0:1, 0:64], esrow[0:1, kvh * 4 + j, :], True, False))
                            fn_d.append(mm(pd[po, cs2], ol, pM[kb][:, j * 128:(j + 1) * 128], False, kb == 2))
                    MM(fn_n, [b_v] + b_pM, [bpn])
                    MM(fn_d, [b_cst, b_esr] + b_pM, [bpd])
                    A(I("activation", rden, pd[:, 0:256].rearrange("p (a q) -> p a q", a=2), AF.Ln), [bpd], [b_rden])
                    A(I("activation", rden, rden, AF.Exp, scale=-1.0), [b_rden], [b_rden])
                    V(I("tensor_tensor", attT[:, kvh * 2:kvh * 2 + 2, qs],
                                                                       pn[:, 0:256].rearrange("p (a q) -> p a q", a=2), rden, ALU.mult),
                      [bpn, b_rden, b_uT2], [b_att])
            if l == 0:
                tap("attT", attT.rearrange("p j t -> p (j t)"), [b_att], 2048)

            ck("att")
            V(I("memset", carry[:], 0.0), [], [b_carry])
            for d in range(2):
                cs_ = slice(d * 16, d * 16 + 16); ci_ = slice(32 + d * 16, 32 + d * 16 + 16)
                cr = hcr[:, 0:16]; cim_ = hcr[:, 16:32]; t0 = hcr[:, 32:48]; t1 = hcr[:, 48:64]
                V(I("memset", hcr[:], 0.0), [], [b_hcr])
                order = [0, 1, 2, 3] if d == 0 else [3, 2, 1, 0]
                hh = [b_hcr]
                for r in order:
                    V(I("scalar_tensor_tensor", out=carry[:, cs_], in0=cr, scalar=sel[:, r:r + 1], in1=carry[:, cs_],
                                                                     op0=ALU.mult, op1=ALU.add), [b_hcr, b_sel], [b_carry])
                    V(I("scalar_tensor_tensor", out=carry[:, ci_], in0=cim_, scalar=sel[:, r:r + 1], in1=carry[:, ci_],
                                                                     op0=ALU.mult, op1=ALU.add), [b_hcr, b_sel], [b_carry])
                    ar_ = Acr[:, cs_]; ai_ = Aci[:, cs_]
                    V(I("tensor_mul", t0, cr, ar_), [b_der], hh)
                    V(I("tensor_mul", t1, cim_, ai_), [b_der], hh)
                    V(I("tensor_sub", t0, t0, t1), [], hh)
                    V(I("tensor_mul", t1, cr, ai_), [b_der], hh)
                    V(I("tensor_mul", cim_, cim_, ar_), [b_der], hh)
                    V(I("tensor_add", cim_, cim_, t1), [], hh)
                    V(I("tensor_add", cr, t0, stg[:, r, cs_]), [b_stg], hh)
                    V(I("tensor_add", cim_, cim_, stg[:, r, ci_]), [b_stg], hh)

            gate([b_q], [b_yT2])
            for bt in range(8):
                ssm_batch(bt, 1)
                for gq in range(2):
                    gp = bt * 2 + gq
                    pt, bp = nps()
                    fns = []
                    for g2_ in range(2):
                        g = gp * 2 + g2_
                        ps_ = slice(g2_ * 64, g2_ * 64 + 64)
                        oc = pt[:, g2_ * 256:g2_ * 256 + 256]
                        fns.append(mm(oc, Tm[:, g, :], U[:, g, :], True, False))
                        for d in range(2):
                            fns.append(mm(oc, Qr[ps_, d, gp, :], Hb16[(d, "r")][ps_, gq, :], False, False))
                            fns.append(mm(oc, Qn[ps_, d, gp, :], Hb16[(d, "i")][ps_, gq, :], False, d == 1))
                    MM(fns, [b_T, b_U, b_Q, b_H], [bp])
                    A(I("activation", Yall[:, gp * 2:gp * 2 + 2, :].rearrange("p g c -> p (g c)"), pt[:, :],
                                                           AF.Gelu_apprx_tanh), [bp, b_Yall], [b_Yb[bt]])
                for g in range(bt * 4, bt * 4 + 4):
                    a_, g8 = g // 8, g % 8
                    for j in range(8):
                        P.dma("sync", I("dma_start", out=yT2[g8 * 16:(g8 + 1) * 16, a_, j, :],
                                        in_=Yall[j * 16:(j + 1) * 16, g, :]), reads=[b_Yb[bt]], writes=[b_yT2])
            if l == 0:
                tap("Yall", Yall.rearrange("p g c -> p (g c)"), b_Yb, 2048)
            ck("ssm1")
            gate(xr_mixer, allx)
            for k in range(8):
                P.dma("sync", I("dma_start", out=xT[:, k, :], in_=xsp[k * 128:(k + 1) * 128, :]), writes=b_x[k])
            gate([b_tab, b_W, b_S, b_H], [b_glu])
            yflat = yT2.rearrange("p a j c -> p a (j c)")
            for cb in range(4):
                cs_ = slice(cb * 512, (cb + 1) * 512)
                for oc in range(4):
                    pv, bpv = nps(); pg, bpg = nps()
                    MM([mm(pv[:, :], wglu[:, k, oc * 128:(oc + 1) * 128], yflat[:, k, cs_], k == 0, k == 3) for k in range(4)],
                       [b_WA, b_yT2], [bpv])
                    MM([mm(pg[:, :], wglu[:, k, 512 + oc * 128:512 + (oc + 1) * 128], yflat[:, k, cs_], k == 0, k == 3) for k in range(4)],
                       [b_WA, b_yT2], [bpg])
                    A(I("activation", sig[:], pg[:, :], AF.Sigmoid), [bpg], [b_sig])
                    dst = gluT[:, oc, :].rearrange("p (c j) -> p j c", j=8)[:, 2 * cb:2 * cb + 2, :]
                    V(I("tensor_tensor", dst, pv[:, :].rearrange("p (j c) -> p j c", j=2),
                                                                sig[:, :].rearrange("p (j c) -> p j c", j=2), ALU.mult),
                      [bpv, b_sig], [b_glu])
            if l == 0:
                tap("gluT", gluT.rearrange("p j t -> p (j t)"), [b_glu], 2048)
            ck("glu")
            for tt in range(NTT):
                ts = slice(tt * 512, (tt + 1) * 512)
                for m in range(8):
                    pt, bp = nps()
                    fns = [mm(pt[:, :], wout[:, k, m * 128:(m + 1) * 128], attT[:, k, ts], k == 0, False) for k in range(4)]
                    fns += [mm(pt[:, :], wout[:, 4 + k, m * 128:(m + 1) * 128], gluT[:, k, ts], False, k == 3) for k in range(4)]
                    MM(fns, [b_WA, b_WB, b_att, b_glu], [bp])
                    V(I("tensor_add", xT[:, m, ts], xT[:, m, ts], pt[:, :]), [bp], [b_x[m][tt]])
            if l == 0:
                tap("xmid", xT[:, 0, :], b_x[0], 2048)

            ck("wout")
            gate(a1_p1, b_h)
            gate([b_att, b_yT2, b_uT2, b_q], [])
            gate(misc_att, misc_ffn)
            for tt in range(NTT):
                rms_tile(l, tt, g2)
            b_ws = [b_WA, b_WB]
            for gI in range(8):
                s = gI % 2
                w1, w2 = wsl[s]
                P.dma("gpsimd", I("dma_start",
                    out=w1, in_=w_ff1_d[l][:, gI * 512:(gI + 1) * 512].rearrange("(k p) f -> p k f", p=128)), writes=[b_ws[s]])
                P.dma("gpsimd", I("dma_start",
                    out=w2, in_=w_ff2_d[l][gI * 512:(gI + 1) * 512, :].rearrange("(k p) f -> p k f", p=128)), writes=[b_ws[s]])
                for tt in range(NTT):
                    ts = slice(tt * 512, (tt + 1) * 512)
                    for fc in range(4):
                        pt, bp = nps()
                        MM([mm(pt[:, :], w1[:, k, fc * 128:(fc + 1) * 128], hT[:, k, ts], k == 0, k == 7) for k in range(8)],
                           [b_ws[s], b_h[tt]], [bp])
                        ri_ = fc % 2
                        A(I("activation", rl[ri_], pt[:, :], AF.Relu), [bp], [b_rl[ri_]])
                        V(I("tensor_mul", aT[:, fc, :], rl[ri_], rl[ri_]), [b_rl[ri_]], [b_a[fc]])
                    for m in range(8):
                        pt, bp = nps()
                        MM([mm(pt[:, :], w2[:, k, m * 128:(m + 1) * 128], aT[:, k, :], k == 0, k == 3) for k in range(4)],
                           [b_ws[s]] + b_a, [bp])
                        V(I("tensor_add", xT[:, m, ts], xT[:, m, ts], pt[:, :]), [bp], [b_x[m][tt]])
            gate(misc_ffn, misc_att)

        try:
            for l in range(nlayers):
                layer(l)
        except _Stop:
            gate(xr_mixer, allx)
        emit_taps()
        for k in range(8):
            P.dma("sync", I("dma_start", out=out_d[k * 128:(k + 1) * 128, :], in_=xT[:, k, :]), reads=b_x[k])
        P.wait_all("sync", allx + b_tS)
        P.finish("sync")
        P.run()
    return nc


def _prep_inputs(x, norm1, w_in, q_gain, k_gain, sink, lam_re, lam_im, log_dt, b_re, b_im, c_re, c_im,
                 d_skip, w_glu, w_out, norm2, w_ff1, w_ff2, nlayers=DEPTH):
    f = lambda a: np.ascontiguousarray(np.asarray(a, dtype=np.float32))
    L = DEPTH
    perm = []
    for j in range(4):
        perm += list(range(j * 64, j * 64 + 64)) + list(range((4 + j) * 64, (4 + j) * 64 + 64))
    perm += list(range(512, 1280))
    shared = {
        "w_in": f(np.asarray(w_in)[:nlayers][:, :, perm]), "w_glu": f(np.asarray(w_glu)[:nlayers]),
        "w_out": f(np.asarray(w_out)[:nlayers]),
        "w_ff1": f(np.asarray(w_ff1)[:nlayers]), "w_ff2": f(np.asarray(w_ff2)[:nlayers]),
        "g1": f(np.asarray(norm1).reshape(L, 8, 128).transpose(2, 0, 1)),
        "g2": f(np.asarray(norm2).reshape(L, 8, 128).transpose(2, 0, 1)),
        "qg": f(np.tile(np.asarray(q_gain).T, (2, 1))),
        "kg": f(np.tile(np.asarray(k_gain).T, (2, 1))),
        "sinkr": f(np.broadcast_to(np.asarray(sink)[None], (128, L, 8))),
    }

    def gn(a):
        a = np.asarray(a).reshape(L, 2, 16, 2, 64)
        return f(a.transpose(3, 4, 0, 1, 2).reshape(128, L, 32))
    shared["lre"] = gn(lam_re)
    shared["lim"] = gn(lam_im)
    shared["ldt"] = gn(np.broadcast_to(np.asarray(log_dt)[:, :, :, None], (L, 2, 32, 64)))

    def bb(a):
        a = np.asarray(a).reshape(L, 16, 2, 64, 16)
        return f(a.transpose(2, 3, 0, 1, 4).reshape(128, L, 256))
    shared["bre"] = bb(b_re)
    shared["bim"] = bb(b_im)

    def cc(a):
        a = np.asarray(a).reshape(L, 2, 16, 2, 16, 64)
        return f(a.transpose(3, 5, 0, 1, 2, 4).reshape(128, L, 512))
    shared["cre"] = cc(c_re)
    shared["cim"] = cc(c_im)
    dsk = np.asarray(d_skip).reshape(L, 32, 16)
    shared["dcol"] = f(np.broadcast_to(dsk.transpose(2, 0, 1)[None], (8, 16, L, 32)).reshape(128, L, 32))
    slopes = np.exp2(-8.0 * np.arange(1, 9) / 8.0)
    ci = np.arange(128)[:, None]; qi = np.arange(128)[None, :]
    et = np.zeros((128, 3, 8, 128), np.float32)
    for kb in range(3):
        dist = np.abs(qi - ci - (kb - 1) * 128)
        valid = dist <= 128
        for h in range(8):
            et[:, kb, h, :] = np.where(valid, np.exp(-slopes[h] * dist), 0.0)
    shared["etab"] = f(et.reshape(128, 3072))
    cst = np.zeros((128, 656), np.float32)
    cst[:, 0:128] = np.eye(128)
    bi = np.arange(128)[:, None] // 16; bj = np.arange(128)[None, :] // 16
    cst[:, 128:256] = (bj >= bi)
    cst[:, 256:384] = (bi >= bj)
    cst[:, 384:393] = np.arange(9)[None, :]
    cst[:, 400:656] = np.arange(1, 257)[None, :]
    shared["cst"] = cst
    xs = np.asarray(x)
    in_maps = []
    for r in range(8):
        b, q = r // 4, r % 4
        m = dict(shared)
        m["xT"] = f(xs[b, q * T:(q + 1) * T, :].T)
        s = np.zeros((128, 16), np.float32)
        s[:, q] = 1.0
        if q > 0:
            s[:, 4 + q - 1] = 1.0; s[:, 12] = 1.0
        if q < 3:
            s[:, 8 + q + 1] = 1.0; s[:, 13] = 1.0
        m["sel"] = s
        in_maps.append(m)
    return in_maps


_NC_CACHE = {}


def kernel(**inputs):
    in_maps = _prep_inputs(**inputs)
    if "nc" not in _NC_CACHE:
        _NC_CACHE["nc"] = build_nc()
    nc = _NC_CACHE["nc"]
    res = run_bass_kernel_spmd(nc, in_maps, core_ids=list(range(8)))
    out = np.zeros((2, 4 * T, 1024), np.float32)
    for r in range(8):
        b, q = r // 4, r % 4
        out[b, q * T:(q + 1) * T, :] = np.asarray(res.results[r]["outT"]).T
    return out
```

```python
import math
import os
import numpy as np
from contextlib import ExitStack
import concourse.bass as bass
import concourse.mybir as mybir
from concourse.bass_utils import run_bass_kernel_spmd

F32 = mybir.dt.float32
BF16 = mybir.dt.bfloat16
I32 = mybir.dt.int32
AF = mybir.ActivationFunctionType
ALU = mybir.AluOpType

DEPTH = 4
T = 2048
NTT = 4
EPS = 1e-6
TWO_PI = 2.0 * math.pi


class Buf:
    def __init__(self, name=""):
        self.name = name
        self.w = None
        self.r = []


class EngQ:
    def __init__(self, name, sem):
        self.name = name
        self.sem = sem
        self.count = 0
        self.ops = []
        self.seen = {}


class Prog:
    ENGS = ["sync", "scalar", "gpsimd", "vector", "tensor"]

    def __init__(self, nc, es, ndma=16):
        self.nc = nc
        self.q = {e: EngQ(e, es.enter_context(nc.semaphore("s_" + e))) for e in self.ENGS}
        self.dq = {e: [EngQ(f"d_{e}{k}", es.enter_context(nc.semaphore(f"d_{e}{k}"))) for k in range(ndma)]
                   for e in ["sync", "gpsimd"]}
        self.rr = {e: 0 for e in self.dq}
        self.nops = 0

    def _waits(self, eng, reads, writes):
        q = self.q[eng]
        need = {}

        def add(tok):
            if tok is None:
                return
            s, v = tok
            if eng == "tensor" and s is q:
                return
            if need.get(s, 0) < v:
                need[s] = v
        for b in reads:
            add(b.w)
        for b in writes:
            add(b.w)
            for t in b.r:
                add(t)
        for s, v in need.items():
            if q.seen.get(s, 0) >= v:
                continue
            q.seen[s] = v
            q.ops.append(lambda e, s=s, v=v: e.wait_ge(s.sem, v))

    def _mark(self, tok, reads, writes):
        for b in reads:
            b.r = [t for t in b.r if t[0] is not tok[0]] + [tok]
        for b in writes:
            b.w = tok
            b.r = []

    def op(self, eng, fn, reads=(), writes=()):
        return self.group(eng, [fn], reads, writes)

    def group(self, eng, fns, reads=(), writes=()):
        q = self.q[eng]
        self._waits(eng, reads, writes)
        for fn in fns[:-1]:
            q.ops.append(lambda e, fn=fn: fn(e))
        q.count += 1
        tok = (q, q.count)
        q.ops.append(lambda e, fn=fns[-1], q=q: fn(e).then_inc(q.sem, 1))
        self._mark(tok, reads, writes)
        self.nops += len(fns)
        return tok

    def dma(self, eng, fn, reads=(), writes=()):
        self._waits(eng, reads, writes)
        k = self.rr[eng]
        self.rr[eng] = (k + 1) % len(self.dq[eng])
        d = self.dq[eng][k]
        q_ = self.q[eng]
        if d.count > 0 and q_.seen.get(d, 0) < d.count:
            q_.seen[d] = d.count
            q_.ops.append(lambda e, d=d, v=d.count: e.wait_ge(d.sem, v))
        d.count += 16
        tok = (d, d.count)
        self.q[eng].ops.append(lambda e, fn=fn, d=d: fn(e).then_inc(d.sem, 16))
        self._mark(tok, reads, writes)
        self.nops += 1
        return tok

    def wait_all(self, eng, bufs):
        self._waits(eng, [], bufs)

    def finish(self, eng="sync"):
        q = self.q[eng]
        allq = [x for x in self.q.values() if x is not q] + [d for ds in self.dq.values() for d in ds]
        for s_ in allq:
            if s_.count > 0:
                q.ops.append(lambda e, s_=s_, v=s_.count: e.wait_ge(s_.sem, v))

    def run(self):
        with self.nc.Block() as block:
            for e in self.ENGS:
                ops = self.q[e].ops

                def body(eng, ops=ops):
                    for o in ops:
                        o(eng)
                getattr(block, e)(body)


class _Stop(Exception):
    pass


def build_nc(nlayers=DEPTH, taps=None, stop_after=None):
    nc = bass.Bass("TRN2", target_bir_lowering=False)
    L = DEPTH

    def din(name, shape, dt=F32):
        return nc.dram_tensor(name, list(shape), dt, kind="ExternalInput").ap()

    xT_d = din("xT", [1024, T])
    LW = nlayers
    w_in_d = din("w_in", [LW, 1024, 1280])
    w_glu_d = din("w_glu", [LW, 512, 1024])
    w_out_d = din("w_out", [LW, 1024, 1024])
    w_ff1_d = din("w_ff1", [LW, 1024, 4096])
    w_ff2_d = din("w_ff2", [LW, 4096, 1024])
    g1_d = din("g1", [128, L, 8])
    g2_d = din("g2", [128, L, 8])
    qg_d = din("qg", [128, L])
    kg_d = din("kg", [128, L])
    sink_d = din("sinkr", [128, L, 8])
    lre_d = din("lre", [128, L, 32])
    lim_d = din("lim", [128, L, 32])
    ldt_d = din("ldt", [128, L, 32])
    bre_d = din("bre", [128, L, 256])
    bim_d = din("bim", [128, L, 256])
    cre_d = din("cre", [128, L, 512])
    cim_d = din("cim", [128, L, 512])
    dcol_d = din("dcol", [128, L, 32])
    etab_d = din("etab", [128, 3072])
    cst_d = din("cst", [128, 656])
    sel_d = din("sel", [128, 16])
    out_d = nc.dram_tensor("outT", [1024, T], F32, kind="ExternalOutput").ap()
    taps = list(taps) if taps else []
    dbg_d = nc.dram_tensor("dbg", [max(1, len(taps)), 128, 2048], F32, kind="ExternalOutput").ap() if taps else None

    xsp = nc.dram_tensor("xsp", [1024, T], F32).ap()
    hal_src = nc.dram_tensor("hal_src", [128, 512], F32).ap()
    hal_dst = nc.dram_tensor("hal_dst", [4 * 128, 512], F32).ap()
    st_src = nc.dram_tensor("st_src", [128, 64], F32).ap()
    st_dst = nc.dram_tensor("st_dst", [4 * 128, 64], F32).ap()
    GROUPS = [[0, 1, 2, 3], [4, 5, 6, 7]]

    with ExitStack() as es:
        P = Prog(nc, es)

        def sb(name, shape, dt=F32):
            return es.enter_context(nc.sbuf_tensor(name, list(shape), dt))

        XR = sb("XR", [128, 16384], F32)
        XRb = XR[:, :].bitcast(BF16)
        A1 = sb("A1", [128, 16384], BF16)
        A1f = A1[:, :].bitcast(F32)
        WA = sb("WA", [128, 16384], BF16)
        WAf = WA[:, :].bitcast(F32)
        A3 = sb("A3", [128, 8192], BF16)
        A4 = sb("A4", [128, 8192], BF16)
        MISC = sb("MISC", [128, 4096], BF16)

        xT = XR[:, :].rearrange("p (k t) -> p k t", k=8)
        b_x = [[Buf() for _ in range(NTT)] for _ in range(8)]
        allx = [b for row in b_x for b in row]
        U = XRb[:, 0:8192].rearrange("p (g c) -> p g c", g=32); b_U = Buf()
        PT = XRb[:, 8192:16384].rearrange("p (d g r n) -> p d g r n", d=2, g=32, r=2); b_PT = Buf()
        Qr = XRb[:, 16384:20480].rearrange("p (d g f) -> p d g f", d=2, g=16)
        Qn = XRb[:, 20480:24576].rearrange("p (d g f) -> p d g f", d=2, g=16); b_Q = Buf()
        Tm = XRb[:, 24576:28672].rearrange("p (g f) -> p g f", g=32); b_T = Buf()
        etab = XRb[:, 28672:31744].rearrange("p (k h q) -> p k h q", k=3, h=8); b_et = Buf()
        xr_mixer = [b_U, b_PT, b_Q, b_T, b_et]

        hT = A1[:, :].rearrange("p (k t) -> p k t", k=8); b_h = [Buf() for _ in range(NTT)]
        bre = A1f[:, 0:256].rearrange("p (g h) -> p g h", g=16)
        bim = A1f[:, 256:512].rearrange("p (g h) -> p g h", g=16)
        cre = A1f[:, 512:1024].rearrange("p (d g h) -> p d g h", d=2, g=16)
        cim = A1f[:, 1024:1536].rearrange("p (d g h) -> p d g h", d=2, g=16)
        b_par2 = Buf()
        halg = A1f[:, 1536:3584].rearrange("p (r f) -> p r f", r=4); b_halg = Buf()
        tA = A1f[:, 4096:6144]; tB = A1f[:, 6144:8192]; b_tA = Buf(); b_tB = Buf()
        Yall = A1[:, 0:8192].rearrange("p (g c) -> p g c", g=32); b_Yall = Buf()
        yT2 = A3[:, :].rearrange("p (a j c) -> p a j c", a=4, j=8); b_yT2 = Buf()
        cosT = A1f[:, 4096:4608].rearrange("p (g c) -> p g c", g=2)
        sinT = A1f[:, 4608:5120].rearrange("p (g c) -> p g c", g=2); b_tab = Buf()
        Wr = A1f[:, 5120:5632].rearrange("p (g c) -> p g c", g=2)
        Wi = A1f[:, 5632:6144].rearrange("p (g c) -> p g c", g=2); b_W = Buf()
        Sr = A1f[:, 6144:6656].rearrange("p (g c) -> p g c", g=2)
        Si = A1f[:, 6656:7168].rearrange("p (g c) -> p g c", g=2); b_S = Buf()
        Hb16 = {}
        for d in range(2):
            for ni, n in enumerate("ri"):
                o = 14336 + (d * 2 + ni) * 512
                Hb16[(d, n)] = A1[:, o:o + 512].rearrange("p (g c) -> p g c", g=2)
        b_H = Buf()
        a1_s2 = [b_par2, b_halg, b_tA, b_tB]
        b_Yb = [Buf() for _ in range(8)]

        win = WA[:, 0:10240].rearrange("p (k f) -> p k f", k=8); b_WA = Buf(); b_WB = Buf()
        X_r = WA[:, 0:4096].rearrange("p (d g f) -> p d g f", d=2, g=16)
        X_i = WA[:, 4096:8192].rearrange("p (d g f) -> p d g f", d=2, g=16); b_X = Buf()
        pw = {n: WAf[:, 4096 + i * 288:4096 + (i + 1) * 288].rearrange("p (k c) -> p k c", k=9)
              for i, n in enumerate(["pr", "pi", "mr", "mi"])}
        b_pw = Buf()
        cs_s = WAf[:, 5248:5536]; cs_c = WAf[:, 5536:5824]
        mg_p = WAf[:, 5824:6112]; mg_m = WAf[:, 6112:6400]
        Bb = {"r": WAf[:, 6400:6912].rearrange("p (d g h) -> p d g h", d=2, g=16),
              "i": WAf[:, 6912:7424].rearrange("p (d g h) -> p d g h", d=2, g=16)}
        b_Bb = Buf()
        zt = [WAf[:, 7424 + i * 32:7424 + (i + 1) * 32] for i in range(6)]; b_zt = Buf()
        wa_s2 = [b_X, b_pw, b_Bb, b_zt]
        E0c = WAf[:, 6144:6656].rearrange("p (g c) -> p g c", g=32)
        E0s = WAf[:, 6656:7168].rearrange("p (g c) -> p g c", g=32)
        E1c = WAf[:, 7168:7680].rearrange("p (g c) -> p g c", g=32)
        E1s = WAf[:, 7680:8192].rearrange("p (g c) -> p g c", g=32)
        b_E = Buf()
        wglu = WA[:, 0:4096].rearrange("p (k f) -> p k f", k=4)
        wout = WA[:, 4096:12288].rearrange("p (k f) -> p k f", k=8)
        wsl = [(WA[:, s * 8192:s * 8192 + 4096].rearrange("p (k f) -> p k f", k=8),
                WA[:, s * 8192 + 4096:s * 8192 + 8192].rearrange("p (k f) -> p k f", k=4)) for s in range(2)]

        qT = A3[:, :].rearrange("p (j t) -> p j t", j=4); b_q = Buf()
        gluT = A1[:, 8192:16384].rearrange("p (j t) -> p j t", j=4); b_glu = Buf()
        a1_p1 = b_Yb + [b_Yall, b_glu, b_tab, b_W, b_S, b_H]
        uT2 = A4[:, :].rearrange("p (a i c) -> p a i c", a=4, i=8); b_uT2 = Buf()
        attT = A4[:, :].rearrange("p (j t) -> p j t", j=4); b_att = Buf()

        pM2 = [[MISC[:, (s_ * 3 + i) * 512:(s_ * 3 + i + 1) * 512] for i in range(3)] for s_ in range(2)]
        b_pM2 = [[Buf() for _ in range(3)] for _ in range(2)]
        rden = MISC[:, 3072:3584].bitcast(F32).rearrange("p (a q) -> p a q", a=2); b_rden = Buf()
        rl = [MISC[:, i * 512:(i + 1) * 512] for i in range(2)]; b_rl = [Buf(), Buf()]
        aT = MISC[:, 1024:3072].rearrange("p (k t) -> p k t", k=4); b_a = [Buf() for _ in range(4)]
        misc_att = b_pM2[0] + b_pM2[1] + [b_rden]
        misc_ffn = b_rl + b_a

        kT = sb("kT", [128, T + 256], BF16); b_k = Buf()
        vtm = sb("vtm", [128, 18, 128], BF16); b_v = Buf()
        cst = sb("cst_sb", [128, 656]); b_cst = Buf()
        cstb = sb("cstb_sb", [128, 128], BF16)
        onesb = sb("onesb", [128, 128], BF16)
        blk2 = sb("blk2", [128, 128], BF16)
        ones_pn = sb("ones_pn", [128, 2, 64], BF16)
        sel = sb("sel_sb", [128, 16]); b_sel = Buf()
        g1 = sb("g1_sb", [128, L, 8]); g2 = sb("g2_sb", [128, L, 8])
        qg = sb("qg_sb", [128, L]); kg = sb("kg_sb", [128, L])
        sinkr = sb("sink_sb", [128, L * 8]); esink = sb("esink", [128, L, 8])
        b_small = Buf()
        epsb = sb("epsb", [128, 1])
        lre = sb("lre_sb", [128, 32]); lim = sb("lim_sb", [128, 32]); ldt = sb("ldt_sb", [128, 32])
        dcol = sb("dcol_sb", [128, 32]); b_par = Buf()
        lrdt = sb("lrdt", [128, 32]); th = sb("th", [128, 32]); Thu = sb("Thu", [128, 32])
        R8 = sb("R8", [128, 32]); Acr = sb("Acr", [128, 32]); Aci = sb("Aci", [128, 32])
        finr = sb("finr", [128, 32]); fini = sb("fini", [128, 32]); mag2048 = sb("mag2048", [128, 32])
        b_der = Buf()
        tS = [sb(f"tS{i}", [128, 512]) for i in range(3)]; b_tS = [Buf() for _ in range(3)]
        tSI = sb("tSI", [128, 512], I32); b_tSI = Buf()
        Fst = sb("Fst", [128, 64]); b_F = Buf()
        stg = sb("stg", [128, 4, 64]); b_stg = Buf()
        carry = sb("carry", [128, 64]); b_carry = Buf()
        hcr = sb("hcr", [128, 64]); b_hcr = Buf()
        halt = sb("halt", [128, 512]); b_halt = Buf()
        sq = sb("sq", [128, 2, 512], BF16); b_sq = [Buf(), Buf()]
        rstd = sb("rstd", [128, 512]); b_rstd = Buf()
        sig = sb("sig", [128, 512]); b_sig = Buf()
        gatec = sb("gatec", [128, 8]); b_gate = Buf()

        psb = [es.enter_context(nc.psum_tensor(f"ps{i}", [128, 512], F32)) for i in range(8)]
        b_ps = [Buf() for _ in range(8)]
        ps_rr = [0]

        def nps():
            i = ps_rr[0]
            ps_rr[0] = (i + 1) % 8
            return psb[i], b_ps[i]

        def V(fn, r=(), w=()):
            return P.op("vector", fn, r, w)

        def A(fn, r=(), w=()):
            return P.op("scalar", fn, r, w)

        def MM(fns, r=(), w=()):
            return P.group("tensor", fns, r, w)

        def I(method, *a, **k):
            return lambda e: getattr(e, method)(*a, **k)

        def mm(out, lhsT, rhs, start=True, stop=True):
            return I("matmul", out, lhsT=lhsT, rhs=rhs, start=start, stop=stop)

        def gate(old, new):
            V(I("memset", gatec[:, 0:1], 0.0), [], list(old) + list(new) + [b_gate])

        tap_i = [0]

        deferred_taps = []

        def tap(name, ap, bufs, n):
            if name in taps:
                deferred_taps.append((name, ap, bufs, n))

        def emit_taps():
            if not deferred_taps:
                return
            P.finish("vector")
            P.finish("sync")
            for (name, ap, bufs, n) in deferred_taps:
                idx = taps.index(name)
                for c0 in range(0, n, 512):
                    c1 = min(n, c0 + 512)
                    V(I("tensor_copy", tS[2][:, 0:c1 - c0], ap[:, c0:c1]), list(bufs), [b_tS[2]])
                    P.dma("sync", I("dma_start", out=dbg_d[idx, :, c0:c1], in_=tS[2][:, 0:c1 - c0]), reads=[b_tS[2]])

        P.dma("sync", I("dma_start", out=cst[:], in_=cst_d), writes=[b_cst])
        P.dma("sync", I("dma_start", out=sel[:], in_=sel_d), writes=[b_sel])
        for (t_sb, t_d) in [(g1, g1_d), (g2, g2_d), (qg, qg_d), (kg, kg_d)]:
            P.dma("sync", I("dma_start", out=t_sb[:], in_=t_d), writes=[b_small])
        P.dma("sync", I("dma_start", out=sinkr[:], in_=sink_d.rearrange("p l h -> p (l h)")), writes=[b_small])
        for k in range(8):
            P.dma("sync", I("dma_start", out=xT[:, k, :], in_=xT_d[k * 128:(k + 1) * 128, :]),
                  writes=b_x[k])
        ident = cst[:, 0:128]; Mf = cst[:, 128:256]; Mb = cst[:, 256:384]; kvec = cst[:, 384:393]
        cpos = cst[:, 400:656]
        V(I("tensor_copy", cstb[:], ident), [b_cst], [b_cst])
        V(I("memset", onesb[:], 1.0), [], [b_cst])
        V(I("memset", blk2[:], 0.0), [], [b_cst])
        V(I("memset", blk2[0:64, 0:64], 1.0), [], [b_cst])
        V(I("memset", blk2[64:128, 64:128], 1.0), [], [b_cst])
        V(I("memset", epsb[:], EPS), [], [b_cst])
        V(I("tensor_copy", ones_pn[:, 0, :], sel[:, 12:13].to_broadcast([128, 64])), [b_sel], [b_cst])
        V(I("tensor_copy", ones_pn[:, 1, :], sel[:, 13:14].to_broadcast([128, 64])), [b_sel], [b_cst])
        A(I("activation", esink[:, :, :].rearrange("p l h -> p (l h)"), sinkr[:], AF.Exp), [b_small], [b_small])
        identb = cstb
        c16 = sb("c16", [128, 16])
        V(I("tensor_scalar", c16[:], cpos[:, 0:16], -1.0, 16.0, ALU.add, ALU.mult), [b_cst], [b_cst])
        tabsets = [(cosT, sinT, b_tab), (rstd[:, :].rearrange("p (g c) -> p g c", g=2), sig[:, :].rearrange("p (g c) -> p g c", g=2), None)]
        ssm_it = [0]
        esrow = sb("esrow", [1, 8, 128], BF16); b_esr = Buf()

        def rms_tile(l, tt, gain):
            ts = slice(tt * 512, (tt + 1) * 512)
            pt, bp = nps()
            for k in range(8):
                s = k % 2
                A(I("activation", sq[:, s, :], xT[:, k, ts], AF.Square), [b_x[k][tt]], [b_sq[s]])
                P.op("tensor", mm(pt[:, :], onesb[:, :], sq[:, s, :], start=(k == 0), stop=(k == 7)), [b_sq[s], b_cst], [bp])
            A(I("activation", rstd[:], pt[:, :], AF.Ln, bias=epsb[:], scale=1.0 / 1024.0), [bp, b_cst], [b_rstd])
            A(I("activation", rstd[:], rstd[:], AF.Exp, scale=-0.5), [b_rstd], [b_rstd])
            for k in range(8):
                V(I("scalar_tensor_tensor", out=hT[:, k, ts], in0=xT[:, k, ts], scalar=gain[:, l, k:k + 1],
                                                        in1=rstd[:], op0=ALU.mult, op1=ALU.mult),
                  [b_x[k][tt], b_rstd, b_small], [b_h[tt]])

        def sincos_turns(tin, n, out_s, out_c, rb, wb):
            a_ = tS[0][:, 0:n]; b_ = tS[1][:, 0:n]; i_ = tSI[:, 0:n]
            for (off, outp) in [(0.0, out_s), (0.25, out_c)]:
                V(I("tensor_scalar_add", a_, tin, off), rb, [b_tS[0]])
                V(I("tensor_copy", i_, a_), [b_tS[0]], [b_tSI])
                V(I("tensor_copy", b_, i_), [b_tSI], [b_tS[1]])
                V(I("tensor_sub", a_, a_, b_), [b_tS[0], b_tS[1]], [b_tS[0]])
                V(I("tensor_scalar", a_, a_, -0.4999999, 0.4999999, ALU.max, ALU.min), [b_tS[0]], [b_tS[0]])
                A(I("activation", outp, a_, AF.Sin, scale=TWO_PI), [b_tS[0]], wb)

        def ck(name):
            if stop_after == name:
                raise _Stop()

        if os.environ.get("HALO_FIRST"):
            V(I("memset", halt[:], 1.0), [], [b_halt])
            P.dma("gpsimd", I("dma_start", out=hal_src, in_=halt[:]), reads=[b_halt], writes=[b_halg])
            P.op("gpsimd", I("collective_compute", "AllGather", ALU.bypass, replica_groups=GROUPS,
                             ins=[hal_src.opt()], outs=[hal_dst.opt()]), [b_halg], [b_halg])
            P.dma("gpsimd", I("dma_start", out=halg, in_=hal_dst.rearrange("(r p) f -> p r f", p=128)),
                  reads=[b_halg], writes=[b_halg])
            if os.environ.get("HALO_FIRST") == "only":
                raise_stop = True

        def layer(l):
            P.dma("gpsimd", I("dma_start", out=win, in_=w_in_d[l].rearrange("(k p) f -> p k f", p=128)),
                  writes=[b_WA, b_WB])
            for (t_sb, t_d) in [(lre, lre_d), (lim, lim_d), (ldt, ldt_d), (dcol, dcol_d)]:
                P.dma("sync", I("dma_start", out=t_sb[:], in_=t_d[:, l]), writes=[b_par])

            for tt in range(NTT):
                ts = slice(tt * 512, (tt + 1) * 512)
                rms_tile(l, tt, g1)
                for k in range(8 if not os.environ.get("NOSPILL") else 0):
                    P.dma("sync", I("dma_start", out=xsp[k * 128:(k + 1) * 128, ts], in_=xT[:, k, ts]),
                          reads=[b_x[k][tt]])
                for oc in range(5):
                    pt, bp = nps()
                    MM([mm(pt[:, :], win[:, k, oc * 128:(oc + 1) * 128], hT[:, k, ts], k == 0, k == 7) for k in range(8)],
                       [b_WA, b_h[tt]], [bp])
                    A(I("activation", sq[:, 0, :], pt[:, :], AF.Square), [bp], [b_sq[0]])
                    p2, bp2 = nps()
                    MM([mm(p2[:, :], blk2[:, :], sq[:, 0, :])], [b_sq[0], b_cst], [bp2])
                    A(I("activation", rstd[:], p2[:, :], AF.Ln, bias=epsb[:], scale=1.0 / 64.0),
                      [bp2, b_cst], [b_rstd])
                    A(I("activation", rstd[:], rstd[:], AF.Exp, scale=-0.5), [b_rstd], [b_rstd])
                    if oc < 4:
                        V(I("scalar_tensor_tensor", out=qT[:, oc, ts], in0=pt[:, :], scalar=qg[:, l:l + 1],
                                                                               in1=rstd[:], op0=ALU.mult, op1=ALU.mult),
                          [bp, b_rstd, b_small], [b_q])
                    else:
                        V(I("scalar_tensor_tensor", out=kT[:, 128 + tt * 512:128 + (tt + 1) * 512], in0=pt[:, :],
                                                                         scalar=kg[:, l:l + 1], in1=rstd[:], op0=ALU.mult, op1=ALU.mult),
                          [bp, b_rstd, b_small], [b_k])
                        if tt == 0:
                            V(I("scalar_tensor_tensor", out=halt[:, 0:128], in0=pt[:, 0:128], scalar=kg[:, l:l + 1],
                                                                      in1=rstd[:, 0:128], op0=ALU.mult, op1=ALU.mult),
                              [bp, b_rstd, b_small], [b_halt])
                        if tt == NTT - 1:
                            V(I("scalar_tensor_tensor", out=halt[:, 128:256], in0=pt[:, 384:512], scalar=kg[:, l:l + 1],
                                                                      in1=rstd[:, 384:512], op0=ALU.mult, op1=ALU.mult),
                              [bp, b_rstd, b_small], [b_halt])
                pt, bp = nps()
                fns = []
                for b4 in range(4):
                    for k in range(8):
                        fns.append(mm(pt[:, b4 * 128:(b4 + 1) * 128], hT[:, k, tt * 512 + b4 * 128: tt * 512 + (b4 + 1) * 128],
                                      win[:, k, 640:768], k == 0, k == 7))
                MM(fns, [b_WA, b_h[tt]], [bp])
                A(I("activation", vtm[:, 1 + tt * 4:1 + (tt + 1) * 4, :].rearrange("p b f -> p (b f)"),
                                                       pt[:, :], AF.Copy), [bp], [b_v])
                if tt == 0:
                    V(I("tensor_copy", halt[:, 256:384], pt[:, 0:128]), [bp], [b_halt])
                if tt == NTT - 1:
                    V(I("tensor_copy", halt[:, 384:512], pt[:, 384:512]), [bp], [b_halt])
                for a_ in range(4):
                    oc = 6 + a_
                    pt, bp = nps()
                    MM([mm(pt[:, :], win[:, k, oc * 128:(oc + 1) * 128], hT[:, k, ts], k == 0, k == 7) for k in range(8)],
                       [b_WA, b_h[tt]], [bp])
                    V(I("tensor_copy", uT2[:, a_, :, tt * 64:(tt + 1) * 64], pt[:, :].rearrange("p (c i) -> p i c", i=8)),
                      [bp], [b_uT2])
            if l == 0:
                tap("hT", hT.rearrange("p k t -> p (k t)"), b_h, 2048)
                tap("qT", qT.rearrange("p j t -> p (j t)"), [b_q], 2048)
                tap("kT", kT[:, 128:128 + 2048], [b_k], 2048)
                tap("uT2", uT2.rearrange("p a i c -> p (a i c)"), [b_uT2], 2048)

            ck("S1")
            gate(b_h, a1_s2)
            P.dma("gpsimd", I("dma_start", out=hal_src, in_=halt[:]), reads=[b_halt], writes=[b_halg])
            if not os.environ.get("NOCC"):
                P.op("gpsimd", I("collective_compute", "AllGather", ALU.bypass, replica_groups=GROUPS,
                                 ins=[hal_src.opt()], outs=[hal_dst.opt()]), [b_halg], [b_halg])
            P.dma("gpsimd", I("dma_start", out=halg, in_=hal_dst.rearrange("(r p) f -> p r f", p=128)),
                  reads=[b_halg], writes=[b_halg])

            ck("halo")
            gate([b_WA, b_WB, b_E], wa_s2)
            gate(allx, xr_mixer)
            P.dma("gpsimd", I("dma_start", out=etab.rearrange("p k h q -> p (k h q)"), in_=etab_d), writes=[b_et])
            for (t_ap, t_d) in [(bre, bre_d), (bim, bim_d), (cre, cre_d), (cim, cim_d)]:
                P.dma("sync", I("dma_start",
                    out=t_ap.rearrange("p g h -> p (g h)") if len(t_ap.shape) == 3 else t_ap.rearrange("p d g h -> p (d g h)"),
                    in_=t_d[:, l]), writes=[b_par2])
            A(I("activation", lrdt[:], ldt[:], AF.Exp), [b_par], [b_der])
            V(I("tensor_mul", th[:], lim[:], lrdt[:]), [b_par, b_der], [b_der])
            V(I("tensor_mul", lrdt[:], lre[:], lrdt[:]), [b_par, b_der], [b_der])
            V(I("tensor_scalar_mul", th[:], th[:], 1.0 / TWO_PI), [b_der], [b_der])
            kb9 = kvec.unsqueeze(2).to_broadcast([128, 9, 32])
            ph9 = tS[2][:, 0:288]
            V(I("tensor_tensor", ph9.rearrange("p (k c) -> p k c", k=9), th[:].unsqueeze(1).to_broadcast([128, 9, 32]),
                                        kb9, ALU.mult), [b_der, b_cst], [b_tS[2]])
            sincos_turns(ph9, 288, cs_s, cs_c, [b_tS[2]], [b_pw])
            V(I("tensor_tensor", ph9.rearrange("p (k c) -> p k c", k=9), lrdt[:].unsqueeze(1).to_broadcast([128, 9, 32]),
                                        kb9, ALU.mult), [b_der, b_cst, b_pw], [b_tS[2]])
            A(I("activation", mg_p, ph9, AF.Exp), [b_tS[2]], [b_pw])
            A(I("activation", mg_m, ph9, AF.Exp, scale=-1.0), [b_tS[2]], [b_pw])
            fl = lambda t: t.rearrange("p k c -> p (k c)")
            V(I("tensor_mul", fl(pw["pr"]), mg_p, cs_c), [b_pw], [b_pw])
            V(I("tensor_mul", fl(pw["pi"]), mg_p, cs_s), [b_pw], [b_pw])
            V(I("tensor_mul", fl(pw["mr"]), mg_m, cs_c), [b_pw], [b_pw])
            V(I("scalar_tensor_tensor", out=fl(pw["mi"]), in0=mg_m, scalar=-1.0, in1=cs_s,
                                               op0=ALU.mult, op1=ALU.mult), [b_pw], [b_pw])
            ck("S2a")
            ar1, ai1 = pw["pr"][:, 1, :], pw["pi"][:, 1, :]
            zin = [b_zt, b_par, b_pw]
            V(I("tensor_mul", zt[0], lre[:], lre[:]), zin, [b_zt])
            V(I("tensor_mul", zt[1], lim[:], lim[:]), zin, [b_zt])
            V(I("tensor_add", zt[0], zt[0], zt[1]), zin, [b_zt])
            V(I("reciprocal", zt[0], zt[0]), zin, [b_zt])
            V(I("tensor_scalar_add", zt[1], ar1, -1.0), zin, [b_zt])
            V(I("tensor_mul", zt[2], zt[1], lre[:]), zin, [b_zt])
            V(I("tensor_mul", zt[3], ai1, lim[:]), zin, [b_zt])
            V(I("tensor_add", zt[2], zt[2], zt[3]), zin, [b_zt])
            V(I("tensor_mul", zt[2], zt[2], zt[0]), zin, [b_zt])
            V(I("tensor_mul", zt[3], ai1, lre[:]), zin, [b_zt])
            V(I("tensor_mul", zt[4], zt[1], lim[:]), zin, [b_zt])
            V(I("tensor_sub", zt[3], zt[3], zt[4]), zin, [b_zt])
            V(I("tensor_mul", zt[3], zt[3], zt[0]), zin, [b_zt])
            for d in range(2):
                zr = zt[2][:, d * 16:(d + 1) * 16].unsqueeze(2).to_broadcast([128, 16, 16])
                zi = zt[3][:, d * 16:(d + 1) * 16].unsqueeze(2).to_broadcast([128, 16, 16])
                t1 = tA[:, 0:256].rearrange("p (a b) -> p a b", a=16)
                t2 = tB[:, 0:256].rearrange("p (a b) -> p a b", a=16)
                V(I("tensor_tensor", t1, bre, zr, ALU.mult), [b_par2, b_zt], [b_tA])
                V(I("tensor_tensor", t2, bim, zi, ALU.mult), [b_par2, b_zt], [b_tB])
                V(I("tensor_sub", Bb["r"][:, d], t1, t2), [b_tA, b_tB], [b_Bb])
                V(I("tensor_tensor", t1, bim, zr, ALU.mult), [b_par2, b_zt], [b_tA])
                V(I("tensor_tensor", t2, bre, zi, ALU.mult), [b_par2, b_zt], [b_tB])
                V(I("tensor_add", Bb["i"][:, d], t1, t2), [b_tA, b_tB], [b_Bb])

            ck("S2b")

            def cmul_tab(out_r, out_i, tr, ti, Xr_, Xi_, wb, neg_i=False):
                trb = tr.rearrange("p k g -> p g k").unsqueeze(3).to_broadcast([128, 16, 8, 16])
                tib = ti.rearrange("p k g -> p g k").unsqueeze(3).to_broadcast([128, 16, 8, 16])
                Xrb = Xr_.unsqueeze(2).to_broadcast([128, 16, 8, 16])
                Xib = Xi_.unsqueeze(2).to_broadcast([128, 16, 8, 16])
                v4 = lambda t: t.rearrange("p (g k h) -> p g k h", g=16, k=8)
                o4 = lambda t: t.rearrange("p g (k h) -> p g k h", k=8)
                rd = [b_pw, b_Bb, b_par2]
                V(I("tensor_tensor", v4(tA), trb, Xrb, ALU.mult), rd, [b_tA])
                V(I("tensor_tensor", v4(tB), tib, Xib, ALU.mult), rd, [b_tB])
                V(I("tensor_sub", o4(out_r), v4(tA), v4(tB)), [b_tA, b_tB], [wb])
                V(I("tensor_tensor", v4(tA), trb, Xib, ALU.mult), rd, [b_tA])
                V(I("tensor_tensor", v4(tB), tib, Xrb, ALU.mult), rd, [b_tB])
                if neg_i:
                    V(I("scalar_tensor_tensor", out=o4(out_i), in0=v4(tA), scalar=-1.0, in1=v4(tB),
                                                       op0=ALU.mult, op1=ALU.subtract), [b_tA, b_tB], [wb])
                else:
                    V(I("tensor_add", o4(out_i), v4(tA), v4(tB)), [b_tA, b_tB], [wb])

            def tab(name, lo, hi, rev, d):
                t = pw[name][:, lo:hi, d * 16:(d + 1) * 16]
                return t[:, ::-1, :] if rev else t

            for d in range(2):
                rev = (d == 1)
                cmul_tab(Qr[:, d], Qn[:, d], tab("pr", 1, 9, rev, d), tab("pi", 1, 9, rev, d), cre[:, d], cim[:, d], b_Q, neg_i=True)
                cmul_tab(X_r[:, d], X_i[:, d], tab("mr", 1, 9, rev, d), tab("mi", 1, 9, rev, d), Bb["r"][:, d], Bb["i"][:, d], b_X)
            ck("S2c")
            for gq4 in range(4):
                for g2_ in range(2):
                    ps_ = slice(g2_ * 64, g2_ * 64 + 64)
                    ptf, bpf = nps(); ptb, bpb = nps()
                    for (d, pt, bp) in [(0, ptf, bpf), (1, ptb, bpb)]:
                        fns = []
                        for k4 in range(4):
                            gp = gq4 * 4 + k4
                            fns.append(mm(pt[:, k4 * 128:(k4 + 1) * 128], X_r[ps_, d, gp, :], Qr[ps_, d, gp, :], True, False))
                            fns.append(mm(pt[:, k4 * 128:(k4 + 1) * 128], X_i[ps_, d, gp, :], Qn[ps_, d, gp, :], False, True))
                        MM(fns, [b_X, b_Q], [bp])
                    m4 = lambda m: m.unsqueeze(1).to_broadcast([128, 4, 128])
                    v3 = lambda t: t.rearrange("p (a b) -> p a b", a=4)
                    V(I("tensor_tensor", v3(tA[:, 0:512]), v3(ptf[:, :]), m4(Mf), ALU.mult), [bpf, b_cst], [b_tA])
                    V(I("tensor_tensor", v3(tB[:, 0:512]), v3(ptb[:, :]), m4(Mb), ALU.mult), [bpb, b_cst], [b_tB])
                    V(I("tensor_add", tA[:, 0:512], tA[:, 0:512], tB[:, 0:512]), [b_tA, b_tB], [b_tA])
                    for k4 in range(4):
                        g = 2 * (gq4 * 4 + k4) + g2_
                        V(I("scalar_tensor_tensor", out=Tm[:, g, :], in0=ident, scalar=dcol[:, g:g + 1],
                            in1=tA[:, k4 * 128:(k4 + 1) * 128], op0=ALU.mult, op1=ALU.add),
                          [b_tA, b_par, b_cst], [b_T])
            ck("S2d")
            for d in range(2):
                rev = (d == 1)
                cmul_tab(X_r[:, d], X_i[:, d], tab("pr", 0, 8, not rev, d), tab("pi", 0, 8, not rev, d),
                         Bb["r"][:, d], Bb["i"][:, d], b_X)
            for d in range(2):
                for gq4 in range(4):
                    for g2_ in range(2):
                        ps_ = slice(g2_ * 64, g2_ * 64 + 64)
                        pt, bp = nps()
                        fns = []
                        for k4 in range(4):
                            gp = gq4 * 4 + k4
                            for ri, Pm in enumerate([X_r, X_i]):
                                c0 = (k4 * 2 + ri) * 64
                                fns.append(mm(pt[:, c0:c0 + 64], Pm[ps_, d, gp, :], identb[ps_, g2_ * 64:g2_ * 64 + 64]))
                        MM(fns, [b_X, b_cst], [bp])
                        g0 = 2 * gq4 * 4 + g2_
                        V(I("tensor_copy", PT[:, d, g0:g0 + 7:2, :, :].rearrange("p g r n -> p g (r n)"),
                            pt[:, :].rearrange("p (g f) -> p g f", g=4)), [bp], [b_PT])
            ck("S2e")
            V(I("tensor_scalar_mul", Thu[:], th[:], 8.0), [b_der], [b_der])
            V(I("tensor_copy", tSI[:, 0:32], Thu[:]), [b_der], [b_tSI])
            V(I("tensor_copy", tS[1][:, 0:32], tSI[:, 0:32]), [b_tSI], [b_tS[1]])
            V(I("tensor_sub", Thu[:], Thu[:], tS[1][:, 0:32]), [b_tS[1], b_der], [b_der])
            A(I("activation", R8[:], lrdt[:], AF.Exp, scale=8.0), [b_der], [b_der])
            V(I("tensor_scalar_mul", tS[2][:, 0:32], Thu[:], 256.0), [b_der], [b_tS[2]])
            sincos_turns(tS[2][:, 0:32], 32, fini[:], finr[:], [b_tS[2]], [b_der])
            A(I("activation", mag2048[:], lrdt[:], AF.Exp, scale=2048.0), [b_der], [b_der])
            V(I("tensor_mul", Acr[:], mag2048[:], finr[:]), [b_der], [b_der])
            V(I("tensor_mul", Aci[:], mag2048[:], fini[:]), [b_der], [b_der])
            gate([b_Bb, b_zt], [b_E])
            for (Ec, Es, pos) in [(E0c, E0s, cpos[:, 0:16]), (E1c, E1s, c16[:])]:
                V(I("tensor_tensor", tS[2][:, 0:512].rearrange("p (g c) -> p g c", g=32),
                    Thu[:].unsqueeze(2).to_broadcast([128, 32, 16]), pos.unsqueeze(1).to_broadcast([128, 32, 16]), ALU.mult),
                  [b_der, b_cst], [b_tS[2]])
                sincos_turns(tS[2][:, 0:512], 512, Es.rearrange("p g c -> p (g c)"), Ec.rearrange("p g c -> p (g c)"), [b_tS[2]], [b_E])
            if l == 0:
                tap("Tm", Tm.rearrange("p g f -> p (g f)"), [b_T], 2048)
                tap("Qr", Qr.rearrange("p d g f -> p (d g f)"), [b_Q], 2048)
                tap("PT", PT.rearrange("p d g r n -> p (d g r n)"), [b_PT], 2048)

            ck("S2")
            for (so, kcols, vcols, kdst, vblk) in [(4, slice(128, 256), slice(384, 512), slice(0, 128), 0),
                                                   (8, slice(0, 128), slice(256, 384), slice(T + 128, T + 256), 17)]:
                for (cols, dst_ap, wb) in [(kcols, kT[:, kdst], b_k), (vcols, vtm[:, vblk, :], b_v)]:
                    acc = tS[2][:, 0:128]
                    V(I("tensor_scalar_mul", acc, halg[:, 0, cols], sel[:, so:so + 1]),
                      [b_halg, b_sel], [b_tS[2]])
                    for r in range(1, 4):
                        V(I("scalar_tensor_tensor", out=acc, in0=halg[:, r, cols],
                                                                                  scalar=sel[:, so + r:so + r + 1], in1=acc,
                                                                                  op0=ALU.mult, op1=ALU.add),
                          [b_halg, b_sel], [b_tS[2]])
                    V(I("tensor_copy", dst_ap, acc), [b_tS[2]], [wb])

            ck("halosel")
            for i in range(8):
                for g8 in range(8):
                    P.dma("sync", I("dma_start",
                        out=U[i * 16:(i + 1) * 16, :, :].rearrange("p (a g) c -> p a g c", a=4)[:, :, g8, :],
                        in_=uT2[g8 * 16:(g8 + 1) * 16, :, i, :]), reads=[b_uT2], writes=[b_U])
            if l == 0:
                tap("U", U.rearrange("p g c -> p (g c)"), [b_U], 2048)
            ck("relayout")
            gate(a1_s2, a1_p1)
            gate(wa_s2, [b_WA, b_WB])
            P.dma("gpsimd", I("dma_start", out=wglu, in_=w_glu_d[l].rearrange("(k p) f -> p k f", p=128)), writes=[b_WA])
            P.dma("gpsimd", I("dma_start", out=wout, in_=w_out_d[l].rearrange("(k p) f -> p k f", p=128)), writes=[b_WA, b_WB])

            def ssm_batch(bt, phase):
                for d in range(2):
                    c0 = d * 16 + bt * 2
                    gsl = slice(c0, c0 + 2)
                    cosT_, sinT_, btab_ = tabsets[ssm_it[0] % 2]
                    ssm_it[0] += 1
                    if btab_ is None:
                        rd_t = [b_rstd, b_sig]; wr_c = [b_rstd]; wr_s = [b_sig]
                    else:
                        rd_t = [btab_]; wr_c = [btab_]; wr_s = [btab_]
                    e1c = E1c[:, gsl, :].unsqueeze(3).to_broadcast([128, 2, 16, 16])
                    e1s = E1s[:, gsl, :].unsqueeze(3).to_broadcast([128, 2, 16, 16])
                    e0c = E0c[:, gsl, :].unsqueeze(2).to_broadcast([128, 2, 16, 16])
                    e0s = E0s[:, gsl, :].unsqueeze(2).to_broadcast([128, 2, 16, 16])
                    q4 = lambda t: t.rearrange("p (g a b) -> p g a b", g=2, a=16)
                    q3 = lambda t: t.rearrange("p g (a b) -> p g a b", a=16)
                    pa = tS[2][:, 0:512]; pb = tSI[:, 0:512].bitcast(F32)
                    G = lambda fn, r, w: P.op("gpsimd", fn, r, w)
                    G(I("tensor_tensor", q4(pa), e1c, e0c, ALU.mult), [b_E], [b_tS[2]])
                    G(I("tensor_tensor", q4(pb), e1s, e0s, ALU.mult), [b_E], [b_tSI])
                    G(I("tensor_sub", q3(cosT_), q4(pa), q4(pb)), [b_tS[2], b_tSI], wr_c)
                    G(I("tensor_tensor", q4(pa), e1s, e0c, ALU.mult), [b_E], [b_tS[2]])
                    G(I("tensor_tensor", q4(pb), e1c, e0s, ALU.mult), [b_E], [b_tSI])
                    G(I("tensor_add", q3(sinT_), q4(pa), q4(pb)), [b_tS[2], b_tSI], wr_s)
                    zb = []
                    for ri in range(2):
                        pt, bp = nps()
                        fns = []
                        for gq in range(2):
                            gp = bt * 2 + gq
                            for g2_ in range(2):
                                g = gp * 2 + g2_
                                fns.append(mm(pt[g2_ * 64:(g2_ + 1) * 64, gq * 256:(gq + 1) * 256], PT[:, d, g, ri, :], U[:, g, :]))
                        MM(fns, [b_PT, b_U], [bp])
                        zb.append((pt, bp))
                    (pzr, bzr), (pzi, bzi) = zb
                    zr_ = pzr[:, :].rearrange("p (g c) -> p g c", g=2)
                    zi_ = pzi[:, :].rearrange("p (g c) -> p g c", g=2)
                    if d == 1:
                        zr_ = zr_[:, :, ::-1]; zi_ = zi_[:, :, ::-1]
                    a3 = tS[0][:, 0:512].rearrange("p (g c) -> p g c", g=2)
                    b3 = tS[1][:, 0:512].rearrange("p (g c) -> p g c", g=2)
                    V(I("tensor_tensor", a3, zr_, cosT_, ALU.mult), [bzr, *rd_t], [b_tS[0]])
                    V(I("tensor_tensor", b3, zi_, sinT_, ALU.mult), [bzi, *rd_t], [b_tS[1]])
                    V(I("tensor_add", Wr, a3, b3), [b_tS[0], b_tS[1]], [b_W])
                    V(I("tensor_tensor", a3, zi_, cosT_, ALU.mult), [bzi, *rd_t], [b_tS[0]])
                    V(I("tensor_tensor", b3, zr_, sinT_, ALU.mult), [bzr, *rd_t], [b_tS[1]])
                    V(I("tensor_sub", Wi, a3, b3), [b_tS[0], b_tS[1]], [b_W])
                    for gq in range(2):
                        col = c0 + gq
                        for (Wt, St, ci) in [(Wr, Sr, 0), (Wi, Si, 1)]:
                            cc = ci * 32 + col
                            init = 0.0 if phase == 0 else carry[:, cc:cc + 1]
                            V(I("tensor_tensor_scan",
                                St[:, gq, :], R8[:, col:col + 1].to_broadcast([128, 256]), Wt[:, gq, :], init, ALU.mult, ALU.add),
                              [b_W, b_der, b_carry], [b_S])
                    if phase == 0:
                        fr = finr[:, gsl]; fi = fini[:, gsl]
                        s_r = Sr[:, :, 255]; s_i = Si[:, :, 255]
                        o_r = Fst[:, c0:c0 + 2]; o_i = Fst[:, 32 + c0:32 + c0 + 2]
                        t0 = tS[0][:, 0:2]; t1 = tS[1][:, 0:2]
                        V(I("tensor_mul", t0, s_r, fr), [b_S, b_der], [b_tS[0]])
                        V(I("tensor_mul", t1, s_i, fi), [b_S, b_der], [b_tS[1]])
                        V(I("tensor_sub", o_r, t0, t1), [b_tS[0], b_tS[1]], [b_F])
                        V(I("tensor_mul", t0, s_r, fi), [b_S, b_der], [b_tS[0]])
                        V(I("tensor_mul", t1, s_i, fr), [b_S, b_der], [b_tS[1]])
                        V(I("tensor_add", o_i, t0, t1), [b_tS[0], b_tS[1]], [b_F])
                    else:
                        Hr_ = Hb16[(d, "r")]; Hi_ = Hb16[(d, "i")]
                        if d == 0:
                            o_r = Hr_[:, :, 1:256]; o_i = Hi_[:, :, 1:256]
                            o_r0 = Hr_[:, :, 0]; o_i0 = Hi_[:, :, 0]
                        else:
                            o_r = Hr_[:, :, 254::-1]; o_i = Hi_[:, :, 254::-1]
                            o_r0 = Hr_[:, :, 255]; o_i0 = Hi_[:, :, 255]
                        a3s = a3[:, :, 0:255]; b3s = b3[:, :, 0:255]
                        V(I("tensor_tensor", a3s, Sr[:, :, 0:255], cosT_[:, :, 0:255], ALU.mult), [b_S, *rd_t], [b_tS[0]])
                        V(I("tensor_tensor", b3s, Si[:, :, 0:255], sinT_[:, :, 0:255], ALU.mult), [b_S, *rd_t], [b_tS[1]])
                        V(I("tensor_sub", o_r, a3s, b3s), [b_tS[0], b_tS[1]], [b_H])
                        V(I("tensor_tensor", a3s, Sr[:, :, 0:255], sinT_[:, :, 0:255], ALU.mult), [b_S, *rd_t], [b_tS[0]])
                        V(I("tensor_tensor", b3s, Si[:, :, 0:255], cosT_[:, :, 0:255], ALU.mult), [b_S, *rd_t], [b_tS[1]])
                        V(I("tensor_add", o_i, a3s, b3s), [b_tS[0], b_tS[1]], [b_H])
                        V(I("tensor_copy", o_r0, carry[:, c0:c0 + 2]), [b_carry], [b_H])
                        V(I("tensor_copy", o_i0, carry[:, 32 + c0:32 + c0 + 2]), [b_carry], [b_H])

            for bt in range(8):
                ssm_batch(bt, 0)
            ck("ssm0")
            P.dma("gpsimd", I("dma_start", out=st_src, in_=Fst[:]), reads=[b_F], writes=[b_stg])
            P.op("gpsimd", I("collective_compute", "AllGather", ALU.bypass, replica_groups=GROUPS,
                                                         ins=[st_src.opt()], outs=[st_dst.opt()]), [b_stg], [b_stg])
            P.dma("gpsimd", I("dma_start", out=stg[:], in_=st_dst.rearrange("(r p) f -> p r f", p=128)),
                  reads=[b_stg], writes=[b_stg])

            ck("stx")
            V(I("tensor_copy", esrow[:, :, :], esink[0:1, l, :].unsqueeze(2).to_broadcast([1, 8, 128])), [b_small], [b_esr])
            gate([b_uT2], [b_att])
            for qb in range(16):
                qs = slice(qb * 128, (qb + 1) * 128)
                for kvh in range(2):
                    hs_ = slice(kvh * 64, kvh * 64 + 64)
                    set_ = (qb * 2 + kvh) % 2
                    pM = pM2[set_]; b_pM = b_pM2[set_]
                    for kb in range(3):
                        pt, bp = nps()
                        MM([mm(pt[:, :].rearrange("p (j q) -> p j q", j=4), kT[hs_, (qb + kb) * 128:(qb + kb + 1) * 128], qT[hs_, :, qs])],
                           [b_k, b_q], [bp])
                        A(I("activation", pM[kb], pt[:, :], AF.Exp, scale=0.125), [bp], [b_pM[kb]])
                        V(I("tensor_tensor", pM[kb], pM[kb],
                            etab[:, kb, kvh * 4:(kvh + 1) * 4, :].rearrange("p j q -> p (j q)"), ALU.mult),
                          [b_et], [b_pM[kb]])
                    pn, bpn = nps(); pd, bpd = nps()
                    fn_n = []; fn_d = []
                    for j in range(4):
                        po = slice((j % 2) * 64, (j % 2) * 64 + 64)
                        cs2 = slice((j // 2) * 128, (j // 2) * 128 + 128)
                        for kb in range(3):
                            if kb == 0 and qb == 0:
                                ol = ones_pn[:, 0, :]
                            elif kb == 2 and qb == 15:
                                ol = ones_pn[:, 1, :]
                            else:
                                ol = onesb[:, 0:64]
                            fn_n.append(mm(pn[po, cs2], vtm[:, qb + kb, hs_], pM[kb][:, j * 128:(j + 1) * 128], kb == 0, kb == 2))
                            if kb == 0:
                                fn_d.append(mm(pd[po, cs2], onesb[0:1, 0:64], esrow[0:1, kvh * 4 + j, :], True, False))
                            fn_d.append(mm(pd[po, cs2], ol, pM[kb][:, j * 128:(j + 1) * 128], False, kb == 2))
                    MM(fn_n, [b_v] + b_pM, [bpn])
                    MM(fn_d, [b_cst, b_esr] + b_pM, [bpd])
                    A(I("activation", rden, pd[:, 0:256].rearrange("p (a q) -> p a q", a=2), AF.Ln), [bpd], [b_rden])
                    A(I("activation", rden, rden, AF.Exp, scale=-1.0), [b_rden], [b_rden])
                    V(I("tensor_tensor", attT[:, kvh * 2:kvh * 2 + 2, qs],
                                                                       pn[:, 0:256].rearrange("p (a q) -> p a q", a=2), rden, ALU.mult),
                      [bpn, b_rden, b_uT2], [b_att])
            if l == 0:
                tap("attT", attT.rearrange("p j t -> p (j t)"), [b_att], 2048)

            ck("att")
            V(I("memset", carry[:], 0.0), [], [b_carry])
            for d in range(2):
                cs_ = slice(d * 16, d * 16 + 16); ci_ = slice(32 + d * 16, 32 + d * 16 + 16)
                cr = hcr[:, 0:16]; cim_ = hcr[:, 16:32]; t0 = hcr[:, 32:48]; t1 = hcr[:, 48:64]
                V(I("memset", hcr[:], 0.0), [], [b_hcr])
                order = [0, 1, 2, 3] if d == 0 else [3, 2, 1, 0]
                hh = [b_hcr]
                for r in order:
                    V(I("scalar_tensor_tensor", out=carry[:, cs_], in0=cr, scalar=sel[:, r:r + 1], in1=carry[:, cs_],
                                                                     op0=ALU.mult, op1=ALU.add), [b_hcr, b_sel], [b_carry])
                    V(I("scalar_tensor_tensor", out=carry[:, ci_], in0=cim_, scalar=sel[:, r:r + 1], in1=carry[:, ci_],
                                                                     op0=ALU.mult, op1=ALU.add), [b_hcr, b_sel], [b_carry])
                    ar_ = Acr[:, cs_]; ai_ = Aci[:, cs_]
                    V(I("tensor_mul", t0, cr, ar_), [b_der], hh)
                    V(I("tensor_mul", t1, cim_, ai_), [b_der], hh)
                    V(I("tensor_sub", t0, t0, t1), [], hh)
                    V(I("tensor_mul", t1, cr, ai_), [b_der], hh)
                    V(I("tensor_mul", cim_, cim_, ar_), [b_der], hh)
                    V(I("tensor_add", cim_, cim_, t1), [], hh)
                    V(I("tensor_add", cr, t0, stg[:, r, cs_]), [b_stg], hh)
                    V(I("tensor_add", cim_, cim_, stg[:, r, ci_]), [b_stg], hh)

            gate([b_q], [b_yT2])
            for bt in range(8):
                ssm_batch(bt, 1)
                for gq in range(2):
                    gp = bt * 2 + gq
                    pt, bp = nps()
                    fns = []
                    for g2_ in range(2):
                        g = gp * 2 + g2_
                        ps_ = slice(g2_ * 64, g2_ * 64 + 64)
                        oc = pt[:, g2_ * 256:g2_ * 256 + 256]
                        fns.append(mm(oc, Tm[:, g, :], U[:, g, :], True, False))
                        for d in range(2):
                            fns.append(mm(oc, Qr[ps_, d, gp, :], Hb16[(d, "r")][ps_, gq, :], False, False))
                            fns.append(mm(oc, Qn[ps_, d, gp, :], Hb16[(d, "i")][ps_, gq, :], False, d == 1))
                    MM(fns, [b_T, b_U, b_Q, b_H], [bp])
                    A(I("activation", Yall[:, gp * 2:gp * 2 + 2, :].rearrange("p g c -> p (g c)"), pt[:, :],
                                                           AF.Gelu_apprx_tanh), [bp, b_Yall], [b_Yb[bt]])
                if bt % 4 == 3:
                    a0 = (bt // 4) * 2
                    for j in range(8):
                        for g8 in range(8):
                            P.dma("sync", I("dma_start", out=yT2[g8 * 16:(g8 + 1) * 16, a0:a0 + 2, j, :],
                                            in_=Yall[j * 16:(j + 1) * 16, :, :].rearrange("p (a g) c -> p a g c", a=4)[:, a0:a0 + 2, g8, :]),
                                  reads=b_Yb[bt - 3:bt + 1], writes=[b_yT2])
            if l == 0:
                tap("Yall", Yall.rearrange("p g c -> p (g c)"), b_Yb, 2048)
            ck("ssm1")
            gate(xr_mixer, allx)
            for k in range(8):
                P.dma("gpsimd", I("dma_start", out=xT[:, k, :], in_=xsp[k * 128:(k + 1) * 128, :]), writes=b_x[k])
            gate([b_tab, b_W, b_S, b_H], [b_glu])
            yflat = yT2.rearrange("p a j c -> p a (j c)")
            for cb in range(4):
                cs_ = slice(cb * 512, (cb + 1) * 512)
                for oc in range(4):
                    pv, bpv = nps(); pg, bpg = nps()
                    MM([mm(pv[:, :], wglu[:, k, oc * 128:(oc + 1) * 128], yflat[:, k, cs_], k == 0, k == 3) for k in range(4)],
                       [b_WA, b_yT2], [bpv])
                    MM([mm(pg[:, :], wglu[:, k, 512 + oc * 128:512 + (oc + 1) * 128], yflat[:, k, cs_], k == 0, k == 3) for k in range(4)],
                       [b_WA, b_yT2], [bpg])
                    A(I("activation", sig[:], pg[:, :], AF.Sigmoid), [bpg], [b_sig])
                    dst = gluT[:, oc, :].rearrange("p (c j) -> p j c", j=8)[:, 2 * cb:2 * cb + 2, :]
                    V(I("tensor_tensor", dst, pv[:, :].rearrange("p (j c) -> p j c", j=2),
                                                                sig[:, :].rearrange("p (j c) -> p j c", j=2), ALU.mult),
                      [bpv, b_sig], [b_glu])
            if l == 0:
                tap("gluT", gluT.rearrange("p j t -> p (j t)"), [b_glu], 2048)
            ck("glu")
            for tt in range(NTT):
                ts = slice(tt * 512, (tt + 1) * 512)
                for m in range(8):
                    pt, bp = nps()
                    fns = [mm(pt[:, :], wout[:, k, m * 128:(m + 1) * 128], attT[:, k, ts], k == 0, False) for k in range(4)]
                    fns += [mm(pt[:, :], wout[:, 4 + k, m * 128:(m + 1) * 128], gluT[:, k, ts], False, k == 3) for k in range(4)]
                    MM(fns, [b_WA, b_WB, b_att, b_glu], [bp])
                    V(I("tensor_add", xT[:, m, ts], xT[:, m, ts], pt[:, :]), [bp], [b_x[m][tt]])
            if l == 0:
                tap("xmid", xT[:, 0, :], b_x[0], 2048)

            ck("wout")
            gate(a1_p1, b_h)
            gate([b_att, b_yT2, b_uT2, b_q], [])
            gate(misc_att, misc_ffn)
            gate([b_E], [b_WB])
            for tt in range(NTT):
                rms_tile(l, tt, g2)
            b_ws = [b_WA, b_WB]
            for gI in range(8):
                s = gI % 2
                w1, w2 = wsl[s]
                P.dma("gpsimd", I("dma_start",
                    out=w1, in_=w_ff1_d[l][:, gI * 512:(gI + 1) * 512].rearrange("(k p) f -> p k f", p=128)), writes=[b_ws[s]])
                P.dma("gpsimd", I("dma_start",
                    out=w2, in_=w_ff2_d[l][gI * 512:(gI + 1) * 512, :].rearrange("(k p) f -> p k f", p=128)), writes=[b_ws[s]])
                for tt in range(NTT):
                    ts = slice(tt * 512, (tt + 1) * 512)
                    for fc in range(4):
                        pt, bp = nps()
                        MM([mm(pt[:, :], w1[:, k, fc * 128:(fc + 1) * 128], hT[:, k, ts], k == 0, k == 7) for k in range(8)],
                           [b_ws[s], b_h[tt]], [bp])
                        ri_ = fc % 2
                        A(I("activation", rl[ri_], pt[:, :], AF.Relu), [bp], [b_rl[ri_]])
                        V(I("tensor_mul", aT[:, fc, :], rl[ri_], rl[ri_]), [b_rl[ri_]], [b_a[fc]])
                    for m in range(8):
                        pt, bp = nps()
                        MM([mm(pt[:, :], w2[:, k, m * 128:(m + 1) * 128], aT[:, k, :], k == 0, k == 3) for k in range(4)],
                           [b_ws[s]] + b_a, [bp])
                        V(I("tensor_add", xT[:, m, ts], xT[:, m, ts], pt[:, :]), [bp], [b_x[m][tt]])
            gate(misc_ffn, misc_att)

        try:
            for l in range(nlayers):
                layer(l)
        except _Stop:
            gate(xr_mixer, allx)
        emit_taps()
        for k in range(8):
            P.dma("sync", I("dma_start", out=out_d[k * 128:(k + 1) * 128, :], in_=xT[:, k, :]), reads=b_x[k])
        P.wait_all("sync", allx + b_tS)
        P.finish("sync")
        P.run()
    return nc


def _prep_inputs(x, norm1, w_in, q_gain, k_gain, sink, lam_re, lam_im, log_dt, b_re, b_im, c_re, c_im,
                 d_skip, w_glu, w_out, norm2, w_ff1, w_ff2, nlayers=DEPTH):
    f = lambda a: np.ascontiguousarray(np.asarray(a, dtype=np.float32))
    L = DEPTH
    perm = []
    for j in range(4):
        perm += list(range(j * 64, j * 64 + 64)) + list(range((4 + j) * 64, (4 + j) * 64 + 64))
    perm += list(range(512, 1280))
    shared = {
        "w_in": f(np.asarray(w_in)[:nlayers][:, :, perm]), "w_glu": f(np.asarray(w_glu)[:nlayers]),
        "w_out": f(np.asarray(w_out)[:nlayers]),
        "w_ff1": f(np.asarray(w_ff1)[:nlayers]), "w_ff2": f(np.asarray(w_ff2)[:nlayers]),
        "g1": f(np.asarray(norm1).reshape(L, 8, 128).transpose(2, 0, 1)),
        "g2": f(np.asarray(norm2).reshape(L, 8, 128).transpose(2, 0, 1)),
        "qg": f(np.tile(np.asarray(q_gain).T, (2, 1))),
        "kg": f(np.tile(np.asarray(k_gain).T, (2, 1))),
        "sinkr": f(np.broadcast_to(np.asarray(sink)[None], (128, L, 8))),
    }

    def gn(a):
        a = np.asarray(a).reshape(L, 2, 16, 2, 64)
        return f(a.transpose(3, 4, 0, 1, 2).reshape(128, L, 32))
    shared["lre"] = gn(lam_re)
    shared["lim"] = gn(lam_im)
    shared["ldt"] = gn(np.broadcast_to(np.asarray(log_dt)[:, :, :, None], (L, 2, 32, 64)))

    def bb(a):
        a = np.asarray(a).reshape(L, 16, 2, 64, 16)
        return f(a.transpose(2, 3, 0, 1, 4).reshape(128, L, 256))
    shared["bre"] = bb(b_re)
    shared["bim"] = bb(b_im)

    def cc(a):
        a = np.asarray(a).reshape(L, 2, 16, 2, 16, 64)
        return f(a.transpose(3, 5, 0, 1, 2, 4).reshape(128, L, 512))
    shared["cre"] = cc(c_re)
    shared["cim"] = cc(c_im)
    dsk = np.asarray(d_skip).reshape(L, 32, 16)
    shared["dcol"] = f(np.broadcast_to(dsk.transpose(2, 0, 1)[None], (8, 16, L, 32)).reshape(128, L, 32))
    slopes = np.exp2(-8.0 * np.arange(1, 9) / 8.0)
    ci = np.arange(128)[:, None]; qi = np.arange(128)[None, :]
    et = np.zeros((128, 3, 8, 128), np.float32)
    for kb in range(3):
        dist = np.abs(qi - ci - (kb - 1) * 128)
        valid = dist <= 128
        for h in range(8):
            et[:, kb, h, :] = np.where(valid, np.exp(-slopes[h] * dist), 0.0)
    shared["etab"] = f(et.reshape(128, 3072))
    cst = np.zeros((128, 656), np.float32)
    cst[:, 0:128] = np.eye(128)
    bi = np.arange(128)[:, None] // 16; bj = np.arange(128)[None, :] // 16
    cst[:, 128:256] = (bj >= bi)
    cst[:, 256:384] = (bi >= bj)
    cst[:, 384:393] = np.arange(9)[None, :]
    cst[:, 400:656] = np.arange(1, 257)[None, :]
    shared["cst"] = cst
    xs = np.asarray(x)
    in_maps = []
    for r in range(8):
        b, q = r // 4, r % 4
        m = dict(shared)
        m["xT"] = f(xs[b, q * T:(q + 1) * T, :].T)
        s = np.zeros((128, 16), np.float32)
        s[:, q] = 1.0
        if q > 0:
            s[:, 4 + q - 1] = 1.0; s[:, 12] = 1.0
        if q < 3:
            s[:, 8 + q + 1] = 1.0; s[:, 13] = 1.0
        m["sel"] = s
        in_maps.append(m)
    return in_maps


_NC_CACHE = {}


def kernel(**inputs):
    in_maps = _prep_inputs(**inputs)
    if "nc" not in _NC_CACHE:
        _NC_CACHE["nc"] = build_nc()
    nc = _NC_CACHE["nc"]
    res = run_bass_kernel_spmd(nc, in_maps, core_ids=list(range(8)))
    out = np.zeros((2, 4 * T, 1024), np.float32)
    for r in range(8):
        b, q = r // 4, r % 4
        out[b, q * T:(q + 1) * T, :] = np.asarray(res.results[r]["outT"]).T
    return out
```

```python
import math
import os
import numpy as np
from contextlib import ExitStack
import concourse.bass as bass
import concourse.mybir as mybir
from concourse.bass_utils import run_bass_kernel_spmd

F32 = mybir.dt.float32
BF16 = mybir.dt.bfloat16
I32 = mybir.dt.int32
AF = mybir.ActivationFunctionType
ALU = mybir.AluOpType

DEPTH = 4
T = 2048
NTT = 4
EPS = 1e-6
TWO_PI = 2.0 * math.pi


class Buf:
    def __init__(self, name=""):
        self.name = name
        self.w = None
        self.r = []


class EngQ:
    def __init__(self, name, sem):
        self.name = name
        self.sem = sem
        self.count = 0
        self.ops = []
        self.seen = {}


class Prog:
    ENGS = ["sync", "scalar", "gpsimd", "vector", "tensor"]

    def __init__(self, nc, es, ndma=16):
        self.nc = nc
        self.q = {e: EngQ(e, es.enter_context(nc.semaphore("s_" + e))) for e in self.ENGS}
        self.dq = {e: [EngQ(f"d_{e}{k}", es.enter_context(nc.semaphore(f"d_{e}{k}"))) for k in range(ndma)]
                   for e in ["sync", "gpsimd"]}
        self.rr = {e: 0 for e in self.dq}
        self.nops = 0

    def _waits(self, eng, reads, writes):
        q = self.q[eng]
        need = {}

        def add(tok):
            if tok is None:
                return
            s, v = tok
            if eng == "tensor" and s is q:
                return
            if need.get(s, 0) < v:
                need[s] = v
        for b in reads:
            add(b.w)
        for b in writes:
            add(b.w)
            for t in b.r:
                add(t)
        for s, v in need.items():
            if q.seen.get(s, 0) >= v:
                continue
            q.seen[s] = v
            q.ops.append(lambda e, s=s, v=v: e.wait_ge(s.sem, v))

    def _mark(self, tok, reads, writes):
        for b in reads:
            b.r = [t for t in b.r if t[0] is not tok[0]] + [tok]
        for b in writes:
            b.w = tok
            b.r = []

    def op(self, eng, fn, reads=(), writes=()):
        return self.group(eng, [fn], reads, writes)

    def group(self, eng, fns, reads=(), writes=()):
        q = self.q[eng]
        self._waits(eng, reads, writes)
        for fn in fns[:-1]:
            q.ops.append(lambda e, fn=fn: fn(e))
        q.count += 1
        tok = (q, q.count)
        q.ops.append(lambda e, fn=fns[-1], q=q: fn(e).then_inc(q.sem, 1))
        self._mark(tok, reads, writes)
        self.nops += len(fns)
        return tok

    def dma(self, eng, fn, reads=(), writes=()):
        self._waits(eng, reads, writes)
        k = self.rr[eng]
        self.rr[eng] = (k + 1) % len(self.dq[eng])
        d = self.dq[eng][k]
        q_ = self.q[eng]
        if d.count > 0 and q_.seen.get(d, 0) < d.count:
            q_.seen[d] = d.count
            q_.ops.append(lambda e, d=d, v=d.count: e.wait_ge(d.sem, v))
        d.count += 16
        tok = (d, d.count)
        self.q[eng].ops.append(lambda e, fn=fn, d=d: fn(e).then_inc(d.sem, 16))
        self._mark(tok, reads, writes)
        self.nops += 1
        return tok

    def wait_all(self, eng, bufs):
        self._waits(eng, [], bufs)

    def finish(self, eng="sync"):
        q = self.q[eng]
        allq = [x for x in self.q.values() if x is not q] + [d for ds in self.dq.values() for d in ds]
        for s_ in allq:
            if s_.count > 0:
                q.ops.append(lambda e, s_=s_, v=s_.count: e.wait_ge(s_.sem, v))

    def run(self):
        with self.nc.Block() as block:
            for e in self.ENGS:
                ops = self.q[e].ops

                def body(eng, ops=ops):
                    for o in ops:
                        o(eng)
                getattr(block, e)(body)


class _Stop(Exception):
    pass


def build_nc(nlayers=DEPTH, taps=None, stop_after=None):
    nc = bass.Bass("TRN2", target_bir_lowering=False)
    L = DEPTH

    def din(name, shape, dt=F32):
        return nc.dram_tensor(name, list(shape), dt, kind="ExternalInput").ap()

    xT_d = din("xT", [1024, T])
    LW = nlayers
    w_in_d = din("w_in", [LW, 1024, 1280])
    w_glu_d = din("w_glu", [LW, 512, 1024])
    w_out_d = din("w_out", [LW, 1024, 1024])
    w_ff1_d = din("w_ff1", [LW, 1024, 4096])
    w_ff2_d = din("w_ff2", [LW, 4096, 1024])
    g1_d = din("g1", [128, L, 8])
    g2_d = din("g2", [128, L, 8])
    qg_d = din("qg", [128, L])
    kg_d = din("kg", [128, L])
    sink_d = din("sinkr", [128, L, 8])
    lre_d = din("lre", [128, L, 32])
    lim_d = din("lim", [128, L, 32])
    ldt_d = din("ldt", [128, L, 32])
    bre_d = din("bre", [128, L, 256])
    bim_d = din("bim", [128, L, 256])
    cre_d = din("cre", [128, L, 512])
    cim_d = din("cim", [128, L, 512])
    dcol_d = din("dcol", [128, L, 32])
    etab_d = din("etab", [128, 3072])
    cst_d = din("cst", [128, 656])
    sel_d = din("sel", [128, 16])
    out_d = nc.dram_tensor("outT", [1024, T], F32, kind="ExternalOutput").ap()
    taps = list(taps) if taps else []
    dbg_d = nc.dram_tensor("dbg", [max(1, len(taps)), 128, 2048], F32, kind="ExternalOutput").ap() if taps else None

    xsp = nc.dram_tensor("xsp", [1024, T], F32).ap()
    hal_src = nc.dram_tensor("hal_src", [128, 512], F32).ap()
    hal_dst = nc.dram_tensor("hal_dst", [4 * 128, 512], F32).ap()
    st_src = nc.dram_tensor("st_src", [128, 64], F32).ap()
    st_dst = nc.dram_tensor("st_dst", [4 * 128, 64], F32).ap()
    GROUPS = [[0, 1, 2, 3], [4, 5, 6, 7]]

    with ExitStack() as es:
        P = Prog(nc, es)

        def sb(name, shape, dt=F32):
            return es.enter_context(nc.sbuf_tensor(name, list(shape), dt))

        XR = sb("XR", [128, 16384], F32)
        XRb = XR[:, :].bitcast(BF16)
        A1 = sb("A1", [128, 16384], BF16)
        A1f = A1[:, :].bitcast(F32)
        WA = sb("WA", [128, 16384], BF16)
        WAf = WA[:, :].bitcast(F32)
        A3 = sb("A3", [128, 8192], BF16)
        A4 = sb("A4", [128, 8192], BF16)
        MISC = sb("MISC", [128, 4096], BF16)

        xT = XR[:, :].rearrange("p (k t) -> p k t", k=8)
        b_x = [[Buf() for _ in range(NTT)] for _ in range(8)]
        allx = [b for row in b_x for b in row]
        U = XRb[:, 0:8192].rearrange("p (g c) -> p g c", g=32); b_U = Buf()
        PT = XRb[:, 8192:16384].rearrange("p (d g r n) -> p d g r n", d=2, g=32, r=2); b_PT = Buf()
        Qr = XRb[:, 16384:20480].rearrange("p (d g f) -> p d g f", d=2, g=16)
        Qn = XRb[:, 20480:24576].rearrange("p (d g f) -> p d g f", d=2, g=16); b_Q = Buf()
        Tm = XRb[:, 24576:28672].rearrange("p (g f) -> p g f", g=32); b_T = Buf()
        etab = XRb[:, 28672:31744].rearrange("p (k h q) -> p k h q", k=3, h=8); b_et = Buf()
        xr_mixer = [b_U, b_PT, b_Q, b_T, b_et]

        hT = A1[:, :].rearrange("p (k t) -> p k t", k=8); b_h = [Buf() for _ in range(NTT)]
        bre = A1f[:, 0:256].rearrange("p (g h) -> p g h", g=16)
        bim = A1f[:, 256:512].rearrange("p (g h) -> p g h", g=16)
        cre = A1f[:, 512:1024].rearrange("p (d g h) -> p d g h", d=2, g=16)
        cim = A1f[:, 1024:1536].rearrange("p (d g h) -> p d g h", d=2, g=16)
        b_par2 = Buf()
        halg = A1f[:, 1536:3584].rearrange("p (r f) -> p r f", r=4); b_halg = Buf()
        tA = A1f[:, 4096:6144]; tB = A1f[:, 6144:8192]; b_tA = Buf(); b_tB = Buf()
        Yall = A1[:, 0:8192].rearrange("p (g c) -> p g c", g=32); b_Yall = Buf()
        yT2 = A3[:, :].rearrange("p (a j c) -> p a j c", a=4, j=8); b_yT2 = Buf()
        cosT = A1f[:, 4096:4608].rearrange("p (g c) -> p g c", g=2)
        sinT = A1f[:, 4608:5120].rearrange("p (g c) -> p g c", g=2); b_tab = Buf()
        Wr = A1f[:, 5120:5632].rearrange("p (g c) -> p g c", g=2)
        Wi = A1f[:, 5632:6144].rearrange("p (g c) -> p g c", g=2); b_W = Buf()
        Sr = A1f[:, 6144:6656].rearrange("p (g c) -> p g c", g=2)
        Si = A1f[:, 6656:7168].rearrange("p (g c) -> p g c", g=2); b_S = Buf()
        Hb16 = {}
        for d in range(2):
            for ni, n in enumerate("ri"):
                o = 14336 + (d * 2 + ni) * 512
                Hb16[(d, n)] = A1[:, o:o + 512].rearrange("p (g c) -> p g c", g=2)
        b_H = Buf()
        a1_s2 = [b_par2, b_halg, b_tA, b_tB]
        b_Yb = [Buf() for _ in range(8)]

        win = WA[:, 0:10240].rearrange("p (k f) -> p k f", k=8); b_WA = Buf(); b_WB = Buf()
        X_r = WA[:, 0:4096].rearrange("p (d g f) -> p d g f", d=2, g=16)
        X_i = WA[:, 4096:8192].rearrange("p (d g f) -> p d g f", d=2, g=16); b_X = Buf()
        pw = {n: WAf[:, 4096 + i * 288:4096 + (i + 1) * 288].rearrange("p (k c) -> p k c", k=9)
              for i, n in enumerate(["pr", "pi", "mr", "mi"])}
        b_pw = Buf()
        cs_s = WAf[:, 5248:5536]; cs_c = WAf[:, 5536:5824]
        mg_p = WAf[:, 5824:6112]; mg_m = WAf[:, 6112:6400]
        Bb = {"r": WAf[:, 6400:6912].rearrange("p (d g h) -> p d g h", d=2, g=16),
              "i": WAf[:, 6912:7424].rearrange("p (d g h) -> p d g h", d=2, g=16)}
        b_Bb = Buf()
        zt = [WAf[:, 7424 + i * 32:7424 + (i + 1) * 32] for i in range(6)]; b_zt = Buf()
        wa_s2 = [b_X, b_pw, b_Bb, b_zt]
        E0c = WAf[:, 6144:6656].rearrange("p (g c) -> p g c", g=32)
        E0s = WAf[:, 6656:7168].rearrange("p (g c) -> p g c", g=32)
        E1c = WAf[:, 7168:7680].rearrange("p (g c) -> p g c", g=32)
        E1s = WAf[:, 7680:8192].rearrange("p (g c) -> p g c", g=32)
        b_E = Buf()
        wglu = WA[:, 0:4096].rearrange("p (k f) -> p k f", k=4)
        wout = WA[:, 4096:12288].rearrange("p (k f) -> p k f", k=8)
        wsl = [(WA[:, s * 8192:s * 8192 + 4096].rearrange("p (k f) -> p k f", k=8),
                WA[:, s * 8192 + 4096:s * 8192 + 8192].rearrange("p (k f) -> p k f", k=4)) for s in range(2)]

        qT = A3[:, :].rearrange("p (j t) -> p j t", j=4); b_q = Buf()
        gluT = A1[:, 8192:16384].rearrange("p (j t) -> p j t", j=4); b_glu = Buf()
        a1_p1 = b_Yb + [b_Yall, b_glu, b_tab, b_W, b_S, b_H]
        uT2 = A4[:, :].rearrange("p (a i c) -> p a i c", a=4, i=8); b_uT2 = Buf()
        attT = A4[:, :].rearrange("p (j t) -> p j t", j=4); b_att = Buf()

        pM2 = [[MISC[:, (s_ * 3 + i) * 512:(s_ * 3 + i + 1) * 512] for i in range(3)] for s_ in range(2)]
        b_pM2 = [[Buf() for _ in range(3)] for _ in range(2)]
        rden = MISC[:, 3072:3584].bitcast(F32).rearrange("p (a q) -> p a q", a=2); b_rden = Buf()
        rl = [MISC[:, i * 512:(i + 1) * 512] for i in range(2)]; b_rl = [Buf(), Buf()]
        aT = MISC[:, 1024:3072].rearrange("p (k t) -> p k t", k=4); b_a = [Buf() for _ in range(4)]
        misc_att = b_pM2[0] + b_pM2[1] + [b_rden]
        misc_ffn = b_rl + b_a

        kT = sb("kT", [128, T + 256], BF16); b_k = Buf()
        vtm = sb("vtm", [128, 18, 128], BF16); b_v = Buf()
        cst = sb("cst_sb", [128, 656]); b_cst = Buf()
        cstb = sb("cstb_sb", [128, 128], BF16)
        onesb = sb("onesb", [128, 128], BF16)
        blk2 = sb("blk2", [128, 128], BF16)
        ones_pn = sb("ones_pn", [128, 2, 64], BF16)
        sel = sb("sel_sb", [128, 16]); b_sel = Buf()
        g1 = sb("g1_sb", [128, L, 8]); g2 = sb("g2_sb", [128, L, 8])
        qg = sb("qg_sb", [128, L]); kg = sb("kg_sb", [128, L])
        sinkr = sb("sink_sb", [128, L * 8]); esink = sb("esink", [128, L, 8])
        b_small = Buf()
        epsb = sb("epsb", [128, 1])
        lre = sb("lre_sb", [128, 32]); lim = sb("lim_sb", [128, 32]); ldt = sb("ldt_sb", [128, 32])
        dcol = sb("dcol_sb", [128, 32]); b_par = Buf()
        lrdt = sb("lrdt", [128, 32]); th = sb("th", [128, 32]); Thu = sb("Thu", [128, 32])
        R8 = sb("R8", [128, 32]); Acr = sb("Acr", [128, 32]); Aci = sb("Aci", [128, 32])
        finr = sb("finr", [128, 32]); fini = sb("fini", [128, 32]); mag2048 = sb("mag2048", [128, 32])
        b_der = Buf()
        tS = [sb(f"tS{i}", [128, 512]) for i in range(3)]; b_tS = [Buf() for _ in range(3)]
        tSI = sb("tSI", [128, 512], I32); b_tSI = Buf()
        Fst = sb("Fst", [128, 64]); b_F = Buf()
        stg = sb("stg", [128, 4, 64]); b_stg = Buf()
        carry = sb("carry", [128, 64]); b_carry = Buf()
        hcr = sb("hcr", [128, 64]); b_hcr = Buf()
        halt = sb("halt", [128, 512]); b_halt = Buf()
        sq = sb("sq", [128, 2, 512], BF16); b_sq = [Buf(), Buf()]
        rstd = sb("rstd", [128, 512]); b_rstd = Buf()
        sig = sb("sig", [128, 512]); b_sig = Buf()
        gatec = sb("gatec", [128, 8]); b_gate = Buf()

        psb = [es.enter_context(nc.psum_tensor(f"ps{i}", [128, 512], F32)) for i in range(8)]
        b_ps = [Buf() for _ in range(8)]
        ps_rr = [0]

        def nps():
            i = ps_rr[0]
            ps_rr[0] = (i + 1) % 8
            return psb[i], b_ps[i]

        def V(fn, r=(), w=()):
            return P.op("vector", fn, r, w)

        def A(fn, r=(), w=()):
            return P.op("scalar", fn, r, w)

        def MM(fns, r=(), w=()):
            return P.group("tensor", fns, r, w)

        def I(method, *a, **k):
            return lambda e: getattr(e, method)(*a, **k)

        def mm(out, lhsT, rhs, start=True, stop=True):
            return I("matmul", out, lhsT=lhsT, rhs=rhs, start=start, stop=stop)

        def gate(old, new):
            V(I("memset", gatec[:, 0:1], 0.0), [], list(old) + list(new) + [b_gate])

        tap_i = [0]

        deferred_taps = []

        def tap(name, ap, bufs, n):
            if name in taps:
                deferred_taps.append((name, ap, bufs, n))

        def emit_taps():
            if not deferred_taps:
                return
            P.finish("vector")
            P.finish("sync")
            for (name, ap, bufs, n) in deferred_taps:
                idx = taps.index(name)
                for c0 in range(0, n, 512):
                    c1 = min(n, c0 + 512)
                    V(I("tensor_copy", tS[2][:, 0:c1 - c0], ap[:, c0:c1]), list(bufs), [b_tS[2]])
                    P.dma("sync", I("dma_start", out=dbg_d[idx, :, c0:c1], in_=tS[2][:, 0:c1 - c0]), reads=[b_tS[2]])

        P.dma("sync", I("dma_start", out=cst[:], in_=cst_d), writes=[b_cst])
        P.dma("sync", I("dma_start", out=sel[:], in_=sel_d), writes=[b_sel])
        for (t_sb, t_d) in [(g1, g1_d), (g2, g2_d), (qg, qg_d), (kg, kg_d)]:
            P.dma("sync", I("dma_start", out=t_sb[:], in_=t_d), writes=[b_small])
        P.dma("sync", I("dma_start", out=sinkr[:], in_=sink_d.rearrange("p l h -> p (l h)")), writes=[b_small])
        for k in range(8):
            P.dma("sync", I("dma_start", out=xT[:, k, :], in_=xT_d[k * 128:(k + 1) * 128, :]),
                  writes=b_x[k])
        ident = cst[:, 0:128]; Mf = cst[:, 128:256]; Mb = cst[:, 256:384]; kvec = cst[:, 384:393]
        cpos = cst[:, 400:656]
        V(I("tensor_copy", cstb[:], ident), [b_cst], [b_cst])
        V(I("memset", onesb[:], 1.0), [], [b_cst])
        V(I("memset", blk2[:], 0.0), [], [b_cst])
        V(I("memset", blk2[0:64, 0:64], 1.0), [], [b_cst])
        V(I("memset", blk2[64:128, 64:128], 1.0), [], [b_cst])
        V(I("memset", epsb[:], EPS), [], [b_cst])
        V(I("tensor_copy", ones_pn[:, 0, :], sel[:, 12:13].to_broadcast([128, 64])), [b_sel], [b_cst])
        V(I("tensor_copy", ones_pn[:, 1, :], sel[:, 13:14].to_broadcast([128, 64])), [b_sel], [b_cst])
        A(I("activation", esink[:, :, :].rearrange("p l h -> p (l h)"), sinkr[:], AF.Exp), [b_small], [b_small])
        identb = cstb
        c16 = sb("c16", [128, 16])
        V(I("tensor_scalar", c16[:], cpos[:, 0:16], -1.0, 16.0, ALU.add, ALU.mult), [b_cst], [b_cst])
        tabsets = [(cosT, sinT, b_tab), (rstd[:, :].rearrange("p (g c) -> p g c", g=2), sig[:, :].rearrange("p (g c) -> p g c", g=2), None)]
        ssm_it = [0]
        esrow = sb("esrow", [1, 8, 128], BF16); b_esr = Buf()

        def rms_tile(l, tt, gain):
            ts = slice(tt * 512, (tt + 1) * 512)
            pt, bp = nps()
            for k in range(8):
                s = k % 2
                A(I("activation", sq[:, s, :], xT[:, k, ts], AF.Square), [b_x[k][tt]], [b_sq[s]])
                P.op("tensor", mm(pt[:, :], onesb[:, :], sq[:, s, :], start=(k == 0), stop=(k == 7)), [b_sq[s], b_cst], [bp])
            A(I("activation", rstd[:], pt[:, :], AF.Ln, bias=epsb[:], scale=1.0 / 1024.0), [bp, b_cst], [b_rstd])
            A(I("activation", rstd[:], rstd[:], AF.Exp, scale=-0.5), [b_rstd], [b_rstd])
            for k in range(8):
                V(I("scalar_tensor_tensor", out=hT[:, k, ts], in0=xT[:, k, ts], scalar=gain[:, l, k:k + 1],
                                                        in1=rstd[:], op0=ALU.mult, op1=ALU.mult),
                  [b_x[k][tt], b_rstd, b_small], [b_h[tt]])

        def sincos_turns(tin, n, out_s, out_c, rb, wb):
            a_ = tS[0][:, 0:n]; b_ = tS[1][:, 0:n]; i_ = tSI[:, 0:n]
            for (off, outp) in [(0.0, out_s), (0.25, out_c)]:
                V(I("tensor_scalar_add", a_, tin, off), rb, [b_tS[0]])
                V(I("tensor_copy", i_, a_), [b_tS[0]], [b_tSI])
                V(I("tensor_copy", b_, i_), [b_tSI], [b_tS[1]])
                V(I("tensor_sub", a_, a_, b_), [b_tS[0], b_tS[1]], [b_tS[0]])
                V(I("tensor_scalar", a_, a_, -0.4999999, 0.4999999, ALU.max, ALU.min), [b_tS[0]], [b_tS[0]])
                A(I("activation", outp, a_, AF.Sin, scale=TWO_PI), [b_tS[0]], wb)

        def ck(name):
            if stop_after == name:
                raise _Stop()

        if os.environ.get("HALO_FIRST"):
            V(I("memset", halt[:], 1.0), [], [b_halt])
            P.dma("gpsimd", I("dma_start", out=hal_src, in_=halt[:]), reads=[b_halt], writes=[b_halg])
            P.op("gpsimd", I("collective_compute", "AllGather", ALU.bypass, replica_groups=GROUPS,
                             ins=[hal_src.opt()], outs=[hal_dst.opt()]), [b_halg], [b_halg])
            P.dma("gpsimd", I("dma_start", out=halg, in_=hal_dst.rearrange("(r p) f -> p r f", p=128)),
                  reads=[b_halg], writes=[b_halg])
            if os.environ.get("HALO_FIRST") == "only":
                raise_stop = True

        def layer(l):
            P.dma("gpsimd", I("dma_start", out=win, in_=w_in_d[l].rearrange("(k p) f -> p k f", p=128)),
                  writes=[b_WA, b_WB])
            for (t_sb, t_d) in [(lre, lre_d), (lim, lim_d), (ldt, ldt_d), (dcol, dcol_d)]:
                P.dma("sync", I("dma_start", out=t_sb[:], in_=t_d[:, l]), writes=[b_par])

            for tt in range(NTT):
                ts = slice(tt * 512, (tt + 1) * 512)
                rms_tile(l, tt, g1)
                for k in range(8 if not os.environ.get("NOSPILL") else 0):
                    P.dma("sync", I("dma_start", out=xsp[k * 128:(k + 1) * 128, ts], in_=xT[:, k, ts]),
                          reads=[b_x[k][tt]])
                for oc in range(5):
                    pt, bp = nps()
                    MM([mm(pt[:, :], win[:, k, oc * 128:(oc + 1) * 128], hT[:, k, ts], k == 0, k == 7) for k in range(8)],
                       [b_WA, b_h[tt]], [bp])
                    A(I("activation", sq[:, 0, :], pt[:, :], AF.Square), [bp], [b_sq[0]])
                    p2, bp2 = nps()
                    MM([mm(p2[:, :], blk2[:, :], sq[:, 0, :])], [b_sq[0], b_cst], [bp2])
                    A(I("activation", rstd[:], p2[:, :], AF.Ln, bias=epsb[:], scale=1.0 / 64.0),
                      [bp2, b_cst], [b_rstd])
                    A(I("activation", rstd[:], rstd[:], AF.Exp, scale=-0.5), [b_rstd], [b_rstd])
                    if oc < 4:
                        V(I("scalar_tensor_tensor", out=qT[:, oc, ts], in0=pt[:, :], scalar=qg[:, l:l + 1],
                                                                               in1=rstd[:], op0=ALU.mult, op1=ALU.mult),
                          [bp, b_rstd, b_small], [b_q])
                    else:
                        V(I("scalar_tensor_tensor", out=kT[:, 128 + tt * 512:128 + (tt + 1) * 512], in0=pt[:, :],
                                                                         scalar=kg[:, l:l + 1], in1=rstd[:], op0=ALU.mult, op1=ALU.mult),
                          [bp, b_rstd, b_small], [b_k])
                        if tt == 0:
                            V(I("scalar_tensor_tensor", out=halt[:, 0:128], in0=pt[:, 0:128], scalar=kg[:, l:l + 1],
                                                                      in1=rstd[:, 0:128], op0=ALU.mult, op1=ALU.mult),
                              [bp, b_rstd, b_small], [b_halt])
                        if tt == NTT - 1:
                            V(I("scalar_tensor_tensor", out=halt[:, 128:256], in0=pt[:, 384:512], scalar=kg[:, l:l + 1],
                                                                      in1=rstd[:, 384:512], op0=ALU.mult, op1=ALU.mult),
                              [bp, b_rstd, b_small], [b_halt])
                pt, bp = nps()
                fns = []
                for b4 in range(4):
                    for k in range(8):
                        fns.append(mm(pt[:, b4 * 128:(b4 + 1) * 128], hT[:, k, tt * 512 + b4 * 128: tt * 512 + (b4 + 1) * 128],
                                      win[:, k, 640:768], k == 0, k == 7))
                MM(fns, [b_WA, b_h[tt]], [bp])
                A(I("activation", vtm[:, 1 + tt * 4:1 + (tt + 1) * 4, :].rearrange("p b f -> p (b f)"),
                                                       pt[:, :], AF.Copy), [bp], [b_v])
                if tt == 0:
                    V(I("tensor_copy", halt[:, 256:384], pt[:, 0:128]), [bp], [b_halt])
                if tt == NTT - 1:
                    V(I("tensor_copy", halt[:, 384:512], pt[:, 384:512]), [bp], [b_halt])
                for a_ in range(4):
                    oc = 6 + a_
                    pt, bp = nps()
                    MM([mm(pt[:, :], win[:, k, oc * 128:(oc + 1) * 128], hT[:, k, ts], k == 0, k == 7) for k in range(8)],
                       [b_WA, b_h[tt]], [bp])
                    V(I("tensor_copy", uT2[:, a_, :, tt * 64:(tt + 1) * 64], pt[:, :].rearrange("p (c i) -> p i c", i=8)),
                      [bp], [b_uT2])
            if l == 0:
                tap("hT", hT.rearrange("p k t -> p (k t)"), b_h, 2048)
                tap("qT", qT.rearrange("p j t -> p (j t)"), [b_q], 2048)
                tap("kT", kT[:, 128:128 + 2048], [b_k], 2048)
                tap("uT2", uT2.rearrange("p a i c -> p (a i c)"), [b_uT2], 2048)

            ck("S1")
            gate(b_h, a1_s2)
            P.dma("gpsimd", I("dma_start", out=hal_src, in_=halt[:]), reads=[b_halt], writes=[b_halg])
            if not os.environ.get("NOCC"):
                P.op("gpsimd", I("collective_compute", "AllGather", ALU.bypass, replica_groups=GROUPS,
                                 ins=[hal_src.opt()], outs=[hal_dst.opt()]), [b_halg], [b_halg])
            P.dma("gpsimd", I("dma_start", out=halg, in_=hal_dst.rearrange("(r p) f -> p r f", p=128)),
                  reads=[b_halg], writes=[b_halg])

            ck("halo")
            gate([b_WA, b_WB, b_E], wa_s2)
            gate(allx, xr_mixer)
            P.dma("gpsimd", I("dma_start", out=etab.rearrange("p k h q -> p (k h q)"), in_=etab_d), writes=[b_et])
            for (t_ap, t_d) in [(bre, bre_d), (bim, bim_d), (cre, cre_d), (cim, cim_d)]:
                P.dma("sync", I("dma_start",
                    out=t_ap.rearrange("p g h -> p (g h)") if len(t_ap.shape) == 3 else t_ap.rearrange("p d g h -> p (d g h)"),
                    in_=t_d[:, l]), writes=[b_par2])
            A(I("activation", lrdt[:], ldt[:], AF.Exp), [b_par], [b_der])
            V(I("tensor_mul", th[:], lim[:], lrdt[:]), [b_par, b_der], [b_der])
            V(I("tensor_mul", lrdt[:], lre[:], lrdt[:]), [b_par, b_der], [b_der])
            V(I("tensor_scalar_mul", th[:], th[:], 1.0 / TWO_PI), [b_der], [b_der])
            kb9 = kvec.unsqueeze(2).to_broadcast([128, 9, 32])
            ph9 = tS[2][:, 0:288]
            V(I("tensor_tensor", ph9.rearrange("p (k c) -> p k c", k=9), th[:].unsqueeze(1).to_broadcast([128, 9, 32]),
                                        kb9, ALU.mult), [b_der, b_cst], [b_tS[2]])
            sincos_turns(ph9, 288, cs_s, cs_c, [b_tS[2]], [b_pw])
            V(I("tensor_tensor", ph9.rearrange("p (k c) -> p k c", k=9), lrdt[:].unsqueeze(1).to_broadcast([128, 9, 32]),
                                        kb9, ALU.mult), [b_der, b_cst, b_pw], [b_tS[2]])
            A(I("activation", mg_p, ph9, AF.Exp), [b_tS[2]], [b_pw])
            A(I("activation", mg_m, ph9, AF.Exp, scale=-1.0), [b_tS[2]], [b_pw])
            fl = lambda t: t.rearrange("p k c -> p (k c)")
            V(I("tensor_mul", fl(pw["pr"]), mg_p, cs_c), [b_pw], [b_pw])
            V(I("tensor_mul", fl(pw["pi"]), mg_p, cs_s), [b_pw], [b_pw])
            V(I("tensor_mul", fl(pw["mr"]), mg_m, cs_c), [b_pw], [b_pw])
            V(I("scalar_tensor_tensor", out=fl(pw["mi"]), in0=mg_m, scalar=-1.0, in1=cs_s,
                                               op0=ALU.mult, op1=ALU.mult), [b_pw], [b_pw])
            ck("S2a")
            ar1, ai1 = pw["pr"][:, 1, :], pw["pi"][:, 1, :]
            zin = [b_zt, b_par, b_pw]
            V(I("tensor_mul", zt[0], lre[:], lre[:]), zin, [b_zt])
            V(I("tensor_mul", zt[1], lim[:], lim[:]), zin, [b_zt])
            V(I("tensor_add", zt[0], zt[0], zt[1]), zin, [b_zt])
            V(I("reciprocal", zt[0], zt[0]), zin, [b_zt])
            V(I("tensor_scalar_add", zt[1], ar1, -1.0), zin, [b_zt])
            V(I("tensor_mul", zt[2], zt[1], lre[:]), zin, [b_zt])
            V(I("tensor_mul", zt[3], ai1, lim[:]), zin, [b_zt])
            V(I("tensor_add", zt[2], zt[2], zt[3]), zin, [b_zt])
            V(I("tensor_mul", zt[2], zt[2], zt[0]), zin, [b_zt])
            V(I("tensor_mul", zt[3], ai1, lre[:]), zin, [b_zt])
            V(I("tensor_mul", zt[4], zt[1], lim[:]), zin, [b_zt])
            V(I("tensor_sub", zt[3], zt[3], zt[4]), zin, [b_zt])
            V(I("tensor_mul", zt[3], zt[3], zt[0]), zin, [b_zt])
            for d in range(2):
                zr = zt[2][:, d * 16:(d + 1) * 16].unsqueeze(2).to_broadcast([128, 16, 16])
                zi = zt[3][:, d * 16:(d + 1) * 16].unsqueeze(2).to_broadcast([128, 16, 16])
                t1 = tA[:, 0:256].rearrange("p (a b) -> p a b", a=16)
                t2 = tB[:, 0:256].rearrange("p (a b) -> p a b", a=16)
                V(I("tensor_tensor", t1, bre, zr, ALU.mult), [b_par2, b_zt], [b_tA])
                V(I("tensor_tensor", t2, bim, zi, ALU.mult), [b_par2, b_zt], [b_tB])
                V(I("tensor_sub", Bb["r"][:, d], t1, t2), [b_tA, b_tB], [b_Bb])
                V(I("tensor_tensor", t1, bim, zr, ALU.mult), [b_par2, b_zt], [b_tA])
                V(I("tensor_tensor", t2, bre, zi, ALU.mult), [b_par2, b_zt], [b_tB])
                V(I("tensor_add", Bb["i"][:, d], t1, t2), [b_tA, b_tB], [b_Bb])

            ck("S2b")

            def cmul_tab(out_r, out_i, tr, ti, Xr_, Xi_, wb, neg_i=False):
                trb = tr.rearrange("p k g -> p g k").unsqueeze(3).to_broadcast([128, 16, 8, 16])
                tib = ti.rearrange("p k g -> p g k").unsqueeze(3).to_broadcast([128, 16, 8, 16])
                Xrb = Xr_.unsqueeze(2).to_broadcast([128, 16, 8, 16])
                Xib = Xi_.unsqueeze(2).to_broadcast([128, 16, 8, 16])
                v4 = lambda t: t.rearrange("p (g k h) -> p g k h", g=16, k=8)
                o4 = lambda t: t.rearrange("p g (k h) -> p g k h", k=8)
                rd = [b_pw, b_Bb, b_par2]
                V(I("tensor_tensor", v4(tA), trb, Xrb, ALU.mult), rd, [b_tA])
                V(I("tensor_tensor", v4(tB), tib, Xib, ALU.mult), rd, [b_tB])
                V(I("tensor_sub", o4(out_r), v4(tA), v4(tB)), [b_tA, b_tB], [wb])
                V(I("tensor_tensor", v4(tA), trb, Xib, ALU.mult), rd, [b_tA])
                V(I("tensor_tensor", v4(tB), tib, Xrb, ALU.mult), rd, [b_tB])
                if neg_i:
                    V(I("scalar_tensor_tensor", out=o4(out_i), in0=v4(tA), scalar=-1.0, in1=v4(tB),
                                                       op0=ALU.mult, op1=ALU.subtract), [b_tA, b_tB], [wb])
                else:
                    V(I("tensor_add", o4(out_i), v4(tA), v4(tB)), [b_tA, b_tB], [wb])

            def tab(name, lo, hi, rev, d):
                t = pw[name][:, lo:hi, d * 16:(d + 1) * 16]
                return t[:, ::-1, :] if rev else t

            for d in range(2):
                rev = (d == 1)
                cmul_tab(Qr[:, d], Qn[:, d], tab("pr", 1, 9, rev, d), tab("pi", 1, 9, rev, d), cre[:, d], cim[:, d], b_Q, neg_i=True)
                cmul_tab(X_r[:, d], X_i[:, d], tab("mr", 1, 9, rev, d), tab("mi", 1, 9, rev, d), Bb["r"][:, d], Bb["i"][:, d], b_X)
            ck("S2c")
            for gq4 in range(4):
                for g2_ in range(2):
                    ps_ = slice(g2_ * 64, g2_ * 64 + 64)
                    ptf, bpf = nps(); ptb, bpb = nps()
                    for (d, pt, bp) in [(0, ptf, bpf), (1, ptb, bpb)]:
                        fns = []
                        for k4 in range(4):
                            gp = gq4 * 4 + k4
                            fns.append(mm(pt[:, k4 * 128:(k4 + 1) * 128], X_r[ps_, d, gp, :], Qr[ps_, d, gp, :], True, False))
                            fns.append(mm(pt[:, k4 * 128:(k4 + 1) * 128], X_i[ps_, d, gp, :], Qn[ps_, d, gp, :], False, True))
                        MM(fns, [b_X, b_Q], [bp])
                    m4 = lambda m: m.unsqueeze(1).to_broadcast([128, 4, 128])
                    v3 = lambda t: t.rearrange("p (a b) -> p a b", a=4)
                    V(I("tensor_tensor", v3(tA[:, 0:512]), v3(ptf[:, :]), m4(Mf), ALU.mult), [bpf, b_cst], [b_tA])
                    V(I("tensor_tensor", v3(tB[:, 0:512]), v3(ptb[:, :]), m4(Mb), ALU.mult), [bpb, b_cst], [b_tB])
                    V(I("tensor_add", tA[:, 0:512], tA[:, 0:512], tB[:, 0:512]), [b_tA, b_tB], [b_tA])
                    for k4 in range(4):
                        g = 2 * (gq4 * 4 + k4) + g2_
                        V(I("scalar_tensor_tensor", out=Tm[:, g, :], in0=ident, scalar=dcol[:, g:g + 1],
                            in1=tA[:, k4 * 128:(k4 + 1) * 128], op0=ALU.mult, op1=ALU.add),
                          [b_tA, b_par, b_cst], [b_T])
            ck("S2d")
            for d in range(2):
                rev = (d == 1)
                cmul_tab(X_r[:, d], X_i[:, d], tab("pr", 0, 8, not rev, d), tab("pi", 0, 8, not rev, d),
                         Bb["r"][:, d], Bb["i"][:, d], b_X)
            for d in range(2):
                for gq4 in range(4):
                    for g2_ in range(2):
                        ps_ = slice(g2_ * 64, g2_ * 64 + 64)
                        pt, bp = nps()
                        fns = []
                        for k4 in range(4):
                            gp = gq4 * 4 + k4
                            for ri, Pm in enumerate([X_r, X_i]):
                                c0 = (k4 * 2 + ri) * 64
                                fns.append(mm(pt[:, c0:c0 + 64], Pm[ps_, d, gp, :], identb[ps_, g2_ * 64:g2_ * 64 + 64]))
                        MM(fns, [b_X, b_cst], [bp])
                        g0 = 2 * gq4 * 4 + g2_
                        V(I("tensor_copy", PT[:, d, g0:g0 + 7:2, :, :].rearrange("p g r n -> p g (r n)"),
                            pt[:, :].rearrange("p (g f) -> p g f", g=4)), [bp], [b_PT])
            ck("S2e")
            V(I("tensor_scalar_mul", Thu[:], th[:], 8.0), [b_der], [b_der])
            V(I("tensor_copy", tSI[:, 0:32], Thu[:]), [b_der], [b_tSI])
            V(I("tensor_copy", tS[1][:, 0:32], tSI[:, 0:32]), [b_tSI], [b_tS[1]])
            V(I("tensor_sub", Thu[:], Thu[:], tS[1][:, 0:32]), [b_tS[1], b_der], [b_der])
            A(I("activation", R8[:], lrdt[:], AF.Exp, scale=8.0), [b_der], [b_der])
            V(I("tensor_scalar_mul", tS[2][:, 0:32], Thu[:], 256.0), [b_der], [b_tS[2]])
            sincos_turns(tS[2][:, 0:32], 32, fini[:], finr[:], [b_tS[2]], [b_der])
            A(I("activation", mag2048[:], lrdt[:], AF.Exp, scale=2048.0), [b_der], [b_der])
            V(I("tensor_mul", Acr[:], mag2048[:], finr[:]), [b_der], [b_der])
            V(I("tensor_mul", Aci[:], mag2048[:], fini[:]), [b_der], [b_der])
            gate([b_Bb, b_zt], [b_E])
            for (Ec, Es, pos) in [(E0c, E0s, cpos[:, 0:16]), (E1c, E1s, c16[:])]:
                V(I("tensor_tensor", tS[2][:, 0:512].rearrange("p (g c) -> p g c", g=32),
                    Thu[:].unsqueeze(2).to_broadcast([128, 32, 16]), pos.unsqueeze(1).to_broadcast([128, 32, 16]), ALU.mult),
                  [b_der, b_cst], [b_tS[2]])
                sincos_turns(tS[2][:, 0:512], 512, Es.rearrange("p g c -> p (g c)"), Ec.rearrange("p g c -> p (g c)"), [b_tS[2]], [b_E])
            if l == 0:
                tap("Tm", Tm.rearrange("p g f -> p (g f)"), [b_T], 2048)
                tap("Qr", Qr.rearrange("p d g f -> p (d g f)"), [b_Q], 2048)
                tap("PT", PT.rearrange("p d g r n -> p (d g r n)"), [b_PT], 2048)

            ck("S2")
            for (so, kcols, vcols, kdst, vblk) in [(4, slice(128, 256), slice(384, 512), slice(0, 128), 0),
                                                   (8, slice(0, 128), slice(256, 384), slice(T + 128, T + 256), 17)]:
                for (cols, dst_ap, wb) in [(kcols, kT[:, kdst], b_k), (vcols, vtm[:, vblk, :], b_v)]:
                    acc = tS[2][:, 0:128]
                    V(I("tensor_scalar_mul", acc, halg[:, 0, cols], sel[:, so:so + 1]),
                      [b_halg, b_sel], [b_tS[2]])
                    for r in range(1, 4):
                        V(I("scalar_tensor_tensor", out=acc, in0=halg[:, r, cols],
                                                                                  scalar=sel[:, so + r:so + r + 1], in1=acc,
                                                                                  op0=ALU.mult, op1=ALU.add),
                          [b_halg, b_sel], [b_tS[2]])
                    V(I("tensor_copy", dst_ap, acc), [b_tS[2]], [wb])

            ck("halosel")
            for i in range(8):
                for g8 in range(8):
                    P.dma("sync", I("dma_start",
                        out=U[i * 16:(i + 1) * 16, :, :].rearrange("p (a g) c -> p a g c", a=4)[:, :, g8, :],
                        in_=uT2[g8 * 16:(g8 + 1) * 16, :, i, :]), reads=[b_uT2], writes=[b_U])
            if l == 0:
                tap("U", U.rearrange("p g c -> p (g c)"), [b_U], 2048)
            ck("relayout")
            gate(a1_s2, a1_p1)
            gate(wa_s2, [b_WA, b_WB])
            P.dma("gpsimd", I("dma_start", out=wglu, in_=w_glu_d[l].rearrange("(k p) f -> p k f", p=128)), writes=[b_WA])
            P.dma("gpsimd", I("dma_start", out=wout, in_=w_out_d[l].rearrange("(k p) f -> p k f", p=128)), writes=[b_WA, b_WB])

            def ssm_batch(bt, phase):
                for d in range(2):
                    c0 = d * 16 + bt * 2
                    gsl = slice(c0, c0 + 2)
                    cosT_, sinT_, btab_ = tabsets[ssm_it[0] % 2]
                    ssm_it[0] += 1
                    if btab_ is None:
                        rd_t = [b_rstd, b_sig]; wr_c = [b_rstd]; wr_s = [b_sig]
                    else:
                        rd_t = [btab_]; wr_c = [btab_]; wr_s = [btab_]
                    e1c = E1c[:, gsl, :].unsqueeze(3).to_broadcast([128, 2, 16, 16])
                    e1s = E1s[:, gsl, :].unsqueeze(3).to_broadcast([128, 2, 16, 16])
                    e0c = E0c[:, gsl, :].unsqueeze(2).to_broadcast([128, 2, 16, 16])
                    e0s = E0s[:, gsl, :].unsqueeze(2).to_broadcast([128, 2, 16, 16])
                    q4 = lambda t: t.rearrange("p (g a b) -> p g a b", g=2, a=16)
                    q3 = lambda t: t.rearrange("p g (a b) -> p g a b", a=16)
                    pa = tS[2][:, 0:512]; pb = tSI[:, 0:512].bitcast(F32)
                    G = lambda fn, r, w: P.op("gpsimd", fn, r, w)
                    G(I("tensor_tensor", q4(pa), e1c, e0c, ALU.mult), [b_E], [b_tS[2]])
                    G(I("tensor_tensor", q4(pb), e1s, e0s, ALU.mult), [b_E], [b_tSI])
                    G(I("tensor_sub", q3(cosT_), q4(pa), q4(pb)), [b_tS[2], b_tSI], wr_c)
                    G(I("tensor_tensor", q4(pa), e1s, e0c, ALU.mult), [b_E], [b_tS[2]])
                    G(I("tensor_tensor", q4(pb), e1c, e0s, ALU.mult), [b_E], [b_tSI])
                    G(I("tensor_add", q3(sinT_), q4(pa), q4(pb)), [b_tS[2], b_tSI], wr_s)
                    zb = []
                    for ri in range(2):
                        pt, bp = nps()
                        fns = []
                        for gq in range(2):
                            gp = bt * 2 + gq
                            for g2_ in range(2):
                                g = gp * 2 + g2_
                                fns.append(mm(pt[g2_ * 64:(g2_ + 1) * 64, gq * 256:(gq + 1) * 256], PT[:, d, g, ri, :], U[:, g, :]))
                        MM(fns, [b_PT, b_U], [bp])
                        zb.append((pt, bp))
                    (pzr, bzr), (pzi, bzi) = zb
                    zr_ = pzr[:, :].rearrange("p (g c) -> p g c", g=2)
                    zi_ = pzi[:, :].rearrange("p (g c) -> p g c", g=2)
                    if d == 1:
                        zr_ = zr_[:, :, ::-1]; zi_ = zi_[:, :, ::-1]
                    a3 = tS[0][:, 0:512].rearrange("p (g c) -> p g c", g=2)
                    b3 = tS[1][:, 0:512].rearrange("p (g c) -> p g c", g=2)
                    V(I("tensor_tensor", a3, zr_, cosT_, ALU.mult), [bzr, *rd_t], [b_tS[0]])
                    V(I("tensor_tensor", b3, zi_, sinT_, ALU.mult), [bzi, *rd_t], [b_tS[1]])
                    V(I("tensor_add", Wr, a3, b3), [b_tS[0], b_tS[1]], [b_W])
                    V(I("tensor_tensor", a3, zi_, cosT_, ALU.mult), [bzi, *rd_t], [b_tS[0]])
                    V(I("tensor_tensor", b3, zr_, sinT_, ALU.mult), [bzr, *rd_t], [b_tS[1]])
                    V(I("tensor_sub", Wi, a3, b3), [b_tS[0], b_tS[1]], [b_W])
                    for gq in range(2):
                        col = c0 + gq
                        for (Wt, St, ci) in [(Wr, Sr, 0), (Wi, Si, 1)]:
                            cc = ci * 32 + col
                            init = 0.0 if phase == 0 else carry[:, cc:cc + 1]
                            V(I("tensor_tensor_scan",
                                St[:, gq, :], R8[:, col:col + 1].to_broadcast([128, 256]), Wt[:, gq, :], init, ALU.mult, ALU.add),
                              [b_W, b_der, b_carry], [b_S])
                    if phase == 0:
                        fr = finr[:, gsl]; fi = fini[:, gsl]
                        s_r = Sr[:, :, 255]; s_i = Si[:, :, 255]
                        o_r = Fst[:, c0:c0 + 2]; o_i = Fst[:, 32 + c0:32 + c0 + 2]
                        t0 = tS[0][:, 0:2]; t1 = tS[1][:, 0:2]
                        V(I("tensor_mul", t0, s_r, fr), [b_S, b_der], [b_tS[0]])
                        V(I("tensor_mul", t1, s_i, fi), [b_S, b_der], [b_tS[1]])
                        V(I("tensor_sub", o_r, t0, t1), [b_tS[0], b_tS[1]], [b_F])
                        V(I("tensor_mul", t0, s_r, fi), [b_S, b_der], [b_tS[0]])
                        V(I("tensor_mul", t1, s_i, fr), [b_S, b_der], [b_tS[1]])
                        V(I("tensor_add", o_i, t0, t1), [b_tS[0], b_tS[1]], [b_F])
                    else:
                        Hr_ = Hb16[(d, "r")]; Hi_ = Hb16[(d, "i")]
                        if d == 0:
                            o_r = Hr_[:, :, 1:256]; o_i = Hi_[:, :, 1:256]
                            o_r0 = Hr_[:, :, 0]; o_i0 = Hi_[:, :, 0]
                        else:
                            o_r = Hr_[:, :, 254::-1]; o_i = Hi_[:, :, 254::-1]
                            o_r0 = Hr_[:, :, 255]; o_i0 = Hi_[:, :, 255]
                        a3s = a3[:, :, 0:255]; b3s = b3[:, :, 0:255]
                        V(I("tensor_tensor", a3s, Sr[:, :, 0:255], cosT_[:, :, 0:255], ALU.mult), [b_S, *rd_t], [b_tS[0]])
                        V(I("tensor_tensor", b3s, Si[:, :, 0:255], sinT_[:, :, 0:255], ALU.mult), [b_S, *rd_t], [b_tS[1]])
                        V(I("tensor_sub", o_r, a3s, b3s), [b_tS[0], b_tS[1]], [b_H])
                        V(I("tensor_tensor", a3s, Sr[:, :, 0:255], sinT_[:, :, 0:255], ALU.mult), [b_S, *rd_t], [b_tS[0]])
                        V(I("tensor_tensor", b3s, Si[:, :, 0:255], cosT_[:, :, 0:255], ALU.mult), [b_S, *rd_t], [b_tS[1]])
                        V(I("tensor_add", o_i, a3s, b3s), [b_tS[0], b_tS[1]], [b_H])
                        V(I("tensor_copy", o_r0, carry[:, c0:c0 + 2]), [b_carry], [b_H])
                        V(I("tensor_copy", o_i0, carry[:, 32 + c0:32 + c0 + 2]), [b_carry], [b_H])

            V(I("tensor_copy", esrow[:, :, :], esink[0:1, l, :].unsqueeze(2).to_broadcast([1, 8, 128])), [b_small], [b_esr])
            gate([b_uT2], [b_att])
            def att_block(qb):
                qs = slice(qb * 128, (qb + 1) * 128)
                for kvh in range(2):
                    hs_ = slice(kvh * 64, kvh * 64 + 64)
                    set_ = (qb * 2 + kvh) % 2
                    pM = pM2[set_]; b_pM = b_pM2[set_]
                    for kb in range(3):
                        pt, bp = nps()
                        MM([mm(pt[:, :].rearrange("p (j q) -> p j q", j=4), kT[hs_, (qb + kb) * 128:(qb + kb + 1) * 128], qT[hs_, :, qs])],
                           [b_k, b_q], [bp])
                        A(I("activation", pM[kb], pt[:, :], AF.Exp, scale=0.125), [bp], [b_pM[kb]])
                        V(I("tensor_tensor", pM[kb], pM[kb],
                            etab[:, kb, kvh * 4:(kvh + 1) * 4, :].rearrange("p j q -> p (j q)"), ALU.mult),
                          [b_et], [b_pM[kb]])
                    pn, bpn = nps(); pd, bpd = nps()
                    fn_n = []; fn_d = []
                    for j in range(4):
                        po = slice((j % 2) * 64, (j % 2) * 64 + 64)
                        cs2 = slice((j // 2) * 128, (j // 2) * 128 + 128)
                        for kb in range(3):
                            if kb == 0 and qb == 0:
                                ol = ones_pn[:, 0, :]
                            elif kb == 2 and qb == 15:
                                ol = ones_pn[:, 1, :]
                            else:
                                ol = onesb[:, 0:64]
                            fn_n.append(mm(pn[po, cs2], vtm[:, qb + kb, hs_], pM[kb][:, j * 128:(j + 1) * 128], kb == 0, kb == 2))
                            if kb == 0:
                                fn_d.append(mm(pd[po, cs2], onesb[0:1, 0:64], esrow[0:1, kvh * 4 + j, :], True, False))
                            fn_d.append(mm(pd[po, cs2], ol, pM[kb][:, j * 128:(j + 1) * 128], False, kb == 2))
                    MM(fn_n, [b_v] + b_pM, [bpn])
                    MM(fn_d, [b_cst, b_esr] + b_pM, [bpd])
                    A(I("activation", rden, pd[:, 0:256].rearrange("p (a q) -> p a q", a=2), AF.Ln), [bpd], [b_rden])
                    A(I("activation", rden, rden, AF.Exp, scale=-1.0), [b_rden], [b_rden])
                    V(I("tensor_tensor", attT[:, kvh * 2:kvh * 2 + 2, qs],
                                                                       pn[:, 0:256].rearrange("p (a q) -> p a q", a=2), rden, ALU.mult),
                      [bpn, b_rden, b_uT2], [b_att])
            att_order = [1, 2, 3, 4, 5, 6, 7, 8, 9, 10, 11, 12, 13, 14, 0, 15]
            for bt in range(8):
                ssm_batch(bt, 0)
                att_block(att_order[bt])
            ck("ssm0")
            P.dma("gpsimd", I("dma_start", out=st_src, in_=Fst[:]), reads=[b_F], writes=[b_stg])
            P.op("gpsimd", I("collective_compute", "AllGather", ALU.bypass, replica_groups=GROUPS,
                                                         ins=[st_src.opt()], outs=[st_dst.opt()]), [b_stg], [b_stg])
            P.dma("gpsimd", I("dma_start", out=stg[:], in_=st_dst.rearrange("(r p) f -> p r f", p=128)),
                  reads=[b_stg], writes=[b_stg])

            ck("stx")
            if l == 0:
                tap("attT", attT.rearrange("p j t -> p (j t)"), [b_att], 2048)

            ck("att")
            V(I("memset", carry[:], 0.0), [], [b_carry])
            for d in range(2):
                cs_ = slice(d * 16, d * 16 + 16); ci_ = slice(32 + d * 16, 32 + d * 16 + 16)
                cr = hcr[:, 0:16]; cim_ = hcr[:, 16:32]; t0 = hcr[:, 32:48]; t1 = hcr[:, 48:64]
                V(I("memset", hcr[:], 0.0), [], [b_hcr])
                order = [0, 1, 2, 3] if d == 0 else [3, 2, 1, 0]
                hh = [b_hcr]
                for r in order:
                    V(I("scalar_tensor_tensor", out=carry[:, cs_], in0=cr, scalar=sel[:, r:r + 1], in1=carry[:, cs_],
                                                                     op0=ALU.mult, op1=ALU.add), [b_hcr, b_sel], [b_carry])
                    V(I("scalar_tensor_tensor", out=carry[:, ci_], in0=cim_, scalar=sel[:, r:r + 1], in1=carry[:, ci_],
                                                                     op0=ALU.mult, op1=ALU.add), [b_hcr, b_sel], [b_carry])
                    ar_ = Acr[:, cs_]; ai_ = Aci[:, cs_]
                    V(I("tensor_mul", t0, cr, ar_), [b_der], hh)
                    V(I("tensor_mul", t1, cim_, ai_), [b_der], hh)
                    V(I("tensor_sub", t0, t0, t1), [], hh)
                    V(I("tensor_mul", t1, cr, ai_), [b_der], hh)
                    V(I("tensor_mul", cim_, cim_, ar_), [b_der], hh)
                    V(I("tensor_add", cim_, cim_, t1), [], hh)
                    V(I("tensor_add", cr, t0, stg[:, r, cs_]), [b_stg], hh)
                    V(I("tensor_add", cim_, cim_, stg[:, r, ci_]), [b_stg], hh)

            for bt in range(8):
                ssm_batch(bt, 1)
                att_block(att_order[8 + bt])
                for gq in range(2):
                    gp = bt * 2 + gq
                    pt, bp = nps()
                    fns = []
                    for g2_ in range(2):
                        g = gp * 2 + g2_
                        ps_ = slice(g2_ * 64, g2_ * 64 + 64)
                        oc = pt[:, g2_ * 256:g2_ * 256 + 256]
                        fns.append(mm(oc, Tm[:, g, :], U[:, g, :], True, False))
                        for d in range(2):
                            fns.append(mm(oc, Qr[ps_, d, gp, :], Hb16[(d, "r")][ps_, gq, :], False, False))
                            fns.append(mm(oc, Qn[ps_, d, gp, :], Hb16[(d, "i")][ps_, gq, :], False, d == 1))
                    MM(fns, [b_T, b_U, b_Q, b_H], [bp])
                    A(I("activation", Yall[:, gp * 2:gp * 2 + 2, :].rearrange("p g c -> p (g c)"), pt[:, :],
                                                           AF.Gelu_apprx_tanh), [bp, b_Yall], [b_Yb[bt]])
            if l == 0:
                tap("Yall", Yall.rearrange("p g c -> p (g c)"), b_Yb, 2048)
            ck("ssm1")
            gate(xr_mixer, allx)
            for k in range(8):
                P.dma("gpsimd", I("dma_start", out=xT[:, k, :], in_=xsp[k * 128:(k + 1) * 128, :]), writes=b_x[k])
            b_yTok = Buf()
            gate([b_tab, b_W, b_S, b_H], [b_yTok])
            gate([b_q], [b_yT2])
            yTok = A1[:, 8192:16384].rearrange("p (k a j g h) -> p k a j g h", k=2, a=4, j=8, g=8)
            ev = [0]

            def evac(out_ap, in_ap, r, w):
                if ev[0] % 2 == 0:
                    A(I("activation", out_ap, in_ap, AF.Copy), r, w)
                else:
                    V(I("tensor_copy", out_ap, in_ap), r, w)
                ev[0] += 1
            for cblk in range(2):
                for g4 in range(8):
                    pt, bp = nps()
                    MM([mm(pt[:, k4 * 128:(k4 + 1) * 128], Yall[:, g4 * 4 + k4, cblk * 128:(cblk + 1) * 128], identb[:, :])
                        for k4 in range(4)], b_Yb + [b_cst], [bp])
                    evac(yTok[:, cblk, g4 // 2, :, (g4 % 2) * 4:(g4 % 2) * 4 + 4, :],
                         pt[:, :].rearrange("p (k j h) -> p j k h", k=4, j=8), [bp], [b_yTok])
            for a_ in range(4):
                for jp in range(4):
                    pt, bp = nps()
                    fns = []
                    for jj in range(2):
                        j = jp * 2 + jj
                        for cblk in range(2):
                            c0 = (jj * 2 + cblk) * 128
                            fns.append(mm(pt[:, c0:c0 + 128], yTok[:, cblk, a_, j, :, :].rearrange("p g h -> p (g h)"), identb[:, :]))
                    MM(fns, [b_yTok, b_cst], [bp])
                    evac(yT2[:, a_, jp * 2:jp * 2 + 2, :].rearrange("p j c -> p (j c)"), pt[:, :], [bp], [b_yT2])
            gate([b_yTok], [b_glu])
            yflat = yT2.rearrange("p a j c -> p a (j c)")
            for cb in range(4):
                cs_ = slice(cb * 512, (cb + 1) * 512)
                for oc in range(4):
                    pv, bpv = nps(); pg, bpg = nps()
                    MM([mm(pv[:, :], wglu[:, k, oc * 128:(oc + 1) * 128], yflat[:, k, cs_], k == 0, k == 3) for k in range(4)],
                       [b_WA, b_yT2], [bpv])
                    MM([mm(pg[:, :], wglu[:, k, 512 + oc * 128:512 + (oc + 1) * 128], yflat[:, k, cs_], k == 0, k == 3) for k in range(4)],
                       [b_WA, b_yT2], [bpg])
                    A(I("activation", sig[:], pg[:, :], AF.Sigmoid), [bpg], [b_sig])
                    dst = gluT[:, oc, :].rearrange("p (c j) -> p j c", j=8)[:, 2 * cb:2 * cb + 2, :]
                    V(I("tensor_tensor", dst, pv[:, :].rearrange("p (j c) -> p j c", j=2),
                                                                sig[:, :].rearrange("p (j c) -> p j c", j=2), ALU.mult),
                      [bpv, b_sig], [b_glu])
            if l == 0:
                tap("gluT", gluT.rearrange("p j t -> p (j t)"), [b_glu], 2048)
            ck("glu")
            for tt in range(NTT):
                ts = slice(tt * 512, (tt + 1) * 512)
                for m in range(8):
                    pt, bp = nps()
                    fns = [mm(pt[:, :], wout[:, k, m * 128:(m + 1) * 128], attT[:, k, ts], k == 0, False) for k in range(4)]
                    fns += [mm(pt[:, :], wout[:, 4 + k, m * 128:(m + 1) * 128], gluT[:, k, ts], False, k == 3) for k in range(4)]
                    MM(fns, [b_WA, b_WB, b_att, b_glu], [bp])
                    V(I("tensor_add", xT[:, m, ts], xT[:, m, ts], pt[:, :]), [bp], [b_x[m][tt]])
            if l == 0:
                tap("xmid", xT[:, 0, :], b_x[0], 2048)

            ck("wout")
            gate(a1_p1, b_h)
            gate([b_att, b_yT2, b_uT2, b_q], [])
            gate(misc_att, misc_ffn)
            gate([b_E], [b_WB])
            for tt in range(NTT):
                rms_tile(l, tt, g2)
            b_ws = [b_WA, b_WB]
            for gI in range(8):
                s = gI % 2
                w1, w2 = wsl[s]
                P.dma("gpsimd", I("dma_start",
                    out=w1, in_=w_ff1_d[l][:, gI * 512:(gI + 1) * 512].rearrange("(k p) f -> p k f", p=128)), writes=[b_ws[s]])
                P.dma("gpsimd", I("dma_start",
                    out=w2, in_=w_ff2_d[l][gI * 512:(gI + 1) * 512, :].rearrange("(k p) f -> p k f", p=128)), writes=[b_ws[s]])
                for tt in range(NTT):
                    ts = slice(tt * 512, (tt + 1) * 512)
                    for fc in range(4):
                        pt, bp = nps()
                        MM([mm(pt[:, :], w1[:, k, fc * 128:(fc + 1) * 128], hT[:, k, ts], k == 0, k == 7) for k in range(8)],
                           [b_ws[s], b_h[tt]], [bp])
                        ri_ = fc % 2
                        A(I("activation", rl[ri_], pt[:, :], AF.Relu), [bp], [b_rl[ri_]])
                        V(I("tensor_mul", aT[:, fc, :], rl[ri_], rl[ri_]), [b_rl[ri_]], [b_a[fc]])
                    for m in range(8):
                        pt, bp = nps()
                        MM([mm(pt[:, :], w2[:, k, m * 128:(m + 1) * 128], aT[:, k, :], k == 0, k == 3) for k in range(4)],
                           [b_ws[s]] + b_a, [bp])
                        V(I("tensor_add", xT[:, m, ts], xT[:, m, ts], pt[:, :]), [bp], [b_x[m][tt]])
            gate(misc_ffn, misc_att)

        try:
            for l in range(nlayers):
                layer(l)
        except _Stop:
            gate(xr_mixer, allx)
        emit_taps()
        for k in range(8):
            P.dma("sync", I("dma_start", out=out_d[k * 128:(k + 1) * 128, :], in_=xT[:, k, :]), reads=b_x[k])
        P.wait_all("sync", allx + b_tS)
        P.finish("sync")
        P.run()
    return nc


def _prep_inputs(x, norm1, w_in, q_gain, k_gain, sink, lam_re, lam_im, log_dt, b_re, b_im, c_re, c_im,
                 d_skip, w_glu, w_out, norm2, w_ff1, w_ff2, nlayers=DEPTH):
    f = lambda a: np.ascontiguousarray(np.asarray(a, dtype=np.float32))
    L = DEPTH
    perm = []
    for j in range(4):
        perm += list(range(j * 64, j * 64 + 64)) + list(range((4 + j) * 64, (4 + j) * 64 + 64))
    perm += list(range(512, 1280))
    shared = {
        "w_in": f(np.asarray(w_in)[:nlayers][:, :, perm]), "w_glu": f(np.asarray(w_glu)[:nlayers]),
        "w_out": f(np.asarray(w_out)[:nlayers]),
        "w_ff1": f(np.asarray(w_ff1)[:nlayers]), "w_ff2": f(np.asarray(w_ff2)[:nlayers]),
        "g1": f(np.asarray(norm1).reshape(L, 8, 128).transpose(2, 0, 1)),
        "g2": f(np.asarray(norm2).reshape(L, 8, 128).transpose(2, 0, 1)),
        "qg": f(np.tile(np.asarray(q_gain).T, (2, 1))),
        "kg": f(np.tile(np.asarray(k_gain).T, (2, 1))),
        "sinkr": f(np.broadcast_to(np.asarray(sink)[None], (128, L, 8))),
    }

    def gn(a):
        a = np.asarray(a).reshape(L, 2, 16, 2, 64)
        return f(a.transpose(3, 4, 0, 1, 2).reshape(128, L, 32))
    shared["lre"] = gn(lam_re)
    shared["lim"] = gn(lam_im)
    shared["ldt"] = gn(np.broadcast_to(np.asarray(log_dt)[:, :, :, None], (L, 2, 32, 64)))

    def bb(a):
        a = np.asarray(a).reshape(L, 16, 2, 64, 16)
        return f(a.transpose(2, 3, 0, 1, 4).reshape(128, L, 256))
    shared["bre"] = bb(b_re)
    shared["bim"] = bb(b_im)

    def cc(a):
        a = np.asarray(a).reshape(L, 2, 16, 2, 16, 64)
        return f(a.transpose(3, 5, 0, 1, 2, 4).reshape(128, L, 512))
    shared["cre"] = cc(c_re)
    shared["cim"] = cc(c_im)
    dsk = np.asarray(d_skip).reshape(L, 32, 16)
    shared["dcol"] = f(np.broadcast_to(dsk.transpose(2, 0, 1)[None], (8, 16, L, 32)).reshape(128, L, 32))
    slopes = np.exp2(-8.0 * np.arange(1, 9) / 8.0)
    ci = np.arange(128)[:, None]; qi = np.arange(128)[None, :]
    et = np.zeros((128, 3, 8, 128), np.float32)
    for kb in range(3):
        dist = np.abs(qi - ci - (kb - 1) * 128)
        valid = dist <= 128
        for h in range(8):
            et[:, kb, h, :] = np.where(valid, np.exp(-slopes[h] * dist), 0.0)
    shared["etab"] = f(et.reshape(128, 3072))
    cst = np.zeros((128, 656), np.float32)
    cst[:, 0:128] = np.eye(128)
    bi = np.arange(128)[:, None] // 16; bj = np.arange(128)[None, :] // 16
    cst[:, 128:256] = (bj >= bi)
    cst[:, 256:384] = (bi >= bj)
    cst[:, 384:393] = np.arange(9)[None, :]
    cst[:, 400:656] = np.arange(1, 257)[None, :]
    shared["cst"] = cst
    xs = np.asarray(x)
    in_maps = []
    for r in range(8):
        b, q = r // 4, r % 4
        m = dict(shared)
        m["xT"] = f(xs[b, q * T:(q + 1) * T, :].T)
        s = np.zeros((128, 16), np.float32)
        s[:, q] = 1.0
        if q > 0:
            s[:, 4 + q - 1] = 1.0; s[:, 12] = 1.0
        if q < 3:
            s[:, 8 + q + 1] = 1.0; s[:, 13] = 1.0
        m["sel"] = s
        in_maps.append(m)
    return in_maps


_NC_CACHE = {}


def kernel(**inputs):
    in_maps = _prep_inputs(**inputs)
    if "nc" not in _NC_CACHE:
        _NC_CACHE["nc"] = build_nc()
    nc = _NC_CACHE["nc"]
    res = run_bass_kernel_spmd(nc, in_maps, core_ids=list(range(8)))
    out = np.zeros((2, 4 * T, 1024), np.float32)
    for r in range(8):
        b, q = r // 4, r % 4
        out[b, q * T:(q + 1) * T, :] = np.asarray(res.results[r]["outT"]).T
    return out
```

```python
import math
import os
import numpy as np
from contextlib import ExitStack
import concourse.bass as bass
import concourse.mybir as mybir
from concourse.bass_utils import run_bass_kernel_spmd

F32 = mybir.dt.float32
BF16 = mybir.dt.bfloat16
I32 = mybir.dt.int32
AF = mybir.ActivationFunctionType
ALU = mybir.AluOpType

DEPTH = 4
T = 2048
NTT = 4
EPS = 1e-6
TWO_PI = 2.0 * math.pi


class Buf:
    def __init__(self, name=""):
        self.name = name
        self.w = None
        self.r = []


class EngQ:
    def __init__(self, name, sem):
        self.name = name
        self.sem = sem
        self.count = 0
        self.ops = []
        self.seen = {}


class Prog:
    ENGS = ["sync", "scalar", "gpsimd", "vector", "tensor"]

    def __init__(self, nc, es, ndma=16):
        self.nc = nc
        self.q = {e: EngQ(e, es.enter_context(nc.semaphore("s_" + e))) for e in self.ENGS}
        self.dq = {e: [EngQ(f"d_{e}{k}", es.enter_context(nc.semaphore(f"d_{e}{k}"))) for k in range(ndma)]
                   for e in ["sync", "gpsimd"]}
        self.rr = {e: 0 for e in self.dq}
        self.nops = 0

    def _waits(self, eng, reads, writes):
        q = self.q[eng]
        need = {}

        def add(tok):
            if tok is None:
                return
            s, v = tok
            if eng == "tensor" and s is q:
                return
            if need.get(s, 0) < v:
                need[s] = v
        for b in reads:
            add(b.w)
        for b in writes:
            add(b.w)
            for t in b.r:
                add(t)
        for s, v in need.items():
            if q.seen.get(s, 0) >= v:
                continue
            q.seen[s] = v
            q.ops.append(lambda e, s=s, v=v: e.wait_ge(s.sem, v))

    def _mark(self, tok, reads, writes):
        for b in reads:
            b.r = [t for t in b.r if t[0] is not tok[0]] + [tok]
        for b in writes:
            b.w = tok
            b.r = []

    def op(self, eng, fn, reads=(), writes=()):
        return self.group(eng, [fn], reads, writes)

    def group(self, eng, fns, reads=(), writes=()):
        q = self.q[eng]
        self._waits(eng, reads, writes)
        for fn in fns[:-1]:
            q.ops.append(lambda e, fn=fn: fn(e))
        q.count += 1
        tok = (q, q.count)
        q.ops.append(lambda e, fn=fns[-1], q=q: fn(e).then_inc(q.sem, 1))
        self._mark(tok, reads, writes)
        self.nops += len(fns)
        return tok

    def dma(self, eng, fn, reads=(), writes=()):
        self._waits(eng, reads, writes)
        k = self.rr[eng]
        self.rr[eng] = (k + 1) % len(self.dq[eng])
        d = self.dq[eng][k]
        q_ = self.q[eng]
        if d.count > 0 and q_.seen.get(d, 0) < d.count:
            q_.seen[d] = d.count
            q_.ops.append(lambda e, d=d, v=d.count: e.wait_ge(d.sem, v))
        d.count += 16
        tok = (d, d.count)
        self.q[eng].ops.append(lambda e, fn=fn, d=d: fn(e).then_inc(d.sem, 16))
        self._mark(tok, reads, writes)
        self.nops += 1
        return tok

    def wait_all(self, eng, bufs):
        self._waits(eng, [], bufs)

    def finish(self, eng="sync"):
        q = self.q[eng]
        allq = [x for x in self.q.values() if x is not q] + [d for ds in self.dq.values() for d in ds]
        for s_ in allq:
            if s_.count > 0:
                q.ops.append(lambda e, s_=s_, v=s_.count: e.wait_ge(s_.sem, v))

    def run(self):
        with self.nc.Block() as block:
            for e in self.ENGS:
                ops = self.q[e].ops

                def body(eng, ops=ops):
                    for o in ops:
                        o(eng)
                getattr(block, e)(body)


class _Stop(Exception):
    pass


def build_nc(nlayers=DEPTH, taps=None, stop_after=None):
    nc = bass.Bass("TRN2", target_bir_lowering=False)
    L = DEPTH

    def din(name, shape, dt=F32):
        return nc.dram_tensor(name, list(shape), dt, kind="ExternalInput").ap()

    xT_d = din("xT", [1024, T])
    LW = nlayers
    w_in_d = din("w_in", [LW, 1024, 1280])
    w_glu_d = din("w_glu", [LW, 512, 1024])
    w_out_d = din("w_out", [LW, 1024, 1024])
    w_ff1_d = din("w_ff1", [LW, 1024, 4096])
    w_ff2_d = din("w_ff2", [LW, 4096, 1024])
    g1_d = din("g1", [128, L, 8])
    g2_d = din("g2", [128, L, 8])
    qg_d = din("qg", [128, L])
    kg_d = din("kg", [128, L])
    sink_d = din("sinkr", [128, L, 8])
    lre_d = din("lre", [128, L, 32])
    lim_d = din("lim", [128, L, 32])
    ldt_d = din("ldt", [128, L, 32])
    bre_d = din("bre", [128, L, 256])
    bim_d = din("bim", [128, L, 256])
    cre_d = din("cre", [128, L, 512])
    cim_d = din("cim", [128, L, 512])
    dcol_d = din("dcol", [128, L, 32])
    etab_d = din("etab", [128, 3072])
    cst_d = din("cst", [128, 656])
    sel_d = din("sel", [128, 16])
    out_d = nc.dram_tensor("outT", [1024, T], F32, kind="ExternalOutput").ap()
    taps = list(taps) if taps else []
    dbg_d = nc.dram_tensor("dbg", [max(1, len(taps)), 128, 2048], F32, kind="ExternalOutput").ap() if taps else None

    xsp = nc.dram_tensor("xsp", [1024, T], F32).ap()
    hal_src = nc.dram_tensor("hal_src", [128, 512], F32).ap()
    hal_dst = nc.dram_tensor("hal_dst", [4 * 128, 512], F32).ap()
    st_src = nc.dram_tensor("st_src", [128, 64], F32).ap()
    st_dst = nc.dram_tensor("st_dst", [4 * 128, 64], F32).ap()
    GROUPS = [[0, 1, 2, 3], [4, 5, 6, 7]]

    with ExitStack() as es:
        P = Prog(nc, es)

        def sb(name, shape, dt=F32):
            return es.enter_context(nc.sbuf_tensor(name, list(shape), dt))

        XR = sb("XR", [128, 16384], F32)
        XRb = XR[:, :].bitcast(BF16)
        A1 = sb("A1", [128, 16384], BF16)
        A1f = A1[:, :].bitcast(F32)
        WA = sb("WA", [128, 16384], BF16)
        WAf = WA[:, :].bitcast(F32)
        A3 = sb("A3", [128, 8192], BF16)
        A4 = sb("A4", [128, 8192], BF16)
        MISC = sb("MISC", [128, 4096], BF16)

        xT = XR[:, :].rearrange("p (k t) -> p k t", k=8)
        b_x = [[Buf() for _ in range(NTT)] for _ in range(8)]
        allx = [b for row in b_x for b in row]
        U = XRb[:, 0:8192].rearrange("p (g c) -> p g c", g=32); b_U = Buf()
        PT = XRb[:, 8192:16384].rearrange("p (d g r n) -> p d g r n", d=2, g=32, r=2); b_PT = Buf()
        Qr = XRb[:, 16384:20480].rearrange("p (d g f) -> p d g f", d=2, g=16)
        Qn = XRb[:, 20480:24576].rearrange("p (d g f) -> p d g f", d=2, g=16); b_Qh = [Buf(), Buf()]
        Tm = XRb[:, 24576:28672].rearrange("p (g f) -> p g f", g=32); b_T = Buf()
        etab = XRb[:, 28672:31744].rearrange("p (k h q) -> p k h q", k=3, h=8); b_et = Buf()
        xr_mixer = [b_U, b_PT, b_T, b_et] + b_Qh

        hT = A1[:, :].rearrange("p (k t) -> p k t", k=8); b_h = [Buf() for _ in range(NTT)]
        bre = A1f[:, 0:256].rearrange("p (g h) -> p g h", g=16)
        bim = A1f[:, 256:512].rearrange("p (g h) -> p g h", g=16)
        cre = A1f[:, 512:1024].rearrange("p (d g h) -> p d g h", d=2, g=16)
        cim = A1f[:, 1024:1536].rearrange("p (d g h) -> p d g h", d=2, g=16)
        b_par2 = Buf()
        halg = A1f[:, 1536:3584].rearrange("p (r f) -> p r f", r=4); b_halg = Buf()
        tA = A1f[:, 4096:6144]; tB = A1f[:, 6144:8192]; b_tA = Buf(); b_tB = Buf(); b_tA2 = Buf(); b_tB2 = Buf()
        Yall = A1[:, 0:8192].rearrange("p (g c) -> p g c", g=32); b_Yall = Buf()
        yT2 = A3[:, :].rearrange("p (a j c) -> p a j c", a=4, j=8); b_yT2 = Buf()
        cosT = A1f[:, 4096:4608].rearrange("p (g c) -> p g c", g=2)
        sinT = A1f[:, 4608:5120].rearrange("p (g c) -> p g c", g=2); b_tab = Buf()
        Wr = A1f[:, 5120:5632].rearrange("p (g c) -> p g c", g=2)
        Wi = A1f[:, 5632:6144].rearrange("p (g c) -> p g c", g=2); b_W = Buf()
        Sr = A1f[:, 6144:6656].rearrange("p (g c) -> p g c", g=2)
        Si = A1f[:, 6656:7168].rearrange("p (g c) -> p g c", g=2); b_S = Buf()
        Hb16 = {}
        for d in range(2):
            for ni, n in enumerate("ri"):
                o = 14336 + (d * 2 + ni) * 512
                Hb16[(d, n)] = A1[:, o:o + 512].rearrange("p (g c) -> p g c", g=2)
        b_H = Buf()
        a1_s2 = [b_par2, b_halg, b_tA, b_tB, b_tA2, b_tB2]
        b_Yb = [Buf() for _ in range(8)]

        win = WA[:, 0:10240].rearrange("p (k f) -> p k f", k=8); b_WA = Buf(); b_WB = Buf()
        X_r = WA[:, 0:4096].rearrange("p (d g f) -> p d g f", d=2, g=16)
        X_i = WA[:, 4096:8192].rearrange("p (d g f) -> p d g f", d=2, g=16); b_Xh = [Buf(), Buf()]
        pw = {n: WAf[:, 4096 + i * 288:4096 + (i + 1) * 288].rearrange("p (k c) -> p k c", k=9)
              for i, n in enumerate(["pr", "pi", "mr", "mi"])}
        b_pw = Buf()
        cs_s = WAf[:, 5248:5536]; cs_c = WAf[:, 5536:5824]
        mg_p = WAf[:, 5824:6112]; mg_m = WAf[:, 6112:6400]
        Bb = {"r": WAf[:, 6400:6912].rearrange("p (d g h) -> p d g h", d=2, g=16),
              "i": WAf[:, 6912:7424].rearrange("p (d g h) -> p d g h", d=2, g=16)}
        b_Bb = Buf()
        zt = [WAf[:, 7424 + i * 32:7424 + (i + 1) * 32] for i in range(6)]; b_zt = Buf()
        wa_s2 = [b_pw, b_Bb, b_zt] + b_Xh
        E0c = WAf[:, 6144:6656].rearrange("p (g c) -> p g c", g=32)
        E0s = WAf[:, 6656:7168].rearrange("p (g c) -> p g c", g=32)
        E1c = WAf[:, 7168:7680].rearrange("p (g c) -> p g c", g=32)
        E1s = WAf[:, 7680:8192].rearrange("p (g c) -> p g c", g=32)
        b_E = Buf()
        wglu = WA[:, 0:4096].rearrange("p (k f) -> p k f", k=4)
        wout = WA[:, 4096:12288].rearrange("p (k f) -> p k f", k=8)
        wsl = [(WA[:, s * 8192:s * 8192 + 4096].rearrange("p (k f) -> p k f", k=8),
                WA[:, s * 8192 + 4096:s * 8192 + 8192].rearrange("p (k f) -> p k f", k=4)) for s in range(2)]

        qT = A3[:, :].rearrange("p (j t) -> p j t", j=4); b_q = Buf()
        gluT = A1[:, 8192:16384].rearrange("p (j t) -> p j t", j=4); b_glu = Buf()
        a1_p1 = b_Yb + [b_Yall, b_glu, b_tab, b_W, b_S, b_H]
        uT2 = A4[:, :].rearrange("p (a i c) -> p a i c", a=4, i=8); b_uT2 = Buf()
        attT = A4[:, :].rearrange("p (j t) -> p j t", j=4); b_att = Buf()

        pM2 = [[MISC[:, (s_ * 3 + i) * 512:(s_ * 3 + i + 1) * 512] for i in range(3)] for s_ in range(2)]
        b_pM2 = [[Buf() for _ in range(3)] for _ in range(2)]
        rden = MISC[:, 3072:3584].bitcast(F32).rearrange("p (a q) -> p a q", a=2); b_rden = Buf()
        rl = [MISC[:, i * 512:(i + 1) * 512] for i in range(2)]; b_rl = [Buf(), Buf()]
        aT = MISC[:, 1024:3072].rearrange("p (k t) -> p k t", k=4); b_a = [Buf() for _ in range(4)]
        misc_att = b_pM2[0] + b_pM2[1] + [b_rden]
        misc_ffn = b_rl + b_a

        kT = sb("kT", [128, T + 256], BF16); b_k = Buf()
        vtm = sb("vtm", [128, 18, 128], BF16); b_v = Buf()
        cst = sb("cst_sb", [128, 656]); b_cst = Buf()
        cstb = sb("cstb_sb", [128, 128], BF16)
        onesb = sb("onesb", [128, 128], BF16)
        blk2 = sb("blk2", [128, 128], BF16)
        ones_pn = sb("ones_pn", [128, 2, 64], BF16)
        sel = sb("sel_sb", [128, 16]); b_sel = Buf()
        g1 = sb("g1_sb", [128, L, 8]); g2 = sb("g2_sb", [128, L, 8])
        qg = sb("qg_sb", [128, L]); kg = sb("kg_sb", [128, L])
        sinkr = sb("sink_sb", [128, L * 8]); esink = sb("esink", [128, L, 8])
        b_small = Buf()
        epsb = sb("epsb", [128, 1])
        lre = sb("lre_sb", [128, 32]); lim = sb("lim_sb", [128, 32]); ldt = sb("ldt_sb", [128, 32])
        dcol = sb("dcol_sb", [128, 32]); b_par = Buf()
        lrdt = sb("lrdt", [128, 32]); th = sb("th", [128, 32]); Thu = sb("Thu", [128, 32])
        R8 = sb("R8", [128, 32]); Acr = sb("Acr", [128, 32]); Aci = sb("Aci", [128, 32])
        finr = sb("finr", [128, 32]); fini = sb("fini", [128, 32]); mag2048 = sb("mag2048", [128, 32])
        b_der = Buf()
        tS = [sb(f"tS{i}", [128, 512]) for i in range(3)]; b_tS = [Buf() for _ in range(3)]
        tSI = sb("tSI", [128, 512], I32); b_tSI = Buf()
        Fst = sb("Fst", [128, 64]); b_F = Buf()
        stg = sb("stg", [128, 4, 64]); b_stg = Buf()
        carry = sb("carry", [128, 64]); b_carry = Buf()
        hcr = sb("hcr", [128, 64]); b_hcr = Buf()
        halt = sb("halt", [128, 512]); b_halt = Buf()
        sq = sb("sq", [128, 2, 512], BF16); b_sq = [Buf(), Buf()]
        rstd = sb("rstd", [128, 512]); b_rstd = Buf()
        sig = sb("sig", [128, 512]); b_sig = Buf()
        gatec = sb("gatec", [128, 8]); b_gate = Buf()

        psb = [es.enter_context(nc.psum_tensor(f"ps{i}", [128, 512], F32)) for i in range(8)]
        b_ps = [Buf() for _ in range(8)]
        ps_pools = {"all": list(range(8)), "ssm": [0, 1, 2], "att_s": [5, 6, 7], "att_o": [3, 4]}
        ps_rr = {k: 0 for k in ps_pools}

        def nps(pool="all"):
            pool = "all"
            lst = ps_pools[pool]
            i = lst[ps_rr[pool] % len(lst)]
            ps_rr[pool] += 1
            return psb[i], b_ps[i]

        def V(fn, r=(), w=()):
            return P.op("vector", fn, r, w)

        def A(fn, r=(), w=()):
            return P.op("scalar", fn, r, w)

        def MM(fns, r=(), w=()):
            return P.group("tensor", fns, r, w)

        def I(method, *a, **k):
            return lambda e: getattr(e, method)(*a, **k)

        def mm(out, lhsT, rhs, start=True, stop=True):
            return I("matmul", out, lhsT=lhsT, rhs=rhs, start=start, stop=stop)

        def gate(old, new):
            V(I("memset", gatec[:, 0:1], 0.0), [], list(old) + list(new) + [b_gate])

        tap_i = [0]

        deferred_taps = []

        def tap(name, ap, bufs, n):
            if name in taps:
                deferred_taps.append((name, ap, bufs, n))

        def emit_taps():
            if not deferred_taps:
                return
            P.finish("vector")
            P.finish("sync")
            for (name, ap, bufs, n) in deferred_taps:
                idx = taps.index(name)
                for c0 in range(0, n, 512):
                    c1 = min(n, c0 + 512)
                    V(I("tensor_copy", tS[2][:, 0:c1 - c0], ap[:, c0:c1]), list(bufs), [b_tS[2]])
                    P.dma("sync", I("dma_start", out=dbg_d[idx, :, c0:c1], in_=tS[2][:, 0:c1 - c0]), reads=[b_tS[2]])

        P.dma("sync", I("dma_start", out=cst[:], in_=cst_d), writes=[b_cst])
        P.dma("sync", I("dma_start", out=sel[:], in_=sel_d), writes=[b_sel])
        for (t_sb, t_d) in [(g1, g1_d), (g2, g2_d), (qg, qg_d), (kg, kg_d)]:
            P.dma("sync", I("dma_start", out=t_sb[:], in_=t_d), writes=[b_small])
        P.dma("sync", I("dma_start", out=sinkr[:], in_=sink_d.rearrange("p l h -> p (l h)")), writes=[b_small])
        for k in range(8):
            P.dma("sync", I("dma_start", out=xT[:, k, :], in_=xT_d[k * 128:(k + 1) * 128, :]),
                  writes=b_x[k])
        ident = cst[:, 0:128]; Mf = cst[:, 128:256]; Mb = cst[:, 256:384]; kvec = cst[:, 384:393]
        cpos = cst[:, 400:656]
        V(I("tensor_copy", cstb[:], ident), [b_cst], [b_cst])
        V(I("memset", onesb[:], 1.0), [], [b_cst])
        V(I("memset", blk2[:], 0.0), [], [b_cst])
        V(I("memset", blk2[0:64, 0:64], 1.0), [], [b_cst])
        V(I("memset", blk2[64:128, 64:128], 1.0), [], [b_cst])
        V(I("memset", epsb[:], EPS), [], [b_cst])
        V(I("tensor_copy", ones_pn[:, 0, :], sel[:, 12:13].to_broadcast([128, 64])), [b_sel], [b_cst])
        V(I("tensor_copy", ones_pn[:, 1, :], sel[:, 13:14].to_broadcast([128, 64])), [b_sel], [b_cst])
        A(I("activation", esink[:, :, :].rearrange("p l h -> p (l h)"), sinkr[:], AF.Exp), [b_small], [b_small])
        identb = cstb
        c16 = sb("c16", [128, 16])
        V(I("tensor_scalar", c16[:], cpos[:, 0:16], -1.0, 16.0, ALU.add, ALU.mult), [b_cst], [b_cst])
        tabsets = [(cosT, sinT, b_tab), (rstd[:, :].rearrange("p (g c) -> p g c", g=2), sig[:, :].rearrange("p (g c) -> p g c", g=2), None)]
        ssm_it = [0]
        esrow = sb("esrow", [1, 8, 128], BF16); b_esr = Buf()

        def rms_tile(l, tt, gain):
            ts = slice(tt * 512, (tt + 1) * 512)
            pt, bp = nps()
            for k in range(8):
                s = k % 2
                A(I("activation", sq[:, s, :], xT[:, k, ts], AF.Square), [b_x[k][tt]], [b_sq[s]])
                P.op("tensor", mm(pt[:, :], onesb[:, :], sq[:, s, :], start=(k == 0), stop=(k == 7)), [b_sq[s], b_cst], [bp])
            A(I("activation", rstd[:], pt[:, :], AF.Ln, bias=epsb[:], scale=1.0 / 1024.0), [bp, b_cst], [b_rstd])
            A(I("activation", rstd[:], rstd[:], AF.Exp, scale=-0.5), [b_rstd], [b_rstd])
            for k in range(8):
                V(I("scalar_tensor_tensor", out=hT[:, k, ts], in0=xT[:, k, ts], scalar=gain[:, l, k:k + 1],
                                                        in1=rstd[:], op0=ALU.mult, op1=ALU.mult),
                  [b_x[k][tt], b_rstd, b_small], [b_h[tt]])

        def sincos_turns(tin, n, out_s, out_c, rb, wb):
            a_ = tS[0][:, 0:n]; b_ = tS[1][:, 0:n]; i_ = tSI[:, 0:n]
            for (off, outp) in [(0.0, out_s), (0.25, out_c)]:
                V(I("tensor_scalar_add", a_, tin, off), rb, [b_tS[0]])
                V(I("tensor_copy", i_, a_), [b_tS[0]], [b_tSI])
                V(I("tensor_copy", b_, i_), [b_tSI], [b_tS[1]])
                V(I("tensor_sub", a_, a_, b_), [b_tS[0], b_tS[1]], [b_tS[0]])
                V(I("tensor_scalar", a_, a_, -0.4999999, 0.4999999, ALU.max, ALU.min), [b_tS[0]], [b_tS[0]])
                A(I("activation", outp, a_, AF.Sin, scale=TWO_PI), [b_tS[0]], wb)

        def ck(name):
            if stop_after == name:
                raise _Stop()

        if os.environ.get("HALO_FIRST"):
            V(I("memset", halt[:], 1.0), [], [b_halt])
            P.dma("gpsimd", I("dma_start", out=hal_src, in_=halt[:]), reads=[b_halt], writes=[b_halg])
            P.op("gpsimd", I("collective_compute", "AllGather", ALU.bypass, replica_groups=GROUPS,
                             ins=[hal_src.opt()], outs=[hal_dst.opt()]), [b_halg], [b_halg])
            P.dma("gpsimd", I("dma_start", out=halg, in_=hal_dst.rearrange("(r p) f -> p r f", p=128)),
                  reads=[b_halg], writes=[b_halg])
            if os.environ.get("HALO_FIRST") == "only":
                raise_stop = True

        def layer(l):
            P.dma("gpsimd", I("dma_start", out=win, in_=w_in_d[l].rearrange("(k p) f -> p k f", p=128)),
                  writes=[b_WA, b_WB])
            for (t_sb, t_d) in [(lre, lre_d), (lim, lim_d), (ldt, ldt_d), (dcol, dcol_d)]:
                P.dma("sync", I("dma_start", out=t_sb[:], in_=t_d[:, l]), writes=[b_par])

            for tt in range(NTT):
                ts = slice(tt * 512, (tt + 1) * 512)
                rms_tile(l, tt, g1)
                for k in range(8 if not os.environ.get("NOSPILL") else 0):
                    P.dma("sync", I("dma_start", out=xsp[k * 128:(k + 1) * 128, ts], in_=xT[:, k, ts]),
                          reads=[b_x[k][tt]])
                for oc in range(5):
                    pt, bp = nps()
                    MM([mm(pt[:, :], win[:, k, oc * 128:(oc + 1) * 128], hT[:, k, ts], k == 0, k == 7) for k in range(8)],
                       [b_WA, b_h[tt]], [bp])
                    A(I("activation", sq[:, 0, :], pt[:, :], AF.Square), [bp], [b_sq[0]])
                    p2, bp2 = nps()
                    MM([mm(p2[:, :], blk2[:, :], sq[:, 0, :])], [b_sq[0], b_cst], [bp2])
                    A(I("activation", rstd[:], p2[:, :], AF.Ln, bias=epsb[:], scale=1.0 / 64.0),
                      [bp2, b_cst], [b_rstd])
                    A(I("activation", rstd[:], rstd[:], AF.Exp, scale=-0.5), [b_rstd], [b_rstd])
                    if oc < 4:
                        V(I("scalar_tensor_tensor", out=qT[:, oc, ts], in0=pt[:, :], scalar=qg[:, l:l + 1],
                                                                               in1=rstd[:], op0=ALU.mult, op1=ALU.mult),
                          [bp, b_rstd, b_small], [b_q])
                    else:
                        V(I("scalar_tensor_tensor", out=kT[:, 128 + tt * 512:128 + (tt + 1) * 512], in0=pt[:, :],
                                                                         scalar=kg[:, l:l + 1], in1=rstd[:], op0=ALU.mult, op1=ALU.mult),
                          [bp, b_rstd, b_small], [b_k])
                        if tt == 0:
                            V(I("scalar_tensor_tensor", out=halt[:, 0:128], in0=pt[:, 0:128], scalar=kg[:, l:l + 1],
                                                                      in1=rstd[:, 0:128], op0=ALU.mult, op1=ALU.mult),
                              [bp, b_rstd, b_small], [b_halt])
                        if tt == NTT - 1:
                            V(I("scalar_tensor_tensor", out=halt[:, 128:256], in0=pt[:, 384:512], scalar=kg[:, l:l + 1],
                                                                      in1=rstd[:, 384:512], op0=ALU.mult, op1=ALU.mult),
                              [bp, b_rstd, b_small], [b_halt])
                pt, bp = nps()
                fns = []
                for b4 in range(4):
                    for k in range(8):
                        fns.append(mm(pt[:, b4 * 128:(b4 + 1) * 128], hT[:, k, tt * 512 + b4 * 128: tt * 512 + (b4 + 1) * 128],
                                      win[:, k, 640:768], k == 0, k == 7))
                MM(fns, [b_WA, b_h[tt]], [bp])
                A(I("activation", vtm[:, 1 + tt * 4:1 + (tt + 1) * 4, :].rearrange("p b f -> p (b f)"),
                                                       pt[:, :], AF.Copy), [bp], [b_v])
                if tt == 0:
                    V(I("tensor_copy", halt[:, 256:384], pt[:, 0:128]), [bp], [b_halt])
                if tt == NTT - 1:
                    V(I("tensor_copy", halt[:, 384:512], pt[:, 384:512]), [bp], [b_halt])
                for a_ in range(4):
                    oc = 6 + a_
                    pt, bp = nps()
                    MM([mm(pt[:, :], win[:, k, oc * 128:(oc + 1) * 128], hT[:, k, ts], k == 0, k == 7) for k in range(8)],
                       [b_WA, b_h[tt]], [bp])
                    V(I("tensor_copy", uT2[:, a_, :, tt * 64:(tt + 1) * 64], pt[:, :].rearrange("p (c i) -> p i c", i=8)),
                      [bp], [b_uT2])
            if l == 0:
                tap("hT", hT.rearrange("p k t -> p (k t)"), b_h, 2048)
                tap("qT", qT.rearrange("p j t -> p (j t)"), [b_q], 2048)
                tap("kT", kT[:, 128:128 + 2048], [b_k], 2048)
                tap("uT2", uT2.rearrange("p a i c -> p (a i c)"), [b_uT2], 2048)

            ck("S1")
            gate(b_h, a1_s2)
            P.dma("gpsimd", I("dma_start", out=hal_src, in_=halt[:]), reads=[b_halt], writes=[b_halg])
            if not os.environ.get("NOCC"):
                P.op("gpsimd", I("collective_compute", "AllGather", ALU.bypass, replica_groups=GROUPS,
                                 ins=[hal_src.opt()], outs=[hal_dst.opt()]), [b_halg], [b_halg])
            P.dma("gpsimd", I("dma_start", out=halg, in_=hal_dst.rearrange("(r p) f -> p r f", p=128)),
                  reads=[b_halg], writes=[b_halg])

            ck("halo")
            gate([b_WA, b_WB, b_E], wa_s2)
            gate(allx, xr_mixer)
            P.dma("gpsimd", I("dma_start", out=etab.rearrange("p k h q -> p (k h q)"), in_=etab_d), writes=[b_et])
            for (t_ap, t_d) in [(bre, bre_d), (bim, bim_d), (cre, cre_d), (cim, cim_d)]:
                P.dma("sync", I("dma_start",
                    out=t_ap.rearrange("p g h -> p (g h)") if len(t_ap.shape) == 3 else t_ap.rearrange("p d g h -> p (d g h)"),
                    in_=t_d[:, l]), writes=[b_par2])
            A(I("activation", lrdt[:], ldt[:], AF.Exp), [b_par], [b_der])
            V(I("tensor_mul", th[:], lim[:], lrdt[:]), [b_par, b_der], [b_der])
            V(I("tensor_mul", lrdt[:], lre[:], lrdt[:]), [b_par, b_der], [b_der])
            V(I("tensor_scalar_mul", th[:], th[:], 1.0 / TWO_PI), [b_der], [b_der])
            kb9 = kvec.unsqueeze(2).to_broadcast([128, 9, 32])
            ph9 = tS[2][:, 0:288]
            V(I("tensor_tensor", ph9.rearrange("p (k c) -> p k c", k=9), th[:].unsqueeze(1).to_broadcast([128, 9, 32]),
                                        kb9, ALU.mult), [b_der, b_cst], [b_tS[2]])
            sincos_turns(ph9, 288, cs_s, cs_c, [b_tS[2]], [b_pw])
            V(I("tensor_tensor", ph9.rearrange("p (k c) -> p k c", k=9), lrdt[:].unsqueeze(1).to_broadcast([128, 9, 32]),
                                        kb9, ALU.mult), [b_der, b_cst, b_pw], [b_tS[2]])
            A(I("activation", mg_p, ph9, AF.Exp), [b_tS[2]], [b_pw])
            A(I("activation", mg_m, ph9, AF.Exp, scale=-1.0), [b_tS[2]], [b_pw])
            fl = lambda t: t.rearrange("p k c -> p (k c)")
            V(I("tensor_mul", fl(pw["pr"]), mg_p, cs_c), [b_pw], [b_pw])
            V(I("tensor_mul", fl(pw["pi"]), mg_p, cs_s), [b_pw], [b_pw])
            V(I("tensor_mul", fl(pw["mr"]), mg_m, cs_c), [b_pw], [b_pw])
            V(I("scalar_tensor_tensor", out=fl(pw["mi"]), in0=mg_m, scalar=-1.0, in1=cs_s,
                                               op0=ALU.mult, op1=ALU.mult), [b_pw], [b_pw])
            ck("S2a")
            ar1, ai1 = pw["pr"][:, 1, :], pw["pi"][:, 1, :]
            zin = [b_zt, b_par, b_pw]
            V(I("tensor_mul", zt[0], lre[:], lre[:]), zin, [b_zt])
            V(I("tensor_mul", zt[1], lim[:], lim[:]), zin, [b_zt])
            V(I("tensor_add", zt[0], zt[0], zt[1]), zin, [b_zt])
            V(I("reciprocal", zt[0], zt[0]), zin, [b_zt])
            V(I("tensor_scalar_add", zt[1], ar1, -1.0), zin, [b_zt])
            V(I("tensor_mul", zt[2], zt[1], lre[:]), zin, [b_zt])
            V(I("tensor_mul", zt[3], ai1, lim[:]), zin, [b_zt])
            V(I("tensor_add", zt[2], zt[2], zt[3]), zin, [b_zt])
            V(I("tensor_mul", zt[2], zt[2], zt[0]), zin, [b_zt])
            V(I("tensor_mul", zt[3], ai1, lre[:]), zin, [b_zt])
            V(I("tensor_mul", zt[4], zt[1], lim[:]), zin, [b_zt])
            V(I("tensor_sub", zt[3], zt[3], zt[4]), zin, [b_zt])
            V(I("tensor_mul", zt[3], zt[3], zt[0]), zin, [b_zt])
            for d in range(2):
                zr = zt[2][:, d * 16:(d + 1) * 16].unsqueeze(2).to_broadcast([128, 16, 16])
                zi = zt[3][:, d * 16:(d + 1) * 16].unsqueeze(2).to_broadcast([128, 16, 16])
                t1 = tA[:, 0:256].rearrange("p (a b) -> p a b", a=16)
                t2 = tB[:, 0:256].rearrange("p (a b) -> p a b", a=16)
                V(I("tensor_tensor", t1, bre, zr, ALU.mult), [b_par2, b_zt], [b_tA])
                V(I("tensor_tensor", t2, bim, zi, ALU.mult), [b_par2, b_zt], [b_tB])
                V(I("tensor_sub", Bb["r"][:, d], t1, t2), [b_tA, b_tB], [b_Bb])
                V(I("tensor_tensor", t1, bim, zr, ALU.mult), [b_par2, b_zt], [b_tA])
                V(I("tensor_tensor", t2, bre, zi, ALU.mult), [b_par2, b_zt], [b_tB])
                V(I("tensor_add", Bb["i"][:, d], t1, t2), [b_tA, b_tB], [b_Bb])

            ck("S2b")

            def cmul_tab(out_r, out_i, tr, ti, Xr_, Xi_, wbs, neg_i=False):
                for hf, (eng, btA, btB) in enumerate([("vector", b_tA, b_tB), ("vector", b_tA2, b_tB2)]):
                    gs_ = slice(hf * 8, hf * 8 + 8)
                    ta = tA[:, hf * 1024:(hf + 1) * 1024]; tb = tB[:, hf * 1024:(hf + 1) * 1024]
                    trb = tr[:, :, gs_].rearrange("p k g -> p g k").unsqueeze(3).to_broadcast([128, 8, 8, 16])
                    tib = ti[:, :, gs_].rearrange("p k g -> p g k").unsqueeze(3).to_broadcast([128, 8, 8, 16])
                    Xrb = Xr_[:, gs_, :].unsqueeze(2).to_broadcast([128, 8, 8, 16])
                    Xib = Xi_[:, gs_, :].unsqueeze(2).to_broadcast([128, 8, 8, 16])
                    v4 = lambda t: t.rearrange("p (g k h) -> p g k h", g=8, k=8)
                    o4 = lambda t: t[:, gs_, :].rearrange("p g (k h) -> p g k h", k=8)
                    rd = [b_pw, b_Bb, b_par2]
                    wb = [wbs[hf]]
                    E_ = lambda fn, r, w, eng=eng: P.op(eng, fn, r, w)
                    E_(I("tensor_tensor", v4(ta), trb, Xrb, ALU.mult), rd, [btA])
                    E_(I("tensor_tensor", v4(tb), tib, Xib, ALU.mult), rd, [btB])
                    E_(I("tensor_sub", o4(out_r), v4(ta), v4(tb)), [btA, btB], wb)
                    E_(I("tensor_tensor", v4(ta), trb, Xib, ALU.mult), rd, [btA])
                    E_(I("tensor_tensor", v4(tb), tib, Xrb, ALU.mult), rd, [btB])
                    if neg_i and eng == "vector":
                        E_(I("scalar_tensor_tensor", out=o4(out_i), in0=v4(ta), scalar=-1.0, in1=v4(tb),
                             op0=ALU.mult, op1=ALU.subtract), [btA, btB], wb)
                    elif neg_i:
                        E_(I("tensor_add", v4(ta), v4(ta), v4(tb)), [btA, btB], [btA])
                        E_(I("tensor_scalar_mul", o4(out_i), v4(ta), -1.0), [btA], wb)
                    else:
                        E_(I("tensor_add", o4(out_i), v4(ta), v4(tb)), [btA, btB], wb)

            def tab(name, lo, hi, rev, d):
                t = pw[name][:, lo:hi, d * 16:(d + 1) * 16]
                return t[:, ::-1, :] if rev else t

            for d in range(2):
                rev = (d == 1)
                cmul_tab(Qr[:, d], Qn[:, d], tab("pr", 1, 9, rev, d), tab("pi", 1, 9, rev, d), cre[:, d], cim[:, d], b_Qh, neg_i=True)
                cmul_tab(X_r[:, d], X_i[:, d], tab("mr", 1, 9, rev, d), tab("mi", 1, 9, rev, d), Bb["r"][:, d], Bb["i"][:, d], b_Xh)
            ck("S2c")
            for gq4 in range(4):
                for g2_ in range(2):
                    ps_ = slice(g2_ * 64, g2_ * 64 + 64)
                    ptf, bpf = nps(); ptb, bpb = nps()
                    for (d, pt, bp) in [(0, ptf, bpf), (1, ptb, bpb)]:
                        fns = []
                        for k4 in range(4):
                            gp = gq4 * 4 + k4
                            fns.append(mm(pt[:, k4 * 128:(k4 + 1) * 128], X_r[ps_, d, gp, :], Qr[ps_, d, gp, :], True, False))
                            fns.append(mm(pt[:, k4 * 128:(k4 + 1) * 128], X_i[ps_, d, gp, :], Qn[ps_, d, gp, :], False, True))
                        MM(fns, b_Xh + b_Qh, [bp])
                    m4 = lambda m: m.unsqueeze(1).to_broadcast([128, 4, 128])
                    v3 = lambda t: t.rearrange("p (a b) -> p a b", a=4)
                    V(I("tensor_tensor", v3(tA[:, 0:512]), v3(ptf[:, :]), m4(Mf), ALU.mult), [bpf, b_cst], [b_tA])
                    V(I("tensor_tensor", v3(tB[:, 0:512]), v3(ptb[:, :]), m4(Mb), ALU.mult), [bpb, b_cst], [b_tB])
                    V(I("tensor_add", tA[:, 0:512], tA[:, 0:512], tB[:, 0:512]), [b_tA, b_tB], [b_tA])
                    for k4 in range(4):
                        g = 2 * (gq4 * 4 + k4) + g2_
                        V(I("scalar_tensor_tensor", out=Tm[:, g, :], in0=ident, scalar=dcol[:, g:g + 1],
                            in1=tA[:, k4 * 128:(k4 + 1) * 128], op0=ALU.mult, op1=ALU.add),
                          [b_tA, b_par, b_cst], [b_T])
            ck("S2d")
            for d in range(2):
                rev = (d == 1)
                cmul_tab(X_r[:, d], X_i[:, d], tab("pr", 0, 8, not rev, d), tab("pi", 0, 8, not rev, d),
                         Bb["r"][:, d], Bb["i"][:, d], b_Xh)
            for d in range(2):
                for gq4 in range(4):
                    for g2_ in range(2):
                        ps_ = slice(g2_ * 64, g2_ * 64 + 64)
                        pt, bp = nps()
                        fns = []
                        for k4 in range(4):
                            gp = gq4 * 4 + k4
                            for ri, Pm in enumerate([X_r, X_i]):
                                c0 = (k4 * 2 + ri) * 64
                                fns.append(mm(pt[:, c0:c0 + 64], Pm[ps_, d, gp, :], identb[ps_, g2_ * 64:g2_ * 64 + 64]))
                        MM(fns, b_Xh + [b_cst], [bp])
                        g0 = 2 * gq4 * 4 + g2_
                        V(I("tensor_copy", PT[:, d, g0:g0 + 7:2, :, :].rearrange("p g r n -> p g (r n)"),
                            pt[:, :].rearrange("p (g f) -> p g f", g=4)), [bp], [b_PT])
            ck("S2e")
            V(I("tensor_scalar_mul", Thu[:], th[:], 8.0), [b_der], [b_der])
            V(I("tensor_copy", tSI[:, 0:32], Thu[:]), [b_der], [b_tSI])
            V(I("tensor_copy", tS[1][:, 0:32], tSI[:, 0:32]), [b_tSI], [b_tS[1]])
            V(I("tensor_sub", Thu[:], Thu[:], tS[1][:, 0:32]), [b_tS[1], b_der], [b_der])
            A(I("activation", R8[:], lrdt[:], AF.Exp, scale=8.0), [b_der], [b_der])
            V(I("tensor_scalar_mul", tS[2][:, 0:32], Thu[:], 256.0), [b_der], [b_tS[2]])
            sincos_turns(tS[2][:, 0:32], 32, fini[:], finr[:], [b_tS[2]], [b_der])
            A(I("activation", mag2048[:], lrdt[:], AF.Exp, scale=2048.0), [b_der], [b_der])
            V(I("tensor_mul", Acr[:], mag2048[:], finr[:]), [b_der], [b_der])
            V(I("tensor_mul", Aci[:], mag2048[:], fini[:]), [b_der], [b_der])
            gate([b_Bb, b_zt], [b_E])
            for (Ec, Es, pos) in [(E0c, E0s, cpos[:, 0:16]), (E1c, E1s, c16[:])]:
                V(I("tensor_tensor", tS[2][:, 0:512].rearrange("p (g c) -> p g c", g=32),
                    Thu[:].unsqueeze(2).to_broadcast([128, 32, 16]), pos.unsqueeze(1).to_broadcast([128, 32, 16]), ALU.mult),
                  [b_der, b_cst], [b_tS[2]])
                sincos_turns(tS[2][:, 0:512], 512, Es.rearrange("p g c -> p (g c)"), Ec.rearrange("p g c -> p (g c)"), [b_tS[2]], [b_E])
            if l == 0:
                tap("Tm", Tm.rearrange("p g f -> p (g f)"), [b_T], 2048)
                tap("Qr", Qr.rearrange("p d g f -> p (d g f)"), b_Qh, 2048)
                tap("PT", PT.rearrange("p d g r n -> p (d g r n)"), [b_PT], 2048)

            ck("S2")
            for (so, kcols, vcols, kdst, vblk) in [(4, slice(128, 256), slice(384, 512), slice(0, 128), 0),
                                                   (8, slice(0, 128), slice(256, 384), slice(T + 128, T + 256), 17)]:
                for (cols, dst_ap, wb) in [(kcols, kT[:, kdst], b_k), (vcols, vtm[:, vblk, :], b_v)]:
                    acc = tS[2][:, 0:128]
                    V(I("tensor_scalar_mul", acc, halg[:, 0, cols], sel[:, so:so + 1]),
                      [b_halg, b_sel], [b_tS[2]])
                    for r in range(1, 4):
                        V(I("scalar_tensor_tensor", out=acc, in0=halg[:, r, cols],
                                                                                  scalar=sel[:, so + r:so + r + 1], in1=acc,
                                                                                  op0=ALU.mult, op1=ALU.add),
                          [b_halg, b_sel], [b_tS[2]])
                    V(I("tensor_copy", dst_ap, acc), [b_tS[2]], [wb])

            ck("halosel")
            for i in range(8):
                for g8 in range(8):
                    P.dma("sync", I("dma_start",
                        out=U[i * 16:(i + 1) * 16, :, :].rearrange("p (a g) c -> p a g c", a=4)[:, :, g8, :],
                        in_=uT2[g8 * 16:(g8 + 1) * 16, :, i, :]), reads=[b_uT2], writes=[b_U])
            if l == 0:
                tap("U", U.rearrange("p g c -> p (g c)"), [b_U], 2048)
            ck("relayout")
            gate(a1_s2, a1_p1)
            gate(wa_s2, [b_WA, b_WB])
            P.dma("gpsimd", I("dma_start", out=wglu, in_=w_glu_d[l].rearrange("(k p) f -> p k f", p=128)), writes=[b_WA])
            P.dma("gpsimd", I("dma_start", out=wout, in_=w_out_d[l].rearrange("(k p) f -> p k f", p=128)), writes=[b_WA, b_WB])

            def ssm_batch(bt, phase):
                for d in range(2):
                    c0 = d * 16 + bt * 2
                    gsl = slice(c0, c0 + 2)
                    cosT_, sinT_, btab_ = tabsets[ssm_it[0] % 2]
                    ssm_it[0] += 1
                    if btab_ is None:
                        rd_t = [b_rstd, b_sig]; wr_c = [b_rstd]; wr_s = [b_sig]
                    else:
                        rd_t = [btab_]; wr_c = [btab_]; wr_s = [btab_]
                    e1c = E1c[:, gsl, :].unsqueeze(3).to_broadcast([128, 2, 16, 16])
                    e1s = E1s[:, gsl, :].unsqueeze(3).to_broadcast([128, 2, 16, 16])
                    e0c = E0c[:, gsl, :].unsqueeze(2).to_broadcast([128, 2, 16, 16])
                    e0s = E0s[:, gsl, :].unsqueeze(2).to_broadcast([128, 2, 16, 16])
                    q4 = lambda t: t.rearrange("p (g a b) -> p g a b", g=2, a=16)
                    q3 = lambda t: t.rearrange("p g (a b) -> p g a b", a=16)
                    pa = tS[2][:, 0:512]; pb = tSI[:, 0:512].bitcast(F32)
                    G = lambda fn, r, w: P.op("gpsimd", fn, r, w)
                    G(I("tensor_tensor", q4(pa), e1c, e0c, ALU.mult), [b_E], [b_tS[2]])
                    G(I("tensor_tensor", q4(pb), e1s, e0s, ALU.mult), [b_E], [b_tSI])
                    G(I("tensor_sub", q3(cosT_), q4(pa), q4(pb)), [b_tS[2], b_tSI], wr_c)
                    G(I("tensor_tensor", q4(pa), e1s, e0c, ALU.mult), [b_E], [b_tS[2]])
                    G(I("tensor_tensor", q4(pb), e1c, e0s, ALU.mult), [b_E], [b_tSI])
                    G(I("tensor_add", q3(sinT_), q4(pa), q4(pb)), [b_tS[2], b_tSI], wr_s)
                    zb = []
                    for ri in range(2):
                        pt, bp = nps("ssm")
                        fns = []
                        for gq in range(2):
                            gp = bt * 2 + gq
                            for g2_ in range(2):
                                g = gp * 2 + g2_
                                fns.append(mm(pt[g2_ * 64:(g2_ + 1) * 64, gq * 256:(gq + 1) * 256], PT[:, d, g, ri, :], U[:, g, :]))
                        MM(fns, [b_PT, b_U], [bp])
                        zb.append((pt, bp))
                    (pzr, bzr), (pzi, bzi) = zb
                    zr_ = pzr[:, :].rearrange("p (g c) -> p g c", g=2)
                    zi_ = pzi[:, :].rearrange("p (g c) -> p g c", g=2)
                    if d == 1:
                        zr_ = zr_[:, :, ::-1]; zi_ = zi_[:, :, ::-1]
                    a3 = tS[0][:, 0:512].rearrange("p (g c) -> p g c", g=2)
                    b3 = tS[1][:, 0:512].rearrange("p (g c) -> p g c", g=2)
                    V(I("tensor_tensor", a3, zr_, cosT_, ALU.mult), [bzr, *rd_t], [b_tS[0]])
                    V(I("tensor_tensor", b3, zi_, sinT_, ALU.mult), [bzi, *rd_t], [b_tS[1]])
                    V(I("tensor_add", Wr, a3, b3), [b_tS[0], b_tS[1]], [b_W])
                    V(I("tensor_tensor", a3, zi_, cosT_, ALU.mult), [bzi, *rd_t], [b_tS[0]])
                    V(I("tensor_tensor", b3, zr_, sinT_, ALU.mult), [bzr, *rd_t], [b_tS[1]])
                    V(I("tensor_sub", Wi, a3, b3), [b_tS[0], b_tS[1]], [b_W])
                    for gq in range(2):
                        col = c0 + gq
                        for (Wt, St, ci) in [(Wr, Sr, 0), (Wi, Si, 1)]:
                            cc = ci * 32 + col
                            init = 0.0 if phase == 0 else carry[:, cc:cc + 1]
                            V(I("tensor_tensor_scan",
                                St[:, gq, :], R8[:, col:col + 1].to_broadcast([128, 256]), Wt[:, gq, :], init, ALU.mult, ALU.add),
                              [b_W, b_der, b_carry], [b_S])
                    if phase == 0:
                        fr = finr[:, gsl]; fi = fini[:, gsl]
                        s_r = Sr[:, :, 255]; s_i = Si[:, :, 255]
                        o_r = Fst[:, c0:c0 + 2]; o_i = Fst[:, 32 + c0:32 + c0 + 2]
                        t0 = tS[0][:, 0:2]; t1 = tS[1][:, 0:2]
                        V(I("tensor_mul", t0, s_r, fr), [b_S, b_der], [b_tS[0]])
                        V(I("tensor_mul", t1, s_i, fi), [b_S, b_der], [b_tS[1]])
                        V(I("tensor_sub", o_r, t0, t1), [b_tS[0], b_tS[1]], [b_F])
                        V(I("tensor_mul", t0, s_r, fi), [b_S, b_der], [b_tS[0]])
                        V(I("tensor_mul", t1, s_i, fr), [b_S, b_der], [b_tS[1]])
                        V(I("tensor_add", o_i, t0, t1), [b_tS[0], b_tS[1]], [b_F])
                    else:
                        Hr_ = Hb16[(d, "r")]; Hi_ = Hb16[(d, "i")]
                        if d == 0:
                            o_r = Hr_[:, :, 1:256]; o_i = Hi_[:, :, 1:256]
                            o_r0 = Hr_[:, :, 0]; o_i0 = Hi_[:, :, 0]
                        else:
                            o_r = Hr_[:, :, 254::-1]; o_i = Hi_[:, :, 254::-1]
                            o_r0 = Hr_[:, :, 255]; o_i0 = Hi_[:, :, 255]
                        a3s = a3[:, :, 0:255]; b3s = b3[:, :, 0:255]
                        V(I("tensor_tensor", a3s, Sr[:, :, 0:255], cosT_[:, :, 0:255], ALU.mult), [b_S, *rd_t], [b_tS[0]])
                        V(I("tensor_tensor", b3s, Si[:, :, 0:255], sinT_[:, :, 0:255], ALU.mult), [b_S, *rd_t], [b_tS[1]])
                        V(I("tensor_sub", o_r, a3s, b3s), [b_tS[0], b_tS[1]], [b_H])
                        V(I("tensor_tensor", a3s, Sr[:, :, 0:255], sinT_[:, :, 0:255], ALU.mult), [b_S, *rd_t], [b_tS[0]])
                        V(I("tensor_tensor", b3s, Si[:, :, 0:255], cosT_[:, :, 0:255], ALU.mult), [b_S, *rd_t], [b_tS[1]])
                        V(I("tensor_add", o_i, a3s, b3s), [b_tS[0], b_tS[1]], [b_H])
                        V(I("tensor_copy", o_r0, carry[:, c0:c0 + 2]), [b_carry], [b_H])
                        V(I("tensor_copy", o_i0, carry[:, 32 + c0:32 + c0 + 2]), [b_carry], [b_H])

            V(I("tensor_copy", esrow[:, :, :], esink[0:1, l, :].unsqueeze(2).to_broadcast([1, 8, 128])), [b_small], [b_esr])
            gate([b_uT2], [b_att])
            att_pending = []

            def att_flush():
                while att_pending:
                    att_pending.pop(0)()

            def att_block(qb):
                qs = slice(qb * 128, (qb + 1) * 128)
                for kvh in range(2):
                    hs_ = slice(kvh * 64, kvh * 64 + 64)
                    set_ = (qb * 2 + kvh) % 2
                    pM = pM2[set_]; b_pM = b_pM2[set_]
                    for kb in range(3):
                        pt, bp = nps("att_s")
                        MM([mm(pt[:, :].rearrange("p (j q) -> p j q", j=4), kT[hs_, (qb + kb) * 128:(qb + kb + 1) * 128], qT[hs_, :, qs])],
                           [b_k, b_q], [bp])
                        A(I("activation", pM[kb], pt[:, :], AF.Exp, scale=0.125), [bp], [b_pM[kb]])
                        V(I("tensor_tensor", pM[kb], pM[kb],
                            etab[:, kb, kvh * 4:(kvh + 1) * 4, :].rearrange("p j q -> p (j q)"), ALU.mult),
                          [b_et], [b_pM[kb]])
                    pno, bpn = nps("att_o")
                    pn = pno[:, 0:256]; pd = pno[:, 256:512]; bpd = bpn
                    fn_n = []; fn_d = []
                    for j in range(4):
                        po = slice((j % 2) * 64, (j % 2) * 64 + 64)
                        cs2 = slice((j // 2) * 128, (j // 2) * 128 + 128)
                        for kb in range(3):
                            if kb == 0 and qb == 0:
                                ol = ones_pn[:, 0, :]
                            elif kb == 2 and qb == 15:
                                ol = ones_pn[:, 1, :]
                            else:
                                ol = onesb[:, 0:64]
                            fn_n.append(mm(pn[po, cs2], vtm[:, qb + kb, hs_], pM[kb][:, j * 128:(j + 1) * 128], kb == 0, kb == 2))
                            if kb == 0:
                                fn_d.append(mm(pd[po, cs2], onesb[0:1, 0:64], esrow[0:1, kvh * 4 + j, :], True, False))
                            fn_d.append(mm(pd[po, cs2], ol, pM[kb][:, j * 128:(j + 1) * 128], False, kb == 2))
                    att_flush()
                    MM(fn_n + fn_d, [b_v, b_cst, b_esr] + b_pM, [bpn])

                    def fin(pn=pn, pd=pd, bpn=bpn, bpd=bpd, kvh=kvh, qs=qs):
                        A(I("activation", rden, pd.rearrange("p (a q) -> p a q", a=2), AF.Ln), [bpd], [b_rden])
                        A(I("activation", rden, rden, AF.Exp, scale=-1.0), [b_rden], [b_rden])
                        V(I("tensor_tensor", attT[:, kvh * 2:kvh * 2 + 2, qs],
                            pn.rearrange("p (a q) -> p a q", a=2), rden, ALU.mult),
                          [bpn, b_rden, b_uT2], [b_att])
                    att_pending.append(fin)
                att_flush()
            att_order = [1, 2, 3, 4, 5, 6, 7, 8, 9, 10, 11, 12, 13, 14, 0, 15]
            for bt in range(8):
                ssm_batch(bt, 0)
                att_block(att_order[bt])
            ck("ssm0")
            P.dma("gpsimd", I("dma_start", out=st_src, in_=Fst[:]), reads=[b_F], writes=[b_stg])
            P.op("gpsimd", I("collective_compute", "AllGather", ALU.bypass, replica_groups=GROUPS,
                                                         ins=[st_src.opt()], outs=[st_dst.opt()]), [b_stg], [b_stg])
            P.dma("gpsimd", I("dma_start", out=stg[:], in_=st_dst.rearrange("(r p) f -> p r f", p=128)),
                  reads=[b_stg], writes=[b_stg])

            ck("stx")
            if l == 0:
                tap("attT", attT.rearrange("p j t -> p (j t)"), [b_att], 2048)

            ck("att")
            V(I("memset", carry[:], 0.0), [], [b_carry])
            for d in range(2):
                cs_ = slice(d * 16, d * 16 + 16); ci_ = slice(32 + d * 16, 32 + d * 16 + 16)
                cr = hcr[:, 0:16]; cim_ = hcr[:, 16:32]; t0 = hcr[:, 32:48]; t1 = hcr[:, 48:64]
                V(I("memset", hcr[:], 0.0), [], [b_hcr])
                order = [0, 1, 2, 3] if d == 0 else [3, 2, 1, 0]
                hh = [b_hcr]
                for r in order:
                    V(I("scalar_tensor_tensor", out=carry[:, cs_], in0=cr, scalar=sel[:, r:r + 1], in1=carry[:, cs_],
                                                                     op0=ALU.mult, op1=ALU.add), [b_hcr, b_sel], [b_carry])
                    V(I("scalar_tensor_tensor", out=carry[:, ci_], in0=cim_, scalar=sel[:, r:r + 1], in1=carry[:, ci_],
                                                                     op0=ALU.mult, op1=ALU.add), [b_hcr, b_sel], [b_carry])
                    ar_ = Acr[:, cs_]; ai_ = Aci[:, cs_]
                    V(I("tensor_mul", t0, cr, ar_), [b_der], hh)
                    V(I("tensor_mul", t1, cim_, ai_), [b_der], hh)
                    V(I("tensor_sub", t0, t0, t1), [], hh)
                    V(I("tensor_mul", t1, cr, ai_), [b_der], hh)
                    V(I("tensor_mul", cim_, cim_, ar_), [b_der], hh)
                    V(I("tensor_add", cim_, cim_, t1), [], hh)
                    V(I("tensor_add", cr, t0, stg[:, r, cs_]), [b_stg], hh)
                    V(I("tensor_add", cim_, cim_, stg[:, r, ci_]), [b_stg], hh)

            for bt in range(8):
                ssm_batch(bt, 1)
                att_block(att_order[8 + bt])
                for gq in range(2):
                    gp = bt * 2 + gq
                    pt, bp = nps("ssm")
                    fns = []
                    for g2_ in range(2):
                        g = gp * 2 + g2_
                        ps_ = slice(g2_ * 64, g2_ * 64 + 64)
                        oc = pt[:, g2_ * 256:g2_ * 256 + 256]
                        fns.append(mm(oc, Tm[:, g, :], U[:, g, :], True, False))
                        for d in range(2):
                            fns.append(mm(oc, Qr[ps_, d, gp, :], Hb16[(d, "r")][ps_, gq, :], False, False))
                            fns.append(mm(oc, Qn[ps_, d, gp, :], Hb16[(d, "i")][ps_, gq, :], False, d == 1))
                    MM(fns, [b_T, b_U, b_H] + b_Qh, [bp])
                    A(I("activation", Yall[:, gp * 2:gp * 2 + 2, :].rearrange("p g c -> p (g c)"), pt[:, :],
                                                           AF.Gelu_apprx_tanh), [bp, b_Yall], [b_Yb[bt]])
            if l == 0:
                tap("Yall", Yall.rearrange("p g c -> p (g c)"), b_Yb, 2048)
            ck("ssm1")
            gate(xr_mixer, allx)
            for k in range(8):
                P.dma("gpsimd", I("dma_start", out=xT[:, k, :], in_=xsp[k * 128:(k + 1) * 128, :]), writes=b_x[k])
            att_flush()
            b_yTok = Buf()
            gate([b_tab, b_W, b_S, b_H], [b_yTok])
            gate([b_q], [b_yT2])
            yTok = A1[:, 8192:16384].rearrange("p (k a j g h) -> p k a j g h", k=2, a=4, j=8, g=8)
            ev = [0]

            def evac(out_ap, in_ap, r, w):
                if ev[0] % 2 == 0:
                    A(I("activation", out_ap, in_ap, AF.Copy), r, w)
                else:
                    V(I("tensor_copy", out_ap, in_ap), r, w)
                ev[0] += 1
            for cblk in range(2):
                for g4 in range(8):
                    pt, bp = nps()
                    MM([mm(pt[:, k4 * 128:(k4 + 1) * 128], Yall[:, g4 * 4 + k4, cblk * 128:(cblk + 1) * 128], identb[:, :])
                        for k4 in range(4)], b_Yb + [b_cst], [bp])
                    evac(yTok[:, cblk, g4 // 2, :, (g4 % 2) * 4:(g4 % 2) * 4 + 4, :],
                         pt[:, :].rearrange("p (k j h) -> p j k h", k=4, j=8), [bp], [b_yTok])
            for a_ in range(4):
                for jp in range(4):
                    pt, bp = nps()
                    fns = []
                    for jj in range(2):
                        j = jp * 2 + jj
                        for cblk in range(2):
                            c0 = (jj * 2 + cblk) * 128
                            fns.append(mm(pt[:, c0:c0 + 128], yTok[:, cblk, a_, j, :, :].rearrange("p g h -> p (g h)"), identb[:, :]))
                    MM(fns, [b_yTok, b_cst], [bp])
                    evac(yT2[:, a_, jp * 2:jp * 2 + 2, :].rearrange("p j c -> p (j c)"), pt[:, :], [bp], [b_yT2])
            gate([b_yTok], [b_glu])
            yflat = yT2.rearrange("p a j c -> p a (j c)")
            for cb in range(4):
                cs_ = slice(cb * 512, (cb + 1) * 512)
                for oc in range(4):
                    pv, bpv = nps(); pg, bpg = nps()
                    MM([mm(pv[:, :], wglu[:, k, oc * 128:(oc + 1) * 128], yflat[:, k, cs_], k == 0, k == 3) for k in range(4)],
                       [b_WA, b_yT2], [bpv])
                    MM([mm(pg[:, :], wglu[:, k, 512 + oc * 128:512 + (oc + 1) * 128], yflat[:, k, cs_], k == 0, k == 3) for k in range(4)],
                       [b_WA, b_yT2], [bpg])
                    A(I("activation", sig[:], pg[:, :], AF.Sigmoid), [bpg], [b_sig])
                    dst = gluT[:, oc, :].rearrange("p (c j) -> p j c", j=8)[:, 2 * cb:2 * cb + 2, :]
                    V(I("tensor_tensor", dst, pv[:, :].rearrange("p (j c) -> p j c", j=2),
                                                                sig[:, :].rearrange("p (j c) -> p j c", j=2), ALU.mult),
                      [bpv, b_sig], [b_glu])
            if l == 0:
                tap("gluT", gluT.rearrange("p j t -> p (j t)"), [b_glu], 2048)
            ck("glu")
            for tt in range(NTT):
                ts = slice(tt * 512, (tt + 1) * 512)
                for m in range(8):
                    pt, bp = nps()
                    fns = [mm(pt[:, :], wout[:, k, m * 128:(m + 1) * 128], attT[:, k, ts], k == 0, False) for k in range(4)]
                    fns += [mm(pt[:, :], wout[:, 4 + k, m * 128:(m + 1) * 128], gluT[:, k, ts], False, k == 3) for k in range(4)]
                    MM(fns, [b_WA, b_WB, b_att, b_glu], [bp])
                    V(I("tensor_add", xT[:, m, ts], xT[:, m, ts], pt[:, :]), [bp], [b_x[m][tt]])
            if l == 0:
                tap("xmid", xT[:, 0, :], b_x[0], 2048)

            ck("wout")
            gate(a1_p1, b_h)
            gate([b_att, b_yT2, b_uT2, b_q], [])
            gate(misc_att, misc_ffn)
            gate([b_E], [b_WB])
            for tt in range(NTT):
                rms_tile(l, tt, g2)
            b_ws = [b_WA, b_WB]
            for gI in range(8):
                s = gI % 2
                w1, w2 = wsl[s]
                P.dma("gpsimd", I("dma_start",
                    out=w1, in_=w_ff1_d[l][:, gI * 512:(gI + 1) * 512].rearrange("(k p) f -> p k f", p=128)), writes=[b_ws[s]])
                P.dma("gpsimd", I("dma_start",
                    out=w2, in_=w_ff2_d[l][gI * 512:(gI + 1) * 512, :].rearrange("(k p) f -> p k f", p=128)), writes=[b_ws[s]])
                for tt in range(NTT):
                    ts = slice(tt * 512, (tt + 1) * 512)
                    for fc in range(4):
                        pt, bp = nps()
                        MM([mm(pt[:, :], w1[:, k, fc * 128:(fc + 1) * 128], hT[:, k, ts], k == 0, k == 7) for k in range(8)],
                           [b_ws[s], b_h[tt]], [bp])
                        ri_ = fc % 2
                        A(I("activation", rl[ri_], pt[:, :], AF.Relu), [bp], [b_rl[ri_]])
                        V(I("tensor_mul", aT[:, fc, :], rl[ri_], rl[ri_]), [b_rl[ri_]], [b_a[fc]])
                    for m in range(8):
                        pt, bp = nps()
                        MM([mm(pt[:, :], w2[:, k, m * 128:(m + 1) * 128], aT[:, k, :], k == 0, k == 3) for k in range(4)],
                           [b_ws[s]] + b_a, [bp])
                        V(I("tensor_add", xT[:, m, ts], xT[:, m, ts], pt[:, :]), [bp], [b_x[m][tt]])
            gate(misc_ffn, misc_att)

        try:
            for l in range(nlayers):
                layer(l)
        except _Stop:
            gate(xr_mixer, allx)
        emit_taps()
        for k in range(8):
            P.dma("sync", I("dma_start", out=out_d[k * 128:(k + 1) * 128, :], in_=xT[:, k, :]), reads=b_x[k])
        P.wait_all("sync", allx + b_tS)
        P.finish("sync")
        P.run()
    return nc


def _prep_inputs(x, norm1, w_in, q_gain, k_gain, sink, lam_re, lam_im, log_dt, b_re, b_im, c_re, c_im,
                 d_skip, w_glu, w_out, norm2, w_ff1, w_ff2, nlayers=DEPTH):
    f = lambda a: np.ascontiguousarray(np.asarray(a, dtype=np.float32))
    L = DEPTH
    perm = []
    for j in range(4):
        perm += list(range(j * 64, j * 64 + 64)) + list(range((4 + j) * 64, (4 + j) * 64 + 64))
    perm += list(range(512, 1280))
    shared = {
        "w_in": f(np.asarray(w_in)[:nlayers][:, :, perm]), "w_glu": f(np.asarray(w_glu)[:nlayers]),
        "w_out": f(np.asarray(w_out)[:nlayers]),
        "w_ff1": f(np.asarray(w_ff1)[:nlayers]), "w_ff2": f(np.asarray(w_ff2)[:nlayers]),
        "g1": f(np.asarray(norm1).reshape(L, 8, 128).transpose(2, 0, 1)),
        "g2": f(np.asarray(norm2).reshape(L, 8, 128).transpose(2, 0, 1)),
        "qg": f(np.tile(np.asarray(q_gain).T, (2, 1))),
        "kg": f(np.tile(np.asarray(k_gain).T, (2, 1))),
        "sinkr": f(np.broadcast_to(np.asarray(sink)[None], (128, L, 8))),
    }

    def gn(a):
        a = np.asarray(a).reshape(L, 2, 16, 2, 64)
        return f(a.transpose(3, 4, 0, 1, 2).reshape(128, L, 32))
    shared["lre"] = gn(lam_re)
    shared["lim"] = gn(lam_im)
    shared["ldt"] = gn(np.broadcast_to(np.asarray(log_dt)[:, :, :, None], (L, 2, 32, 64)))

    def bb(a):
        a = np.asarray(a).reshape(L, 16, 2, 64, 16)
        return f(a.transpose(2, 3, 0, 1, 4).reshape(128, L, 256))
    shared["bre"] = bb(b_re)
    shared["bim"] = bb(b_im)

    def cc(a):
        a = np.asarray(a).reshape(L, 2, 16, 2, 16, 64)
        return f(a.transpose(3, 5, 0, 1, 2, 4).reshape(128, L, 512))
    shared["cre"] = cc(c_re)
    shared["cim"] = cc(c_im)
    dsk = np.asarray(d_skip).reshape(L, 32, 16)
    shared["dcol"] = f(np.broadcast_to(dsk.transpose(2, 0, 1)[None], (8, 16, L, 32)).reshape(128, L, 32))
    slopes = np.exp2(-8.0 * np.arange(1, 9) / 8.0)
    ci = np.arange(128)[:, None]; qi = np.arange(128)[None, :]
    et = np.zeros((128, 3, 8, 128), np.float32)
    for kb in range(3):
        dist = np.abs(qi - ci - (kb - 1) * 128)
        valid = dist <= 128
        for h in range(8):
            et[:, kb, h, :] = np.where(valid, np.exp(-slopes[h] * dist), 0.0)
    shared["etab"] = f(et.reshape(128, 3072))
    cst = np.zeros((128, 656), np.float32)
    cst[:, 0:128] = np.eye(128)
    bi = np.arange(128)[:, None] // 16; bj = np.arange(128)[None, :] // 16
    cst[:, 128:256] = (bj >= bi)
    cst[:, 256:384] = (bi >= bj)
    cst[:, 384:393] = np.arange(9)[None, :]
    cst[:, 400:656] = np.arange(1, 257)[None, :]
    shared["cst"] = cst
    xs = np.asarray(x)
    in_maps = []
    for r in range(8):
        b, q = r // 4, r % 4
        m = dict(shared)
        m["xT"] = f(xs[b, q * T:(q + 1) * T, :].T)
        s = np.zeros((128, 16), np.float32)
        s[:, q] = 1.0
        if q > 0:
            s[:, 4 + q - 1] = 1.0; s[:, 12] = 1.0
        if q < 3:
            s[:, 8 + q + 1] = 1.0; s[:, 13] = 1.0
        m["sel"] = s
        in_maps.append(m)
    return in_maps


_NC_CACHE = {}


def kernel(**inputs):
    in_maps = _prep_inputs(**inputs)
    if "nc" not in _NC_CACHE:
        _NC_CACHE["nc"] = build_nc()
    nc = _NC_CACHE["nc"]
    res = run_bass_kernel_spmd(nc, in_maps, core_ids=list(range(8)))
    out = np.zeros((2, 4 * T, 1024), np.float32)
    for r in range(8):
        b, q = r // 4, r % 4
        out[b, q * T:(q + 1) * T, :] = np.asarray(res.results[r]["outT"]).T
    return out
```

```python
import math
import os
import numpy as np
from contextlib import ExitStack
import concourse.bass as bass
import concourse.mybir as mybir
from concourse.bass_utils import run_bass_kernel_spmd

F32 = mybir.dt.float32
BF16 = mybir.dt.bfloat16
I32 = mybir.dt.int32
AF = mybir.ActivationFunctionType
ALU = mybir.AluOpType

DEPTH = 4
T = 2048
NTT = 4
EPS = 1e-6
TWO_PI = 2.0 * math.pi


class Buf:
    def __init__(self, name=""):
        self.name = name
        self.w = None
        self.r = []


class EngQ:
    def __init__(self, name, sem):
        self.name = name
        self.sem = sem
        self.count = 0
        self.ops = []
        self.seen = {}


class Prog:
    ENGS = ["sync", "scalar", "gpsimd", "vector", "tensor"]

    def __init__(self, nc, es, ndma=16):
        self.nc = nc
        self.q = {e: EngQ(e, es.enter_context(nc.semaphore("s_" + e))) for e in self.ENGS}
        self.dq = {e: [EngQ(f"d_{e}{k}", es.enter_context(nc.semaphore(f"d_{e}{k}"))) for k in range(ndma)]
                   for e in ["sync", "gpsimd"]}
        self.rr = {e: 0 for e in self.dq}
        self.nops = 0

    def _waits(self, eng, reads, writes):
        q = self.q[eng]
        need = {}

        def add(tok):
            if tok is None:
                return
            s, v = tok
            if eng == "tensor" and s is q:
                return
            if need.get(s, 0) < v:
                need[s] = v
        for b in reads:
            add(b.w)
        for b in writes:
            add(b.w)
            for t in b.r:
                add(t)
        for s, v in need.items():
            if q.seen.get(s, 0) >= v:
                continue
            q.seen[s] = v
            q.ops.append(lambda e, s=s, v=v: e.wait_ge(s.sem, v))

    def _mark(self, tok, reads, writes):
        for b in reads:
            b.r = [t for t in b.r if t[0] is not tok[0]] + [tok]
        for b in writes:
            b.w = tok
            b.r = []

    def op(self, eng, fn, reads=(), writes=()):
        return self.group(eng, [fn], reads, writes)

    def group(self, eng, fns, reads=(), writes=()):
        q = self.q[eng]
        self._waits(eng, reads, writes)
        for fn in fns[:-1]:
            q.ops.append(lambda e, fn=fn: fn(e))
        q.count += 1
        tok = (q, q.count)
        q.ops.append(lambda e, fn=fns[-1], q=q: fn(e).then_inc(q.sem, 1))
        self._mark(tok, reads, writes)
        self.nops += len(fns)
        return tok

    def dma(self, eng, fn, reads=(), writes=()):
        self._waits(eng, reads, writes)
        k = self.rr[eng]
        self.rr[eng] = (k + 1) % len(self.dq[eng])
        d = self.dq[eng][k]
        q_ = self.q[eng]
        if d.count > 0 and q_.seen.get(d, 0) < d.count:
            q_.seen[d] = d.count
            q_.ops.append(lambda e, d=d, v=d.count: e.wait_ge(d.sem, v))
        d.count += 16
        tok = (d, d.count)
        self.q[eng].ops.append(lambda e, fn=fn, d=d: fn(e).then_inc(d.sem, 16))
        self._mark(tok, reads, writes)
        self.nops += 1
        return tok

    def wait_all(self, eng, bufs):
        self._waits(eng, [], bufs)

    def finish(self, eng="sync"):
        q = self.q[eng]
        allq = [x for x in self.q.values() if x is not q] + [d for ds in self.dq.values() for d in ds]
        for s_ in allq:
            if s_.count > 0:
                q.ops.append(lambda e, s_=s_, v=s_.count: e.wait_ge(s_.sem, v))

    def run(self):
        with self.nc.Block() as block:
            for e in self.ENGS:
                ops = self.q[e].ops

                def body(eng, ops=ops):
                    for o in ops:
                        o(eng)
                getattr(block, e)(body)


class _Stop(Exception):
    pass


def build_nc(nlayers=DEPTH, taps=None, stop_after=None):
    nc = bass.Bass("TRN2", target_bir_lowering=False)
    L = DEPTH

    def din(name, shape, dt=F32):
        return nc.dram_tensor(name, list(shape), dt, kind="ExternalInput").ap()

    xT_d = din("xT", [1024, T])
    LW = nlayers
    w_in_d = din("w_in", [LW, 1024, 1280])
    w_glu_d = din("w_glu", [LW, 512, 1024])
    w_out_d = din("w_out", [LW, 1024, 1024])
    w_ff1_d = din("w_ff1", [LW, 1024, 4096])
    w_ff2_d = din("w_ff2", [LW, 4096, 1024])
    g1_d = din("g1", [128, L, 8])
    g2_d = din("g2", [128, L, 8])
    qg_d = din("qg", [128, L])
    kg_d = din("kg", [128, L])
    sink_d = din("sinkr", [128, L, 8])
    lre_d = din("lre", [128, L, 32])
    lim_d = din("lim", [128, L, 32])
    ldt_d = din("ldt", [128, L, 32])
    bre_d = din("bre", [128, L, 256])
    bim_d = din("bim", [128, L, 256])
    cre_d = din("cre", [128, L, 512])
    cim_d = din("cim", [128, L, 512])
    dcol_d = din("dcol", [128, L, 32])
    etab_d = din("etab", [128, 3072])
    cst_d = din("cst", [128, 656])
    sel_d = din("sel", [128, 16])
    out_d = nc.dram_tensor("outT", [1024, T], F32, kind="ExternalOutput").ap()
    taps = list(taps) if taps else []
    dbg_d = nc.dram_tensor("dbg", [max(1, len(taps)), 128, 2048], F32, kind="ExternalOutput").ap() if taps else None

    xsp = nc.dram_tensor("xsp", [1024, T], F32).ap()
    hal_src = nc.dram_tensor("hal_src", [128, 512], F32).ap()
    hal_dst = nc.dram_tensor("hal_dst", [4 * 128, 512], F32).ap()
    st_src = nc.dram_tensor("st_src", [128, 64], F32).ap()
    st_dst = nc.dram_tensor("st_dst", [4 * 128, 64], F32).ap()
    GROUPS = [[0, 1, 2, 3], [4, 5, 6, 7]]

    with ExitStack() as es:
        P = Prog(nc, es)

        def sb(name, shape, dt=F32):
            return es.enter_context(nc.sbuf_tensor(name, list(shape), dt))

        XR = sb("XR", [128, 16384], F32)
        XRb = XR[:, :].bitcast(BF16)
        A1 = sb("A1", [128, 16384], BF16)
        A1f = A1[:, :].bitcast(F32)
        WA = sb("WA", [128, 16384], BF16)
        WAf = WA[:, :].bitcast(F32)
        A3 = sb("A3", [128, 8192], BF16)
        A4 = sb("A4", [128, 8192], BF16)
        MISC = sb("MISC", [128, 4096], BF16)

        xT = XR[:, :].rearrange("p (k t) -> p k t", k=8)
        b_x = [[Buf() for _ in range(NTT)] for _ in range(8)]
        allx = [b for row in b_x for b in row]
        U = XRb[:, 0:8192].rearrange("p (g c) -> p g c", g=32); b_U = Buf()
        PT = XRb[:, 8192:16384].rearrange("p (d g r n) -> p d g r n", d=2, g=32, r=2); b_PT = Buf()
        Qr = XRb[:, 16384:20480].rearrange("p (d g f) -> p d g f", d=2, g=16)
        Qn = XRb[:, 20480:24576].rearrange("p (d g f) -> p d g f", d=2, g=16); b_Qh = [Buf(), Buf()]
        Tm = XRb[:, 24576:28672].rearrange("p (g f) -> p g f", g=32); b_T = Buf()
        etab = XRb[:, 28672:31744].rearrange("p (k h q) -> p k h q", k=3, h=8); b_et = Buf()
        xr_mixer = [b_U, b_PT, b_T, b_et] + b_Qh

        hT = A1[:, :].rearrange("p (k t) -> p k t", k=8); b_h = [Buf() for _ in range(NTT)]
        bre = A1f[:, 0:256].rearrange("p (g h) -> p g h", g=16)
        bim = A1f[:, 256:512].rearrange("p (g h) -> p g h", g=16)
        cre = A1f[:, 512:1024].rearrange("p (d g h) -> p d g h", d=2, g=16)
        cim = A1f[:, 1024:1536].rearrange("p (d g h) -> p d g h", d=2, g=16)
        b_par2 = Buf()
        halg = A1f[:, 1536:3584].rearrange("p (r f) -> p r f", r=4); b_halg = Buf()
        tA = A1f[:, 4096:6144]; tB = A1f[:, 6144:8192]; b_tA = Buf(); b_tB = Buf(); b_tA2 = Buf(); b_tB2 = Buf()
        Yall = A1[:, 0:8192].rearrange("p (g c) -> p g c", g=32); b_Yall = Buf()
        yT2 = A3[:, :].rearrange("p (a j c) -> p a j c", a=4, j=8); b_yT2 = Buf()
        cosT = A1f[:, 4096:4608].rearrange("p (g c) -> p g c", g=2)
        sinT = A1f[:, 4608:5120].rearrange("p (g c) -> p g c", g=2); b_tab = Buf()
        Wr = A1f[:, 5120:5632].rearrange("p (g c) -> p g c", g=2)
        Wi = A1f[:, 5632:6144].rearrange("p (g c) -> p g c", g=2); b_W = Buf()
        Sr = A1f[:, 6144:6656].rearrange("p (g c) -> p g c", g=2)
        Si = A1f[:, 6656:7168].rearrange("p (g c) -> p g c", g=2); b_S = Buf()
        Hb16 = {}
        for d in range(2):
            for ni, n in enumerate("ri"):
                o = 14336 + (d * 2 + ni) * 512
                Hb16[(d, n)] = A1[:, o:o + 512].rearrange("p (g c) -> p g c", g=2)
        b_H = Buf()
        a1_s2 = [b_par2, b_halg, b_tA, b_tB, b_tA2, b_tB2]
        b_Yb = [Buf() for _ in range(8)]

        win = WA[:, 0:10240].rearrange("p (k f) -> p k f", k=8); b_WA = Buf(); b_WB = Buf()
        X_r = WA[:, 0:4096].rearrange("p (d g f) -> p d g f", d=2, g=16)
        X_i = WA[:, 4096:8192].rearrange("p (d g f) -> p d g f", d=2, g=16); b_Xh = [Buf(), Buf()]
        pw = {n: WAf[:, 4096 + i * 288:4096 + (i + 1) * 288].rearrange("p (k c) -> p k c", k=9)
              for i, n in enumerate(["pr", "pi", "mr", "mi"])}
        b_pw = Buf()
        cs_s = WAf[:, 5248:5536]; cs_c = WAf[:, 5536:5824]
        mg_p = WAf[:, 5824:6112]; mg_m = WAf[:, 6112:6400]
        Bb = {"r": WAf[:, 6400:6912].rearrange("p (d g h) -> p d g h", d=2, g=16),
              "i": WAf[:, 6912:7424].rearrange("p (d g h) -> p d g h", d=2, g=16)}
        b_Bb = Buf()
        zt = [WAf[:, 7424 + i * 32:7424 + (i + 1) * 32] for i in range(6)]; b_zt = Buf()
        wa_s2 = [b_pw, b_Bb, b_zt] + b_Xh
        E0c = WAf[:, 6144:6656].rearrange("p (g c) -> p g c", g=32)
        E0s = WAf[:, 6656:7168].rearrange("p (g c) -> p g c", g=32)
        E1c = WAf[:, 7168:7680].rearrange("p (g c) -> p g c", g=32)
        E1s = WAf[:, 7680:8192].rearrange("p (g c) -> p g c", g=32)
        b_E = Buf()
        wglu = WA[:, 0:4096].rearrange("p (k f) -> p k f", k=4)
        wout = WA[:, 4096:12288].rearrange("p (k f) -> p k f", k=8)
        wsl = [(WA[:, s * 8192:s * 8192 + 4096].rearrange("p (k f) -> p k f", k=8),
                WA[:, s * 8192 + 4096:s * 8192 + 8192].rearrange("p (k f) -> p k f", k=4)) for s in range(2)]

        qT = A3[:, :].rearrange("p (j t) -> p j t", j=4); b_q = Buf()
        gluT = A1[:, 8192:16384].rearrange("p (j t) -> p j t", j=4); b_glu = Buf()
        a1_p1 = b_Yb + [b_Yall, b_glu, b_tab, b_W, b_S, b_H]
        uT2 = A4[:, :].rearrange("p (a i c) -> p a i c", a=4, i=8); b_uT2 = Buf()
        attT = A4[:, :].rearrange("p (j t) -> p j t", j=4); b_att = Buf()

        pM2 = [[MISC[:, (s_ * 3 + i) * 512:(s_ * 3 + i + 1) * 512] for i in range(3)] for s_ in range(2)]
        b_pM2 = [[Buf() for _ in range(3)] for _ in range(2)]
        rden = MISC[:, 3072:3584].bitcast(F32).rearrange("p (a q) -> p a q", a=2); b_rden = Buf()
        rl = [MISC[:, i * 512:(i + 1) * 512] for i in range(2)]; b_rl = [Buf(), Buf()]
        aT = MISC[:, 1024:3072].rearrange("p (k t) -> p k t", k=4); b_a = [Buf() for _ in range(4)]
        misc_att = b_pM2[0] + b_pM2[1] + [b_rden]
        misc_ffn = b_rl + b_a

        kT = sb("kT", [128, T + 256], BF16); b_k = Buf()
        vtm = sb("vtm", [128, 18, 128], BF16); b_v = Buf()
        cst = sb("cst_sb", [128, 656]); b_cst = Buf()
        cstb = sb("cstb_sb", [128, 128], BF16)
        onesb = sb("onesb", [128, 128], BF16)
        blk2 = sb("blk2", [128, 128], BF16)
        ones_pn = sb("ones_pn", [128, 2, 64], BF16)
        sel = sb("sel_sb", [128, 16]); b_sel = Buf()
        g1 = sb("g1_sb", [128, L, 8]); g2 = sb("g2_sb", [128, L, 8])
        qg = sb("qg_sb", [128, L]); kg = sb("kg_sb", [128, L])
        sinkr = sb("sink_sb", [128, L * 8]); esink = sb("esink", [128, L, 8])
        b_small = Buf()
        epsb = sb("epsb", [128, 1])
        lre = sb("lre_sb", [128, 32]); lim = sb("lim_sb", [128, 32]); ldt = sb("ldt_sb", [128, 32])
        dcol = sb("dcol_sb", [128, 32]); b_par = Buf()
        lrdt = sb("lrdt", [128, 32]); th = sb("th", [128, 32]); Thu = sb("Thu", [128, 32])
        R8 = sb("R8", [128, 32]); Acr = sb("Acr", [128, 32]); Aci = sb("Aci", [128, 32])
        finr = sb("finr", [128, 32]); fini = sb("fini", [128, 32]); mag2048 = sb("mag2048", [128, 32])
        b_der = Buf()
        tS = [sb(f"tS{i}", [128, 512]) for i in range(3)]; b_tS = [Buf() for _ in range(3)]
        tSI = sb("tSI", [128, 512], I32); b_tSI = Buf()
        Fst = sb("Fst", [128, 64]); b_F = Buf()
        stg = sb("stg", [128, 4, 64]); b_stg = Buf()
        carry = sb("carry", [128, 64]); b_carry = Buf()
        hcr = sb("hcr", [128, 64]); b_hcr = Buf()
        halt = sb("halt", [128, 512]); b_halt = Buf()
        sq = sb("sq", [128, 2, 512], BF16); b_sq = [Buf(), Buf()]
        rstd = sb("rstd", [128, 512]); b_rstd = Buf()
        sig = sb("sig", [128, 512]); b_sig = Buf()
        gatec = sb("gatec", [128, 8]); b_gate = Buf()

        psb = [es.enter_context(nc.psum_tensor(f"ps{i}", [128, 512], F32)) for i in range(8)]
        b_ps = [Buf() for _ in range(8)]
        ps_pools = {"all": list(range(8)), "ssm": [0, 1, 2], "att_s": [5, 6, 7], "att_o": [3, 4]}
        ps_rr = {k: 0 for k in ps_pools}

        def nps(pool="all"):
            pool = "all"
            lst = ps_pools[pool]
            i = lst[ps_rr[pool] % len(lst)]
            ps_rr[pool] += 1
            return psb[i], b_ps[i]

        def V(fn, r=(), w=()):
            return P.op("vector", fn, r, w)

        def A(fn, r=(), w=()):
            return P.op("scalar", fn, r, w)

        def MM(fns, r=(), w=()):
            return P.group("tensor", fns, r, w)

        def I(method, *a, **k):
            return lambda e: getattr(e, method)(*a, **k)

        def mm(out, lhsT, rhs, start=True, stop=True):
            return I("matmul", out, lhsT=lhsT, rhs=rhs, start=start, stop=stop)

        def gate(old, new):
            V(I("memset", gatec[:, 0:1], 0.0), [], list(old) + list(new) + [b_gate])

        tap_i = [0]

        deferred_taps = []

        def tap(name, ap, bufs, n):
            if name in taps:
                deferred_taps.append((name, ap, bufs, n))

        def emit_taps():
            if not deferred_taps:
                return
            P.finish("vector")
            P.finish("sync")
            for (name, ap, bufs, n) in deferred_taps:
                idx = taps.index(name)
                for c0 in range(0, n, 512):
                    c1 = min(n, c0 + 512)
                    V(I("tensor_copy", tS[2][:, 0:c1 - c0], ap[:, c0:c1]), list(bufs), [b_tS[2]])
                    P.dma("sync", I("dma_start", out=dbg_d[idx, :, c0:c1], in_=tS[2][:, 0:c1 - c0]), reads=[b_tS[2]])

        P.dma("sync", I("dma_start", out=cst[:], in_=cst_d), writes=[b_cst])
        P.dma("sync", I("dma_start", out=sel[:], in_=sel_d), writes=[b_sel])
        for (t_sb, t_d) in [(g1, g1_d), (g2, g2_d), (qg, qg_d), (kg, kg_d)]:
            P.dma("sync", I("dma_start", out=t_sb[:], in_=t_d), writes=[b_small])
        P.dma("sync", I("dma_start", out=sinkr[:], in_=sink_d.rearrange("p l h -> p (l h)")), writes=[b_small])
        for k in range(8):
            P.dma("sync", I("dma_start", out=xT[:, k, :], in_=xT_d[k * 128:(k + 1) * 128, :]),
                  writes=b_x[k])
        ident = cst[:, 0:128]; Mf = cst[:, 128:256]; Mb = cst[:, 256:384]; kvec = cst[:, 384:393]
        cpos = cst[:, 400:656]
        V(I("tensor_copy", cstb[:], ident), [b_cst], [b_cst])
        V(I("memset", onesb[:], 1.0), [], [b_cst])
        V(I("memset", blk2[:], 0.0), [], [b_cst])
        V(I("memset", blk2[0:64, 0:64], 1.0), [], [b_cst])
        V(I("memset", blk2[64:128, 64:128], 1.0), [], [b_cst])
        V(I("memset", epsb[:], EPS), [], [b_cst])
        V(I("tensor_copy", ones_pn[:, 0, :], sel[:, 12:13].to_broadcast([128, 64])), [b_sel], [b_cst])
        V(I("tensor_copy", ones_pn[:, 1, :], sel[:, 13:14].to_broadcast([128, 64])), [b_sel], [b_cst])
        A(I("activation", esink[:, :, :].rearrange("p l h -> p (l h)"), sinkr[:], AF.Exp), [b_small], [b_small])
        identb = cstb
        c16 = sb("c16", [128, 16])
        V(I("tensor_scalar", c16[:], cpos[:, 0:16], -1.0, 16.0, ALU.add, ALU.mult), [b_cst], [b_cst])
        tabsets = [(cosT, sinT, b_tab), (rstd[:, :].rearrange("p (g c) -> p g c", g=2), sig[:, :].rearrange("p (g c) -> p g c", g=2), None)]
        ssm_it = [0]
        esrow = sb("esrow", [1, 8, 128], BF16); b_esr = Buf()

        def rms_tile(l, tt, gain):
            ts = slice(tt * 512, (tt + 1) * 512)
            pt, bp = nps()
            for k in range(8):
                s = k % 2
                A(I("activation", sq[:, s, :], xT[:, k, ts], AF.Square), [b_x[k][tt]], [b_sq[s]])
                P.op("tensor", mm(pt[:, :], onesb[:, :], sq[:, s, :], start=(k == 0), stop=(k == 7)), [b_sq[s], b_cst], [bp])
            A(I("activation", rstd[:], pt[:, :], AF.Ln, bias=epsb[:], scale=1.0 / 1024.0), [bp, b_cst], [b_rstd])
            A(I("activation", rstd[:], rstd[:], AF.Exp, scale=-0.5), [b_rstd], [b_rstd])
            for k in range(8):
                V(I("scalar_tensor_tensor", out=hT[:, k, ts], in0=xT[:, k, ts], scalar=gain[:, l, k:k + 1],
                                                        in1=rstd[:], op0=ALU.mult, op1=ALU.mult),
                  [b_x[k][tt], b_rstd, b_small], [b_h[tt]])

        def sincos_turns(tin, n, out_s, out_c, rb, wb):
            a_ = tS[0][:, 0:n]; b_ = tS[1][:, 0:n]; i_ = tSI[:, 0:n]
            for (off, outp) in [(0.0, out_s), (0.25, out_c)]:
                V(I("tensor_scalar_add", a_, tin, off), rb, [b_tS[0]])
                V(I("tensor_copy", i_, a_), [b_tS[0]], [b_tSI])
                V(I("tensor_copy", b_, i_), [b_tSI], [b_tS[1]])
                V(I("tensor_sub", a_, a_, b_), [b_tS[0], b_tS[1]], [b_tS[0]])
                V(I("tensor_scalar", a_, a_, -0.4999999, 0.4999999, ALU.max, ALU.min), [b_tS[0]], [b_tS[0]])
                A(I("activation", outp, a_, AF.Sin, scale=TWO_PI), [b_tS[0]], wb)

        def ck(name):
            if stop_after == name:
                raise _Stop()

        if os.environ.get("HALO_FIRST"):
            V(I("memset", halt[:], 1.0), [], [b_halt])
            P.dma("gpsimd", I("dma_start", out=hal_src, in_=halt[:]), reads=[b_halt], writes=[b_halg])
            P.op("gpsimd", I("collective_compute", "AllGather", ALU.bypass, replica_groups=GROUPS,
                             ins=[hal_src.opt()], outs=[hal_dst.opt()]), [b_halg], [b_halg])
            P.dma("gpsimd", I("dma_start", out=halg, in_=hal_dst.rearrange("(r p) f -> p r f", p=128)),
                  reads=[b_halg], writes=[b_halg])
            if os.environ.get("HALO_FIRST") == "only":
                raise_stop = True

        def layer(l):
            P.dma("gpsimd", I("dma_start", out=win, in_=w_in_d[l].rearrange("(k p) f -> p k f", p=128)),
                  writes=[b_WA, b_WB])
            for (t_sb, t_d) in [(lre, lre_d), (lim, lim_d), (ldt, ldt_d), (dcol, dcol_d)]:
                P.dma("sync", I("dma_start", out=t_sb[:], in_=t_d[:, l]), writes=[b_par])

            for tt in range(NTT):
                ts = slice(tt * 512, (tt + 1) * 512)
                rms_tile(l, tt, g1)
                for k in range(8 if not os.environ.get("NOSPILL") else 0):
                    P.dma("sync", I("dma_start", out=xsp[k * 128:(k + 1) * 128, ts], in_=xT[:, k, ts]),
                          reads=[b_x[k][tt]])
                for oc in range(5):
                    pt, bp = nps()
                    MM([mm(pt[:, :], win[:, k, oc * 128:(oc + 1) * 128], hT[:, k, ts], k == 0, k == 7) for k in range(8)],
                       [b_WA, b_h[tt]], [bp])
                    A(I("activation", sq[:, 0, :], pt[:, :], AF.Square), [bp], [b_sq[0]])
                    p2, bp2 = nps()
                    MM([mm(p2[:, :], blk2[:, :], sq[:, 0, :])], [b_sq[0], b_cst], [bp2])
                    A(I("activation", rstd[:], p2[:, :], AF.Ln, bias=epsb[:], scale=1.0 / 64.0),
                      [bp2, b_cst], [b_rstd])
                    A(I("activation", rstd[:], rstd[:], AF.Exp, scale=-0.5), [b_rstd], [b_rstd])
                    if oc < 4:
                        V(I("scalar_tensor_tensor", out=qT[:, oc, ts], in0=pt[:, :], scalar=qg[:, l:l + 1],
                                                                               in1=rstd[:], op0=ALU.mult, op1=ALU.mult),
                          [bp, b_rstd, b_small], [b_q])
                    else:
                        V(I("scalar_tensor_tensor", out=kT[:, 128 + tt * 512:128 + (tt + 1) * 512], in0=pt[:, :],
                                                                         scalar=kg[:, l:l + 1], in1=rstd[:], op0=ALU.mult, op1=ALU.mult),
                          [bp, b_rstd, b_small], [b_k])
                        if tt == 0:
                            V(I("scalar_tensor_tensor", out=halt[:, 0:128], in0=pt[:, 0:128], scalar=kg[:, l:l + 1],
                                                                      in1=rstd[:, 0:128], op0=ALU.mult, op1=ALU.mult),
                              [bp, b_rstd, b_small], [b_halt])
                        if tt == NTT - 1:
                            V(I("scalar_tensor_tensor", out=halt[:, 128:256], in0=pt[:, 384:512], scalar=kg[:, l:l + 1],
                                                                      in1=rstd[:, 384:512], op0=ALU.mult, op1=ALU.mult),
                              [bp, b_rstd, b_small], [b_halt])
                pt, bp = nps()
                fns = []
                for b4 in range(4):
                    for k in range(8):
                        fns.append(mm(pt[:, b4 * 128:(b4 + 1) * 128], hT[:, k, tt * 512 + b4 * 128: tt * 512 + (b4 + 1) * 128],
                                      win[:, k, 640:768], k == 0, k == 7))
                MM(fns, [b_WA, b_h[tt]], [bp])
                A(I("activation", vtm[:, 1 + tt * 4:1 + (tt + 1) * 4, :].rearrange("p b f -> p (b f)"),
                                                       pt[:, :], AF.Copy), [bp], [b_v])
                if tt == 0:
                    V(I("tensor_copy", halt[:, 256:384], pt[:, 0:128]), [bp], [b_halt])
                if tt == NTT - 1:
                    V(I("tensor_copy", halt[:, 384:512], pt[:, 384:512]), [bp], [b_halt])
                for a_ in range(4):
                    oc = 6 + a_
                    pt, bp = nps()
                    MM([mm(pt[:, :], win[:, k, oc * 128:(oc + 1) * 128], hT[:, k, ts], k == 0, k == 7) for k in range(8)],
                       [b_WA, b_h[tt]], [bp])
                    V(I("tensor_copy", uT2[:, a_, :, tt * 64:(tt + 1) * 64], pt[:, :].rearrange("p (c i) -> p i c", i=8)),
                      [bp], [b_uT2])
            if l == 0:
                tap("hT", hT.rearrange("p k t -> p (k t)"), b_h, 2048)
                tap("qT", qT.rearrange("p j t -> p (j t)"), [b_q], 2048)
                tap("kT", kT[:, 128:128 + 2048], [b_k], 2048)
                tap("uT2", uT2.rearrange("p a i c -> p (a i c)"), [b_uT2], 2048)

            ck("S1")
            gate(b_h, a1_s2)
            P.dma("gpsimd", I("dma_start", out=hal_src, in_=halt[:]), reads=[b_halt], writes=[b_halg])
            if not os.environ.get("NOCC"):
                P.op("gpsimd", I("collective_compute", "AllGather", ALU.bypass, replica_groups=GROUPS,
                                 ins=[hal_src.opt()], outs=[hal_dst.opt()]), [b_halg], [b_halg])
            P.dma("gpsimd", I("dma_start", out=halg, in_=hal_dst.rearrange("(r p) f -> p r f", p=128)),
                  reads=[b_halg], writes=[b_halg])

            ck("halo")
            gate([b_WA, b_WB, b_E], wa_s2)
            gate(allx, xr_mixer)
            P.dma("gpsimd", I("dma_start", out=etab.rearrange("p k h q -> p (k h q)"), in_=etab_d), writes=[b_et])
            for (t_ap, t_d) in [(bre, bre_d), (bim, bim_d), (cre, cre_d), (cim, cim_d)]:
                P.dma("sync", I("dma_start",
                    out=t_ap.rearrange("p g h -> p (g h)") if len(t_ap.shape) == 3 else t_ap.rearrange("p d g h -> p (d g h)"),
                    in_=t_d[:, l]), writes=[b_par2])
            A(I("activation", lrdt[:], ldt[:], AF.Exp), [b_par], [b_der])
            V(I("tensor_mul", th[:], lim[:], lrdt[:]), [b_par, b_der], [b_der])
            V(I("tensor_mul", lrdt[:], lre[:], lrdt[:]), [b_par, b_der], [b_der])
            V(I("tensor_scalar_mul", th[:], th[:], 1.0 / TWO_PI), [b_der], [b_der])
            kb9 = kvec.unsqueeze(2).to_broadcast([128, 9, 32])
            ph9 = tS[2][:, 0:288]
            V(I("tensor_tensor", ph9.rearrange("p (k c) -> p k c", k=9), th[:].unsqueeze(1).to_broadcast([128, 9, 32]),
                                        kb9, ALU.mult), [b_der, b_cst], [b_tS[2]])
            sincos_turns(ph9, 288, cs_s, cs_c, [b_tS[2]], [b_pw])
            V(I("tensor_tensor", ph9.rearrange("p (k c) -> p k c", k=9), lrdt[:].unsqueeze(1).to_broadcast([128, 9, 32]),
                                        kb9, ALU.mult), [b_der, b_cst, b_pw], [b_tS[2]])
            A(I("activation", mg_p, ph9, AF.Exp), [b_tS[2]], [b_pw])
            A(I("activation", mg_m, ph9, AF.Exp, scale=-1.0), [b_tS[2]], [b_pw])
            fl = lambda t: t.rearrange("p k c -> p (k c)")
            V(I("tensor_mul", fl(pw["pr"]), mg_p, cs_c), [b_pw], [b_pw])
            V(I("tensor_mul", fl(pw["pi"]), mg_p, cs_s), [b_pw], [b_pw])
            V(I("tensor_mul", fl(pw["mr"]), mg_m, cs_c), [b_pw], [b_pw])
            V(I("scalar_tensor_tensor", out=fl(pw["mi"]), in0=mg_m, scalar=-1.0, in1=cs_s,
                                               op0=ALU.mult, op1=ALU.mult), [b_pw], [b_pw])
            ck("S2a")
            ar1, ai1 = pw["pr"][:, 1, :], pw["pi"][:, 1, :]
            zin = [b_zt, b_par, b_pw]
            V(I("tensor_mul", zt[0], lre[:], lre[:]), zin, [b_zt])
            V(I("tensor_mul", zt[1], lim[:], lim[:]), zin, [b_zt])
            V(I("tensor_add", zt[0], zt[0], zt[1]), zin, [b_zt])
            V(I("reciprocal", zt[0], zt[0]), zin, [b_zt])
            V(I("tensor_scalar_add", zt[1], ar1, -1.0), zin, [b_zt])
            V(I("tensor_mul", zt[2], zt[1], lre[:]), zin, [b_zt])
            V(I("tensor_mul", zt[3], ai1, lim[:]), zin, [b_zt])
            V(I("tensor_add", zt[2], zt[2], zt[3]), zin, [b_zt])
            V(I("tensor_mul", zt[2], zt[2], zt[0]), zin, [b_zt])
            V(I("tensor_mul", zt[3], ai1, lre[:]), zin, [b_zt])
            V(I("tensor_mul", zt[4], zt[1], lim[:]), zin, [b_zt])
            V(I("tensor_sub", zt[3], zt[3], zt[4]), zin, [b_zt])
            V(I("tensor_mul", zt[3], zt[3], zt[0]), zin, [b_zt])
            for d in range(2):
                zr = zt[2][:, d * 16:(d + 1) * 16].unsqueeze(2).to_broadcast([128, 16, 16])
                zi = zt[3][:, d * 16:(d + 1) * 16].unsqueeze(2).to_broadcast([128, 16, 16])
                t1 = tA[:, 0:256].rearrange("p (a b) -> p a b", a=16)
                t2 = tB[:, 0:256].rearrange("p (a b) -> p a b", a=16)
                V(I("tensor_tensor", t1, bre, zr, ALU.mult), [b_par2, b_zt], [b_tA])
                V(I("tensor_tensor", t2, bim, zi, ALU.mult), [b_par2, b_zt], [b_tB])
                V(I("tensor_sub", Bb["r"][:, d], t1, t2), [b_tA, b_tB], [b_Bb])
                V(I("tensor_tensor", t1, bim, zr, ALU.mult), [b_par2, b_zt], [b_tA])
                V(I("tensor_tensor", t2, bre, zi, ALU.mult), [b_par2, b_zt], [b_tB])
                V(I("tensor_add", Bb["i"][:, d], t1, t2), [b_tA, b_tB], [b_Bb])

            ck("S2b")

            def cmul_tab(out_r, out_i, tr, ti, Xr_, Xi_, wbs, neg_i=False):
                for hf, (eng, btA, btB) in enumerate([("vector", b_tA, b_tB), ("vector", b_tA2, b_tB2)]):
                    gs_ = slice(hf * 8, hf * 8 + 8)
                    ta = tA[:, hf * 1024:(hf + 1) * 1024]; tb = tB[:, hf * 1024:(hf + 1) * 1024]
                    trb = tr[:, :, gs_].rearrange("p k g -> p g k").unsqueeze(3).to_broadcast([128, 8, 8, 16])
                    tib = ti[:, :, gs_].rearrange("p k g -> p g k").unsqueeze(3).to_broadcast([128, 8, 8, 16])
                    Xrb = Xr_[:, gs_, :].unsqueeze(2).to_broadcast([128, 8, 8, 16])
                    Xib = Xi_[:, gs_, :].unsqueeze(2).to_broadcast([128, 8, 8, 16])
                    v4 = lambda t: t.rearrange("p (g k h) -> p g k h", g=8, k=8)
                    o4 = lambda t: t[:, gs_, :].rearrange("p g (k h) -> p g k h", k=8)
                    rd = [b_pw, b_Bb, b_par2]
                    wb = [wbs[hf]]
                    E_ = lambda fn, r, w, eng=eng: P.op(eng, fn, r, w)
                    E_(I("tensor_tensor", v4(ta), trb, Xrb, ALU.mult), rd, [btA])
                    E_(I("tensor_tensor", v4(tb), tib, Xib, ALU.mult), rd, [btB])
                    E_(I("tensor_sub", o4(out_r), v4(ta), v4(tb)), [btA, btB], wb)
                    E_(I("tensor_tensor", v4(ta), trb, Xib, ALU.mult), rd, [btA])
                    E_(I("tensor_tensor", v4(tb), tib, Xrb, ALU.mult), rd, [btB])
                    if neg_i and eng == "vector":
                        E_(I("scalar_tensor_tensor", out=o4(out_i), in0=v4(ta), scalar=-1.0, in1=v4(tb),
                             op0=ALU.mult, op1=ALU.subtract), [btA, btB], wb)
                    elif neg_i:
                        E_(I("tensor_add", v4(ta), v4(ta), v4(tb)), [btA, btB], [btA])
                        E_(I("tensor_scalar_mul", o4(out_i), v4(ta), -1.0), [btA], wb)
                    else:
                        E_(I("tensor_add", o4(out_i), v4(ta), v4(tb)), [btA, btB], wb)

            def tab(name, lo, hi, rev, d):
                t = pw[name][:, lo:hi, d * 16:(d + 1) * 16]
                return t[:, ::-1, :] if rev else t

            for d in range(2):
                rev = (d == 1)
                cmul_tab(Qr[:, d], Qn[:, d], tab("pr", 1, 9, rev, d), tab("pi", 1, 9, rev, d), cre[:, d], cim[:, d], b_Qh, neg_i=True)
                cmul_tab(X_r[:, d], X_i[:, d], tab("mr", 1, 9, rev, d), tab("mi", 1, 9, rev, d), Bb["r"][:, d], Bb["i"][:, d], b_Xh)
            ck("S2c")
            for gq4 in range(4):
                for g2_ in range(2):
                    ps_ = slice(g2_ * 64, g2_ * 64 + 64)
                    ptf, bpf = nps(); ptb, bpb = nps()
                    for (d, pt, bp) in [(0, ptf, bpf), (1, ptb, bpb)]:
                        fns = []
                        for k4 in range(4):
                            gp = gq4 * 4 + k4
                            fns.append(mm(pt[:, k4 * 128:(k4 + 1) * 128], X_r[ps_, d, gp, :], Qr[ps_, d, gp, :], True, False))
                            fns.append(mm(pt[:, k4 * 128:(k4 + 1) * 128], X_i[ps_, d, gp, :], Qn[ps_, d, gp, :], False, True))
                        MM(fns, b_Xh + b_Qh, [bp])
                    m4 = lambda m: m.unsqueeze(1).to_broadcast([128, 4, 128])
                    v3 = lambda t: t.rearrange("p (a b) -> p a b", a=4)
                    V(I("tensor_tensor", v3(tA[:, 0:512]), v3(ptf[:, :]), m4(Mf), ALU.mult), [bpf, b_cst], [b_tA])
                    V(I("tensor_tensor", v3(tB[:, 0:512]), v3(ptb[:, :]), m4(Mb), ALU.mult), [bpb, b_cst], [b_tB])
                    V(I("tensor_add", tA[:, 0:512], tA[:, 0:512], tB[:, 0:512]), [b_tA, b_tB], [b_tA])
                    for k4 in range(4):
                        g = 2 * (gq4 * 4 + k4) + g2_
                        V(I("scalar_tensor_tensor", out=Tm[:, g, :], in0=ident, scalar=dcol[:, g:g + 1],
                            in1=tA[:, k4 * 128:(k4 + 1) * 128], op0=ALU.mult, op1=ALU.add),
                          [b_tA, b_par, b_cst], [b_T])
            ck("S2d")
            for d in range(2):
                rev = (d == 1)
                cmul_tab(X_r[:, d], X_i[:, d], tab("pr", 0, 8, not rev, d), tab("pi", 0, 8, not rev, d),
                         Bb["r"][:, d], Bb["i"][:, d], b_Xh)
            for d in range(2):
                for gq4 in range(4):
                    for g2_ in range(2):
                        ps_ = slice(g2_ * 64, g2_ * 64 + 64)
                        pt, bp = nps()
                        fns = []
                        for k4 in range(4):
                            gp = gq4 * 4 + k4
                            for ri, Pm in enumerate([X_r, X_i]):
                                c0 = (k4 * 2 + ri) * 64
                                fns.append(mm(pt[:, c0:c0 + 64], Pm[ps_, d, gp, :], identb[ps_, g2_ * 64:g2_ * 64 + 64]))
                        MM(fns, b_Xh + [b_cst], [bp])
                        g0 = 2 * gq4 * 4 + g2_
                        V(I("tensor_copy", PT[:, d, g0:g0 + 7:2, :, :].rearrange("p g r n -> p g (r n)"),
                            pt[:, :].rearrange("p (g f) -> p g f", g=4)), [bp], [b_PT])
            ck("S2e")
            V(I("tensor_scalar_mul", Thu[:], th[:], 8.0), [b_der], [b_der])
            V(I("tensor_copy", tSI[:, 0:32], Thu[:]), [b_der], [b_tSI])
            V(I("tensor_copy", tS[1][:, 0:32], tSI[:, 0:32]), [b_tSI], [b_tS[1]])
            V(I("tensor_sub", Thu[:], Thu[:], tS[1][:, 0:32]), [b_tS[1], b_der], [b_der])
            A(I("activation", R8[:], lrdt[:], AF.Exp, scale=8.0), [b_der], [b_der])
            V(I("tensor_scalar_mul", tS[2][:, 0:32], Thu[:], 256.0), [b_der], [b_tS[2]])
            sincos_turns(tS[2][:, 0:32], 32, fini[:], finr[:], [b_tS[2]], [b_der])
            A(I("activation", mag2048[:], lrdt[:], AF.Exp, scale=2048.0), [b_der], [b_der])
            V(I("tensor_mul", Acr[:], mag2048[:], finr[:]), [b_der], [b_der])
            V(I("tensor_mul", Aci[:], mag2048[:], fini[:]), [b_der], [b_der])
            gate([b_Bb, b_zt], [b_E])
            for (Ec, Es, pos) in [(E0c, E0s, cpos[:, 0:16]), (E1c, E1s, c16[:])]:
                V(I("tensor_tensor", tS[2][:, 0:512].rearrange("p (g c) -> p g c", g=32),
                    Thu[:].unsqueeze(2).to_broadcast([128, 32, 16]), pos.unsqueeze(1).to_broadcast([128, 32, 16]), ALU.mult),
                  [b_der, b_cst], [b_tS[2]])
                sincos_turns(tS[2][:, 0:512], 512, Es.rearrange("p g c -> p (g c)"), Ec.rearrange("p g c -> p (g c)"), [b_tS[2]], [b_E])
            if l == 0:
                tap("Tm", Tm.rearrange("p g f -> p (g f)"), [b_T], 2048)
                tap("Qr", Qr.rearrange("p d g f -> p (d g f)"), b_Qh, 2048)
                tap("PT", PT.rearrange("p d g r n -> p (d g r n)"), [b_PT], 2048)

            ck("S2")
            for (so, kcols, vcols, kdst, vblk) in [(4, slice(128, 256), slice(384, 512), slice(0, 128), 0),
                                                   (8, slice(0, 128), slice(256, 384), slice(T + 128, T + 256), 17)]:
                for (cols, dst_ap, wb) in [(kcols, kT[:, kdst], b_k), (vcols, vtm[:, vblk, :], b_v)]:
                    acc = tS[2][:, 0:128]
                    V(I("tensor_scalar_mul", acc, halg[:, 0, cols], sel[:, so:so + 1]),
                      [b_halg, b_sel], [b_tS[2]])
                    for r in range(1, 4):
                        V(I("scalar_tensor_tensor", out=acc, in0=halg[:, r, cols],
                                                                                  scalar=sel[:, so + r:so + r + 1], in1=acc,
                                                                                  op0=ALU.mult, op1=ALU.add),
                          [b_halg, b_sel], [b_tS[2]])
                    V(I("tensor_copy", dst_ap, acc), [b_tS[2]], [wb])

            ck("halosel")
            for i in range(8):
                for g8 in range(8):
                    P.dma("sync", I("dma_start",
                        out=U[i * 16:(i + 1) * 16, :, :].rearrange("p (a g) c -> p a g c", a=4)[:, :, g8, :],
                        in_=uT2[g8 * 16:(g8 + 1) * 16, :, i, :]), reads=[b_uT2], writes=[b_U])
            if l == 0:
                tap("U", U.rearrange("p g c -> p (g c)"), [b_U], 2048)
            ck("relayout")
            gate(a1_s2, a1_p1)
            gate(wa_s2, [b_WA, b_WB])
            P.dma("gpsimd", I("dma_start", out=wglu, in_=w_glu_d[l].rearrange("(k p) f -> p k f", p=128)), writes=[b_WA])
            P.dma("gpsimd", I("dma_start", out=wout, in_=w_out_d[l].rearrange("(k p) f -> p k f", p=128)), writes=[b_WA, b_WB])

            def ssm_batch(bt, phase):
                for d in range(2):
                    c0 = d * 16 + bt * 2
                    gsl = slice(c0, c0 + 2)
                    cosT_, sinT_, btab_ = tabsets[ssm_it[0] % 2]
                    ssm_it[0] += 1
                    if btab_ is None:
                        rd_t = [b_rstd, b_sig]; wr_c = [b_rstd]; wr_s = [b_sig]
                    else:
                        rd_t = [btab_]; wr_c = [btab_]; wr_s = [btab_]
                    e1c = E1c[:, gsl, :].unsqueeze(3).to_broadcast([128, 2, 16, 16])
                    e1s = E1s[:, gsl, :].unsqueeze(3).to_broadcast([128, 2, 16, 16])
                    e0c = E0c[:, gsl, :].unsqueeze(2).to_broadcast([128, 2, 16, 16])
                    e0s = E0s[:, gsl, :].unsqueeze(2).to_broadcast([128, 2, 16, 16])
                    q4 = lambda t: t.rearrange("p (g a b) -> p g a b", g=2, a=16)
                    q3 = lambda t: t.rearrange("p g (a b) -> p g a b", a=16)
                    pa = tS[2][:, 0:512]; pb = tSI[:, 0:512].bitcast(F32)
                    G = lambda fn, r, w: P.op("gpsimd", fn, r, w)
                    G(I("tensor_tensor", q4(pa), e1c, e0c, ALU.mult), [b_E], [b_tS[2]])
                    G(I("tensor_tensor", q4(pb), e1s, e0s, ALU.mult), [b_E], [b_tSI])
                    G(I("tensor_sub", q3(cosT_), q4(pa), q4(pb)), [b_tS[2], b_tSI], wr_c)
                    G(I("tensor_tensor", q4(pa), e1s, e0c, ALU.mult), [b_E], [b_tS[2]])
                    G(I("tensor_tensor", q4(pb), e1c, e0s, ALU.mult), [b_E], [b_tSI])
                    G(I("tensor_add", q3(sinT_), q4(pa), q4(pb)), [b_tS[2], b_tSI], wr_s)
                    zb = []
                    for ri in range(2):
                        pt, bp = nps("ssm")
                        fns = []
                        for gq in range(2):
                            gp = bt * 2 + gq
                            for g2_ in range(2):
                                g = gp * 2 + g2_
                                fns.append(mm(pt[g2_ * 64:(g2_ + 1) * 64, gq * 256:(gq + 1) * 256], PT[:, d, g, ri, :], U[:, g, :]))
                        MM(fns, [b_PT, b_U], [bp])
                        zb.append((pt, bp))
                    (pzr, bzr), (pzi, bzi) = zb
                    zr_ = pzr[:, :].rearrange("p (g c) -> p g c", g=2)
                    zi_ = pzi[:, :].rearrange("p (g c) -> p g c", g=2)
                    if d == 1:
                        zr_ = zr_[:, :, ::-1]; zi_ = zi_[:, :, ::-1]
                    a3 = tS[0][:, 0:512].rearrange("p (g c) -> p g c", g=2)
                    b3 = tS[1][:, 0:512].rearrange("p (g c) -> p g c", g=2)
                    V(I("tensor_tensor", a3, zr_, cosT_, ALU.mult), [bzr, *rd_t], [b_tS[0]])
                    V(I("tensor_tensor", b3, zi_, sinT_, ALU.mult), [bzi, *rd_t], [b_tS[1]])
                    V(I("tensor_add", Wr, a3, b3), [b_tS[0], b_tS[1]], [b_W])
                    V(I("tensor_tensor", a3, zi_, cosT_, ALU.mult), [bzi, *rd_t], [b_tS[0]])
                    V(I("tensor_tensor", b3, zr_, sinT_, ALU.mult), [bzr, *rd_t], [b_tS[1]])
                    V(I("tensor_sub", Wi, a3, b3), [b_tS[0], b_tS[1]], [b_W])
                    for gq in range(2):
                        col = c0 + gq
                        for (Wt, St, ci) in [(Wr, Sr, 0), (Wi, Si, 1)]:
                            cc = ci * 32 + col
                            init = 0.0 if phase == 0 else carry[:, cc:cc + 1]
                            V(I("tensor_tensor_scan",
                                St[:, gq, :], R8[:, col:col + 1].to_broadcast([128, 256]), Wt[:, gq, :], init, ALU.mult, ALU.add),
                              [b_W, b_der, b_carry], [b_S])
                    if phase == 0:
                        fr = finr[:, gsl]; fi = fini[:, gsl]
                        s_r = Sr[:, :, 255]; s_i = Si[:, :, 255]
                        o_r = Fst[:, c0:c0 + 2]; o_i = Fst[:, 32 + c0:32 + c0 + 2]
                        t0 = tS[0][:, 0:2]; t1 = tS[1][:, 0:2]
                        V(I("tensor_mul", t0, s_r, fr), [b_S, b_der], [b_tS[0]])
                        V(I("tensor_mul", t1, s_i, fi), [b_S, b_der], [b_tS[1]])
                        V(I("tensor_sub", o_r, t0, t1), [b_tS[0], b_tS[1]], [b_F])
                        V(I("tensor_mul", t0, s_r, fi), [b_S, b_der], [b_tS[0]])
                        V(I("tensor_mul", t1, s_i, fr), [b_S, b_der], [b_tS[1]])
                        V(I("tensor_add", o_i, t0, t1), [b_tS[0], b_tS[1]], [b_F])
                    else:
                        Hr_ = Hb16[(d, "r")]; Hi_ = Hb16[(d, "i")]
                        if d == 0:
                            o_r = Hr_[:, :, 1:256]; o_i = Hi_[:, :, 1:256]
                            o_r0 = Hr_[:, :, 0]; o_i0 = Hi_[:, :, 0]
                        else:
                            o_r = Hr_[:, :, 254::-1]; o_i = Hi_[:, :, 254::-1]
                            o_r0 = Hr_[:, :, 255]; o_i0 = Hi_[:, :, 255]
                        a3s = a3[:, :, 0:255]; b3s = b3[:, :, 0:255]
                        V(I("tensor_tensor", a3s, Sr[:, :, 0:255], cosT_[:, :, 0:255], ALU.mult), [b_S, *rd_t], [b_tS[0]])
                        V(I("tensor_tensor", b3s, Si[:, :, 0:255], sinT_[:, :, 0:255], ALU.mult), [b_S, *rd_t], [b_tS[1]])
                        V(I("tensor_sub", o_r, a3s, b3s), [b_tS[0], b_tS[1]], [b_H])
                        V(I("tensor_tensor", a3s, Sr[:, :, 0:255], sinT_[:, :, 0:255], ALU.mult), [b_S, *rd_t], [b_tS[0]])
                        V(I("tensor_tensor", b3s, Si[:, :, 0:255], cosT_[:, :, 0:255], ALU.mult), [b_S, *rd_t], [b_tS[1]])
                        V(I("tensor_add", o_i, a3s, b3s), [b_tS[0], b_tS[1]], [b_H])
                        V(I("tensor_copy", o_r0, carry[:, c0:c0 + 2]), [b_carry], [b_H])
                        V(I("tensor_copy", o_i0, carry[:, 32 + c0:32 + c0 + 2]), [b_carry], [b_H])

            V(I("tensor_copy", esrow[:, :, :], esink[0:1, l, :].unsqueeze(2).to_broadcast([1, 8, 128])), [b_small], [b_esr])
            gate([b_uT2], [b_att])
            att_state = {}

            def att_scores(qb):
                qs = slice(qb * 128, (qb + 1) * 128)
                for kvh in range(2):
                    hs_ = slice(kvh * 64, kvh * 64 + 64)
                    pM = pM2[kvh]; b_pM = b_pM2[kvh]
                    for kb in range(3):
                        pt, bp = nps()
                        MM([mm(pt[:, :].rearrange("p (j q) -> p j q", j=4), kT[hs_, (qb + kb) * 128:(qb + kb + 1) * 128], qT[hs_, :, qs])],
                           [b_k, b_q], [bp])
                        A(I("activation", pM[kb], pt[:, :], AF.Exp, scale=0.125), [bp], [b_pM[kb]])

            def att_mid(qb):
                for kvh in range(2):
                    hs_ = slice(kvh * 64, kvh * 64 + 64)
                    pM = pM2[kvh]; b_pM = b_pM2[kvh]
                    for kb in range(3):
                        V(I("tensor_tensor", pM[kb], pM[kb],
                            etab[:, kb, kvh * 4:(kvh + 1) * 4, :].rearrange("p j q -> p (j q)"), ALU.mult),
                          [b_et], [b_pM[kb]])
                    pno, bpn = nps()
                    pn = pno[:, 0:256]; pd = pno[:, 256:512]
                    fn_n = []; fn_d = []
                    for j in range(4):
                        po = slice((j % 2) * 64, (j % 2) * 64 + 64)
                        cs2 = slice((j // 2) * 128, (j // 2) * 128 + 128)
                        for kb in range(3):
                            if kb == 0 and qb == 0:
                                ol = ones_pn[:, 0, :]
                            elif kb == 2 and qb == 15:
                                ol = ones_pn[:, 1, :]
                            else:
                                ol = onesb[:, 0:64]
                            fn_n.append(mm(pn[po, cs2], vtm[:, qb + kb, hs_], pM[kb][:, j * 128:(j + 1) * 128], kb == 0, kb == 2))
                            if kb == 0:
                                fn_d.append(mm(pd[po, cs2], onesb[0:1, 0:64], esrow[0:1, kvh * 4 + j, :], True, False))
                            fn_d.append(mm(pd[po, cs2], ol, pM[kb][:, j * 128:(j + 1) * 128], False, kb == 2))
                    MM(fn_n + fn_d, [b_v, b_cst, b_esr] + b_pM, [bpn])
                    att_state[(qb, kvh)] = (pn, pd, bpn)

            def att_fin(qb):
                qs = slice(qb * 128, (qb + 1) * 128)
                for kvh in range(2):
                    pn, pd, bpn = att_state.pop((qb, kvh))
                    A(I("activation", rden, pd.rearrange("p (a q) -> p a q", a=2), AF.Ln), [bpn], [b_rden])
                    A(I("activation", rden, rden, AF.Exp, scale=-1.0), [b_rden], [b_rden])
                    V(I("tensor_tensor", attT[:, kvh * 2:kvh * 2 + 2, qs],
                        pn.rearrange("p (a q) -> p a q", a=2), rden, ALU.mult),
                      [bpn, b_rden, b_uT2], [b_att])

            def att_flush():
                pass

            att_order = [1, 2, 3, 4, 5, 6, 7, 8, 9, 10, 11, 12, 13, 14, 0, 15]
            att_scores(att_order[0])
            for bt in range(8):
                att_mid(att_order[bt])
                ssm_batch(bt, 0)
                att_fin(att_order[bt])
                att_scores(att_order[bt + 1])
            ck("ssm0")
            P.dma("gpsimd", I("dma_start", out=st_src, in_=Fst[:]), reads=[b_F], writes=[b_stg])
            P.op("gpsimd", I("collective_compute", "AllGather", ALU.bypass, replica_groups=GROUPS,
                                                         ins=[st_src.opt()], outs=[st_dst.opt()]), [b_stg], [b_stg])
            P.dma("gpsimd", I("dma_start", out=stg[:], in_=st_dst.rearrange("(r p) f -> p r f", p=128)),
                  reads=[b_stg], writes=[b_stg])

            ck("stx")
            if l == 0:
                tap("attT", attT.rearrange("p j t -> p (j t)"), [b_att], 2048)

            ck("att")
            V(I("memset", carry[:], 0.0), [], [b_carry])
            for d in range(2):
                cs_ = slice(d * 16, d * 16 + 16); ci_ = slice(32 + d * 16, 32 + d * 16 + 16)
                cr = hcr[:, 0:16]; cim_ = hcr[:, 16:32]; t0 = hcr[:, 32:48]; t1 = hcr[:, 48:64]
                V(I("memset", hcr[:], 0.0), [], [b_hcr])
                order = [0, 1, 2, 3] if d == 0 else [3, 2, 1, 0]
                hh = [b_hcr]
                for r in order:
                    V(I("scalar_tensor_tensor", out=carry[:, cs_], in0=cr, scalar=sel[:, r:r + 1], in1=carry[:, cs_],
                                                                     op0=ALU.mult, op1=ALU.add), [b_hcr, b_sel], [b_carry])
                    V(I("scalar_tensor_tensor", out=carry[:, ci_], in0=cim_, scalar=sel[:, r:r + 1], in1=carry[:, ci_],
                                                                     op0=ALU.mult, op1=ALU.add), [b_hcr, b_sel], [b_carry])
                    ar_ = Acr[:, cs_]; ai_ = Aci[:, cs_]
                    V(I("tensor_mul", t0, cr, ar_), [b_der], hh)
                    V(I("tensor_mul", t1, cim_, ai_), [b_der], hh)
                    V(I("tensor_sub", t0, t0, t1), [], hh)
                    V(I("tensor_mul", t1, cr, ai_), [b_der], hh)
                    V(I("tensor_mul", cim_, cim_, ar_), [b_der], hh)
                    V(I("tensor_add", cim_, cim_, t1), [], hh)
                    V(I("tensor_add", cr, t0, stg[:, r, cs_]), [b_stg], hh)
                    V(I("tensor_add", cim_, cim_, stg[:, r, ci_]), [b_stg], hh)

            for bt in range(8):
                att_mid(att_order[8 + bt])
                ssm_batch(bt, 1)
                att_fin(att_order[8 + bt])
                if bt < 7:
                    att_scores(att_order[9 + bt])
                for gq in range(2):
                    gp = bt * 2 + gq
                    pt, bp = nps("ssm")
                    fns = []
                    for g2_ in range(2):
                        g = gp * 2 + g2_
                        ps_ = slice(g2_ * 64, g2_ * 64 + 64)
                        oc = pt[:, g2_ * 256:g2_ * 256 + 256]
                        fns.append(mm(oc, Tm[:, g, :], U[:, g, :], True, False))
                        for d in range(2):
                            fns.append(mm(oc, Qr[ps_, d, gp, :], Hb16[(d, "r")][ps_, gq, :], False, False))
                            fns.append(mm(oc, Qn[ps_, d, gp, :], Hb16[(d, "i")][ps_, gq, :], False, d == 1))
                    MM(fns, [b_T, b_U, b_H] + b_Qh, [bp])
                    A(I("activation", Yall[:, gp * 2:gp * 2 + 2, :].rearrange("p g c -> p (g c)"), pt[:, :],
                                                           AF.Gelu_apprx_tanh), [bp, b_Yall], [b_Yb[bt]])
            if l == 0:
                tap("Yall", Yall.rearrange("p g c -> p (g c)"), b_Yb, 2048)
            ck("ssm1")
            gate(xr_mixer, allx)
            for k in range(8):
                P.dma("gpsimd", I("dma_start", out=xT[:, k, :], in_=xsp[k * 128:(k + 1) * 128, :]), writes=b_x[k])
            att_flush()
            b_yTok = Buf()
            gate([b_tab, b_W, b_S, b_H], [b_yTok])
            gate([b_q], [b_yT2])
            yTok = A1[:, 8192:16384].rearrange("p (k a j g h) -> p k a j g h", k=2, a=4, j=8, g=8)
            ev = [0]

            def evac(out_ap, in_ap, r, w):
                if ev[0] % 2 == 0:
                    A(I("activation", out_ap, in_ap, AF.Copy), r, w)
                else:
                    V(I("tensor_copy", out_ap, in_ap), r, w)
                ev[0] += 1
            for cblk in range(2):
                for g4 in range(8):
                    pt, bp = nps()
                    MM([mm(pt[:, k4 * 128:(k4 + 1) * 128], Yall[:, g4 * 4 + k4, cblk * 128:(cblk + 1) * 128], identb[:, :])
                        for k4 in range(4)], b_Yb + [b_cst], [bp])
                    evac(yTok[:, cblk, g4 // 2, :, (g4 % 2) * 4:(g4 % 2) * 4 + 4, :],
                         pt[:, :].rearrange("p (k j h) -> p j k h", k=4, j=8), [bp], [b_yTok])
            for a_ in range(4):
                for jp in range(4):
                    pt, bp = nps()
                    fns = []
                    for jj in range(2):
                        j = jp * 2 + jj
                        for cblk in range(2):
                            c0 = (jj * 2 + cblk) * 128
                            fns.append(mm(pt[:, c0:c0 + 128], yTok[:, cblk, a_, j, :, :].rearrange("p g h -> p (g h)"), identb[:, :]))
                    MM(fns, [b_yTok, b_cst], [bp])
                    evac(yT2[:, a_, jp * 2:jp * 2 + 2, :].rearrange("p j c -> p (j c)"), pt[:, :], [bp], [b_yT2])
            gate([b_yTok], [b_glu])
            yflat = yT2.rearrange("p a j c -> p a (j c)")
            for cb in range(4):
                cs_ = slice(cb * 512, (cb + 1) * 512)
                for oc in range(4):
                    pv, bpv = nps(); pg, bpg = nps()
                    MM([mm(pv[:, :], wglu[:, k, oc * 128:(oc + 1) * 128], yflat[:, k, cs_], k == 0, k == 3) for k in range(4)],
                       [b_WA, b_yT2], [bpv])
                    MM([mm(pg[:, :], wglu[:, k, 512 + oc * 128:512 + (oc + 1) * 128], yflat[:, k, cs_], k == 0, k == 3) for k in range(4)],
                       [b_WA, b_yT2], [bpg])
                    A(I("activation", sig[:], pg[:, :], AF.Sigmoid), [bpg], [b_sig])
                    dst = gluT[:, oc, :].rearrange("p (c j) -> p j c", j=8)[:, 2 * cb:2 * cb + 2, :]
                    V(I("tensor_tensor", dst, pv[:, :].rearrange("p (j c) -> p j c", j=2),
                                                                sig[:, :].rearrange("p (j c) -> p j c", j=2), ALU.mult),
                      [bpv, b_sig], [b_glu])
            if l == 0:
                tap("gluT", gluT.rearrange("p j t -> p (j t)"), [b_glu], 2048)
            ck("glu")
            for tt in range(NTT):
                ts = slice(tt * 512, (tt + 1) * 512)
                for m in range(8):
                    pt, bp = nps()
                    fns = [mm(pt[:, :], wout[:, k, m * 128:(m + 1) * 128], attT[:, k, ts], k == 0, False) for k in range(4)]
                    fns += [mm(pt[:, :], wout[:, 4 + k, m * 128:(m + 1) * 128], gluT[:, k, ts], False, k == 3) for k in range(4)]
                    MM(fns, [b_WA, b_WB, b_att, b_glu], [bp])
                    V(I("tensor_add", xT[:, m, ts], xT[:, m, ts], pt[:, :]), [bp], [b_x[m][tt]])
            if l == 0:
                tap("xmid", xT[:, 0, :], b_x[0], 2048)

            ck("wout")
            gate(a1_p1, b_h)
            gate([b_att, b_yT2, b_uT2, b_q], [])
            gate(misc_att, misc_ffn)
            gate([b_E], [b_WB])
            for tt in range(NTT):
                rms_tile(l, tt, g2)
            b_ws = [b_WA, b_WB]
            for gI in range(8):
                s = gI % 2
                w1, w2 = wsl[s]
                P.dma("gpsimd", I("dma_start",
                    out=w1, in_=w_ff1_d[l][:, gI * 512:(gI + 1) * 512].rearrange("(k p) f -> p k f", p=128)), writes=[b_ws[s]])
                P.dma("gpsimd", I("dma_start",
                    out=w2, in_=w_ff2_d[l][gI * 512:(gI + 1) * 512, :].rearrange("(k p) f -> p k f", p=128)), writes=[b_ws[s]])
                for tt in range(NTT):
                    ts = slice(tt * 512, (tt + 1) * 512)
                    for fc in range(4):
                        pt, bp = nps()
                        MM([mm(pt[:, :], w1[:, k, fc * 128:(fc + 1) * 128], hT[:, k, ts], k == 0, k == 7) for k in range(8)],
                           [b_ws[s], b_h[tt]], [bp])
                        ri_ = fc % 2
                        A(I("activation", rl[ri_], pt[:, :], AF.Relu), [bp], [b_rl[ri_]])
                        V(I("tensor_mul", aT[:, fc, :], rl[ri_], rl[ri_]), [b_rl[ri_]], [b_a[fc]])
                    for m in range(8):
                        pt, bp = nps()
                        MM([mm(pt[:, :], w2[:, k, m * 128:(m + 1) * 128], aT[:, k, :], k == 0, k == 3) for k in range(4)],
                           [b_ws[s]] + b_a, [bp])
                        V(I("tensor_add", xT[:, m, ts], xT[:, m, ts], pt[:, :]), [bp], [b_x[m][tt]])
            gate(misc_ffn, misc_att)

        try:
            for l in range(nlayers):
                layer(l)
        except _Stop:
            gate(xr_mixer, allx)
        emit_taps()
        for k in range(8):
            P.dma("sync", I("dma_start", out=out_d[k * 128:(k + 1) * 128, :], in_=xT[:, k, :]), reads=b_x[k])
        P.wait_all("sync", allx + b_tS)
        P.finish("sync")
        P.run()
    return nc


def _prep_inputs(x, norm1, w_in, q_gain, k_gain, sink, lam_re, lam_im, log_dt, b_re, b_im, c_re, c_im,
                 d_skip, w_glu, w_out, norm2, w_ff1, w_ff2, nlayers=DEPTH):
    f = lambda a: np.ascontiguousarray(np.asarray(a, dtype=np.float32))
    L = DEPTH
    perm = []
    for j in range(4):
        perm += list(range(j * 64, j * 64 + 64)) + list(range((4 + j) * 64, (4 + j) * 64 + 64))
    perm += list(range(512, 1280))
    shared = {
        "w_in": f(np.asarray(w_in)[:nlayers][:, :, perm]), "w_glu": f(np.asarray(w_glu)[:nlayers]),
        "w_out": f(np.asarray(w_out)[:nlayers]),
        "w_ff1": f(np.asarray(w_ff1)[:nlayers]), "w_ff2": f(np.asarray(w_ff2)[:nlayers]),
        "g1": f(np.asarray(norm1).reshape(L, 8, 128).transpose(2, 0, 1)),
        "g2": f(np.asarray(norm2).reshape(L, 8, 128).transpose(2, 0, 1)),
        "qg": f(np.tile(np.asarray(q_gain).T, (2, 1))),
        "kg": f(np.tile(np.asarray(k_gain).T, (2, 1))),
        "sinkr": f(np.broadcast_to(np.asarray(sink)[None], (128, L, 8))),
    }

    def gn(a):
        a = np.asarray(a).reshape(L, 2, 16, 2, 64)
        return f(a.transpose(3, 4, 0, 1, 2).reshape(128, L, 32))
    shared["lre"] = gn(lam_re)
    shared["lim"] = gn(lam_im)
    shared["ldt"] = gn(np.broadcast_to(np.asarray(log_dt)[:, :, :, None], (L, 2, 32, 64)))

    def bb(a):
        a = np.asarray(a).reshape(L, 16, 2, 64, 16)
        return f(a.transpose(2, 3, 0, 1, 4).reshape(128, L, 256))
    shared["bre"] = bb(b_re)
    shared["bim"] = bb(b_im)

    def cc(a):
        a = np.asarray(a).reshape(L, 2, 16, 2, 16, 64)
        return f(a.transpose(3, 5, 0, 1, 2, 4).reshape(128, L, 512))
    shared["cre"] = cc(c_re)
    shared["cim"] = cc(c_im)
    dsk = np.asarray(d_skip).reshape(L, 32, 16)
    shared["dcol"] = f(np.broadcast_to(dsk.transpose(2, 0, 1)[None], (8, 16, L, 32)).reshape(128, L, 32))
    slopes = np.exp2(-8.0 * np.arange(1, 9) / 8.0)
    ci = np.arange(128)[:, None]; qi = np.arange(128)[None, :]
    et = np.zeros((128, 3, 8, 128), np.float32)
    for kb in range(3):
        dist = np.abs(qi - ci - (kb - 1) * 128)
        valid = dist <= 128
        for h in range(8):
            et[:, kb, h, :] = np.where(valid, np.exp(-slopes[h] * dist), 0.0)
    shared["etab"] = f(et.reshape(128, 3072))
    cst = np.zeros((128, 656), np.float32)
    cst[:, 0:128] = np.eye(128)
    bi = np.arange(128)[:, None] // 16; bj = np.arange(128)[None, :] // 16
    cst[:, 128:256] = (bj >= bi)
    cst[:, 256:384] = (bi >= bj)
    cst[:, 384:393] = np.arange(9)[None, :]
    cst[:, 400:656] = np.arange(1, 257)[None, :]
    shared["cst"] = cst
    xs = np.asarray(x)
    in_maps = []
    for r in range(8):
        b, q = r // 4, r % 4
        m = dict(shared)
        m["xT"] = f(xs[b, q * T:(q + 1) * T, :].T)
        s = np.zeros((128, 16), np.float32)
        s[:, q] = 1.0
        if q > 0:
            s[:, 4 + q - 1] = 1.0; s[:, 12] = 1.0
        if q < 3:
            s[:, 8 + q + 1] = 1.0; s[:, 13] = 1.0
        m["sel"] = s
        in_maps.append(m)
    return in_maps


_NC_CACHE = {}


def kernel(**inputs):
    in_maps = _prep_inputs(**inputs)
    if "nc" not in _NC_CACHE:
        _NC_CACHE["nc"] = build_nc()
    nc = _NC_CACHE["nc"]
    res = run_bass_kernel_spmd(nc, in_maps, core_ids=list(range(8)))
    out = np.zeros((2, 4 * T, 1024), np.float32)
    for r in range(8):
        b, q = r // 4, r % 4
        out[b, q * T:(q + 1) * T, :] = np.asarray(res.results[r]["outT"]).T
    return out
```

```python
import math
import os
import numpy as np
from contextlib import ExitStack
import concourse.bass as bass
import concourse.mybir as mybir
from concourse.bass_utils import run_bass_kernel_spmd

F32 = mybir.dt.float32
BF16 = mybir.dt.bfloat16
I32 = mybir.dt.int32
AF = mybir.ActivationFunctionType
ALU = mybir.AluOpType

DEPTH = 4
T = 2048
NTT = 4
EPS = 1e-6
TWO_PI = 2.0 * math.pi


class Buf:
    def __init__(self, name=""):
        self.name = name
        self.w = None
        self.r = []


class EngQ:
    def __init__(self, name, sem):
        self.name = name
        self.sem = sem
        self.count = 0
        self.ops = []
        self.seen = {}


class Prog:
    ENGS = ["sync", "scalar", "gpsimd", "vector", "tensor"]

    def __init__(self, nc, es, ndma=16):
        self.nc = nc
        self.q = {e: EngQ(e, es.enter_context(nc.semaphore("s_" + e))) for e in self.ENGS}
        self.dq = {e: [EngQ(f"d_{e}{k}", es.enter_context(nc.semaphore(f"d_{e}{k}"))) for k in range(ndma)]
                   for e in ["sync", "gpsimd"]}
        self.rr = {e: 0 for e in self.dq}
        self.nops = 0

    def _waits(self, eng, reads, writes):
        q = self.q[eng]
        need = {}

        def add(tok):
            if tok is None:
                return
            s, v = tok
            if eng == "tensor" and s is q:
                return
            if need.get(s, 0) < v:
                need[s] = v
        for b in reads:
            add(b.w)
        for b in writes:
            add(b.w)
            for t in b.r:
                add(t)
        for s, v in need.items():
            if q.seen.get(s, 0) >= v:
                continue
            q.seen[s] = v
            q.ops.append(lambda e, s=s, v=v: e.wait_ge(s.sem, v))

    def _mark(self, tok, reads, writes):
        for b in reads:
            b.r = [t for t in b.r if t[0] is not tok[0]] + [tok]
        for b in writes:
            b.w = tok
            b.r = []

    def op(self, eng, fn, reads=(), writes=()):
        return self.group(eng, [fn], reads, writes)

    def group(self, eng, fns, reads=(), writes=()):
        q = self.q[eng]
        self._waits(eng, reads, writes)
        for fn in fns[:-1]:
            q.ops.append(lambda e, fn=fn: fn(e))
        q.count += 1
        tok = (q, q.count)
        q.ops.append(lambda e, fn=fns[-1], q=q: fn(e).then_inc(q.sem, 1))
        self._mark(tok, reads, writes)
        self.nops += len(fns)
        return tok

    def dma(self, eng, fn, reads=(), writes=()):
        self._waits(eng, reads, writes)
        k = self.rr[eng]
        self.rr[eng] = (k + 1) % len(self.dq[eng])
        d = self.dq[eng][k]
        q_ = self.q[eng]
        if d.count > 0 and q_.seen.get(d, 0) < d.count:
            q_.seen[d] = d.count
            q_.ops.append(lambda e, d=d, v=d.count: e.wait_ge(d.sem, v))
        d.count += 16
        tok = (d, d.count)
        self.q[eng].ops.append(lambda e, fn=fn, d=d: fn(e).then_inc(d.sem, 16))
        self._mark(tok, reads, writes)
        self.nops += 1
        return tok

    def wait_all(self, eng, bufs):
        self._waits(eng, [], bufs)

    def finish(self, eng="sync"):
        q = self.q[eng]
        allq = [x for x in self.q.values() if x is not q] + [d for ds in self.dq.values() for d in ds]
        for s_ in allq:
            if s_.count > 0:
                q.ops.append(lambda e, s_=s_, v=s_.count: e.wait_ge(s_.sem, v))

    def run(self):
        with self.nc.Block() as block:
            for e in self.ENGS:
                ops = self.q[e].ops

                def body(eng, ops=ops):
                    for o in ops:
                        o(eng)
                getattr(block, e)(body)


class _Stop(Exception):
    pass


def build_nc(nlayers=DEPTH, taps=None, stop_after=None):
    nc = bass.Bass("TRN2", target_bir_lowering=False)
    L = DEPTH

    def din(name, shape, dt=F32):
        return nc.dram_tensor(name, list(shape), dt, kind="ExternalInput").ap()

    xT_d = din("xT", [1024, T])
    LW = nlayers
    w_in_d = din("w_in", [LW, 1024, 1280])
    w_glu_d = din("w_glu", [LW, 512, 1024])
    w_out_d = din("w_out", [LW, 1024, 1024])
    w_ff1_d = din("w_ff1", [LW, 1024, 4096])
    w_ff2_d = din("w_ff2", [LW, 4096, 1024])
    g1_d = din("g1", [128, L, 8])
    g2_d = din("g2", [128, L, 8])
    qg_d = din("qg", [128, L])
    kg_d = din("kg", [128, L])
    sink_d = din("sinkr", [128, L, 8])
    lre_d = din("lre", [128, L, 32])
    lim_d = din("lim", [128, L, 32])
    ldt_d = din("ldt", [128, L, 32])
    bre_d = din("bre", [128, L, 256])
    bim_d = din("bim", [128, L, 256])
    cre_d = din("cre", [128, L, 512])
    cim_d = din("cim", [128, L, 512])
    dcol_d = din("dcol", [128, L, 32])
    etab_d = din("etab", [128, 3072])
    cst_d = din("cst", [128, 656])
    sel_d = din("sel", [128, 16])
    out_d = nc.dram_tensor("outT", [1024, T], F32, kind="ExternalOutput").ap()
    taps = list(taps) if taps else []
    dbg_d = nc.dram_tensor("dbg", [max(1, len(taps)), 128, 2048], F32, kind="ExternalOutput").ap() if taps else None

    xsp = nc.dram_tensor("xsp", [1024, T], F32).ap()
    hal_src = nc.dram_tensor("hal_src", [128, 512], F32).ap()
    hal_dst = nc.dram_tensor("hal_dst", [4 * 128, 512], F32).ap()
    st_src = nc.dram_tensor("st_src", [128, 64], F32).ap()
    st_dst = nc.dram_tensor("st_dst", [4 * 128, 64], F32).ap()
    GROUPS = [[0, 1, 2, 3], [4, 5, 6, 7]]

    with ExitStack() as es:
        P = Prog(nc, es)

        def sb(name, shape, dt=F32):
            return es.enter_context(nc.sbuf_tensor(name, list(shape), dt))

        XR = sb("XR", [128, 16384], F32)
        XRb = XR[:, :].bitcast(BF16)
        A1 = sb("A1", [128, 16384], BF16)
        A1f = A1[:, :].bitcast(F32)
        WA = sb("WA", [128, 16384], BF16)
        WAf = WA[:, :].bitcast(F32)
        A3 = sb("A3", [128, 8192], BF16)
        A4 = sb("A4", [128, 8192], BF16)
        MISC = sb("MISC", [128, 4096], BF16)

        xT = XR[:, :].rearrange("p (k t) -> p k t", k=8)
        b_x = [[Buf() for _ in range(NTT)] for _ in range(8)]
        allx = [b for row in b_x for b in row]
        U = XRb[:, 0:8192].rearrange("p (g c) -> p g c", g=32); b_U = Buf()
        PT = XRb[:, 8192:16384].rearrange("p (d g r n) -> p d g r n", d=2, g=32, r=2); b_PT = Buf()
        Qr = XRb[:, 16384:20480].rearrange("p (d g f) -> p d g f", d=2, g=16)
        Qn = XRb[:, 20480:24576].rearrange("p (d g f) -> p d g f", d=2, g=16); b_Qh = [Buf(), Buf()]
        Tm = XRb[:, 24576:28672].rearrange("p (g f) -> p g f", g=32); b_T = Buf()
        etab = XRb[:, 28672:31744].rearrange("p (k h q) -> p k h q", k=3, h=8); b_et = Buf()
        xr_mixer = [b_U, b_PT, b_T, b_et] + b_Qh

        hT = A1[:, :].rearrange("p (k t) -> p k t", k=8); b_h = [Buf() for _ in range(NTT)]
        bre = A1f[:, 0:256].rearrange("p (g h) -> p g h", g=16)
        bim = A1f[:, 256:512].rearrange("p (g h) -> p g h", g=16)
        cre = A1f[:, 512:1024].rearrange("p (d g h) -> p d g h", d=2, g=16)
        cim = A1f[:, 1024:1536].rearrange("p (d g h) -> p d g h", d=2, g=16)
        b_par2 = Buf()
        halg = A1f[:, 1536:3584].rearrange("p (r f) -> p r f", r=4); b_halg = Buf()
        tA = A1f[:, 4096:6144]; tB = A1f[:, 6144:8192]; b_tA = Buf(); b_tB = Buf(); b_tA2 = Buf(); b_tB2 = Buf()
        Yall = A1[:, 0:8192].rearrange("p (g c) -> p g c", g=32); b_Yall = Buf()
        yT2 = A3[:, :].rearrange("p (a j c) -> p a j c", a=4, j=8); b_yT2 = Buf()
        cosT = A1f[:, 4096:4608].rearrange("p (g c) -> p g c", g=2)
        sinT = A1f[:, 4608:5120].rearrange("p (g c) -> p g c", g=2); b_tab = Buf()
        Wr = A1f[:, 5120:5632].rearrange("p (g c) -> p g c", g=2)
        Wi = A1f[:, 5632:6144].rearrange("p (g c) -> p g c", g=2); b_W = Buf()
        Sr = A1f[:, 6144:6656].rearrange("p (g c) -> p g c", g=2)
        Si = A1f[:, 6656:7168].rearrange("p (g c) -> p g c", g=2); b_S = Buf()
        Hb16 = {}
        for d in range(2):
            for ni, n in enumerate("ri"):
                o = 14336 + (d * 2 + ni) * 512
                Hb16[(d, n)] = A1[:, o:o + 512].rearrange("p (g c) -> p g c", g=2)
        b_H = Buf()
        a1_s2 = [b_par2, b_halg, b_tA, b_tB, b_tA2, b_tB2]
        b_Yb = [Buf() for _ in range(8)]

        win = WA[:, 0:10240].rearrange("p (k f) -> p k f", k=8); b_WA = Buf(); b_WB = Buf()
        X_r = WA[:, 0:4096].rearrange("p (d g f) -> p d g f", d=2, g=16)
        X_i = WA[:, 4096:8192].rearrange("p (d g f) -> p d g f", d=2, g=16); b_Xh = [Buf(), Buf()]
        pw = {n: WAf[:, 4096 + i * 288:4096 + (i + 1) * 288].rearrange("p (k c) -> p k c", k=9)
              for i, n in enumerate(["pr", "pi", "mr", "mi"])}
        b_pw = Buf()
        cs_s = WAf[:, 5248:5536]; cs_c = WAf[:, 5536:5824]
        mg_p = WAf[:, 5824:6112]; mg_m = WAf[:, 6112:6400]
        Bb = {"r": WAf[:, 6400:6912].rearrange("p (d g h) -> p d g h", d=2, g=16),
              "i": WAf[:, 6912:7424].rearrange("p (d g h) -> p d g h", d=2, g=16)}
        b_Bb = Buf()
        zt = [WAf[:, 7424 + i * 32:7424 + (i + 1) * 32] for i in range(6)]; b_zt = Buf()
        wa_s2 = [b_pw, b_Bb, b_zt] + b_Xh
        E0c = WAf[:, 6144:6656].rearrange("p (g c) -> p g c", g=32)
        E0s = WAf[:, 6656:7168].rearrange("p (g c) -> p g c", g=32)
        E1c = WAf[:, 7168:7680].rearrange("p (g c) -> p g c", g=32)
        E1s = WAf[:, 7680:8192].rearrange("p (g c) -> p g c", g=32)
        b_E = Buf()
        wglu = WA[:, 0:4096].rearrange("p (k f) -> p k f", k=4)
        wout = WA[:, 4096:12288].rearrange("p (k f) -> p k f", k=8)
        wsl = [(WA[:, s * 8192:s * 8192 + 4096].rearrange("p (k f) -> p k f", k=8),
                WA[:, s * 8192 + 4096:s * 8192 + 8192].rearrange("p (k f) -> p k f", k=4)) for s in range(2)]

        qT = A3[:, :].rearrange("p (j t) -> p j t", j=4); b_q = Buf()
        gluT = A1[:, 8192:16384].rearrange("p (j t) -> p j t", j=4); b_glu = Buf()
        a1_p1 = b_Yb + [b_Yall, b_glu, b_tab, b_W, b_S, b_H]
        uT2 = A4[:, :].rearrange("p (a i c) -> p a i c", a=4, i=8); b_uT2 = Buf()
        attT = A4[:, :].rearrange("p (j t) -> p j t", j=4); b_att = Buf()

        pM2 = [[MISC[:, (s_ * 3 + i) * 512:(s_ * 3 + i + 1) * 512] for i in range(3)] for s_ in range(2)]
        b_pM2 = [[Buf() for _ in range(3)] for _ in range(2)]
        rden = MISC[:, 3072:3584].bitcast(F32).rearrange("p (a q) -> p a q", a=2); b_rden = Buf()
        rl = [MISC[:, i * 512:(i + 1) * 512] for i in range(2)]; b_rl = [Buf(), Buf()]
        aT = MISC[:, 1024:3072].rearrange("p (k t) -> p k t", k=4); b_a = [Buf() for _ in range(4)]
        misc_att = b_pM2[0] + b_pM2[1] + [b_rden]
        misc_ffn = b_rl + b_a

        kT = sb("kT", [128, T + 256], BF16); b_k = Buf()
        vtm = sb("vtm", [128, 18, 128], BF16); b_v = Buf()
        cst = sb("cst_sb", [128, 656]); b_cst = Buf()
        cstb = sb("cstb_sb", [128, 128], BF16)
        onesb = sb("onesb", [128, 128], BF16)
        blk2 = sb("blk2", [128, 128], BF16)
        ones_pn = sb("ones_pn", [128, 2, 64], BF16)
        sel = sb("sel_sb", [128, 16]); b_sel = Buf()
        g1 = sb("g1_sb", [128, L, 8]); g2 = sb("g2_sb", [128, L, 8])
        qg = sb("qg_sb", [128, L]); kg = sb("kg_sb", [128, L])
        sinkr = sb("sink_sb", [128, L * 8]); esink = sb("esink", [128, L, 8])
        b_small = Buf()
        epsb = sb("epsb", [128, 1])
        lre = sb("lre_sb", [128, 32]); lim = sb("lim_sb", [128, 32]); ldt = sb("ldt_sb", [128, 32])
        dcol = sb("dcol_sb", [128, 32]); b_par = Buf()
        lrdt = sb("lrdt", [128, 32]); th = sb("th", [128, 32]); Thu = sb("Thu", [128, 32])
        R8 = sb("R8", [128, 32]); Acr = sb("Acr", [128, 32]); Aci = sb("Aci", [128, 32])
        finr = sb("finr", [128, 32]); fini = sb("fini", [128, 32]); mag2048 = sb("mag2048", [128, 32])
        b_der = Buf()
        tS = [sb(f"tS{i}", [128, 512]) for i in range(3)]; b_tS = [Buf() for _ in range(3)]
        tSI = sb("tSI", [128, 512], I32); b_tSI = Buf()
        Fst = sb("Fst", [128, 64]); b_F = Buf()
        stg = sb("stg", [128, 4, 64]); b_stg = Buf()
        carry = sb("carry", [128, 64]); b_carry = Buf()
        hcr = sb("hcr", [128, 64]); b_hcr = Buf()
        halt = sb("halt", [128, 512]); b_halt = Buf()
        sq = sb("sq", [128, 2, 512], BF16); b_sq = [Buf(), Buf()]
        rstd = sb("rstd", [128, 512]); b_rstd = Buf()
        sig = sb("sig", [128, 512]); b_sig = Buf()
        gatec = sb("gatec", [128, 8]); b_gate = Buf()

        psb = [es.enter_context(nc.psum_tensor(f"ps{i}", [128, 512], F32)) for i in range(8)]
        b_ps = [Buf() for _ in range(8)]
        ps_pools = {"all": list(range(8)), "ssm": [0, 1, 2], "att_s": [5, 6, 7], "att_o": [3, 4]}
        ps_rr = {k: 0 for k in ps_pools}

        def nps(pool="all"):
            pool = "all"
            lst = ps_pools[pool]
            i = lst[ps_rr[pool] % len(lst)]
            ps_rr[pool] += 1
            return psb[i], b_ps[i]

        def V(fn, r=(), w=()):
            return P.op("vector", fn, r, w)

        def A(fn, r=(), w=()):
            return P.op("scalar", fn, r, w)

        def MM(fns, r=(), w=()):
            return P.group("tensor", fns, r, w)

        def I(method, *a, **k):
            return lambda e: getattr(e, method)(*a, **k)

        def mm(out, lhsT, rhs, start=True, stop=True):
            return I("matmul", out, lhsT=lhsT, rhs=rhs, start=start, stop=stop)

        def gate(old, new):
            V(I("memset", gatec[:, 0:1], 0.0), [], list(old) + list(new) + [b_gate])

        tap_i = [0]

        deferred_taps = []

        def tap(name, ap, bufs, n):
            if name in taps:
                deferred_taps.append((name, ap, bufs, n))

        def emit_taps():
            if not deferred_taps:
                return
            P.finish("vector")
            P.finish("sync")
            for (name, ap, bufs, n) in deferred_taps:
                idx = taps.index(name)
                for c0 in range(0, n, 512):
                    c1 = min(n, c0 + 512)
                    V(I("tensor_copy", tS[2][:, 0:c1 - c0], ap[:, c0:c1]), list(bufs), [b_tS[2]])
                    P.dma("sync", I("dma_start", out=dbg_d[idx, :, c0:c1], in_=tS[2][:, 0:c1 - c0]), reads=[b_tS[2]])

        P.dma("sync", I("dma_start", out=cst[:], in_=cst_d), writes=[b_cst])
        P.dma("sync", I("dma_start", out=sel[:], in_=sel_d), writes=[b_sel])
        for (t_sb, t_d) in [(g1, g1_d), (g2, g2_d), (qg, qg_d), (kg, kg_d)]:
            P.dma("sync", I("dma_start", out=t_sb[:], in_=t_d), writes=[b_small])
        P.dma("sync", I("dma_start", out=sinkr[:], in_=sink_d.rearrange("p l h -> p (l h)")), writes=[b_small])
        for k in range(8):
            P.dma("sync", I("dma_start", out=xT[:, k, :], in_=xT_d[k * 128:(k + 1) * 128, :]),
                  writes=b_x[k])
        ident = cst[:, 0:128]; Mf = cst[:, 128:256]; Mb = cst[:, 256:384]; kvec = cst[:, 384:393]
        cpos = cst[:, 400:656]
        V(I("tensor_copy", cstb[:], ident), [b_cst], [b_cst])
        V(I("memset", onesb[:], 1.0), [], [b_cst])
        V(I("memset", blk2[:], 0.0), [], [b_cst])
        V(I("memset", blk2[0:64, 0:64], 1.0), [], [b_cst])
        V(I("memset", blk2[64:128, 64:128], 1.0), [], [b_cst])
        V(I("memset", epsb[:], EPS), [], [b_cst])
        V(I("tensor_copy", ones_pn[:, 0, :], sel[:, 12:13].to_broadcast([128, 64])), [b_sel], [b_cst])
        V(I("tensor_copy", ones_pn[:, 1, :], sel[:, 13:14].to_broadcast([128, 64])), [b_sel], [b_cst])
        A(I("activation", esink[:, :, :].rearrange("p l h -> p (l h)"), sinkr[:], AF.Exp), [b_small], [b_small])
        identb = cstb
        c16 = sb("c16", [128, 16])
        V(I("tensor_scalar", c16[:], cpos[:, 0:16], -1.0, 16.0, ALU.add, ALU.mult), [b_cst], [b_cst])
        tabsets = [(cosT, sinT, b_tab), (rstd[:, :].rearrange("p (g c) -> p g c", g=2), sig[:, :].rearrange("p (g c) -> p g c", g=2), None)]
        ssm_it = [0]
        esrow = sb("esrow", [1, 8, 128], BF16); b_esr = Buf()

        def rms_tile(l, tt, gain):
            ts = slice(tt * 512, (tt + 1) * 512)
            pt, bp = nps()
            for k in range(8):
                s = k % 2
                A(I("activation", sq[:, s, :], xT[:, k, ts], AF.Square), [b_x[k][tt]], [b_sq[s]])
                P.op("tensor", mm(pt[:, :], onesb[:, :], sq[:, s, :], start=(k == 0), stop=(k == 7)), [b_sq[s], b_cst], [bp])
            A(I("activation", rstd[:], pt[:, :], AF.Ln, bias=epsb[:], scale=1.0 / 1024.0), [bp, b_cst], [b_rstd])
            A(I("activation", rstd[:], rstd[:], AF.Exp, scale=-0.5), [b_rstd], [b_rstd])
            for k in range(8):
                V(I("scalar_tensor_tensor", out=hT[:, k, ts], in0=xT[:, k, ts], scalar=gain[:, l, k:k + 1],
                                                        in1=rstd[:], op0=ALU.mult, op1=ALU.mult),
                  [b_x[k][tt], b_rstd, b_small], [b_h[tt]])

        def sincos_turns(tin, n, out_s, out_c, rb, wb):
            a_ = tS[0][:, 0:n]; b_ = tS[1][:, 0:n]; i_ = tSI[:, 0:n]
            for (off, outp) in [(0.0, out_s), (0.25, out_c)]:
                V(I("tensor_scalar_add", a_, tin, off), rb, [b_tS[0]])
                V(I("tensor_copy", i_, a_), [b_tS[0]], [b_tSI])
                V(I("tensor_copy", b_, i_), [b_tSI], [b_tS[1]])
                V(I("tensor_sub", a_, a_, b_), [b_tS[0], b_tS[1]], [b_tS[0]])
                V(I("tensor_scalar", a_, a_, -0.4999999, 0.4999999, ALU.max, ALU.min), [b_tS[0]], [b_tS[0]])
                A(I("activation", outp, a_, AF.Sin, scale=TWO_PI), [b_tS[0]], wb)

        def ck(name):
            if stop_after == name:
                raise _Stop()

        if os.environ.get("HALO_FIRST"):
            V(I("memset", halt[:], 1.0), [], [b_halt])
            P.dma("gpsimd", I("dma_start", out=hal_src, in_=halt[:]), reads=[b_halt], writes=[b_halg])
            P.op("gpsimd", I("collective_compute", "AllGather", ALU.bypass, replica_groups=GROUPS,
                             ins=[hal_src.opt()], outs=[hal_dst.opt()]), [b_halg], [b_halg])
            P.dma("gpsimd", I("dma_start", out=halg, in_=hal_dst.rearrange("(r p) f -> p r f", p=128)),
                  reads=[b_halg], writes=[b_halg])
            if os.environ.get("HALO_FIRST") == "only":
                raise_stop = True

        def layer(l):
            P.dma("gpsimd", I("dma_start", out=win, in_=w_in_d[l].rearrange("(k p) f -> p k f", p=128)),
                  writes=[b_WA, b_WB])
            for (t_sb, t_d) in [(lre, lre_d), (lim, lim_d), (ldt, ldt_d), (dcol, dcol_d)]:
                P.dma("sync", I("dma_start", out=t_sb[:], in_=t_d[:, l]), writes=[b_par])

            for tt in range(NTT):
                ts = slice(tt * 512, (tt + 1) * 512)
                rms_tile(l, tt, g1)
                for k in range(8 if not os.environ.get("NOSPILL") else 0):
                    P.dma("sync", I("dma_start", out=xsp[k * 128:(k + 1) * 128, ts], in_=xT[:, k, ts]),
                          reads=[b_x[k][tt]])
                for oc in range(5):
                    pt, bp = nps()
                    MM([mm(pt[:, :], win[:, k, oc * 128:(oc + 1) * 128], hT[:, k, ts], k == 0, k == 7) for k in range(8)],
                       [b_WA, b_h[tt]], [bp])
                    A(I("activation", sq[:, 0, :], pt[:, :], AF.Square), [bp], [b_sq[0]])
                    p2, bp2 = nps()
                    MM([mm(p2[:, :], blk2[:, :], sq[:, 0, :])], [b_sq[0], b_cst], [bp2])
                    A(I("activation", rstd[:], p2[:, :], AF.Ln, bias=epsb[:], scale=1.0 / 64.0),
                      [bp2, b_cst], [b_rstd])
                    A(I("activation", rstd[:], rstd[:], AF.Exp, scale=-0.5), [b_rstd], [b_rstd])
                    if oc < 4:
                        V(I("scalar_tensor_tensor", out=qT[:, oc, ts], in0=pt[:, :], scalar=qg[:, l:l + 1],
                                                                               in1=rstd[:], op0=ALU.mult, op1=ALU.mult),
                          [bp, b_rstd, b_small], [b_q])
                    else:
                        V(I("scalar_tensor_tensor", out=kT[:, 128 + tt * 512:128 + (tt + 1) * 512], in0=pt[:, :],
                                                                         scalar=kg[:, l:l + 1], in1=rstd[:], op0=ALU.mult, op1=ALU.mult),
                          [bp, b_rstd, b_small], [b_k])
                        if tt == 0:
                            V(I("scalar_tensor_tensor", out=halt[:, 0:128], in0=pt[:, 0:128], scalar=kg[:, l:l + 1],
                                                                      in1=rstd[:, 0:128], op0=ALU.mult, op1=ALU.mult),
                              [bp, b_rstd, b_small], [b_halt])
                        if tt == NTT - 1:
                            V(I("scalar_tensor_tensor", out=halt[:, 128:256], in0=pt[:, 384:512], scalar=kg[:, l:l + 1],
                                                                      in1=rstd[:, 384:512], op0=ALU.mult, op1=ALU.mult),
                              [bp, b_rstd, b_small], [b_halt])
                pt, bp = nps()
                fns = []
                for b4 in range(4):
                    for k in range(8):
                        fns.append(mm(pt[:, b4 * 128:(b4 + 1) * 128], hT[:, k, tt * 512 + b4 * 128: tt * 512 + (b4 + 1) * 128],
                                      win[:, k, 640:768], k == 0, k == 7))
                MM(fns, [b_WA, b_h[tt]], [bp])
                A(I("activation", vtm[:, 1 + tt * 4:1 + (tt + 1) * 4, :].rearrange("p b f -> p (b f)"),
                                                       pt[:, :], AF.Copy), [bp], [b_v])
                if tt == 0:
                    V(I("tensor_copy", halt[:, 256:384], pt[:, 0:128]), [bp], [b_halt])
                if tt == NTT - 1:
                    V(I("tensor_copy", halt[:, 384:512], pt[:, 384:512]), [bp], [b_halt])
                for a_ in range(4):
                    oc = 6 + a_
                    pt, bp = nps()
                    MM([mm(pt[:, :], win[:, k, oc * 128:(oc + 1) * 128], hT[:, k, ts], k == 0, k == 7) for k in range(8)],
                       [b_WA, b_h[tt]], [bp])
                    V(I("tensor_copy", uT2[:, a_, :, tt * 64:(tt + 1) * 64], pt[:, :].rearrange("p (c i) -> p i c", i=8)),
                      [bp], [b_uT2])
            if l == 0:
                tap("hT", hT.rearrange("p k t -> p (k t)"), b_h, 2048)
                tap("qT", qT.rearrange("p j t -> p (j t)"), [b_q], 2048)
                tap("kT", kT[:, 128:128 + 2048], [b_k], 2048)
                tap("uT2", uT2.rearrange("p a i c -> p (a i c)"), [b_uT2], 2048)

            ck("S1")
            gate(b_h, a1_s2)
            P.dma("gpsimd", I("dma_start", out=hal_src, in_=halt[:]), reads=[b_halt], writes=[b_halg])
            if not os.environ.get("NOCC"):
                P.op("gpsimd", I("collective_compute", "AllGather", ALU.bypass, replica_groups=GROUPS,
                                 ins=[hal_src.opt()], outs=[hal_dst.opt()]), [b_halg], [b_halg])
            P.dma("gpsimd", I("dma_start", out=halg, in_=hal_dst.rearrange("(r p) f -> p r f", p=128)),
                  reads=[b_halg], writes=[b_halg])

            ck("halo")
            gate([b_WA, b_WB, b_E], wa_s2)
            gate(allx, xr_mixer)
            P.dma("gpsimd", I("dma_start", out=etab.rearrange("p k h q -> p (k h q)"), in_=etab_d), writes=[b_et])
            for (t_ap, t_d) in [(bre, bre_d), (bim, bim_d), (cre, cre_d), (cim, cim_d)]:
                P.dma("sync", I("dma_start",
                    out=t_ap.rearrange("p g h -> p (g h)") if len(t_ap.shape) == 3 else t_ap.rearrange("p d g h -> p (d g h)"),
                    in_=t_d[:, l]), writes=[b_par2])
            A(I("activation", lrdt[:], ldt[:], AF.Exp), [b_par], [b_der])
            V(I("tensor_mul", th[:], lim[:], lrdt[:]), [b_par, b_der], [b_der])
            V(I("tensor_mul", lrdt[:], lre[:], lrdt[:]), [b_par, b_der], [b_der])
            V(I("tensor_scalar_mul", th[:], th[:], 1.0 / TWO_PI), [b_der], [b_der])
            kb9 = kvec.unsqueeze(2).to_broadcast([128, 9, 32])
            ph9 = tS[2][:, 0:288]
            V(I("tensor_tensor", ph9.rearrange("p (k c) -> p k c", k=9), th[:].unsqueeze(1).to_broadcast([128, 9, 32]),
                                        kb9, ALU.mult), [b_der, b_cst], [b_tS[2]])
            sincos_turns(ph9, 288, cs_s, cs_c, [b_tS[2]], [b_pw])
            V(I("tensor_tensor", ph9.rearrange("p (k c) -> p k c", k=9), lrdt[:].unsqueeze(1).to_broadcast([128, 9, 32]),
                                        kb9, ALU.mult), [b_der, b_cst, b_pw], [b_tS[2]])
            A(I("activation", mg_p, ph9, AF.Exp), [b_tS[2]], [b_pw])
            A(I("activation", mg_m, ph9, AF.Exp, scale=-1.0), [b_tS[2]], [b_pw])
            fl = lambda t: t.rearrange("p k c -> p (k c)")
            V(I("tensor_mul", fl(pw["pr"]), mg_p, cs_c), [b_pw], [b_pw])
            V(I("tensor_mul", fl(pw["pi"]), mg_p, cs_s), [b_pw], [b_pw])
            V(I("tensor_mul", fl(pw["mr"]), mg_m, cs_c), [b_pw], [b_pw])
            V(I("scalar_tensor_tensor", out=fl(pw["mi"]), in0=mg_m, scalar=-1.0, in1=cs_s,
                                               op0=ALU.mult, op1=ALU.mult), [b_pw], [b_pw])
            ck("S2a")
            ar1, ai1 = pw["pr"][:, 1, :], pw["pi"][:, 1, :]
            zin = [b_zt, b_par, b_pw]
            V(I("tensor_mul", zt[0], lre[:], lre[:]), zin, [b_zt])
            V(I("tensor_mul", zt[1], lim[:], lim[:]), zin, [b_zt])
            V(I("tensor_add", zt[0], zt[0], zt[1]), zin, [b_zt])
            V(I("reciprocal", zt[0], zt[0]), zin, [b_zt])
            V(I("tensor_scalar_add", zt[1], ar1, -1.0), zin, [b_zt])
            V(I("tensor_mul", zt[2], zt[1], lre[:]), zin, [b_zt])
            V(I("tensor_mul", zt[3], ai1, lim[:]), zin, [b_zt])
            V(I("tensor_add", zt[2], zt[2], zt[3]), zin, [b_zt])
            V(I("tensor_mul", zt[2], zt[2], zt[0]), zin, [b_zt])
            V(I("tensor_mul", zt[3], ai1, lre[:]), zin, [b_zt])
            V(I("tensor_mul", zt[4], zt[1], lim[:]), zin, [b_zt])
            V(I("tensor_sub", zt[3], zt[3], zt[4]), zin, [b_zt])
            V(I("tensor_mul", zt[3], zt[3], zt[0]), zin, [b_zt])
            for d in range(2):
                zr = zt[2][:, d * 16:(d + 1) * 16].unsqueeze(2).to_broadcast([128, 16, 16])
                zi = zt[3][:, d * 16:(d + 1) * 16].unsqueeze(2).to_broadcast([128, 16, 16])
                t1 = tA[:, 0:256].rearrange("p (a b) -> p a b", a=16)
                t2 = tB[:, 0:256].rearrange("p (a b) -> p a b", a=16)
                V(I("tensor_tensor", t1, bre, zr, ALU.mult), [b_par2, b_zt], [b_tA])
                V(I("tensor_tensor", t2, bim, zi, ALU.mult), [b_par2, b_zt], [b_tB])
                V(I("tensor_sub", Bb["r"][:, d], t1, t2), [b_tA, b_tB], [b_Bb])
                V(I("tensor_tensor", t1, bim, zr, ALU.mult), [b_par2, b_zt], [b_tA])
                V(I("tensor_tensor", t2, bre, zi, ALU.mult), [b_par2, b_zt], [b_tB])
                V(I("tensor_add", Bb["i"][:, d], t1, t2), [b_tA, b_tB], [b_Bb])

            ck("S2b")

            def cmul_tab(out_r, out_i, tr, ti, Xr_, Xi_, wbs, neg_i=False):
                for hf, (eng, btA, btB) in enumerate([("vector", b_tA, b_tB), ("vector", b_tA2, b_tB2)]):
                    gs_ = slice(hf * 8, hf * 8 + 8)
                    ta = tA[:, hf * 1024:(hf + 1) * 1024]; tb = tB[:, hf * 1024:(hf + 1) * 1024]
                    trb = tr[:, :, gs_].rearrange("p k g -> p g k").unsqueeze(3).to_broadcast([128, 8, 8, 16])
                    tib = ti[:, :, gs_].rearrange("p k g -> p g k").unsqueeze(3).to_broadcast([128, 8, 8, 16])
                    Xrb = Xr_[:, gs_, :].unsqueeze(2).to_broadcast([128, 8, 8, 16])
                    Xib = Xi_[:, gs_, :].unsqueeze(2).to_broadcast([128, 8, 8, 16])
                    v4 = lambda t: t.rearrange("p (g k h) -> p g k h", g=8, k=8)
                    o4 = lambda t: t[:, gs_, :].rearrange("p g (k h) -> p g k h", k=8)
                    rd = [b_pw, b_Bb, b_par2]
                    wb = [wbs[hf]]
                    E_ = lambda fn, r, w, eng=eng: P.op(eng, fn, r, w)
                    E_(I("tensor_tensor", v4(ta), trb, Xrb, ALU.mult), rd, [btA])
                    E_(I("tensor_tensor", v4(tb), tib, Xib, ALU.mult), rd, [btB])
                    E_(I("tensor_sub", o4(out_r), v4(ta), v4(tb)), [btA, btB], wb)
                    E_(I("tensor_tensor", v4(ta), trb, Xib, ALU.mult), rd, [btA])
                    E_(I("tensor_tensor", v4(tb), tib, Xrb, ALU.mult), rd, [btB])
                    if neg_i and eng == "vector":
                        E_(I("scalar_tensor_tensor", out=o4(out_i), in0=v4(ta), scalar=-1.0, in1=v4(tb),
                             op0=ALU.mult, op1=ALU.subtract), [btA, btB], wb)
                    elif neg_i:
                        E_(I("tensor_add", v4(ta), v4(ta), v4(tb)), [btA, btB], [btA])
                        E_(I("tensor_scalar_mul", o4(out_i), v4(ta), -1.0), [btA], wb)
                    else:
                        E_(I("tensor_add", o4(out_i), v4(ta), v4(tb)), [btA, btB], wb)

            def tab(name, lo, hi, rev, d):
                t = pw[name][:, lo:hi, d * 16:(d + 1) * 16]
                return t[:, ::-1, :] if rev else t

            for d in range(2):
                rev = (d == 1)
                cmul_tab(Qr[:, d], Qn[:, d], tab("pr", 1, 9, rev, d), tab("pi", 1, 9, rev, d), cre[:, d], cim[:, d], b_Qh, neg_i=True)
                cmul_tab(X_r[:, d], X_i[:, d], tab("mr", 1, 9, rev, d), tab("mi", 1, 9, rev, d), Bb["r"][:, d], Bb["i"][:, d], b_Xh)
            ck("S2c")
            for gq4 in range(4):
                for g2_ in range(2):
                    ps_ = slice(g2_ * 64, g2_ * 64 + 64)
                    ptf, bpf = nps(); ptb, bpb = nps()
                    for (d, pt, bp) in [(0, ptf, bpf), (1, ptb, bpb)]:
                        fns = []
                        for k4 in range(4):
                            gp = gq4 * 4 + k4
                            fns.append(mm(pt[:, k4 * 128:(k4 + 1) * 128], X_r[ps_, d, gp, :], Qr[ps_, d, gp, :], True, False))
                            fns.append(mm(pt[:, k4 * 128:(k4 + 1) * 128], X_i[ps_, d, gp, :], Qn[ps_, d, gp, :], False, True))
                        MM(fns, b_Xh + b_Qh, [bp])
                    m4 = lambda m: m.unsqueeze(1).to_broadcast([128, 4, 128])
                    v3 = lambda t: t.rearrange("p (a b) -> p a b", a=4)
                    V(I("tensor_tensor", v3(tA[:, 0:512]), v3(ptf[:, :]), m4(Mf), ALU.mult), [bpf, b_cst], [b_tA])
                    V(I("tensor_tensor", v3(tB[:, 0:512]), v3(ptb[:, :]), m4(Mb), ALU.mult), [bpb, b_cst], [b_tB])
                    V(I("tensor_add", tA[:, 0:512], tA[:, 0:512], tB[:, 0:512]), [b_tA, b_tB], [b_tA])
                    for k4 in range(4):
                        g = 2 * (gq4 * 4 + k4) + g2_
                        V(I("scalar_tensor_tensor", out=Tm[:, g, :], in0=ident, scalar=dcol[:, g:g + 1],
                            in1=tA[:, k4 * 128:(k4 + 1) * 128], op0=ALU.mult, op1=ALU.add),
                          [b_tA, b_par, b_cst], [b_T])
            ck("S2d")
            for d in range(2):
                rev = (d == 1)
                cmul_tab(X_r[:, d], X_i[:, d], tab("pr", 0, 8, not rev, d), tab("pi", 0, 8, not rev, d),
                         Bb["r"][:, d], Bb["i"][:, d], b_Xh)
            for d in range(2):
                for gq4 in range(4):
                    for g2_ in range(2):
                        ps_ = slice(g2_ * 64, g2_ * 64 + 64)
                        pt, bp = nps()
                        fns = []
                        for k4 in range(4):
                            gp = gq4 * 4 + k4
                            for ri, Pm in enumerate([X_r, X_i]):
                                c0 = (k4 * 2 + ri) * 64
                                fns.append(mm(pt[:, c0:c0 + 64], Pm[ps_, d, gp, :], identb[ps_, g2_ * 64:g2_ * 64 + 64]))
                        MM(fns, b_Xh + [b_cst], [bp])
                        g0 = 2 * gq4 * 4 + g2_
                        V(I("tensor_copy", PT[:, d, g0:g0 + 7:2, :, :].rearrange("p g r n -> p g (r n)"),
                            pt[:, :].rearrange("p (g f) -> p g f", g=4)), [bp], [b_PT])
            ck("S2e")
            V(I("tensor_scalar_mul", Thu[:], th[:], 8.0), [b_der], [b_der])
            V(I("tensor_copy", tSI[:, 0:32], Thu[:]), [b_der], [b_tSI])
            V(I("tensor_copy", tS[1][:, 0:32], tSI[:, 0:32]), [b_tSI], [b_tS[1]])
            V(I("tensor_sub", Thu[:], Thu[:], tS[1][:, 0:32]), [b_tS[1], b_der], [b_der])
            A(I("activation", R8[:], lrdt[:], AF.Exp, scale=8.0), [b_der], [b_der])
            V(I("tensor_scalar_mul", tS[2][:, 0:32], Thu[:], 256.0), [b_der], [b_tS[2]])
            sincos_turns(tS[2][:, 0:32], 32, fini[:], finr[:], [b_tS[2]], [b_der])
            A(I("activation", mag2048[:], lrdt[:], AF.Exp, scale=2048.0), [b_der], [b_der])
            V(I("tensor_mul", Acr[:], mag2048[:], finr[:]), [b_der], [b_der])
            V(I("tensor_mul", Aci[:], mag2048[:], fini[:]), [b_der], [b_der])
            gate([b_Bb, b_zt], [b_E])
            for (Ec, Es, pos) in [(E0c, E0s, cpos[:, 0:16]), (E1c, E1s, c16[:])]:
                V(I("tensor_tensor", tS[2][:, 0:512].rearrange("p (g c) -> p g c", g=32),
                    Thu[:].unsqueeze(2).to_broadcast([128, 32, 16]), pos.unsqueeze(1).to_broadcast([128, 32, 16]), ALU.mult),
                  [b_der, b_cst], [b_tS[2]])
                sincos_turns(tS[2][:, 0:512], 512, Es.rearrange("p g c -> p (g c)"), Ec.rearrange("p g c -> p (g c)"), [b_tS[2]], [b_E])
            if l == 0:
                tap("Tm", Tm.rearrange("p g f -> p (g f)"), [b_T], 2048)
                tap("Qr", Qr.rearrange("p d g f -> p (d g f)"), b_Qh, 2048)
                tap("PT", PT.rearrange("p d g r n -> p (d g r n)"), [b_PT], 2048)

            ck("S2")
            for (so, kcols, vcols, kdst, vblk) in [(4, slice(128, 256), slice(384, 512), slice(0, 128), 0),
                                                   (8, slice(0, 128), slice(256, 384), slice(T + 128, T + 256), 17)]:
                for (cols, dst_ap, wb) in [(kcols, kT[:, kdst], b_k), (vcols, vtm[:, vblk, :], b_v)]:
                    acc = tS[2][:, 0:128]
                    V(I("tensor_scalar_mul", acc, halg[:, 0, cols], sel[:, so:so + 1]),
                      [b_halg, b_sel], [b_tS[2]])
                    for r in range(1, 4):
                        V(I("scalar_tensor_tensor", out=acc, in0=halg[:, r, cols],
                                                                                  scalar=sel[:, so + r:so + r + 1], in1=acc,
                                                                                  op0=ALU.mult, op1=ALU.add),
                          [b_halg, b_sel], [b_tS[2]])
                    V(I("tensor_copy", dst_ap, acc), [b_tS[2]], [wb])

            ck("halosel")
            for i in range(8):
                for g8 in range(8):
                    P.dma("sync", I("dma_start",
                        out=U[i * 16:(i + 1) * 16, :, :].rearrange("p (a g) c -> p a g c", a=4)[:, :, g8, :],
                        in_=uT2[g8 * 16:(g8 + 1) * 16, :, i, :]), reads=[b_uT2], writes=[b_U])
            if l == 0:
                tap("U", U.rearrange("p g c -> p (g c)"), [b_U], 2048)
            ck("relayout")
            gate(a1_s2, a1_p1)
            gate(wa_s2, [b_WA, b_WB])
            P.dma("gpsimd", I("dma_start", out=wglu, in_=w_glu_d[l].rearrange("(k p) f -> p k f", p=128)), writes=[b_WA])
            P.dma("gpsimd", I("dma_start", out=wout, in_=w_out_d[l].rearrange("(k p) f -> p k f", p=128)), writes=[b_WA, b_WB])

            ssm_ctx = {}

            def ssm_batch(bt, phase):
                ssm_z(bt, phase)
                ssm_dve(bt, phase)

            def ssm_z(bt, phase):
                for d in range(2):
                    c0 = d * 16 + bt * 2
                    gsl = slice(c0, c0 + 2)
                    cosT_, sinT_, btab_ = tabsets[ssm_it[0] % 2]
                    ssm_it[0] += 1
                    if btab_ is None:
                        rd_t = [b_rstd, b_sig]; wr_c = [b_rstd]; wr_s = [b_sig]
                    else:
                        rd_t = [btab_]; wr_c = [btab_]; wr_s = [btab_]
                    e1c = E1c[:, gsl, :].unsqueeze(3).to_broadcast([128, 2, 16, 16])
                    e1s = E1s[:, gsl, :].unsqueeze(3).to_broadcast([128, 2, 16, 16])
                    e0c = E0c[:, gsl, :].unsqueeze(2).to_broadcast([128, 2, 16, 16])
                    e0s = E0s[:, gsl, :].unsqueeze(2).to_broadcast([128, 2, 16, 16])
                    q4 = lambda t: t.rearrange("p (g a b) -> p g a b", g=2, a=16)
                    q3 = lambda t: t.rearrange("p g (a b) -> p g a b", a=16)
                    pa = tS[2][:, 0:512]; pb = tSI[:, 0:512].bitcast(F32)
                    G = lambda fn, r, w: P.op("gpsimd", fn, r, w)
                    G(I("tensor_tensor", q4(pa), e1c, e0c, ALU.mult), [b_E], [b_tS[2]])
                    G(I("tensor_tensor", q4(pb), e1s, e0s, ALU.mult), [b_E], [b_tSI])
                    G(I("tensor_sub", q3(cosT_), q4(pa), q4(pb)), [b_tS[2], b_tSI], wr_c)
                    G(I("tensor_tensor", q4(pa), e1s, e0c, ALU.mult), [b_E], [b_tS[2]])
                    G(I("tensor_tensor", q4(pb), e1c, e0s, ALU.mult), [b_E], [b_tSI])
                    G(I("tensor_add", q3(sinT_), q4(pa), q4(pb)), [b_tS[2], b_tSI], wr_s)
                    zb = []
                    for ri in range(2):
                        pt, bp = nps("ssm")
                        fns = []
                        for gq in range(2):
                            gp = bt * 2 + gq
                            for g2_ in range(2):
                                g = gp * 2 + g2_
                                fns.append(mm(pt[g2_ * 64:(g2_ + 1) * 64, gq * 256:(gq + 1) * 256], PT[:, d, g, ri, :], U[:, g, :]))
                        MM(fns, [b_PT, b_U], [bp])
                        zb.append((pt, bp))
                    ssm_ctx[(bt, d)] = (zb, cosT_, sinT_, rd_t)

            def ssm_dve(bt, phase):
                for d in range(2):
                    c0 = d * 16 + bt * 2
                    gsl = slice(c0, c0 + 2)
                    zb, cosT_, sinT_, rd_t = ssm_ctx.pop((bt, d))
                    (pzr, bzr), (pzi, bzi) = zb
                    zr_ = pzr[:, :].rearrange("p (g c) -> p g c", g=2)
                    zi_ = pzi[:, :].rearrange("p (g c) -> p g c", g=2)
                    if d == 1:
                        zr_ = zr_[:, :, ::-1]; zi_ = zi_[:, :, ::-1]
                    a3 = tS[0][:, 0:512].rearrange("p (g c) -> p g c", g=2)
                    b3 = tS[1][:, 0:512].rearrange("p (g c) -> p g c", g=2)
                    V(I("tensor_tensor", a3, zr_, cosT_, ALU.mult), [bzr, *rd_t], [b_tS[0]])
                    V(I("tensor_tensor", b3, zi_, sinT_, ALU.mult), [bzi, *rd_t], [b_tS[1]])
                    V(I("tensor_add", Wr, a3, b3), [b_tS[0], b_tS[1]], [b_W])
                    V(I("tensor_tensor", a3, zi_, cosT_, ALU.mult), [bzi, *rd_t], [b_tS[0]])
                    V(I("tensor_tensor", b3, zr_, sinT_, ALU.mult), [bzr, *rd_t], [b_tS[1]])
                    V(I("tensor_sub", Wi, a3, b3), [b_tS[0], b_tS[1]], [b_W])
                    for gq in range(2):
                        col = c0 + gq
                        for (Wt, St, ci) in [(Wr, Sr, 0), (Wi, Si, 1)]:
                            cc = ci * 32 + col
                            init = 0.0 if phase == 0 else carry[:, cc:cc + 1]
                            V(I("tensor_tensor_scan",
                                St[:, gq, :], R8[:, col:col + 1].to_broadcast([128, 256]), Wt[:, gq, :], init, ALU.mult, ALU.add),
                              [b_W, b_der, b_carry], [b_S])
                    if phase == 0:
                        fr = finr[:, gsl]; fi = fini[:, gsl]
                        s_r = Sr[:, :, 255]; s_i = Si[:, :, 255]
                        o_r = Fst[:, c0:c0 + 2]; o_i = Fst[:, 32 + c0:32 + c0 + 2]
                        t0 = tS[0][:, 0:2]; t1 = tS[1][:, 0:2]
                        V(I("tensor_mul", t0, s_r, fr), [b_S, b_der], [b_tS[0]])
                        V(I("tensor_mul", t1, s_i, fi), [b_S, b_der], [b_tS[1]])
                        V(I("tensor_sub", o_r, t0, t1), [b_tS[0], b_tS[1]], [b_F])
                        V(I("tensor_mul", t0, s_r, fi), [b_S, b_der], [b_tS[0]])
                        V(I("tensor_mul", t1, s_i, fr), [b_S, b_der], [b_tS[1]])
                        V(I("tensor_add", o_i, t0, t1), [b_tS[0], b_tS[1]], [b_F])
                    else:
                        Hr_ = Hb16[(d, "r")]; Hi_ = Hb16[(d, "i")]
                        if d == 0:
                            o_r = Hr_[:, :, 1:256]; o_i = Hi_[:, :, 1:256]
                            o_r0 = Hr_[:, :, 0]; o_i0 = Hi_[:, :, 0]
                        else:
                            o_r = Hr_[:, :, 254::-1]; o_i = Hi_[:, :, 254::-1]
                            o_r0 = Hr_[:, :, 255]; o_i0 = Hi_[:, :, 255]
                        a3s = a3[:, :, 0:255]; b3s = b3[:, :, 0:255]
                        V(I("tensor_tensor", a3s, Sr[:, :, 0:255], cosT_[:, :, 0:255], ALU.mult), [b_S, *rd_t], [b_tS[0]])
                        V(I("tensor_tensor", b3s, Si[:, :, 0:255], sinT_[:, :, 0:255], ALU.mult), [b_S, *rd_t], [b_tS[1]])
                        V(I("tensor_sub", o_r, a3s, b3s), [b_tS[0], b_tS[1]], [b_H])
                        V(I("tensor_tensor", a3s, Sr[:, :, 0:255], sinT_[:, :, 0:255], ALU.mult), [b_S, *rd_t], [b_tS[0]])
                        V(I("tensor_tensor", b3s, Si[:, :, 0:255], cosT_[:, :, 0:255], ALU.mult), [b_S, *rd_t], [b_tS[1]])
                        V(I("tensor_add", o_i, a3s, b3s), [b_tS[0], b_tS[1]], [b_H])
                        V(I("tensor_copy", o_r0, carry[:, c0:c0 + 2]), [b_carry], [b_H])
                        V(I("tensor_copy", o_i0, carry[:, 32 + c0:32 + c0 + 2]), [b_carry], [b_H])

            V(I("tensor_copy", esrow[:, :, :], esink[0:1, l, :].unsqueeze(2).to_broadcast([1, 8, 128])), [b_small], [b_esr])
            gate([b_uT2], [b_att])
            att_state = {}

            def att_scores(qb):
                qs = slice(qb * 128, (qb + 1) * 128)
                for kvh in range(2):
                    hs_ = slice(kvh * 64, kvh * 64 + 64)
                    pM = pM2[kvh]; b_pM = b_pM2[kvh]
                    for kb in range(3):
                        pt, bp = nps()
                        MM([mm(pt[:, :].rearrange("p (j q) -> p j q", j=4), kT[hs_, (qb + kb) * 128:(qb + kb + 1) * 128], qT[hs_, :, qs])],
                           [b_k, b_q], [bp])
                        A(I("activation", pM[kb], pt[:, :], AF.Exp, scale=0.125), [bp], [b_pM[kb]])

            def att_mid(qb):
                for kvh in range(2):
                    hs_ = slice(kvh * 64, kvh * 64 + 64)
                    pM = pM2[kvh]; b_pM = b_pM2[kvh]
                    for kb in range(3):
                        V(I("tensor_tensor", pM[kb], pM[kb],
                            etab[:, kb, kvh * 4:(kvh + 1) * 4, :].rearrange("p j q -> p (j q)"), ALU.mult),
                          [b_et], [b_pM[kb]])
                    pno, bpn = nps()
                    pn = pno[:, 0:256]; pd = pno[:, 256:512]
                    fn_n = []; fn_d = []
                    for j in range(4):
                        po = slice((j % 2) * 64, (j % 2) * 64 + 64)
                        cs2 = slice((j // 2) * 128, (j // 2) * 128 + 128)
                        for kb in range(3):
                            if kb == 0 and qb == 0:
                                ol = ones_pn[:, 0, :]
                            elif kb == 2 and qb == 15:
                                ol = ones_pn[:, 1, :]
                            else:
                                ol = onesb[:, 0:64]
                            fn_n.append(mm(pn[po, cs2], vtm[:, qb + kb, hs_], pM[kb][:, j * 128:(j + 1) * 128], kb == 0, kb == 2))
                            if kb == 0:
                                fn_d.append(mm(pd[po, cs2], onesb[0:1, 0:64], esrow[0:1, kvh * 4 + j, :], True, False))
                            fn_d.append(mm(pd[po, cs2], ol, pM[kb][:, j * 128:(j + 1) * 128], False, kb == 2))
                    MM(fn_n + fn_d, [b_v, b_cst, b_esr] + b_pM, [bpn])
                    att_state[(qb, kvh)] = (pn, pd, bpn)

            def att_fin(qb):
                qs = slice(qb * 128, (qb + 1) * 128)
                for kvh in range(2):
                    pn, pd, bpn = att_state.pop((qb, kvh))
                    A(I("activation", rden, pd.rearrange("p (a q) -> p a q", a=2), AF.Ln), [bpn], [b_rden])
                    A(I("activation", rden, rden, AF.Exp, scale=-1.0), [b_rden], [b_rden])
                    V(I("tensor_tensor", attT[:, kvh * 2:kvh * 2 + 2, qs],
                        pn.rearrange("p (a q) -> p a q", a=2), rden, ALU.mult),
                      [bpn, b_rden, b_uT2], [b_att])

            def att_flush():
                pass

            att_order = [1, 2, 3, 4, 5, 6, 7, 8, 9, 10, 11, 12, 13, 14, 0, 15]
            att_scores(att_order[0])
            for bt in range(8):
                ssm_z(bt, 0)
                att_mid(att_order[bt])
                ssm_dve(bt, 0)
                att_fin(att_order[bt])
                att_scores(att_order[bt + 1])
            ck("ssm0")
            P.dma("gpsimd", I("dma_start", out=st_src, in_=Fst[:]), reads=[b_F], writes=[b_stg])
            P.op("gpsimd", I("collective_compute", "AllGather", ALU.bypass, replica_groups=GROUPS,
                                                         ins=[st_src.opt()], outs=[st_dst.opt()]), [b_stg], [b_stg])
            P.dma("gpsimd", I("dma_start", out=stg[:], in_=st_dst.rearrange("(r p) f -> p r f", p=128)),
                  reads=[b_stg], writes=[b_stg])

            ck("stx")
            if l == 0:
                tap("attT", attT.rearrange("p j t -> p (j t)"), [b_att], 2048)

            ck("att")
            V(I("memset", carry[:], 0.0), [], [b_carry])
            for d in range(2):
                cs_ = slice(d * 16, d * 16 + 16); ci_ = slice(32 + d * 16, 32 + d * 16 + 16)
                cr = hcr[:, 0:16]; cim_ = hcr[:, 16:32]; t0 = hcr[:, 32:48]; t1 = hcr[:, 48:64]
                V(I("memset", hcr[:], 0.0), [], [b_hcr])
                order = [0, 1, 2, 3] if d == 0 else [3, 2, 1, 0]
                hh = [b_hcr]
                for r in order:
                    V(I("scalar_tensor_tensor", out=carry[:, cs_], in0=cr, scalar=sel[:, r:r + 1], in1=carry[:, cs_],
                                                                     op0=ALU.mult, op1=ALU.add), [b_hcr, b_sel], [b_carry])
                    V(I("scalar_tensor_tensor", out=carry[:, ci_], in0=cim_, scalar=sel[:, r:r + 1], in1=carry[:, ci_],
                                                                     op0=ALU.mult, op1=ALU.add), [b_hcr, b_sel], [b_carry])
                    ar_ = Acr[:, cs_]; ai_ = Aci[:, cs_]
                    V(I("tensor_mul", t0, cr, ar_), [b_der], hh)
                    V(I("tensor_mul", t1, cim_, ai_), [b_der], hh)
                    V(I("tensor_sub", t0, t0, t1), [], hh)
                    V(I("tensor_mul", t1, cr, ai_), [b_der], hh)
                    V(I("tensor_mul", cim_, cim_, ar_), [b_der], hh)
                    V(I("tensor_add", cim_, cim_, t1), [], hh)
                    V(I("tensor_add", cr, t0, stg[:, r, cs_]), [b_stg], hh)
                    V(I("tensor_add", cim_, cim_, stg[:, r, ci_]), [b_stg], hh)

            for bt in range(8):
                ssm_z(bt, 1)
                att_mid(att_order[8 + bt])
                ssm_dve(bt, 1)
                att_fin(att_order[8 + bt])
                if bt < 7:
                    att_scores(att_order[9 + bt])
                for gq in range(2):
                    gp = bt * 2 + gq
                    pt, bp = nps("ssm")
                    fns = []
                    for g2_ in range(2):
                        g = gp * 2 + g2_
                        ps_ = slice(g2_ * 64, g2_ * 64 + 64)
                        oc = pt[:, g2_ * 256:g2_ * 256 + 256]
                        fns.append(mm(oc, Tm[:, g, :], U[:, g, :], True, False))
                        for d in range(2):
                            fns.append(mm(oc, Qr[ps_, d, gp, :], Hb16[(d, "r")][ps_, gq, :], False, False))
                            fns.append(mm(oc, Qn[ps_, d, gp, :], Hb16[(d, "i")][ps_, gq, :], False, d == 1))
                    MM(fns, [b_T, b_U, b_H] + b_Qh, [bp])
                    A(I("activation", Yall[:, gp * 2:gp * 2 + 2, :].rearrange("p g c -> p (g c)"), pt[:, :],
                                                           AF.Gelu_apprx_tanh), [bp, b_Yall], [b_Yb[bt]])
            if l == 0:
                tap("Yall", Yall.rearrange("p g c -> p (g c)"), b_Yb, 2048)
            ck("ssm1")
            gate(xr_mixer, allx)
            for k in range(8):
                P.dma("gpsimd", I("dma_start", out=xT[:, k, :], in_=xsp[k * 128:(k + 1) * 128, :]), writes=b_x[k])
            att_flush()
            b_yTok = Buf()
            gate([b_tab, b_W, b_S, b_H], [b_yTok])
            gate([b_q], [b_yT2])
            yTok = A1[:, 8192:16384].rearrange("p (k a j g h) -> p k a j g h", k=2, a=4, j=8, g=8)
            ev = [0]

            def evac(out_ap, in_ap, r, w):
                if ev[0] % 2 == 0:
                    A(I("activation", out_ap, in_ap, AF.Copy), r, w)
                else:
                    V(I("tensor_copy", out_ap, in_ap), r, w)
                ev[0] += 1
            for cblk in range(2):
                for g4 in range(8):
                    pt, bp = nps()
                    MM([mm(pt[:, k4 * 128:(k4 + 1) * 128], Yall[:, g4 * 4 + k4, cblk * 128:(cblk + 1) * 128], identb[:, :])
                        for k4 in range(4)], b_Yb + [b_cst], [bp])
                    evac(yTok[:, cblk, g4 // 2, :, (g4 % 2) * 4:(g4 % 2) * 4 + 4, :],
                         pt[:, :].rearrange("p (k j h) -> p j k h", k=4, j=8), [bp], [b_yTok])
            for a_ in range(4):
                for jp in range(4):
                    pt, bp = nps()
                    fns = []
                    for jj in range(2):
                        j = jp * 2 + jj
                        for cblk in range(2):
                            c0 = (jj * 2 + cblk) * 128
                            fns.append(mm(pt[:, c0:c0 + 128], yTok[:, cblk, a_, j, :, :].rearrange("p g h -> p (g h)"), identb[:, :]))
                    MM(fns, [b_yTok, b_cst], [bp])
                    evac(yT2[:, a_, jp * 2:jp * 2 + 2, :].rearrange("p j c -> p (j c)"), pt[:, :], [bp], [b_yT2])
            gate([b_yTok], [b_glu])
            yflat = yT2.rearrange("p a j c -> p a (j c)")
            for cb in range(4):
                cs_ = slice(cb * 512, (cb + 1) * 512)
                for oc in range(4):
                    pv, bpv = nps(); pg, bpg = nps()
                    MM([mm(pv[:, :], wglu[:, k, oc * 128:(oc + 1) * 128], yflat[:, k, cs_], k == 0, k == 3) for k in range(4)],
                       [b_WA, b_yT2], [bpv])
                    MM([mm(pg[:, :], wglu[:, k, 512 + oc * 128:512 + (oc + 1) * 128], yflat[:, k, cs_], k == 0, k == 3) for k in range(4)],
                       [b_WA, b_yT2], [bpg])
                    A(I("activation", sig[:], pg[:, :], AF.Sigmoid), [bpg], [b_sig])
                    dst = gluT[:, oc, :].rearrange("p (c j) -> p j c", j=8)[:, 2 * cb:2 * cb + 2, :]
                    V(I("tensor_tensor", dst, pv[:, :].rearrange("p (j c) -> p j c", j=2),
                                                                sig[:, :].rearrange("p (j c) -> p j c", j=2), ALU.mult),
                      [bpv, b_sig], [b_glu])
            if l == 0:
                tap("gluT", gluT.rearrange("p j t -> p (j t)"), [b_glu], 2048)
            ck("glu")
            for tt in range(NTT):
                ts = slice(tt * 512, (tt + 1) * 512)
                for m in range(8):
                    pt, bp = nps()
                    fns = [mm(pt[:, :], wout[:, k, m * 128:(m + 1) * 128], attT[:, k, ts], k == 0, False) for k in range(4)]
                    fns += [mm(pt[:, :], wout[:, 4 + k, m * 128:(m + 1) * 128], gluT[:, k, ts], False, k == 3) for k in range(4)]
                    MM(fns, [b_WA, b_WB, b_att, b_glu], [bp])
                    V(I("tensor_add", xT[:, m, ts], xT[:, m, ts], pt[:, :]), [bp], [b_x[m][tt]])
            if l == 0:
                tap("xmid", xT[:, 0, :], b_x[0], 2048)

            ck("wout")
            gate(a1_p1, b_h)
            gate([b_att, b_yT2, b_uT2, b_q], [])
            gate(misc_att, misc_ffn)
            gate([b_E], [b_WB])
            for tt in range(NTT):
                rms_tile(l, tt, g2)
            b_ws = [b_WA, b_WB]
            for gI in range(8):
                s = gI % 2
                w1, w2 = wsl[s]
                P.dma("gpsimd", I("dma_start",
                    out=w1, in_=w_ff1_d[l][:, gI * 512:(gI + 1) * 512].rearrange("(k p) f -> p k f", p=128)), writes=[b_ws[s]])
                P.dma("gpsimd", I("dma_start",
                    out=w2, in_=w_ff2_d[l][gI * 512:(gI + 1) * 512, :].rearrange("(k p) f -> p k f", p=128)), writes=[b_ws[s]])
                for tt in range(NTT):
                    ts = slice(tt * 512, (tt + 1) * 512)
                    for fc in range(4):
                        pt, bp = nps()
                        MM([mm(pt[:, :], w1[:, k, fc * 128:(fc + 1) * 128], hT[:, k, ts], k == 0, k == 7) for k in range(8)],
                           [b_ws[s], b_h[tt]], [bp])
                        ri_ = fc % 2
                        A(I("activation", rl[ri_], pt[:, :], AF.Relu), [bp], [b_rl[ri_]])
                        V(I("tensor_mul", aT[:, fc, :], rl[ri_], rl[ri_]), [b_rl[ri_]], [b_a[fc]])
                    for m in range(8):
                        pt, bp = nps()
                        MM([mm(pt[:, :], w2[:, k, m * 128:(m + 1) * 128], aT[:, k, :], k == 0, k == 3) for k in range(4)],
                           [b_ws[s]] + b_a, [bp])
                        V(I("tensor_add", xT[:, m, ts], xT[:, m, ts], pt[:, :]), [bp], [b_x[m][tt]])
            gate(misc_ffn, misc_att)

        try:
            for l in range(nlayers):
                layer(l)
        except _Stop:
            gate(xr_mixer, allx)
        emit_taps()
        for k in range(8):
            P.dma("sync", I("dma_start", out=out_d[k * 128:(k + 1) * 128, :], in_=xT[:, k, :]), reads=b_x[k])
        P.wait_all("sync", allx + b_tS)
        P.finish("sync")
        P.run()
    return nc


def _prep_inputs(x, norm1, w_in, q_gain, k_gain, sink, lam_re, lam_im, log_dt, b_re, b_im, c_re, c_im,
                 d_skip, w_glu, w_out, norm2, w_ff1, w_ff2, nlayers=DEPTH):
    f = lambda a: np.ascontiguousarray(np.asarray(a, dtype=np.float32))
    L = DEPTH
    perm = []
    for j in range(4):
        perm += list(range(j * 64, j * 64 + 64)) + list(range((4 + j) * 64, (4 + j) * 64 + 64))
    perm += list(range(512, 1280))
    shared = {
        "w_in": f(np.asarray(w_in)[:nlayers][:, :, perm]), "w_glu": f(np.asarray(w_glu)[:nlayers]),
        "w_out": f(np.asarray(w_out)[:nlayers]),
        "w_ff1": f(np.asarray(w_ff1)[:nlayers]), "w_ff2": f(np.asarray(w_ff2)[:nlayers]),
        "g1": f(np.asarray(norm1).reshape(L, 8, 128).transpose(2, 0, 1)),
        "g2": f(np.asarray(norm2).reshape(L, 8, 128).transpose(2, 0, 1)),
        "qg": f(np.tile(np.asarray(q_gain).T, (2, 1))),
        "kg": f(np.tile(np.asarray(k_gain).T, (2, 1))),
        "sinkr": f(np.broadcast_to(np.asarray(sink)[None], (128, L, 8))),
    }

    def gn(a):
        a = np.asarray(a).reshape(L, 2, 16, 2, 64)
        return f(a.transpose(3, 4, 0, 1, 2).reshape(128, L, 32))
    shared["lre"] = gn(lam_re)
    shared["lim"] = gn(lam_im)
    shared["ldt"] = gn(np.broadcast_to(np.asarray(log_dt)[:, :, :, None], (L, 2, 32, 64)))

    def bb(a):
        a = np.asarray(a).reshape(L, 16, 2, 64, 16)
        return f(a.transpose(2, 3, 0, 1, 4).reshape(128, L, 256))
    shared["bre"] = bb(b_re)
    shared["bim"] = bb(b_im)

    def cc(a):
        a = np.asarray(a).reshape(L, 2, 16, 2, 16, 64)
        return f(a.transpose(3, 5, 0, 1, 2, 4).reshape(128, L, 512))
    shared["cre"] = cc(c_re)
    shared["cim"] = cc(c_im)
    dsk = np.asarray(d_skip).reshape(L, 32, 16)
    shared["dcol"] = f(np.broadcast_to(dsk.transpose(2, 0, 1)[None], (8, 16, L, 32)).reshape(128, L, 32))
    slopes = np.exp2(-8.0 * np.arange(1, 9) / 8.0)
    ci = np.arange(128)[:, None]; qi = np.arange(128)[None, :]
    et = np.zeros((128, 3, 8, 128), np.float32)
    for kb in range(3):
        dist = np.abs(qi - ci - (kb - 1) * 128)
        valid = dist <= 128
        for h in range(8):
            et[:, kb, h, :] = np.where(valid, np.exp(-slopes[h] * dist), 0.0)
    shared["etab"] = f(et.reshape(128, 3072))
    cst = np.zeros((128, 656), np.float32)
    cst[:, 0:128] = np.eye(128)
    bi = np.arange(128)[:, None] // 16; bj = np.arange(128)[None, :] // 16
    cst[:, 128:256] = (bj >= bi)
    cst[:, 256:384] = (bi >= bj)
    cst[:, 384:393] = np.arange(9)[None, :]
    cst[:, 400:656] = np.arange(1, 257)[None, :]
    shared["cst"] = cst
    xs = np.asarray(x)
    in_maps = []
    for r in range(8):
        b, q = r // 4, r % 4
        m = dict(shared)
        m["xT"] = f(xs[b, q * T:(q + 1) * T, :].T)
        s = np.zeros((128, 16), np.float32)
        s[:, q] = 1.0
        if q > 0:
            s[:, 4 + q - 1] = 1.0; s[:, 12] = 1.0
        if q < 3:
            s[:, 8 + q + 1] = 1.0; s[:, 13] = 1.0
        m["sel"] = s
        in_maps.append(m)
    return in_maps


_NC_CACHE = {}


def kernel(**inputs):
    in_maps = _prep_inputs(**inputs)
    if "nc" not in _NC_CACHE:
        _NC_CACHE["nc"] = build_nc()
    nc = _NC_CACHE["nc"]
    res = run_bass_kernel_spmd(nc, in_maps, core_ids=list(range(8)))
    out = np.zeros((2, 4 * T, 1024), np.float32)
    for r in range(8):
        b, q = r // 4, r % 4
        out[b, q * T:(q + 1) * T, :] = np.asarray(res.results[r]["outT"]).T
    return out
```

```python
import math
import os
import numpy as np
from contextlib import ExitStack
import concourse.bass as bass
import concourse.mybir as mybir
from concourse.bass_utils import run_bass_kernel_spmd

F32 = mybir.dt.float32
BF16 = mybir.dt.bfloat16
I32 = mybir.dt.int32
AF = mybir.ActivationFunctionType
ALU = mybir.AluOpType

DEPTH = 4
T = 2048
NTT = 4
EPS = 1e-6
TWO_PI = 2.0 * math.pi


class Buf:
    def __init__(self, name=""):
        self.name = name
        self.w = None
        self.r = []


class EngQ:
    def __init__(self, name, sem):
        self.name = name
        self.sem = sem
        self.count = 0
        self.ops = []
        self.seen = {}


class Prog:
    ENGS = ["sync", "scalar", "gpsimd", "vector", "tensor"]

    def __init__(self, nc, es, ndma=16):
        self.nc = nc
        self.q = {e: EngQ(e, es.enter_context(nc.semaphore("s_" + e))) for e in self.ENGS}
        self.dq = {e: [EngQ(f"d_{e}{k}", es.enter_context(nc.semaphore(f"d_{e}{k}"))) for k in range(ndma)]
                   for e in ["sync", "gpsimd"]}
        self.rr = {e: 0 for e in self.dq}
        self.nops = 0

    def _waits(self, eng, reads, writes):
        q = self.q[eng]
        need = {}

        def add(tok):
            if tok is None:
                return
            s, v = tok
            if eng == "tensor" and s is q:
                return
            if need.get(s, 0) < v:
                need[s] = v
        for b in reads:
            add(b.w)
        for b in writes:
            add(b.w)
            for t in b.r:
                add(t)
        for s, v in need.items():
            if q.seen.get(s, 0) >= v:
                continue
            q.seen[s] = v
            q.ops.append(lambda e, s=s, v=v: e.wait_ge(s.sem, v))

    def _mark(self, tok, reads, writes):
        for b in reads:
            b.r = [t for t in b.r if t[0] is not tok[0]] + [tok]
        for b in writes:
            b.w = tok
            b.r = []

    def op(self, eng, fn, reads=(), writes=()):
        return self.group(eng, [fn], reads, writes)

    def group(self, eng, fns, reads=(), writes=()):
        q = self.q[eng]
        self._waits(eng, reads, writes)
        for fn in fns[:-1]:
            q.ops.append(lambda e, fn=fn: fn(e))
        q.count += 1
        tok = (q, q.count)
        q.ops.append(lambda e, fn=fns[-1], q=q: fn(e).then_inc(q.sem, 1))
        self._mark(tok, reads, writes)
        self.nops += len(fns)
        return tok

    def dma(self, eng, fn, reads=(), writes=()):
        self._waits(eng, reads, writes)
        k = self.rr[eng]
        self.rr[eng] = (k + 1) % len(self.dq[eng])
        d = self.dq[eng][k]
        q_ = self.q[eng]
        if d.count > 0 and q_.seen.get(d, 0) < d.count:
            q_.seen[d] = d.count
            q_.ops.append(lambda e, d=d, v=d.count: e.wait_ge(d.sem, v))
        d.count += 16
        tok = (d, d.count)
        self.q[eng].ops.append(lambda e, fn=fn, d=d: fn(e).then_inc(d.sem, 16))
        self._mark(tok, reads, writes)
        self.nops += 1
        return tok

    def wait_all(self, eng, bufs):
        self._waits(eng, [], bufs)

    def finish(self, eng="sync"):
        q = self.q[eng]
        allq = [x for x in self.q.values() if x is not q] + [d for ds in self.dq.values() for d in ds]
        for s_ in allq:
            if s_.count > 0:
                q.ops.append(lambda e, s_=s_, v=s_.count: e.wait_ge(s_.sem, v))

    def run(self):
        with self.nc.Block() as block:
            for e in self.ENGS:
                ops = self.q[e].ops

                def body(eng, ops=ops):
                    for o in ops:
                        o(eng)
                getattr(block, e)(body)


class _Stop(Exception):
    pass


def build_nc(nlayers=DEPTH, taps=None, stop_after=None):
    nc = bass.Bass("TRN2", target_bir_lowering=False)
    L = DEPTH

    def din(name, shape, dt=F32):
        return nc.dram_tensor(name, list(shape), dt, kind="ExternalInput").ap()

    xT_d = din("xT", [1024, T])
    LW = nlayers
    w_in_d = din("w_in", [LW, 1024, 1280])
    w_glu_d = din("w_glu", [LW, 512, 1024])
    w_out_d = din("w_out", [LW, 1024, 1024])
    w_ff1_d = din("w_ff1", [LW, 1024, 4096])
    w_ff2_d = din("w_ff2", [LW, 4096, 1024])
    g1_d = din("g1", [128, L, 8])
    g2_d = din("g2", [128, L, 8])
    qg_d = din("qg", [128, L])
    kg_d = din("kg", [128, L])
    sink_d = din("sinkr", [128, L, 8])
    lre_d = din("lre", [128, L, 32])
    lim_d = din("lim", [128, L, 32])
    ldt_d = din("ldt", [128, L, 32])
    bre_d = din("bre", [128, L, 256])
    bim_d = din("bim", [128, L, 256])
    cre_d = din("cre", [128, L, 512])
    cim_d = din("cim", [128, L, 512])
    dcol_d = din("dcol", [128, L, 32])
    etab_d = din("etab", [128, 3072])
    cst_d = din("cst", [128, 656])
    sel_d = din("sel", [128, 16])
    out_d = nc.dram_tensor("outT", [1024, T], F32, kind="ExternalOutput").ap()
    taps = list(taps) if taps else []
    dbg_d = nc.dram_tensor("dbg", [max(1, len(taps)), 128, 2048], F32, kind="ExternalOutput").ap() if taps else None

    xsp = nc.dram_tensor("xsp", [1024, T], F32).ap()
    hal_src = nc.dram_tensor("hal_src", [128, 512], F32).ap()
    hal_dst = nc.dram_tensor("hal_dst", [4 * 128, 512], F32).ap()
    st_src = nc.dram_tensor("st_src", [128, 64], F32).ap()
    st_dst = nc.dram_tensor("st_dst", [4 * 128, 64], F32).ap()
    GROUPS = [[0, 1, 2, 3], [4, 5, 6, 7]]

    with ExitStack() as es:
        P = Prog(nc, es)

        def sb(name, shape, dt=F32):
            return es.enter_context(nc.sbuf_tensor(name, list(shape), dt))

        XR = sb("XR", [128, 16384], F32)
        XRb = XR[:, :].bitcast(BF16)
        A1 = sb("A1", [128, 16384], BF16)
        A1f = A1[:, :].bitcast(F32)
        WA = sb("WA", [128, 16384], BF16)
        WAf = WA[:, :].bitcast(F32)
        A3 = sb("A3", [128, 8192], BF16)
        A4 = sb("A4", [128, 8192], BF16)
        MISC = sb("MISC", [128, 4096], BF16)

        xT = XR[:, :].rearrange("p (k t) -> p k t", k=8)
        b_x = [[Buf() for _ in range(NTT)] for _ in range(8)]
        allx = [b for row in b_x for b in row]
        U = XRb[:, 0:8192].rearrange("p (g c) -> p g c", g=32); b_U = Buf()
        PT = XRb[:, 8192:16384].rearrange("p (d g r n) -> p d g r n", d=2, g=32, r=2); b_PT = Buf()
        Qr = XRb[:, 16384:20480].rearrange("p (d g f) -> p d g f", d=2, g=16)
        Qn = XRb[:, 20480:24576].rearrange("p (d g f) -> p d g f", d=2, g=16); b_Qh = [Buf(), Buf()]
        Tm = XRb[:, 24576:28672].rearrange("p (g f) -> p g f", g=32); b_T = Buf()
        etab = XRb[:, 28672:31744].rearrange("p (k h q) -> p k h q", k=3, h=8); b_et = Buf()
        xr_mixer = [b_U, b_PT, b_T, b_et] + b_Qh

        hT = A1[:, :].rearrange("p (k t) -> p k t", k=8); b_h = [Buf() for _ in range(NTT)]
        bre = A1f[:, 0:256].rearrange("p (g h) -> p g h", g=16)
        bim = A1f[:, 256:512].rearrange("p (g h) -> p g h", g=16)
        cre = A1f[:, 512:1024].rearrange("p (d g h) -> p d g h", d=2, g=16)
        cim = A1f[:, 1024:1536].rearrange("p (d g h) -> p d g h", d=2, g=16)
        b_par2 = Buf()
        halg = A1f[:, 1536:3584].rearrange("p (r f) -> p r f", r=4); b_halg = Buf()
        tA = A1f[:, 4096:6144]; tB = A1f[:, 6144:8192]; b_tA = Buf(); b_tB = Buf(); b_tA2 = Buf(); b_tB2 = Buf()
        Yall = A1[:, 0:8192].rearrange("p (g c) -> p g c", g=32); b_Yall = Buf()
        yT2 = A3[:, :].rearrange("p (a j c) -> p a j c", a=4, j=8); b_yT2 = Buf()
        cosT = A1f[:, 4096:4608].rearrange("p (g c) -> p g c", g=2)
        sinT = A1f[:, 4608:5120].rearrange("p (g c) -> p g c", g=2); b_tab = Buf()
        Wr = A1f[:, 5120:5632].rearrange("p (g c) -> p g c", g=2)
        Wi = A1f[:, 5632:6144].rearrange("p (g c) -> p g c", g=2); b_W = Buf()
        Sr = A1f[:, 6144:6656].rearrange("p (g c) -> p g c", g=2)
        Si = A1f[:, 6656:7168].rearrange("p (g c) -> p g c", g=2); b_S = Buf()
        Hb16 = {}
        for d in range(2):
            for ni, n in enumerate("ri"):
                o = 14336 + (d * 2 + ni) * 512
                Hb16[(d, n)] = A1[:, o:o + 512].rearrange("p (g c) -> p g c", g=2)
        b_H = Buf()
        a1_s2 = [b_par2, b_halg, b_tA, b_tB, b_tA2, b_tB2]
        b_Yb = [Buf() for _ in range(8)]

        win = WA[:, 0:10240].rearrange("p (k f) -> p k f", k=8); b_WA = Buf(); b_WB = Buf()
        X_r = WA[:, 0:4096].rearrange("p (d g f) -> p d g f", d=2, g=16)
        X_i = WA[:, 4096:8192].rearrange("p (d g f) -> p d g f", d=2, g=16); b_Xh = [Buf(), Buf()]
        pw = {n: WAf[:, 4096 + i * 288:4096 + (i + 1) * 288].rearrange("p (k c) -> p k c", k=9)
              for i, n in enumerate(["pr", "pi", "mr", "mi"])}
        b_pw = Buf()
        cs_s = WAf[:, 5248:5536]; cs_c = WAf[:, 5536:5824]
        mg_p = WAf[:, 5824:6112]; mg_m = WAf[:, 6112:6400]
        Bb = {"r": WAf[:, 6400:6912].rearrange("p (d g h) -> p d g h", d=2, g=16),
              "i": WAf[:, 6912:7424].rearrange("p (d g h) -> p d g h", d=2, g=16)}
        b_Bb = Buf()
        zt = [WAf[:, 7424 + i * 32:7424 + (i + 1) * 32] for i in range(6)]; b_zt = Buf()
        wa_s2 = [b_pw, b_Bb, b_zt] + b_Xh
        E0c = WAf[:, 6144:6656].rearrange("p (g c) -> p g c", g=32)
        E0s = WAf[:, 6656:7168].rearrange("p (g c) -> p g c", g=32)
        E1c = WAf[:, 7168:7680].rearrange("p (g c) -> p g c", g=32)
        E1s = WAf[:, 7680:8192].rearrange("p (g c) -> p g c", g=32)
        b_E = Buf()
        wglu = WA[:, 0:4096].rearrange("p (k f) -> p k f", k=4)
        wout = WA[:, 4096:12288].rearrange("p (k f) -> p k f", k=8)
        wsl = [(WA[:, s * 8192:s * 8192 + 4096].rearrange("p (k f) -> p k f", k=8),
                WA[:, s * 8192 + 4096:s * 8192 + 8192].rearrange("p (k f) -> p k f", k=4)) for s in range(2)]

        qT = A3[:, :].rearrange("p (j t) -> p j t", j=4); b_q = Buf()
        gluT = A1[:, 8192:16384].rearrange("p (j t) -> p j t", j=4); b_glu = Buf()
        a1_p1 = b_Yb + [b_Yall, b_glu, b_tab, b_W, b_S, b_H]
        uT2 = A4[:, :].rearrange("p (a i c) -> p a i c", a=4, i=8); b_uT2 = Buf()
        attT = A4[:, :].rearrange("p (j t) -> p j t", j=4); b_att = Buf()

        pM2 = [[MISC[:, (s_ * 3 + i) * 512:(s_ * 3 + i + 1) * 512] for i in range(3)] for s_ in range(2)]
        b_pM2 = [[Buf() for _ in range(3)] for _ in range(2)]
        rden = MISC[:, 3072:3584].bitcast(F32).rearrange("p (a q) -> p a q", a=2); b_rden = Buf()
        rl = [MISC[:, i * 512:(i + 1) * 512] for i in range(2)]; b_rl = [Buf(), Buf()]
        aT = MISC[:, 1024:3072].rearrange("p (k t) -> p k t", k=4); b_a = [Buf() for _ in range(4)]
        misc_att = b_pM2[0] + b_pM2[1] + [b_rden]
        misc_ffn = b_rl + b_a

        kT = sb("kT", [128, T + 256], BF16); b_k = Buf()
        vtm = sb("vtm", [128, 18, 128], BF16); b_v = Buf()
        cst = sb("cst_sb", [128, 656]); b_cst = Buf()
        cstb = sb("cstb_sb", [128, 128], BF16)
        onesb = sb("onesb", [128, 128], BF16)
        blk2 = sb("blk2", [128, 128], BF16)
        ones_pn = sb("ones_pn", [128, 2, 64], BF16)
        sel = sb("sel_sb", [128, 16]); b_sel = Buf()
        g1 = sb("g1_sb", [128, L, 8]); g2 = sb("g2_sb", [128, L, 8])
        qg = sb("qg_sb", [128, L]); kg = sb("kg_sb", [128, L])
        sinkr = sb("sink_sb", [128, L * 8]); esink = sb("esink", [128, L, 8])
        b_small = Buf()
        epsb = sb("epsb", [128, 1])
        lre = sb("lre_sb", [128, 32]); lim = sb("lim_sb", [128, 32]); ldt = sb("ldt_sb", [128, 32])
        dcol = sb("dcol_sb", [128, 32]); b_par = Buf()
        lrdt = sb("lrdt", [128, 32]); th = sb("th", [128, 32]); Thu = sb("Thu", [128, 32])
        R8 = sb("R8", [128, 32]); Acr = sb("Acr", [128, 32]); Aci = sb("Aci", [128, 32])
        finr = sb("finr", [128, 32]); fini = sb("fini", [128, 32]); mag2048 = sb("mag2048", [128, 32])
        b_der = Buf()
        tS = [sb(f"tS{i}", [128, 512]) for i in range(3)]; b_tS = [Buf() for _ in range(3)]
        tSI = sb("tSI", [128, 512], I32); b_tSI = Buf()
        Fst = sb("Fst", [128, 64]); b_F = Buf()
        stg = sb("stg", [128, 4, 64]); b_stg = Buf()
        carry = sb("carry", [128, 64]); b_carry = Buf()
        hcr = sb("hcr", [128, 64]); b_hcr = Buf()
        halt = sb("halt", [128, 512]); b_halt = Buf()
        sq = sb("sq", [128, 2, 512], BF16); b_sq = [Buf(), Buf()]
        rstd = sb("rstd", [128, 512]); b_rstd = Buf()
        sig = sb("sig", [128, 512]); b_sig = Buf()
        gatec = sb("gatec", [128, 8]); b_gate = Buf()

        psb = [es.enter_context(nc.psum_tensor(f"ps{i}", [128, 512], F32)) for i in range(8)]
        b_ps = [Buf() for _ in range(8)]
        ps_pools = {"all": list(range(8)), "ssm": [0, 1, 2], "att_s": [5, 6, 7], "att_o": [3, 4]}
        ps_rr = {k: 0 for k in ps_pools}

        def nps(pool="all"):
            pool = "all"
            lst = ps_pools[pool]
            i = lst[ps_rr[pool] % len(lst)]
            ps_rr[pool] += 1
            return psb[i], b_ps[i]

        def V(fn, r=(), w=()):
            return P.op("vector", fn, r, w)

        def A(fn, r=(), w=()):
            return P.op("scalar", fn, r, w)

        def MM(fns, r=(), w=()):
            return P.group("tensor", fns, r, w)

        def I(method, *a, **k):
            return lambda e: getattr(e, method)(*a, **k)

        def mm(out, lhsT, rhs, start=True, stop=True):
            return I("matmul", out, lhsT=lhsT, rhs=rhs, start=start, stop=stop)

        def gate(old, new):
            V(I("memset", gatec[:, 0:1], 0.0), [], list(old) + list(new) + [b_gate])

        tap_i = [0]

        deferred_taps = []

        def tap(name, ap, bufs, n):
            if name in taps:
                deferred_taps.append((name, ap, bufs, n))

        def emit_taps():
            if not deferred_taps:
                return
            P.finish("vector")
            P.finish("sync")
            for (name, ap, bufs, n) in deferred_taps:
                idx = taps.index(name)
                for c0 in range(0, n, 512):
                    c1 = min(n, c0 + 512)
                    V(I("tensor_copy", tS[2][:, 0:c1 - c0], ap[:, c0:c1]), list(bufs), [b_tS[2]])
                    P.dma("sync", I("dma_start", out=dbg_d[idx, :, c0:c1], in_=tS[2][:, 0:c1 - c0]), reads=[b_tS[2]])

        P.dma("sync", I("dma_start", out=cst[:], in_=cst_d), writes=[b_cst])
        P.dma("sync", I("dma_start", out=sel[:], in_=sel_d), writes=[b_sel])
        for (t_sb, t_d) in [(g1, g1_d), (g2, g2_d), (qg, qg_d), (kg, kg_d)]:
            P.dma("sync", I("dma_start", out=t_sb[:], in_=t_d), writes=[b_small])
        P.dma("sync", I("dma_start", out=sinkr[:], in_=sink_d.rearrange("p l h -> p (l h)")), writes=[b_small])
        for k in range(8):
            P.dma("sync", I("dma_start", out=xT[:, k, :], in_=xT_d[k * 128:(k + 1) * 128, :]),
                  writes=b_x[k])
        ident = cst[:, 0:128]; Mf = cst[:, 128:256]; Mb = cst[:, 256:384]; kvec = cst[:, 384:393]
        cpos = cst[:, 400:656]
        V(I("tensor_copy", cstb[:], ident), [b_cst], [b_cst])
        V(I("memset", onesb[:], 1.0), [], [b_cst])
        V(I("memset", blk2[:], 0.0), [], [b_cst])
        V(I("memset", blk2[0:64, 0:64], 1.0), [], [b_cst])
        V(I("memset", blk2[64:128, 64:128], 1.0), [], [b_cst])
        V(I("memset", epsb[:], EPS), [], [b_cst])
        V(I("tensor_copy", ones_pn[:, 0, :], sel[:, 12:13].to_broadcast([128, 64])), [b_sel], [b_cst])
        V(I("tensor_copy", ones_pn[:, 1, :], sel[:, 13:14].to_broadcast([128, 64])), [b_sel], [b_cst])
        A(I("activation", esink[:, :, :].rearrange("p l h -> p (l h)"), sinkr[:], AF.Exp), [b_small], [b_small])
        identb = cstb
        c16 = sb("c16", [128, 16])
        V(I("tensor_scalar", c16[:], cpos[:, 0:16], -1.0, 16.0, ALU.add, ALU.mult), [b_cst], [b_cst])
        crev = sb("crev", [128, 256])
        V(I("tensor_scalar", crev[:], cpos, -1.0, 256.0, ALU.mult, ALU.add), [b_cst], [b_cst])
        L8 = sb("L8", [128, 32]); facc = sb("facc", [128, 4]); b_facc = Buf()
        tabsets = [(cosT, sinT, b_tab), (rstd[:, :].rearrange("p (g c) -> p g c", g=2), sig[:, :].rearrange("p (g c) -> p g c", g=2), None)]
        ssm_it = [0]
        esrow = sb("esrow", [1, 8, 128], BF16); b_esr = Buf()

        def rms_tile(l, tt, gain):
            ts = slice(tt * 512, (tt + 1) * 512)
            pt, bp = nps()
            for k in range(8):
                s = k % 2
                A(I("activation", sq[:, s, :], xT[:, k, ts], AF.Square), [b_x[k][tt]], [b_sq[s]])
                P.op("tensor", mm(pt[:, :], onesb[:, :], sq[:, s, :], start=(k == 0), stop=(k == 7)), [b_sq[s], b_cst], [bp])
            A(I("activation", rstd[:], pt[:, :], AF.Ln, bias=epsb[:], scale=1.0 / 1024.0), [bp, b_cst], [b_rstd])
            A(I("activation", rstd[:], rstd[:], AF.Exp, scale=-0.5), [b_rstd], [b_rstd])
            for k in range(8):
                V(I("scalar_tensor_tensor", out=hT[:, k, ts], in0=xT[:, k, ts], scalar=gain[:, l, k:k + 1],
                                                        in1=rstd[:], op0=ALU.mult, op1=ALU.mult),
                  [b_x[k][tt], b_rstd, b_small], [b_h[tt]])

        def sincos_turns(tin, n, out_s, out_c, rb, wb):
            a_ = tS[0][:, 0:n]; b_ = tS[1][:, 0:n]; i_ = tSI[:, 0:n]
            for (off, outp) in [(0.0, out_s), (0.25, out_c)]:
                V(I("tensor_scalar_add", a_, tin, off), rb, [b_tS[0]])
                V(I("tensor_copy", i_, a_), [b_tS[0]], [b_tSI])
                V(I("tensor_copy", b_, i_), [b_tSI], [b_tS[1]])
                V(I("tensor_sub", a_, a_, b_), [b_tS[0], b_tS[1]], [b_tS[0]])
                V(I("tensor_scalar", a_, a_, -0.4999999, 0.4999999, ALU.max, ALU.min), [b_tS[0]], [b_tS[0]])
                A(I("activation", outp, a_, AF.Sin, scale=TWO_PI), [b_tS[0]], wb)

        def ck(name):
            if stop_after == name:
                raise _Stop()

        if os.environ.get("HALO_FIRST"):
            V(I("memset", halt[:], 1.0), [], [b_halt])
            P.dma("gpsimd", I("dma_start", out=hal_src, in_=halt[:]), reads=[b_halt], writes=[b_halg])
            P.op("gpsimd", I("collective_compute", "AllGather", ALU.bypass, replica_groups=GROUPS,
                             ins=[hal_src.opt()], outs=[hal_dst.opt()]), [b_halg], [b_halg])
            P.dma("gpsimd", I("dma_start", out=halg, in_=hal_dst.rearrange("(r p) f -> p r f", p=128)),
                  reads=[b_halg], writes=[b_halg])
            if os.environ.get("HALO_FIRST") == "only":
                raise_stop = True

        def layer(l):
            P.dma("gpsimd", I("dma_start", out=win, in_=w_in_d[l].rearrange("(k p) f -> p k f", p=128)),
                  writes=[b_WA, b_WB])
            for (t_sb, t_d) in [(lre, lre_d), (lim, lim_d), (ldt, ldt_d), (dcol, dcol_d)]:
                P.dma("sync", I("dma_start", out=t_sb[:], in_=t_d[:, l]), writes=[b_par])

            for tt in range(NTT):
                ts = slice(tt * 512, (tt + 1) * 512)
                rms_tile(l, tt, g1)
                for k in range(8 if not os.environ.get("NOSPILL") else 0):
                    P.dma("sync", I("dma_start", out=xsp[k * 128:(k + 1) * 128, ts], in_=xT[:, k, ts]),
                          reads=[b_x[k][tt]])
                for oc in range(5):
                    pt, bp = nps()
                    MM([mm(pt[:, :], win[:, k, oc * 128:(oc + 1) * 128], hT[:, k, ts], k == 0, k == 7) for k in range(8)],
                       [b_WA, b_h[tt]], [bp])
                    A(I("activation", sq[:, 0, :], pt[:, :], AF.Square), [bp], [b_sq[0]])
                    p2, bp2 = nps()
                    MM([mm(p2[:, :], blk2[:, :], sq[:, 0, :])], [b_sq[0], b_cst], [bp2])
                    A(I("activation", rstd[:], p2[:, :], AF.Ln, bias=epsb[:], scale=1.0 / 64.0),
                      [bp2, b_cst], [b_rstd])
                    A(I("activation", rstd[:], rstd[:], AF.Exp, scale=-0.5), [b_rstd], [b_rstd])
                    if oc < 4:
                        V(I("scalar_tensor_tensor", out=qT[:, oc, ts], in0=pt[:, :], scalar=qg[:, l:l + 1],
                                                                               in1=rstd[:], op0=ALU.mult, op1=ALU.mult),
                          [bp, b_rstd, b_small], [b_q])
                    else:
                        V(I("scalar_tensor_tensor", out=kT[:, 128 + tt * 512:128 + (tt + 1) * 512], in0=pt[:, :],
                                                                         scalar=kg[:, l:l + 1], in1=rstd[:], op0=ALU.mult, op1=ALU.mult),
                          [bp, b_rstd, b_small], [b_k])
                        if tt == 0:
                            V(I("scalar_tensor_tensor", out=halt[:, 0:128], in0=pt[:, 0:128], scalar=kg[:, l:l + 1],
                                                                      in1=rstd[:, 0:128], op0=ALU.mult, op1=ALU.mult),
                              [bp, b_rstd, b_small], [b_halt])
                        if tt == NTT - 1:
                            V(I("scalar_tensor_tensor", out=halt[:, 128:256], in0=pt[:, 384:512], scalar=kg[:, l:l + 1],
                                                                      in1=rstd[:, 384:512], op0=ALU.mult, op1=ALU.mult),
                              [bp, b_rstd, b_small], [b_halt])
                pt, bp = nps()
                fns = []
                for b4 in range(4):
                    for k in range(8):
                        fns.append(mm(pt[:, b4 * 128:(b4 + 1) * 128], hT[:, k, tt * 512 + b4 * 128: tt * 512 + (b4 + 1) * 128],
                                      win[:, k, 640:768], k == 0, k == 7))
                MM(fns, [b_WA, b_h[tt]], [bp])
                A(I("activation", vtm[:, 1 + tt * 4:1 + (tt + 1) * 4, :].rearrange("p b f -> p (b f)"),
                                                       pt[:, :], AF.Copy), [bp], [b_v])
                if tt == 0:
                    V(I("tensor_copy", halt[:, 256:384], pt[:, 0:128]), [bp], [b_halt])
                if tt == NTT - 1:
                    V(I("tensor_copy", halt[:, 384:512], pt[:, 384:512]), [bp], [b_halt])
                for a_ in range(4):
                    oc = 6 + a_
                    pt, bp = nps()
                    MM([mm(pt[:, :], win[:, k, oc * 128:(oc + 1) * 128], hT[:, k, ts], k == 0, k == 7) for k in range(8)],
                       [b_WA, b_h[tt]], [bp])
                    V(I("tensor_copy", uT2[:, a_, :, tt * 64:(tt + 1) * 64], pt[:, :].rearrange("p (c i) -> p i c", i=8)),
                      [bp], [b_uT2])
            if l == 0:
                tap("hT", hT.rearrange("p k t -> p (k t)"), b_h, 2048)
                tap("qT", qT.rearrange("p j t -> p (j t)"), [b_q], 2048)
                tap("kT", kT[:, 128:128 + 2048], [b_k], 2048)
                tap("uT2", uT2.rearrange("p a i c -> p (a i c)"), [b_uT2], 2048)

            ck("S1")
            gate(b_h, a1_s2)
            P.dma("gpsimd", I("dma_start", out=hal_src, in_=halt[:]), reads=[b_halt], writes=[b_halg])
            if not os.environ.get("NOCC"):
                P.op("gpsimd", I("collective_compute", "AllGather", ALU.bypass, replica_groups=GROUPS,
                                 ins=[hal_src.opt()], outs=[hal_dst.opt()]), [b_halg], [b_halg])
            P.dma("gpsimd", I("dma_start", out=halg, in_=hal_dst.rearrange("(r p) f -> p r f", p=128)),
                  reads=[b_halg], writes=[b_halg])

            ck("halo")
            gate([b_WA, b_WB, b_E], wa_s2)
            gate(allx, xr_mixer)
            P.dma("gpsimd", I("dma_start", out=etab.rearrange("p k h q -> p (k h q)"), in_=etab_d), writes=[b_et])
            for (t_ap, t_d) in [(bre, bre_d), (bim, bim_d), (cre, cre_d), (cim, cim_d)]:
                P.dma("sync", I("dma_start",
                    out=t_ap.rearrange("p g h -> p (g h)") if len(t_ap.shape) == 3 else t_ap.rearrange("p d g h -> p (d g h)"),
                    in_=t_d[:, l]), writes=[b_par2])
            A(I("activation", lrdt[:], ldt[:], AF.Exp), [b_par], [b_der])
            V(I("tensor_mul", th[:], lim[:], lrdt[:]), [b_par, b_der], [b_der])
            V(I("tensor_mul", lrdt[:], lre[:], lrdt[:]), [b_par, b_der], [b_der])
            V(I("tensor_scalar_mul", th[:], th[:], 1.0 / TWO_PI), [b_der], [b_der])
            kb9 = kvec.unsqueeze(2).to_broadcast([128, 9, 32])
            ph9 = tS[2][:, 0:288]
            V(I("tensor_tensor", ph9.rearrange("p (k c) -> p k c", k=9), th[:].unsqueeze(1).to_broadcast([128, 9, 32]),
                                        kb9, ALU.mult), [b_der, b_cst], [b_tS[2]])
            sincos_turns(ph9, 288, cs_s, cs_c, [b_tS[2]], [b_pw])
            V(I("tensor_tensor", ph9.rearrange("p (k c) -> p k c", k=9), lrdt[:].unsqueeze(1).to_broadcast([128, 9, 32]),
                                        kb9, ALU.mult), [b_der, b_cst, b_pw], [b_tS[2]])
            A(I("activation", mg_p, ph9, AF.Exp), [b_tS[2]], [b_pw])
            A(I("activation", mg_m, ph9, AF.Exp, scale=-1.0), [b_tS[2]], [b_pw])
            fl = lambda t: t.rearrange("p k c -> p (k c)")
            V(I("tensor_mul", fl(pw["pr"]), mg_p, cs_c), [b_pw], [b_pw])
            V(I("tensor_mul", fl(pw["pi"]), mg_p, cs_s), [b_pw], [b_pw])
            V(I("tensor_mul", fl(pw["mr"]), mg_m, cs_c), [b_pw], [b_pw])
            V(I("scalar_tensor_tensor", out=fl(pw["mi"]), in0=mg_m, scalar=-1.0, in1=cs_s,
                                               op0=ALU.mult, op1=ALU.mult), [b_pw], [b_pw])
            ck("S2a")
            ar1, ai1 = pw["pr"][:, 1, :], pw["pi"][:, 1, :]
            zin = [b_zt, b_par, b_pw]
            V(I("tensor_mul", zt[0], lre[:], lre[:]), zin, [b_zt])
            V(I("tensor_mul", zt[1], lim[:], lim[:]), zin, [b_zt])
            V(I("tensor_add", zt[0], zt[0], zt[1]), zin, [b_zt])
            V(I("reciprocal", zt[0], zt[0]), zin, [b_zt])
            V(I("tensor_scalar_add", zt[1], ar1, -1.0), zin, [b_zt])
            V(I("tensor_mul", zt[2], zt[1], lre[:]), zin, [b_zt])
            V(I("tensor_mul", zt[3], ai1, lim[:]), zin, [b_zt])
            V(I("tensor_add", zt[2], zt[2], zt[3]), zin, [b_zt])
            V(I("tensor_mul", zt[2], zt[2], zt[0]), zin, [b_zt])
            V(I("tensor_mul", zt[3], ai1, lre[:]), zin, [b_zt])
            V(I("tensor_mul", zt[4], zt[1], lim[:]), zin, [b_zt])
            V(I("tensor_sub", zt[3], zt[3], zt[4]), zin, [b_zt])
            V(I("tensor_mul", zt[3], zt[3], zt[0]), zin, [b_zt])
            for d in range(2):
                zr = zt[2][:, d * 16:(d + 1) * 16].unsqueeze(2).to_broadcast([128, 16, 16])
                zi = zt[3][:, d * 16:(d + 1) * 16].unsqueeze(2).to_broadcast([128, 16, 16])
                t1 = tA[:, 0:256].rearrange("p (a b) -> p a b", a=16)
                t2 = tB[:, 0:256].rearrange("p (a b) -> p a b", a=16)
                V(I("tensor_tensor", t1, bre, zr, ALU.mult), [b_par2, b_zt], [b_tA])
                V(I("tensor_tensor", t2, bim, zi, ALU.mult), [b_par2, b_zt], [b_tB])
                V(I("tensor_sub", Bb["r"][:, d], t1, t2), [b_tA, b_tB], [b_Bb])
                V(I("tensor_tensor", t1, bim, zr, ALU.mult), [b_par2, b_zt], [b_tA])
                V(I("tensor_tensor", t2, bre, zi, ALU.mult), [b_par2, b_zt], [b_tB])
                V(I("tensor_add", Bb["i"][:, d], t1, t2), [b_tA, b_tB], [b_Bb])

            ck("S2b")

            def cmul_tab(out_r, out_i, tr, ti, Xr_, Xi_, wbs, neg_i=False):
                for hf, (eng, btA, btB) in enumerate([("vector", b_tA, b_tB), ("vector", b_tA2, b_tB2)]):
                    gs_ = slice(hf * 8, hf * 8 + 8)
                    ta = tA[:, hf * 1024:(hf + 1) * 1024]; tb = tB[:, hf * 1024:(hf + 1) * 1024]
                    trb = tr[:, :, gs_].rearrange("p k g -> p g k").unsqueeze(3).to_broadcast([128, 8, 8, 16])
                    tib = ti[:, :, gs_].rearrange("p k g -> p g k").unsqueeze(3).to_broadcast([128, 8, 8, 16])
                    Xrb = Xr_[:, gs_, :].unsqueeze(2).to_broadcast([128, 8, 8, 16])
                    Xib = Xi_[:, gs_, :].unsqueeze(2).to_broadcast([128, 8, 8, 16])
                    v4 = lambda t: t.rearrange("p (g k h) -> p g k h", g=8, k=8)
                    o4 = lambda t: t[:, gs_, :].rearrange("p g (k h) -> p g k h", k=8)
                    rd = [b_pw, b_Bb, b_par2]
                    wb = [wbs[hf]]
                    E_ = lambda fn, r, w, eng=eng: P.op(eng, fn, r, w)
                    E_(I("tensor_tensor", v4(ta), trb, Xrb, ALU.mult), rd, [btA])
                    E_(I("tensor_tensor", v4(tb), tib, Xib, ALU.mult), rd, [btB])
                    E_(I("tensor_sub", o4(out_r), v4(ta), v4(tb)), [btA, btB], wb)
                    E_(I("tensor_tensor", v4(ta), trb, Xib, ALU.mult), rd, [btA])
                    E_(I("tensor_tensor", v4(tb), tib, Xrb, ALU.mult), rd, [btB])
                    if neg_i and eng == "vector":
                        E_(I("scalar_tensor_tensor", out=o4(out_i), in0=v4(ta), scalar=-1.0, in1=v4(tb),
                             op0=ALU.mult, op1=ALU.subtract), [btA, btB], wb)
                    elif neg_i:
                        E_(I("tensor_add", v4(ta), v4(ta), v4(tb)), [btA, btB], [btA])
                        E_(I("tensor_scalar_mul", o4(out_i), v4(ta), -1.0), [btA], wb)
                    else:
                        E_(I("tensor_add", o4(out_i), v4(ta), v4(tb)), [btA, btB], wb)

            def tab(name, lo, hi, rev, d):
                t = pw[name][:, lo:hi, d * 16:(d + 1) * 16]
                return t[:, ::-1, :] if rev else t

            for d in range(2):
                rev = (d == 1)
                cmul_tab(Qr[:, d], Qn[:, d], tab("pr", 1, 9, rev, d), tab("pi", 1, 9, rev, d), cre[:, d], cim[:, d], b_Qh, neg_i=True)
                cmul_tab(X_r[:, d], X_i[:, d], tab("mr", 1, 9, rev, d), tab("mi", 1, 9, rev, d), Bb["r"][:, d], Bb["i"][:, d], b_Xh)
            ck("S2c")
            for gq4 in range(4):
                for g2_ in range(2):
                    ps_ = slice(g2_ * 64, g2_ * 64 + 64)
                    ptf, bpf = nps(); ptb, bpb = nps()
                    for (d, pt, bp) in [(0, ptf, bpf), (1, ptb, bpb)]:
                        fns = []
                        for k4 in range(4):
                            gp = gq4 * 4 + k4
                            fns.append(mm(pt[:, k4 * 128:(k4 + 1) * 128], X_r[ps_, d, gp, :], Qr[ps_, d, gp, :], True, False))
                            fns.append(mm(pt[:, k4 * 128:(k4 + 1) * 128], X_i[ps_, d, gp, :], Qn[ps_, d, gp, :], False, True))
                        MM(fns, b_Xh + b_Qh, [bp])
                    m4 = lambda m: m.unsqueeze(1).to_broadcast([128, 4, 128])
                    v3 = lambda t: t.rearrange("p (a b) -> p a b", a=4)
                    V(I("tensor_tensor", v3(tA[:, 0:512]), v3(ptf[:, :]), m4(Mf), ALU.mult), [bpf, b_cst], [b_tA])
                    V(I("tensor_tensor", v3(tB[:, 0:512]), v3(ptb[:, :]), m4(Mb), ALU.mult), [bpb, b_cst], [b_tB])
                    V(I("tensor_add", tA[:, 0:512], tA[:, 0:512], tB[:, 0:512]), [b_tA, b_tB], [b_tA])
                    for k4 in range(4):
                        g = 2 * (gq4 * 4 + k4) + g2_
                        V(I("scalar_tensor_tensor", out=Tm[:, g, :], in0=ident, scalar=dcol[:, g:g + 1],
                            in1=tA[:, k4 * 128:(k4 + 1) * 128], op0=ALU.mult, op1=ALU.add),
                          [b_tA, b_par, b_cst], [b_T])
            ck("S2d")
            for d in range(2):
                rev = (d == 1)
                cmul_tab(X_r[:, d], X_i[:, d], tab("pr", 0, 8, not rev, d), tab("pi", 0, 8, not rev, d),
                         Bb["r"][:, d], Bb["i"][:, d], b_Xh)
            for d in range(2):
                for gq4 in range(4):
                    for g2_ in range(2):
                        ps_ = slice(g2_ * 64, g2_ * 64 + 64)
                        pt, bp = nps()
                        fns = []
                        for k4 in range(4):
                            gp = gq4 * 4 + k4
                            for ri, Pm in enumerate([X_r, X_i]):
                                c0 = (k4 * 2 + ri) * 64
                                fns.append(mm(pt[:, c0:c0 + 64], Pm[ps_, d, gp, :], identb[ps_, g2_ * 64:g2_ * 64 + 64]))
                        MM(fns, b_Xh + [b_cst], [bp])
                        g0 = 2 * gq4 * 4 + g2_
                        V(I("tensor_copy", PT[:, d, g0:g0 + 7:2, :, :].rearrange("p g r n -> p g (r n)"),
                            pt[:, :].rearrange("p (g f) -> p g f", g=4)), [bp], [b_PT])
            ck("S2e")
            V(I("tensor_scalar_mul", Thu[:], th[:], 8.0), [b_der], [b_der])
            V(I("tensor_copy", tSI[:, 0:32], Thu[:]), [b_der], [b_tSI])
            V(I("tensor_copy", tS[1][:, 0:32], tSI[:, 0:32]), [b_tSI], [b_tS[1]])
            V(I("tensor_sub", Thu[:], Thu[:], tS[1][:, 0:32]), [b_tS[1], b_der], [b_der])
            A(I("activation", R8[:], lrdt[:], AF.Exp, scale=8.0), [b_der], [b_der])
            V(I("tensor_scalar_mul", L8[:], lrdt[:], 8.0), [b_der], [b_der])
            V(I("tensor_scalar_mul", tS[2][:, 0:32], Thu[:], 256.0), [b_der], [b_tS[2]])
            sincos_turns(tS[2][:, 0:32], 32, fini[:], finr[:], [b_tS[2]], [b_der])
            A(I("activation", mag2048[:], lrdt[:], AF.Exp, scale=2048.0), [b_der], [b_der])
            V(I("tensor_mul", Acr[:], mag2048[:], finr[:]), [b_der], [b_der])
            V(I("tensor_mul", Aci[:], mag2048[:], fini[:]), [b_der], [b_der])
            gate([b_Bb, b_zt], [b_E])
            for (Ec, Es, pos) in [(E0c, E0s, cpos[:, 0:16]), (E1c, E1s, c16[:])]:
                V(I("tensor_tensor", tS[2][:, 0:512].rearrange("p (g c) -> p g c", g=32),
                    Thu[:].unsqueeze(2).to_broadcast([128, 32, 16]), pos.unsqueeze(1).to_broadcast([128, 32, 16]), ALU.mult),
                  [b_der, b_cst], [b_tS[2]])
                sincos_turns(tS[2][:, 0:512], 512, Es.rearrange("p g c -> p (g c)"), Ec.rearrange("p g c -> p (g c)"), [b_tS[2]], [b_E])
            if l == 0:
                tap("Tm", Tm.rearrange("p g f -> p (g f)"), [b_T], 2048)
                tap("Qr", Qr.rearrange("p d g f -> p (d g f)"), b_Qh, 2048)
                tap("PT", PT.rearrange("p d g r n -> p (d g r n)"), [b_PT], 2048)

            ck("S2")
            for (so, kcols, vcols, kdst, vblk) in [(4, slice(128, 256), slice(384, 512), slice(0, 128), 0),
                                                   (8, slice(0, 128), slice(256, 384), slice(T + 128, T + 256), 17)]:
                for (cols, dst_ap, wb) in [(kcols, kT[:, kdst], b_k), (vcols, vtm[:, vblk, :], b_v)]:
                    acc = tS[2][:, 0:128]
                    V(I("tensor_scalar_mul", acc, halg[:, 0, cols], sel[:, so:so + 1]),
                      [b_halg, b_sel], [b_tS[2]])
                    for r in range(1, 4):
                        V(I("scalar_tensor_tensor", out=acc, in0=halg[:, r, cols],
                                                                                  scalar=sel[:, so + r:so + r + 1], in1=acc,
                                                                                  op0=ALU.mult, op1=ALU.add),
                          [b_halg, b_sel], [b_tS[2]])
                    V(I("tensor_copy", dst_ap, acc), [b_tS[2]], [wb])

            ck("halosel")
            for i in range(8):
                for g8 in range(8):
                    P.dma("sync", I("dma_start",
                        out=U[i * 16:(i + 1) * 16, :, :].rearrange("p (a g) c -> p a g c", a=4)[:, :, g8, :],
                        in_=uT2[g8 * 16:(g8 + 1) * 16, :, i, :]), reads=[b_uT2], writes=[b_U])
            if l == 0:
                tap("U", U.rearrange("p g c -> p (g c)"), [b_U], 2048)
            ck("relayout")
            gate(a1_s2, a1_p1)
            gate(wa_s2, [b_WA, b_WB])
            P.dma("gpsimd", I("dma_start", out=wglu, in_=w_glu_d[l].rearrange("(k p) f -> p k f", p=128)), writes=[b_WA])
            P.dma("gpsimd", I("dma_start", out=wout, in_=w_out_d[l].rearrange("(k p) f -> p k f", p=128)), writes=[b_WA, b_WB])

            ssm_ctx = {}

            def ssm_batch(bt, phase):
                ssm_z(bt, phase)
                ssm_dve(bt, phase)

            def ssm_z(bt, phase):
                for d in range(2):
                    c0 = d * 16 + bt * 2
                    gsl = slice(c0, c0 + 2)
                    cosT_, sinT_, btab_ = tabsets[ssm_it[0] % 2]
                    ssm_it[0] += 1
                    if btab_ is None:
                        rd_t = [b_rstd, b_sig]; wr_c = [b_rstd]; wr_s = [b_sig]
                    else:
                        rd_t = [btab_]; wr_c = [btab_]; wr_s = [btab_]
                    e1c = E1c[:, gsl, :].unsqueeze(3).to_broadcast([128, 2, 16, 16])
                    e1s = E1s[:, gsl, :].unsqueeze(3).to_broadcast([128, 2, 16, 16])
                    e0c = E0c[:, gsl, :].unsqueeze(2).to_broadcast([128, 2, 16, 16])
                    e0s = E0s[:, gsl, :].unsqueeze(2).to_broadcast([128, 2, 16, 16])
                    q4 = lambda t: t.rearrange("p (g a b) -> p g a b", g=2, a=16)
                    q3 = lambda t: t.rearrange("p g (a b) -> p g a b", a=16)
                    pa = tS[2][:, 0:512]; pb = tSI[:, 0:512].bitcast(F32)
                    G = lambda fn, r, w: P.op("gpsimd", fn, r, w)
                    G(I("tensor_tensor", q4(pa), e1c, e0c, ALU.mult), [b_E], [b_tS[2]])
                    G(I("tensor_tensor", q4(pb), e1s, e0s, ALU.mult), [b_E], [b_tSI])
                    G(I("tensor_sub", q3(cosT_), q4(pa), q4(pb)), [b_tS[2], b_tSI], wr_c)
                    G(I("tensor_tensor", q4(pa), e1s, e0c, ALU.mult), [b_E], [b_tS[2]])
                    G(I("tensor_tensor", q4(pb), e1c, e0s, ALU.mult), [b_E], [b_tSI])
                    G(I("tensor_add", q3(sinT_), q4(pa), q4(pb)), [b_tS[2], b_tSI], wr_s)
                    zb = []
                    for ri in range(2):
                        pt, bp = nps("ssm")
                        fns = []
                        for gq in range(2):
                            gp = bt * 2 + gq
                            for g2_ in range(2):
                                g = gp * 2 + g2_
                                fns.append(mm(pt[g2_ * 64:(g2_ + 1) * 64, gq * 256:(gq + 1) * 256], PT[:, d, g, ri, :], U[:, g, :]))
                        MM(fns, [b_PT, b_U], [bp])
                        zb.append((pt, bp))
                    ssm_ctx[(bt, d)] = (zb, cosT_, sinT_, rd_t)

            def ssm_dve(bt, phase):
                for d in range(2):
                    c0 = d * 16 + bt * 2
                    gsl = slice(c0, c0 + 2)
                    zb, cosT_, sinT_, rd_t = ssm_ctx.pop((bt, d))
                    (pzr, bzr), (pzi, bzi) = zb
                    zr_ = pzr[:, :].rearrange("p (g c) -> p g c", g=2)
                    zi_ = pzi[:, :].rearrange("p (g c) -> p g c", g=2)
                    if d == 1:
                        zr_ = zr_[:, :, ::-1]; zi_ = zi_[:, :, ::-1]
                    a3 = tS[0][:, 0:512].rearrange("p (g c) -> p g c", g=2)
                    b3 = tS[1][:, 0:512].rearrange("p (g c) -> p g c", g=2)
                    V(I("tensor_tensor", a3, zr_, cosT_, ALU.mult), [bzr, *rd_t], [b_tS[0]])
                    V(I("tensor_tensor", b3, zi_, sinT_, ALU.mult), [bzi, *rd_t], [b_tS[1]])
                    V(I("tensor_add", Wr, a3, b3), [b_tS[0], b_tS[1]], [b_W])
                    V(I("tensor_tensor", a3, zi_, cosT_, ALU.mult), [bzi, *rd_t], [b_tS[0]])
                    V(I("tensor_tensor", b3, zr_, sinT_, ALU.mult), [bzr, *rd_t], [b_tS[1]])
                    V(I("tensor_sub", Wi, a3, b3), [b_tS[0], b_tS[1]], [b_W])
                    if phase == 0:
                        V(I("memset", facc[:], 0.0), [], [b_facc])
                        for gq in range(2):
                            col = c0 + gq
                            A(I("activation", Sr[:, gq, :], crev[:], AF.Exp, scale=L8[:, col:col + 1]), [b_der, b_cst], [b_S])
                        for gq in range(2):
                            for (Wt, ci) in [(Wr, 0), (Wi, 1)]:
                                V(I("scalar_tensor_tensor", out=Si[:, gq, :], in0=Wt[:, gq, :], scalar=1.0, in1=Sr[:, gq, :],
                                    op0=ALU.mult, op1=ALU.mult, accum_out=facc[:, ci * 2 + gq:ci * 2 + gq + 1]),
                                  [b_W, b_S], [b_S, b_facc])
                    else:
                        for gq in range(2):
                            col = c0 + gq
                            for (Wt, St, ci) in [(Wr, Sr, 0), (Wi, Si, 1)]:
                                cc = ci * 32 + col
                                init = carry[:, cc:cc + 1]
                                V(I("tensor_tensor_scan", St[:, gq, :], R8[:, col:col + 1].to_broadcast([128, 256]), Wt[:, gq, :], init,
                                    ALU.mult, ALU.add), [b_W, b_der, b_carry], [b_S])
                    if phase == 0:
                        fr = finr[:, gsl]; fi = fini[:, gsl]
                        s_r = facc[:, 0:2]; s_i = facc[:, 2:4]
                        o_r = Fst[:, c0:c0 + 2]; o_i = Fst[:, 32 + c0:32 + c0 + 2]
                        t0 = tS[0][:, 0:2]; t1 = tS[1][:, 0:2]
                        V(I("tensor_mul", t0, s_r, fr), [b_facc, b_der], [b_tS[0]])
                        V(I("tensor_mul", t1, s_i, fi), [b_facc, b_der], [b_tS[1]])
                        V(I("tensor_sub", o_r, t0, t1), [b_tS[0], b_tS[1]], [b_F])
                        V(I("tensor_mul", t0, s_r, fi), [b_facc, b_der], [b_tS[0]])
                        V(I("tensor_mul", t1, s_i, fr), [b_facc, b_der], [b_tS[1]])
                        V(I("tensor_add", o_i, t0, t1), [b_tS[0], b_tS[1]], [b_F])
                    else:
                        Hr_ = Hb16[(d, "r")]; Hi_ = Hb16[(d, "i")]
                        if d == 0:
                            o_r = Hr_[:, :, 1:256]; o_i = Hi_[:, :, 1:256]
                            o_r0 = Hr_[:, :, 0]; o_i0 = Hi_[:, :, 0]
                        else:
                            o_r = Hr_[:, :, 254::-1]; o_i = Hi_[:, :, 254::-1]
                            o_r0 = Hr_[:, :, 255]; o_i0 = Hi_[:, :, 255]
                        a3s = a3[:, :, 0:255]; b3s = b3[:, :, 0:255]
                        V(I("tensor_tensor", a3s, Sr[:, :, 0:255], cosT_[:, :, 0:255], ALU.mult), [b_S, *rd_t], [b_tS[0]])
                        V(I("tensor_tensor", b3s, Si[:, :, 0:255], sinT_[:, :, 0:255], ALU.mult), [b_S, *rd_t], [b_tS[1]])
                        V(I("tensor_sub", o_r, a3s, b3s), [b_tS[0], b_tS[1]], [b_H])
                        V(I("tensor_tensor", a3s, Sr[:, :, 0:255], sinT_[:, :, 0:255], ALU.mult), [b_S, *rd_t], [b_tS[0]])
                        V(I("tensor_tensor", b3s, Si[:, :, 0:255], cosT_[:, :, 0:255], ALU.mult), [b_S, *rd_t], [b_tS[1]])
                        V(I("tensor_add", o_i, a3s, b3s), [b_tS[0], b_tS[1]], [b_H])
                        V(I("tensor_copy", o_r0, carry[:, c0:c0 + 2]), [b_carry], [b_H])
                        V(I("tensor_copy", o_i0, carry[:, 32 + c0:32 + c0 + 2]), [b_carry], [b_H])

            V(I("tensor_copy", esrow[:, :, :], esink[0:1, l, :].unsqueeze(2).to_broadcast([1, 8, 128])), [b_small], [b_esr])
            gate([b_uT2], [b_att])
            att_state = {}

            def att_scores(qb):
                qs = slice(qb * 128, (qb + 1) * 128)
                for kvh in range(2):
                    hs_ = slice(kvh * 64, kvh * 64 + 64)
                    pM = pM2[kvh]; b_pM = b_pM2[kvh]
                    for kb in range(3):
                        pt, bp = nps()
                        MM([mm(pt[:, :].rearrange("p (j q) -> p j q", j=4), kT[hs_, (qb + kb) * 128:(qb + kb + 1) * 128], qT[hs_, :, qs])],
                           [b_k, b_q], [bp])
                        A(I("activation", pM[kb], pt[:, :], AF.Exp, scale=0.125), [bp], [b_pM[kb]])

            def att_mid(qb):
                for kvh in range(2):
                    hs_ = slice(kvh * 64, kvh * 64 + 64)
                    pM = pM2[kvh]; b_pM = b_pM2[kvh]
                    for kb in range(3):
                        P.op("gpsimd", I("tensor_tensor", pM[kb], pM[kb],
                                         etab[:, kb, kvh * 4:(kvh + 1) * 4, :].rearrange("p j q -> p (j q)"), ALU.mult),
                             [b_et], [b_pM[kb]])
                    pno, bpn = nps()
                    pn = pno[:, 0:256]; pd = pno[:, 256:512]
                    fn_n = []; fn_d = []
                    for j in range(4):
                        po = slice((j % 2) * 64, (j % 2) * 64 + 64)
                        cs2 = slice((j // 2) * 128, (j // 2) * 128 + 128)
                        for kb in range(3):
                            if kb == 0 and qb == 0:
                                ol = ones_pn[:, 0, :]
                            elif kb == 2 and qb == 15:
                                ol = ones_pn[:, 1, :]
                            else:
                                ol = onesb[:, 0:64]
                            fn_n.append(mm(pn[po, cs2], vtm[:, qb + kb, hs_], pM[kb][:, j * 128:(j + 1) * 128], kb == 0, kb == 2))
                            if kb == 0:
                                fn_d.append(mm(pd[po, cs2], onesb[0:1, 0:64], esrow[0:1, kvh * 4 + j, :], True, False))
                            fn_d.append(mm(pd[po, cs2], ol, pM[kb][:, j * 128:(j + 1) * 128], False, kb == 2))
                    MM(fn_n + fn_d, [b_v, b_cst, b_esr] + b_pM, [bpn])
                    att_state[(qb, kvh)] = (pn, pd, bpn)

            def att_fin(qb):
                qs = slice(qb * 128, (qb + 1) * 128)
                for kvh in range(2):
                    pn, pd, bpn = att_state.pop((qb, kvh))
                    A(I("activation", rden, pd.rearrange("p (a q) -> p a q", a=2), AF.Ln), [bpn], [b_rden])
                    A(I("activation", rden, rden, AF.Exp, scale=-1.0), [b_rden], [b_rden])
                    V(I("tensor_tensor", attT[:, kvh * 2:kvh * 2 + 2, qs],
                        pn.rearrange("p (a q) -> p a q", a=2), rden, ALU.mult),
                      [bpn, b_rden, b_uT2], [b_att])

            def att_flush():
                pass

            att_order = [1, 2, 3, 4, 5, 6, 7, 8, 9, 10, 11, 12, 13, 14, 0, 15]
            att_scores(att_order[0])
            for bt in range(8):
                ssm_z(bt, 0)
                att_mid(att_order[bt])
                ssm_dve(bt, 0)
                att_fin(att_order[bt])
                att_scores(att_order[bt + 1])
            ck("ssm0")
            P.dma("gpsimd", I("dma_start", out=st_src, in_=Fst[:]), reads=[b_F], writes=[b_stg])
            P.op("gpsimd", I("collective_compute", "AllGather", ALU.bypass, replica_groups=GROUPS,
                                                         ins=[st_src.opt()], outs=[st_dst.opt()]), [b_stg], [b_stg])
            P.dma("gpsimd", I("dma_start", out=stg[:], in_=st_dst.rearrange("(r p) f -> p r f", p=128)),
                  reads=[b_stg], writes=[b_stg])

            ck("stx")
            if l == 0:
                tap("attT", attT.rearrange("p j t -> p (j t)"), [b_att], 2048)

            ck("att")
            V(I("memset", carry[:], 0.0), [], [b_carry])
            for d in range(2):
                cs_ = slice(d * 16, d * 16 + 16); ci_ = slice(32 + d * 16, 32 + d * 16 + 16)
                cr = hcr[:, 0:16]; cim_ = hcr[:, 16:32]; t0 = hcr[:, 32:48]; t1 = hcr[:, 48:64]
                V(I("memset", hcr[:], 0.0), [], [b_hcr])
                order = [0, 1, 2, 3] if d == 0 else [3, 2, 1, 0]
                hh = [b_hcr]
                for r in order:
                    V(I("scalar_tensor_tensor", out=carry[:, cs_], in0=cr, scalar=sel[:, r:r + 1], in1=carry[:, cs_],
                                                                     op0=ALU.mult, op1=ALU.add), [b_hcr, b_sel], [b_carry])
                    V(I("scalar_tensor_tensor", out=carry[:, ci_], in0=cim_, scalar=sel[:, r:r + 1], in1=carry[:, ci_],
                                                                     op0=ALU.mult, op1=ALU.add), [b_hcr, b_sel], [b_carry])
                    ar_ = Acr[:, cs_]; ai_ = Aci[:, cs_]
                    V(I("tensor_mul", t0, cr, ar_), [b_der], hh)
                    V(I("tensor_mul", t1, cim_, ai_), [b_der], hh)
                    V(I("tensor_sub", t0, t0, t1), [], hh)
                    V(I("tensor_mul", t1, cr, ai_), [b_der], hh)
                    V(I("tensor_mul", cim_, cim_, ar_), [b_der], hh)
                    V(I("tensor_add", cim_, cim_, t1), [], hh)
                    V(I("tensor_add", cr, t0, stg[:, r, cs_]), [b_stg], hh)
                    V(I("tensor_add", cim_, cim_, stg[:, r, ci_]), [b_stg], hh)

            for bt in range(8):
                ssm_z(bt, 1)
                att_mid(att_order[8 + bt])
                ssm_dve(bt, 1)
                att_fin(att_order[8 + bt])
                if bt < 7:
                    att_scores(att_order[9 + bt])
                for gq in range(2):
                    gp = bt * 2 + gq
                    pt, bp = nps("ssm")
                    fns = []
                    for g2_ in range(2):
                        g = gp * 2 + g2_
                        ps_ = slice(g2_ * 64, g2_ * 64 + 64)
                        oc = pt[:, g2_ * 256:g2_ * 256 + 256]
                        fns.append(mm(oc, Tm[:, g, :], U[:, g, :], True, False))
                        for d in range(2):
                            fns.append(mm(oc, Qr[ps_, d, gp, :], Hb16[(d, "r")][ps_, gq, :], False, False))
                            fns.append(mm(oc, Qn[ps_, d, gp, :], Hb16[(d, "i")][ps_, gq, :], False, d == 1))
                    MM(fns, [b_T, b_U, b_H] + b_Qh, [bp])
                    A(I("activation", Yall[:, gp * 2:gp * 2 + 2, :].rearrange("p g c -> p (g c)"), pt[:, :],
                                                           AF.Gelu_apprx_tanh), [bp, b_Yall], [b_Yb[bt]])
            if l == 0:
                tap("Yall", Yall.rearrange("p g c -> p (g c)"), b_Yb, 2048)
            ck("ssm1")
            gate(xr_mixer, allx)
            for k in range(8):
                P.dma("gpsimd", I("dma_start", out=xT[:, k, :], in_=xsp[k * 128:(k + 1) * 128, :]), writes=b_x[k])
            att_flush()
            b_yTok = Buf()
            gate([b_tab, b_W, b_S, b_H], [b_yTok])
            gate([b_q], [b_yT2])
            yTok = A1[:, 8192:16384].rearrange("p (k a j g h) -> p k a j g h", k=2, a=4, j=8, g=8)
            ev = [0]

            def evac(out_ap, in_ap, r, w):
                if ev[0] % 2 == 0:
                    A(I("activation", out_ap, in_ap, AF.Copy), r, w)
                else:
                    V(I("tensor_copy", out_ap, in_ap), r, w)
                ev[0] += 1
            for cblk in range(2):
                for g4 in range(8):
                    pt, bp = nps()
                    MM([mm(pt[:, k4 * 128:(k4 + 1) * 128], Yall[:, g4 * 4 + k4, cblk * 128:(cblk + 1) * 128], identb[:, :])
                        for k4 in range(4)], b_Yb + [b_cst], [bp])
                    evac(yTok[:, cblk, g4 // 2, :, (g4 % 2) * 4:(g4 % 2) * 4 + 4, :],
                         pt[:, :].rearrange("p (k j h) -> p j k h", k=4, j=8), [bp], [b_yTok])
            for a_ in range(4):
                for jp in range(4):
                    pt, bp = nps()
                    fns = []
                    for jj in range(2):
                        j = jp * 2 + jj
                        for cblk in range(2):
                            c0 = (jj * 2 + cblk) * 128
                            fns.append(mm(pt[:, c0:c0 + 128], yTok[:, cblk, a_, j, :, :].rearrange("p g h -> p (g h)"), identb[:, :]))
                    MM(fns, [b_yTok, b_cst], [bp])
                    evac(yT2[:, a_, jp * 2:jp * 2 + 2, :].rearrange("p j c -> p (j c)"), pt[:, :], [bp], [b_yT2])
            gate([b_yTok], [b_glu])
            yflat = yT2.rearrange("p a j c -> p a (j c)")
            for cb in range(4):
                cs_ = slice(cb * 512, (cb + 1) * 512)
                for oc in range(4):
                    pv, bpv = nps(); pg, bpg = nps()
                    MM([mm(pv[:, :], wglu[:, k, oc * 128:(oc + 1) * 128], yflat[:, k, cs_], k == 0, k == 3) for k in range(4)],
                       [b_WA, b_yT2], [bpv])
                    MM([mm(pg[:, :], wglu[:, k, 512 + oc * 128:512 + (oc + 1) * 128], yflat[:, k, cs_], k == 0, k == 3) for k in range(4)],
                       [b_WA, b_yT2], [bpg])
                    A(I("activation", sig[:], pg[:, :], AF.Sigmoid), [bpg], [b_sig])
                    dst = gluT[:, oc, :].rearrange("p (c j) -> p j c", j=8)[:, 2 * cb:2 * cb + 2, :]
                    V(I("tensor_tensor", dst, pv[:, :].rearrange("p (j c) -> p j c", j=2),
                                                                sig[:, :].rearrange("p (j c) -> p j c", j=2), ALU.mult),
                      [bpv, b_sig], [b_glu])
            if l == 0:
                tap("gluT", gluT.rearrange("p j t -> p (j t)"), [b_glu], 2048)
            ck("glu")
            for tt in range(NTT):
                ts = slice(tt * 512, (tt + 1) * 512)
                for m in range(8):
                    pt, bp = nps()
                    fns = [mm(pt[:, :], wout[:, k, m * 128:(m + 1) * 128], attT[:, k, ts], k == 0, False) for k in range(4)]
                    fns += [mm(pt[:, :], wout[:, 4 + k, m * 128:(m + 1) * 128], gluT[:, k, ts], False, k == 3) for k in range(4)]
                    MM(fns, [b_WA, b_WB, b_att, b_glu], [bp])
                    V(I("tensor_add", xT[:, m, ts], xT[:, m, ts], pt[:, :]), [bp], [b_x[m][tt]])
            if l == 0:
                tap("xmid", xT[:, 0, :], b_x[0], 2048)

            ck("wout")
            gate(a1_p1, b_h)
            gate([b_att, b_yT2, b_uT2, b_q], [])
            gate(misc_att, misc_ffn)
            gate([b_E], [b_WB])
            for tt in range(NTT):
                rms_tile(l, tt, g2)
            b_ws = [b_WA, b_WB]
            for gI in range(8):
                s = gI % 2
                w1, w2 = wsl[s]
                P.dma("gpsimd", I("dma_start",
                    out=w1, in_=w_ff1_d[l][:, gI * 512:(gI + 1) * 512].rearrange("(k p) f -> p k f", p=128)), writes=[b_ws[s]])
                P.dma("gpsimd", I("dma_start",
                    out=w2, in_=w_ff2_d[l][gI * 512:(gI + 1) * 512, :].rearrange("(k p) f -> p k f", p=128)), writes=[b_ws[s]])
                for tt in range(NTT):
                    ts = slice(tt * 512, (tt + 1) * 512)
                    for fc in range(4):
                        pt, bp = nps()
                        MM([mm(pt[:, :], w1[:, k, fc * 128:(fc + 1) * 128], hT[:, k, ts], k == 0, k == 7) for k in range(8)],
                           [b_ws[s], b_h[tt]], [bp])
                        ri_ = fc % 2
                        A(I("activation", rl[ri_], pt[:, :], AF.Relu), [bp], [b_rl[ri_]])
                        V(I("tensor_mul", aT[:, fc, :], rl[ri_], rl[ri_]), [b_rl[ri_]], [b_a[fc]])
                    for m in range(8):
                        pt, bp = nps()
                        MM([mm(pt[:, :], w2[:, k, m * 128:(m + 1) * 128], aT[:, k, :], k == 0, k == 3) for k in range(4)],
                           [b_ws[s]] + b_a, [bp])
                        V(I("tensor_add", xT[:, m, ts], xT[:, m, ts], pt[:, :]), [bp], [b_x[m][tt]])
            gate(misc_ffn, misc_att)

        try:
            for l in range(nlayers):
                layer(l)
        except _Stop:
            gate(xr_mixer, allx)
        emit_taps()
        for k in range(8):
            P.dma("sync", I("dma_start", out=out_d[k * 128:(k + 1) * 128, :], in_=xT[:, k, :]), reads=b_x[k])
        P.wait_all("sync", allx + b_tS)
        P.finish("sync")
        P.run()
    return nc


def _prep_inputs(x, norm1, w_in, q_gain, k_gain, sink, lam_re, lam_im, log_dt, b_re, b_im, c_re, c_im,
                 d_skip, w_glu, w_out, norm2, w_ff1, w_ff2, nlayers=DEPTH):
    f = lambda a: np.ascontiguousarray(np.asarray(a, dtype=np.float32))
    L = DEPTH
    perm = []
    for j in range(4):
        perm += list(range(j * 64, j * 64 + 64)) + list(range((4 + j) * 64, (4 + j) * 64 + 64))
    perm += list(range(512, 1280))
    shared = {
        "w_in": f(np.asarray(w_in)[:nlayers][:, :, perm]), "w_glu": f(np.asarray(w_glu)[:nlayers]),
        "w_out": f(np.asarray(w_out)[:nlayers]),
        "w_ff1": f(np.asarray(w_ff1)[:nlayers]), "w_ff2": f(np.asarray(w_ff2)[:nlayers]),
        "g1": f(np.asarray(norm1).reshape(L, 8, 128).transpose(2, 0, 1)),
        "g2": f(np.asarray(norm2).reshape(L, 8, 128).transpose(2, 0, 1)),
        "qg": f(np.tile(np.asarray(q_gain).T, (2, 1))),
        "kg": f(np.tile(np.asarray(k_gain).T, (2, 1))),
        "sinkr": f(np.broadcast_to(np.asarray(sink)[None], (128, L, 8))),
    }

    def gn(a):
        a = np.asarray(a).reshape(L, 2, 16, 2, 64)
        return f(a.transpose(3, 4, 0, 1, 2).reshape(128, L, 32))
    shared["lre"] = gn(lam_re)
    shared["lim"] = gn(lam_im)
    shared["ldt"] = gn(np.broadcast_to(np.asarray(log_dt)[:, :, :, None], (L, 2, 32, 64)))

    def bb(a):
        a = np.asarray(a).reshape(L, 16, 2, 64, 16)
        return f(a.transpose(2, 3, 0, 1, 4).reshape(128, L, 256))
    shared["bre"] = bb(b_re)
    shared["bim"] = bb(b_im)

    def cc(a):
        a = np.asarray(a).reshape(L, 2, 16, 2, 16, 64)
        return f(a.transpose(3, 5, 0, 1, 2, 4).reshape(128, L, 512))
    shared["cre"] = cc(c_re)
    shared["cim"] = cc(c_im)
    dsk = np.asarray(d_skip).reshape(L, 32, 16)
    shared["dcol"] = f(np.broadcast_to(dsk.transpose(2, 0, 1)[None], (8, 16, L, 32)).reshape(128, L, 32))
    slopes = np.exp2(-8.0 * np.arange(1, 9) / 8.0)
    ci = np.arange(128)[:, None]; qi = np.arange(128)[None, :]
    et = np.zeros((128, 3, 8, 128), np.float32)
    for kb in range(3):
        dist = np.abs(qi - ci - (kb - 1) * 128)
        valid = dist <= 128
        for h in range(8):
            et[:, kb, h, :] = np.where(valid, np.exp(-slopes[h] * dist), 0.0)
    shared["etab"] = f(et.reshape(128, 3072))
    cst = np.zeros((128, 656), np.float32)
    cst[:, 0:128] = np.eye(128)
    bi = np.arange(128)[:, None] // 16; bj = np.arange(128)[None, :] // 16
    cst[:, 128:256] = (bj >= bi)
    cst[:, 256:384] = (bi >= bj)
    cst[:, 384:393] = np.arange(9)[None, :]
    cst[:, 400:656] = np.arange(1, 257)[None, :]
    shared["cst"] = cst
    xs = np.asarray(x)
    in_maps = []
    for r in range(8):
        b, q = r // 4, r % 4
        m = dict(shared)
        m["xT"] = f(xs[b, q * T:(q + 1) * T, :].T)
        s = np.zeros((128, 16), np.float32)
        s[:, q] = 1.0
        if q > 0:
            s[:, 4 + q - 1] = 1.0; s[:, 12] = 1.0
        if q < 3:
            s[:, 8 + q + 1] = 1.0; s[:, 13] = 1.0
        m["sel"] = s
        in_maps.append(m)
    return in_maps


_NC_CACHE = {}


def kernel(**inputs):
    in_maps = _prep_inputs(**inputs)
    if "nc" not in _NC_CACHE:
        _NC_CACHE["nc"] = build_nc()
    nc = _NC_CACHE["nc"]
    res = run_bass_kernel_spmd(nc, in_maps, core_ids=list(range(8)))
    out = np.zeros((2, 4 * T, 1024), np.float32)
    for r in range(8):
        b, q = r // 4, r % 4
        out[b, q * T:(q + 1) * T, :] = np.asarray(res.results[r]["outT"]).T
    return out
```

```python
import math
import os
import numpy as np
from contextlib import ExitStack
import concourse.bass as bass
import concourse.mybir as mybir
from concourse.bass_utils import run_bass_kernel_spmd

F32 = mybir.dt.float32
BF16 = mybir.dt.bfloat16
I32 = mybir.dt.int32
AF = mybir.ActivationFunctionType
ALU = mybir.AluOpType

DEPTH = 4
T = 2048
NTT = 4
EPS = 1e-6
TWO_PI = 2.0 * math.pi


class Buf:
    def __init__(self, name=""):
        self.name = name
        self.w = None
        self.r = []


class EngQ:
    def __init__(self, name, sem):
        self.name = name
        self.sem = sem
        self.count = 0
        self.ops = []
        self.seen = {}


class Prog:
    ENGS = ["sync", "scalar", "gpsimd", "vector", "tensor"]

    def __init__(self, nc, es, ndma=16):
        self.nc = nc
        self.q = {e: EngQ(e, es.enter_context(nc.semaphore("s_" + e))) for e in self.ENGS}
        self.dq = {e: [EngQ(f"d_{e}{k}", es.enter_context(nc.semaphore(f"d_{e}{k}"))) for k in range(ndma)]
                   for e in ["sync", "gpsimd"]}
        self.rr = {e: 0 for e in self.dq}
        self.nops = 0

    def _waits(self, eng, reads, writes):
        q = self.q[eng]
        need = {}

        def add(tok):
            if tok is None:
                return
            s, v = tok
            if eng == "tensor" and s is q:
                return
            if need.get(s, 0) < v:
                need[s] = v
        for b in reads:
            add(b.w)
        for b in writes:
            add(b.w)
            for t in b.r:
                add(t)
        for s, v in need.items():
            if q.seen.get(s, 0) >= v:
                continue
            q.seen[s] = v
            q.ops.append(lambda e, s=s, v=v: e.wait_ge(s.sem, v))

    def _mark(self, tok, reads, writes):
        for b in reads:
            b.r = [t for t in b.r if t[0] is not tok[0]] + [tok]
        for b in writes:
            b.w = tok
            b.r = []

    def op(self, eng, fn, reads=(), writes=()):
        return self.group(eng, [fn], reads, writes)

    def group(self, eng, fns, reads=(), writes=()):
        q = self.q[eng]
        self._waits(eng, reads, writes)
        for fn in fns[:-1]:
            q.ops.append(lambda e, fn=fn: fn(e))
        q.count += 1
        tok = (q, q.count)
        q.ops.append(lambda e, fn=fns[-1], q=q: fn(e).then_inc(q.sem, 1))
        self._mark(tok, reads, writes)
        self.nops += len(fns)
        return tok

    def dma(self, eng, fn, reads=(), writes=()):
        self._waits(eng, reads, writes)
        k = self.rr[eng]
        self.rr[eng] = (k + 1) % len(self.dq[eng])
        d = self.dq[eng][k]
        q_ = self.q[eng]
        if d.count > 0 and q_.seen.get(d, 0) < d.count:
            q_.seen[d] = d.count
            q_.ops.append(lambda e, d=d, v=d.count: e.wait_ge(d.sem, v))
        d.count += 16
        tok = (d, d.count)
        self.q[eng].ops.append(lambda e, fn=fn, d=d: fn(e).then_inc(d.sem, 16))
        self._mark(tok, reads, writes)
        self.nops += 1
        return tok

    def wait_all(self, eng, bufs):
        self._waits(eng, [], bufs)

    def finish(self, eng="sync"):
        q = self.q[eng]
        allq = [x for x in self.q.values() if x is not q] + [d for ds in self.dq.values() for d in ds]
        for s_ in allq:
            if s_.count > 0:
                q.ops.append(lambda e, s_=s_, v=s_.count: e.wait_ge(s_.sem, v))

    def run(self):
        with self.nc.Block() as block:
            for e in self.ENGS:
                ops = self.q[e].ops

                def body(eng, ops=ops):
                    for o in ops:
                        o(eng)
                getattr(block, e)(body)


class _Stop(Exception):
    pass


def build_nc(nlayers=DEPTH, taps=None, stop_after=None):
    nc = bass.Bass("TRN2", target_bir_lowering=False)
    L = DEPTH

    def din(name, shape, dt=F32):
        return nc.dram_tensor(name, list(shape), dt, kind="ExternalInput").ap()

    xT_d = din("xT", [1024, T])
    LW = nlayers
    w_in_d = din("w_in", [LW, 1024, 1280])
    w_glu_d = din("w_glu", [LW, 512, 1024])
    w_out_d = din("w_out", [LW, 1024, 1024])
    w_ff1_d = din("w_ff1", [LW, 1024, 4096])
    w_ff2_d = din("w_ff2", [LW, 4096, 1024])
    g1_d = din("g1", [128, L, 8])
    g2_d = din("g2", [128, L, 8])
    qg_d = din("qg", [128, L])
    kg_d = din("kg", [128, L])
    sink_d = din("sinkr", [128, L, 8])
    lre_d = din("lre", [128, L, 32])
    lim_d = din("lim", [128, L, 32])
    ldt_d = din("ldt", [128, L, 32])
    bre_d = din("bre", [128, L, 256])
    bim_d = din("bim", [128, L, 256])
    cre_d = din("cre", [128, L, 512])
    cim_d = din("cim", [128, L, 512])
    dcol_d = din("dcol", [128, L, 32])
    etab_d = din("etab", [128, 3072])
    cst_d = din("cst", [128, 656])
    sel_d = din("sel", [128, 16])
    out_d = nc.dram_tensor("outT", [1024, T], F32, kind="ExternalOutput").ap()
    taps = list(taps) if taps else []
    dbg_d = nc.dram_tensor("dbg", [max(1, len(taps)), 128, 2048], F32, kind="ExternalOutput").ap() if taps else None

    xsp = nc.dram_tensor("xsp", [1024, T], F32).ap()
    hal_src = nc.dram_tensor("hal_src", [128, 512], F32).ap()
    hal_dst = nc.dram_tensor("hal_dst", [4 * 128, 512], F32).ap()
    st_src = nc.dram_tensor("st_src", [128, 64], F32).ap()
    st_dst = nc.dram_tensor("st_dst", [4 * 128, 64], F32).ap()
    GROUPS = [[0, 1, 2, 3], [4, 5, 6, 7]]

    with ExitStack() as es:
        P = Prog(nc, es)

        def sb(name, shape, dt=F32):
            return es.enter_context(nc.sbuf_tensor(name, list(shape), dt))

        XR = sb("XR", [128, 16384], F32)
        XRb = XR[:, :].bitcast(BF16)
        A1 = sb("A1", [128, 16384], BF16)
        A1f = A1[:, :].bitcast(F32)
        WA = sb("WA", [128, 16384], BF16)
        WAf = WA[:, :].bitcast(F32)
        A3 = sb("A3", [128, 8192], BF16)
        A4 = sb("A4", [128, 8192], BF16)
        MISC = sb("MISC", [128, 4096], BF16)

        xT = XR[:, :].rearrange("p (k t) -> p k t", k=8)
        b_x = [[Buf() for _ in range(NTT)] for _ in range(8)]
        allx = [b for row in b_x for b in row]
        U = XRb[:, 0:8192].rearrange("p (g c) -> p g c", g=32); b_U = Buf()
        PT = XRb[:, 8192:16384].rearrange("p (d g r n) -> p d g r n", d=2, g=32, r=2); b_PT = Buf()
        Qr = XRb[:, 16384:20480].rearrange("p (d g f) -> p d g f", d=2, g=16)
        Qn = XRb[:, 20480:24576].rearrange("p (d g f) -> p d g f", d=2, g=16); b_Qh = [Buf(), Buf()]
        Tm = XRb[:, 24576:28672].rearrange("p (g f) -> p g f", g=32); b_T = Buf()
        etab = XRb[:, 28672:31744].rearrange("p (k h q) -> p k h q", k=3, h=8); b_et = Buf()
        xr_mixer = [b_U, b_PT, b_T, b_et] + b_Qh

        hT = A1[:, :].rearrange("p (k t) -> p k t", k=8); b_h = [Buf() for _ in range(NTT)]
        bre = A1f[:, 0:256].rearrange("p (g h) -> p g h", g=16)
        bim = A1f[:, 256:512].rearrange("p (g h) -> p g h", g=16)
        cre = A1f[:, 512:1024].rearrange("p (d g h) -> p d g h", d=2, g=16)
        cim = A1f[:, 1024:1536].rearrange("p (d g h) -> p d g h", d=2, g=16)
        b_par2 = Buf()
        halg = A1f[:, 1536:3584].rearrange("p (r f) -> p r f", r=4); b_halg = Buf()
        tA = A1f[:, 4096:6144]; tB = A1f[:, 6144:8192]; b_tA = Buf(); b_tB = Buf(); b_tA2 = Buf(); b_tB2 = Buf()
        Yall = A1[:, 0:8192].rearrange("p (g c) -> p g c", g=32); b_Yall = Buf()
        yT2 = A3[:, :].rearrange("p (a j c) -> p a j c", a=4, j=8); b_yT2 = Buf()
        cosT = A1f[:, 4096:4608].rearrange("p (g c) -> p g c", g=2)
        sinT = A1f[:, 4608:5120].rearrange("p (g c) -> p g c", g=2); b_tab = Buf()
        Wr = A1f[:, 5120:5632].rearrange("p (g c) -> p g c", g=2)
        Wi = A1f[:, 5632:6144].rearrange("p (g c) -> p g c", g=2); b_W = Buf()
        Sr = A1f[:, 6144:6656].rearrange("p (g c) -> p g c", g=2)
        Si = A1f[:, 6656:7168].rearrange("p (g c) -> p g c", g=2); b_S = Buf()
        Hb16 = {}
        for d in range(2):
            for ni, n in enumerate("ri"):
                o = 14336 + (d * 2 + ni) * 512
                Hb16[(d, n)] = A1[:, o:o + 512].rearrange("p (g c) -> p g c", g=2)
        b_H = Buf()
        a1_s2 = [b_par2, b_halg, b_tA, b_tB, b_tA2, b_tB2]
        b_Yb = [Buf() for _ in range(8)]

        win = WA[:, 0:10240].rearrange("p (k f) -> p k f", k=8); b_WA = Buf(); b_WB = Buf()
        X_r = WA[:, 0:4096].rearrange("p (d g f) -> p d g f", d=2, g=16)
        X_i = WA[:, 4096:8192].rearrange("p (d g f) -> p d g f", d=2, g=16); b_Xh = [Buf(), Buf()]
        pw = {n: WAf[:, 4096 + i * 288:4096 + (i + 1) * 288].rearrange("p (k c) -> p k c", k=9)
              for i, n in enumerate(["pr", "pi", "mr", "mi"])}
        b_pw = Buf()
        cs_s = WAf[:, 5248:5536]; cs_c = WAf[:, 5536:5824]
        mg_p = WAf[:, 5824:6112]; mg_m = WAf[:, 6112:6400]
        Bb = {"r": WAf[:, 6400:6912].rearrange("p (d g h) -> p d g h", d=2, g=16),
              "i": WAf[:, 6912:7424].rearrange("p (d g h) -> p d g h", d=2, g=16)}
        b_Bb = Buf()
        zt = [WAf[:, 7424 + i * 32:7424 + (i + 1) * 32] for i in range(6)]; b_zt = Buf()
        wa_s2 = [b_pw, b_Bb, b_zt] + b_Xh
        E0c = WAf[:, 6144:6656].rearrange("p (g c) -> p g c", g=32)
        E0s = WAf[:, 6656:7168].rearrange("p (g c) -> p g c", g=32)
        E1c = WAf[:, 7168:7680].rearrange("p (g c) -> p g c", g=32)
        E1s = WAf[:, 7680:8192].rearrange("p (g c) -> p g c", g=32)
        b_E = Buf()
        wglu = WA[:, 0:4096].rearrange("p (k f) -> p k f", k=4)
        wout = WA[:, 4096:12288].rearrange("p (k f) -> p k f", k=8)
        wsl = [(WA[:, s * 8192:s * 8192 + 4096].rearrange("p (k f) -> p k f", k=8),
                WA[:, s * 8192 + 4096:s * 8192 + 8192].rearrange("p (k f) -> p k f", k=4)) for s in range(2)]

        qT = A3[:, :].rearrange("p (j t) -> p j t", j=4); b_q = Buf()
        gluT = A1[:, 8192:16384].rearrange("p (j t) -> p j t", j=4); b_glu = Buf()
        a1_p1 = b_Yb + [b_Yall, b_glu, b_tab, b_W, b_S, b_H]
        uT2 = A4[:, :].rearrange("p (a i c) -> p a i c", a=4, i=8); b_uT2 = Buf()
        attT = A4[:, :].rearrange("p (j t) -> p j t", j=4); b_att = Buf()

        pM2 = [[MISC[:, (s_ * 3 + i) * 512:(s_ * 3 + i + 1) * 512] for i in range(3)] for s_ in range(2)]
        b_pM2 = [[Buf() for _ in range(3)] for _ in range(2)]
        rden = MISC[:, 3072:3584].bitcast(F32).rearrange("p (a q) -> p a q", a=2); b_rden = Buf()
        rl = [MISC[:, i * 512:(i + 1) * 512] for i in range(2)]; b_rl = [Buf(), Buf()]
        aT = MISC[:, 1024:3072].rearrange("p (k t) -> p k t", k=4); b_a = [Buf() for _ in range(4)]
        misc_att = b_pM2[0] + b_pM2[1] + [b_rden]
        misc_ffn = b_rl + b_a

        kT = sb("kT", [128, T + 256], BF16); b_k = Buf()
        vtm = sb("vtm", [128, 18, 128], BF16); b_v = Buf()
        cst = sb("cst_sb", [128, 656]); b_cst = Buf()
        cstb = sb("cstb_sb", [128, 128], BF16)
        onesb = sb("onesb", [128, 128], BF16)
        blk2 = sb("blk2", [128, 128], BF16)
        ones_pn = sb("ones_pn", [128, 2, 64], BF16)
        sel = sb("sel_sb", [128, 16]); b_sel = Buf()
        g1 = sb("g1_sb", [128, L, 8]); g2 = sb("g2_sb", [128, L, 8])
        qg = sb("qg_sb", [128, L]); kg = sb("kg_sb", [128, L])
        sinkr = sb("sink_sb", [128, L * 8]); esink = sb("esink", [128, L, 8])
        b_small = Buf()
        epsb = sb("epsb", [128, 1])
        lre = sb("lre_sb", [128, 32]); lim = sb("lim_sb", [128, 32]); ldt = sb("ldt_sb", [128, 32])
        dcol = sb("dcol_sb", [128, 32]); b_par = Buf()
        lrdt = sb("lrdt", [128, 32]); th = sb("th", [128, 32]); Thu = sb("Thu", [128, 32])
        R8 = sb("R8", [128, 32]); Acr = sb("Acr", [128, 32]); Aci = sb("Aci", [128, 32])
        finr = sb("finr", [128, 32]); fini = sb("fini", [128, 32]); mag2048 = sb("mag2048", [128, 32])
        b_der = Buf()
        tS = [sb(f"tS{i}", [128, 512]) for i in range(3)]; b_tS = [Buf() for _ in range(3)]
        tSI = sb("tSI", [128, 512], I32); b_tSI = Buf()
        Fst = sb("Fst", [128, 64]); b_F = Buf()
        stg = sb("stg", [128, 4, 64]); b_stg = Buf()
        carry = sb("carry", [128, 64]); b_carry = Buf()
        hcr = sb("hcr", [128, 64]); b_hcr = Buf()
        halt = sb("halt", [128, 512]); b_halt = Buf()
        sq = sb("sq", [128, 2, 512], BF16); b_sq = [Buf(), Buf()]
        rstd = sb("rstd", [128, 512]); b_rstd = Buf()
        sig = sb("sig", [128, 512]); b_sig = Buf()
        gatec = sb("gatec", [128, 8]); b_gate = Buf()

        psb = [es.enter_context(nc.psum_tensor(f"ps{i}", [128, 512], F32)) for i in range(8)]
        b_ps = [Buf() for _ in range(8)]
        ps_pools = {"all": list(range(8)), "ssm": [0, 1, 2], "att_s": [5, 6, 7], "att_o": [3, 4]}
        ps_rr = {k: 0 for k in ps_pools}

        def nps(pool="all"):
            pool = "all"
            lst = ps_pools[pool]
            i = lst[ps_rr[pool] % len(lst)]
            ps_rr[pool] += 1
            return psb[i], b_ps[i]

        def V(fn, r=(), w=()):
            return P.op("vector", fn, r, w)

        def A(fn, r=(), w=()):
            return P.op("scalar", fn, r, w)

        def MM(fns, r=(), w=()):
            return P.group("tensor", fns, r, w)

        def I(method, *a, **k):
            return lambda e: getattr(e, method)(*a, **k)

        def mm(out, lhsT, rhs, start=True, stop=True):
            return I("matmul", out, lhsT=lhsT, rhs=rhs, start=start, stop=stop)

        def gate(old, new):
            V(I("memset", gatec[:, 0:1], 0.0), [], list(old) + list(new) + [b_gate])

        tap_i = [0]

        deferred_taps = []

        def tap(name, ap, bufs, n):
            if name in taps:
                deferred_taps.append((name, ap, bufs, n))

        def emit_taps():
            if not deferred_taps:
                return
            P.finish("vector")
            P.finish("sync")
            for (name, ap, bufs, n) in deferred_taps:
                idx = taps.index(name)
                for c0 in range(0, n, 512):
                    c1 = min(n, c0 + 512)
                    V(I("tensor_copy", tS[2][:, 0:c1 - c0], ap[:, c0:c1]), list(bufs), [b_tS[2]])
                    P.dma("sync", I("dma_start", out=dbg_d[idx, :, c0:c1], in_=tS[2][:, 0:c1 - c0]), reads=[b_tS[2]])

        P.dma("sync", I("dma_start", out=cst[:], in_=cst_d), writes=[b_cst])
        P.dma("sync", I("dma_start", out=sel[:], in_=sel_d), writes=[b_sel])
        for (t_sb, t_d) in [(g1, g1_d), (g2, g2_d), (qg, qg_d), (kg, kg_d)]:
            P.dma("sync", I("dma_start", out=t_sb[:], in_=t_d), writes=[b_small])
        P.dma("sync", I("dma_start", out=sinkr[:], in_=sink_d.rearrange("p l h -> p (l h)")), writes=[b_small])
        for k in range(8):
            P.dma("sync", I("dma_start", out=xT[:, k, :], in_=xT_d[k * 128:(k + 1) * 128, :]),
                  writes=b_x[k])
        ident = cst[:, 0:128]; Mf = cst[:, 128:256]; Mb = cst[:, 256:384]; kvec = cst[:, 384:393]
        cpos = cst[:, 400:656]
        V(I("tensor_copy", cstb[:], ident), [b_cst], [b_cst])
        V(I("memset", onesb[:], 1.0), [], [b_cst])
        V(I("memset", blk2[:], 0.0), [], [b_cst])
        V(I("memset", blk2[0:64, 0:64], 1.0), [], [b_cst])
        V(I("memset", blk2[64:128, 64:128], 1.0), [], [b_cst])
        V(I("memset", epsb[:], EPS), [], [b_cst])
        V(I("tensor_copy", ones_pn[:, 0, :], sel[:, 12:13].to_broadcast([128, 64])), [b_sel], [b_cst])
        V(I("tensor_copy", ones_pn[:, 1, :], sel[:, 13:14].to_broadcast([128, 64])), [b_sel], [b_cst])
        A(I("activation", esink[:, :, :].rearrange("p l h -> p (l h)"), sinkr[:], AF.Exp), [b_small], [b_small])
        identb = cstb
        c16 = sb("c16", [128, 16])
        V(I("tensor_scalar", c16[:], cpos[:, 0:16], -1.0, 16.0, ALU.add, ALU.mult), [b_cst], [b_cst])
        crev = sb("crev", [128, 256])
        V(I("tensor_scalar", crev[:], cpos, -1.0, 256.0, ALU.mult, ALU.add), [b_cst], [b_cst])
        L8 = sb("L8", [128, 32]); facc = sb("facc", [128, 4]); b_facc = Buf()
        tabsets = [(cosT, sinT, b_tab), (rstd[:, :].rearrange("p (g c) -> p g c", g=2), sig[:, :].rearrange("p (g c) -> p g c", g=2), None)]
        ssm_it = [0]
        esrow = sb("esrow", [1, 8, 128], BF16); b_esr = Buf()

        def rms_tile(l, tt, gain):
            ts = slice(tt * 512, (tt + 1) * 512)
            pt, bp = nps()
            for k in range(8):
                s = k % 2
                A(I("activation", sq[:, s, :], xT[:, k, ts], AF.Square), [b_x[k][tt]], [b_sq[s]])
                P.op("tensor", mm(pt[:, :], onesb[:, :], sq[:, s, :], start=(k == 0), stop=(k == 7)), [b_sq[s], b_cst], [bp])
            A(I("activation", rstd[:], pt[:, :], AF.Ln, bias=epsb[:], scale=1.0 / 1024.0), [bp, b_cst], [b_rstd])
            A(I("activation", rstd[:], rstd[:], AF.Exp, scale=-0.5), [b_rstd], [b_rstd])
            for k in range(8):
                V(I("scalar_tensor_tensor", out=hT[:, k, ts], in0=xT[:, k, ts], scalar=gain[:, l, k:k + 1],
                                                        in1=rstd[:], op0=ALU.mult, op1=ALU.mult),
                  [b_x[k][tt], b_rstd, b_small], [b_h[tt]])

        def sincos_turns(tin, n, out_s, out_c, rb, wb):
            a_ = tS[0][:, 0:n]; b_ = tS[1][:, 0:n]; i_ = tSI[:, 0:n]
            for (off, outp) in [(0.0, out_s), (0.25, out_c)]:
                V(I("tensor_scalar_add", a_, tin, off), rb, [b_tS[0]])
                V(I("tensor_copy", i_, a_), [b_tS[0]], [b_tSI])
                V(I("tensor_copy", b_, i_), [b_tSI], [b_tS[1]])
                V(I("tensor_sub", a_, a_, b_), [b_tS[0], b_tS[1]], [b_tS[0]])
                V(I("tensor_scalar", a_, a_, -0.4999999, 0.4999999, ALU.max, ALU.min), [b_tS[0]], [b_tS[0]])
                A(I("activation", outp, a_, AF.Sin, scale=TWO_PI), [b_tS[0]], wb)

        def ck(name):
            if stop_after == name:
                raise _Stop()

        if os.environ.get("HALO_FIRST"):
            V(I("memset", halt[:], 1.0), [], [b_halt])
            P.dma("gpsimd", I("dma_start", out=hal_src, in_=halt[:]), reads=[b_halt], writes=[b_halg])
            P.op("gpsimd", I("collective_compute", "AllGather", ALU.bypass, replica_groups=GROUPS,
                             ins=[hal_src.opt()], outs=[hal_dst.opt()]), [b_halg], [b_halg])
            P.dma("gpsimd", I("dma_start", out=halg, in_=hal_dst.rearrange("(r p) f -> p r f", p=128)),
                  reads=[b_halg], writes=[b_halg])
            if os.environ.get("HALO_FIRST") == "only":
                raise_stop = True

        def layer(l):
            P.dma("gpsimd", I("dma_start", out=win, in_=w_in_d[l].rearrange("(k p) f -> p k f", p=128)),
                  writes=[b_WA, b_WB])
            for (t_sb, t_d) in [(lre, lre_d), (lim, lim_d), (ldt, ldt_d), (dcol, dcol_d)]:
                P.dma("sync", I("dma_start", out=t_sb[:], in_=t_d[:, l]), writes=[b_par])

            for tt in range(NTT):
                ts = slice(tt * 512, (tt + 1) * 512)
                rms_tile(l, tt, g1)
                for k in range(8 if not os.environ.get("NOSPILL") else 0):
                    P.dma("sync", I("dma_start", out=xsp[k * 128:(k + 1) * 128, ts], in_=xT[:, k, ts]),
                          reads=[b_x[k][tt]])
                for oc in range(5):
                    pt, bp = nps()
                    MM([mm(pt[:, :], win[:, k, oc * 128:(oc + 1) * 128], hT[:, k, ts], k == 0, k == 7) for k in range(8)],
                       [b_WA, b_h[tt]], [bp])
                    A(I("activation", sq[:, 0, :], pt[:, :], AF.Square), [bp], [b_sq[0]])
                    p2, bp2 = nps()
                    MM([mm(p2[:, :], blk2[:, :], sq[:, 0, :])], [b_sq[0], b_cst], [bp2])
                    A(I("activation", rstd[:], p2[:, :], AF.Ln, bias=epsb[:], scale=1.0 / 64.0),
                      [bp2, b_cst], [b_rstd])
                    A(I("activation", rstd[:], rstd[:], AF.Exp, scale=-0.5), [b_rstd], [b_rstd])
                    if oc < 4:
                        V(I("scalar_tensor_tensor", out=qT[:, oc, ts], in0=pt[:, :], scalar=qg[:, l:l + 1],
                                                                               in1=rstd[:], op0=ALU.mult, op1=ALU.mult),
                          [bp, b_rstd, b_small], [b_q])
                    else:
                        V(I("scalar_tensor_tensor", out=kT[:, 128 + tt * 512:128 + (tt + 1) * 512], in0=pt[:, :],
                                                                         scalar=kg[:, l:l + 1], in1=rstd[:], op0=ALU.mult, op1=ALU.mult),
                          [bp, b_rstd, b_small], [b_k])
                        if tt == 0:
                            V(I("scalar_tensor_tensor", out=halt[:, 0:128], in0=pt[:, 0:128], scalar=kg[:, l:l + 1],
                                                                      in1=rstd[:, 0:128], op0=ALU.mult, op1=ALU.mult),
                              [bp, b_rstd, b_small], [b_halt])
                        if tt == NTT - 1:
                            V(I("scalar_tensor_tensor", out=halt[:, 128:256], in0=pt[:, 384:512], scalar=kg[:, l:l + 1],
                                                                      in1=rstd[:, 384:512], op0=ALU.mult, op1=ALU.mult),
                              [bp, b_rstd, b_small], [b_halt])
                pt, bp = nps()
                fns = []
                for b4 in range(4):
                    for k in range(8):
                        fns.append(mm(pt[:, b4 * 128:(b4 + 1) * 128], hT[:, k, tt * 512 + b4 * 128: tt * 512 + (b4 + 1) * 128],
                                      win[:, k, 640:768], k == 0, k == 7))
                MM(fns, [b_WA, b_h[tt]], [bp])
                A(I("activation", vtm[:, 1 + tt * 4:1 + (tt + 1) * 4, :].rearrange("p b f -> p (b f)"),
                                                       pt[:, :], AF.Copy), [bp], [b_v])
                if tt == 0:
                    V(I("tensor_copy", halt[:, 256:384], pt[:, 0:128]), [bp], [b_halt])
                if tt == NTT - 1:
                    V(I("tensor_copy", halt[:, 384:512], pt[:, 384:512]), [bp], [b_halt])
                for a_ in range(4):
                    oc = 6 + a_
                    pt, bp = nps()
                    MM([mm(pt[:, :], win[:, k, oc * 128:(oc + 1) * 128], hT[:, k, ts], k == 0, k == 7) for k in range(8)],
                       [b_WA, b_h[tt]], [bp])
                    V(I("tensor_copy", uT2[:, a_, :, tt * 64:(tt + 1) * 64], pt[:, :].rearrange("p (c i) -> p i c", i=8)),
                      [bp], [b_uT2])
            if l == 0:
                tap("hT", hT.rearrange("p k t -> p (k t)"), b_h, 2048)
                tap("qT", qT.rearrange("p j t -> p (j t)"), [b_q], 2048)
                tap("kT", kT[:, 128:128 + 2048], [b_k], 2048)
                tap("uT2", uT2.rearrange("p a i c -> p (a i c)"), [b_uT2], 2048)

            ck("S1")
            gate(b_h, a1_s2)
            P.dma("gpsimd", I("dma_start", out=hal_src, in_=halt[:]), reads=[b_halt], writes=[b_halg])
            if not os.environ.get("NOCC"):
                P.op("gpsimd", I("collective_compute", "AllGather", ALU.bypass, replica_groups=GROUPS,
                                 ins=[hal_src.opt()], outs=[hal_dst.opt()]), [b_halg], [b_halg])
            P.dma("gpsimd", I("dma_start", out=halg, in_=hal_dst.rearrange("(r p) f -> p r f", p=128)),
                  reads=[b_halg], writes=[b_halg])

            ck("halo")
            gate([b_WA, b_WB, b_E], wa_s2)
            gate(allx, xr_mixer)
            P.dma("gpsimd", I("dma_start", out=etab.rearrange("p k h q -> p (k h q)"), in_=etab_d), writes=[b_et])
            for (t_ap, t_d) in [(bre, bre_d), (bim, bim_d), (cre, cre_d), (cim, cim_d)]:
                P.dma("sync", I("dma_start",
                    out=t_ap.rearrange("p g h -> p (g h)") if len(t_ap.shape) == 3 else t_ap.rearrange("p d g h -> p (d g h)"),
                    in_=t_d[:, l]), writes=[b_par2])
            A(I("activation", lrdt[:], ldt[:], AF.Exp), [b_par], [b_der])
            V(I("tensor_mul", th[:], lim[:], lrdt[:]), [b_par, b_der], [b_der])
            V(I("tensor_mul", lrdt[:], lre[:], lrdt[:]), [b_par, b_der], [b_der])
            V(I("tensor_scalar_mul", th[:], th[:], 1.0 / TWO_PI), [b_der], [b_der])
            kb9 = kvec.unsqueeze(2).to_broadcast([128, 9, 32])
            ph9 = tS[2][:, 0:288]
            V(I("tensor_tensor", ph9.rearrange("p (k c) -> p k c", k=9), th[:].unsqueeze(1).to_broadcast([128, 9, 32]),
                                        kb9, ALU.mult), [b_der, b_cst], [b_tS[2]])
            sincos_turns(ph9, 288, cs_s, cs_c, [b_tS[2]], [b_pw])
            V(I("tensor_tensor", ph9.rearrange("p (k c) -> p k c", k=9), lrdt[:].unsqueeze(1).to_broadcast([128, 9, 32]),
                                        kb9, ALU.mult), [b_der, b_cst, b_pw], [b_tS[2]])
            A(I("activation", mg_p, ph9, AF.Exp), [b_tS[2]], [b_pw])
            A(I("activation", mg_m, ph9, AF.Exp, scale=-1.0), [b_tS[2]], [b_pw])
            fl = lambda t: t.rearrange("p k c -> p (k c)")
            V(I("tensor_mul", fl(pw["pr"]), mg_p, cs_c), [b_pw], [b_pw])
            V(I("tensor_mul", fl(pw["pi"]), mg_p, cs_s), [b_pw], [b_pw])
            V(I("tensor_mul", fl(pw["mr"]), mg_m, cs_c), [b_pw], [b_pw])
            V(I("scalar_tensor_tensor", out=fl(pw["mi"]), in0=mg_m, scalar=-1.0, in1=cs_s,
                                               op0=ALU.mult, op1=ALU.mult), [b_pw], [b_pw])
            ck("S2a")
            ar1, ai1 = pw["pr"][:, 1, :], pw["pi"][:, 1, :]
            zin = [b_zt, b_par, b_pw]
            V(I("tensor_mul", zt[0], lre[:], lre[:]), zin, [b_zt])
            V(I("tensor_mul", zt[1], lim[:], lim[:]), zin, [b_zt])
            V(I("tensor_add", zt[0], zt[0], zt[1]), zin, [b_zt])
            V(I("reciprocal", zt[0], zt[0]), zin, [b_zt])
            V(I("tensor_scalar_add", zt[1], ar1, -1.0), zin, [b_zt])
            V(I("tensor_mul", zt[2], zt[1], lre[:]), zin, [b_zt])
            V(I("tensor_mul", zt[3], ai1, lim[:]), zin, [b_zt])
            V(I("tensor_add", zt[2], zt[2], zt[3]), zin, [b_zt])
            V(I("tensor_mul", zt[2], zt[2], zt[0]), zin, [b_zt])
            V(I("tensor_mul", zt[3], ai1, lre[:]), zin, [b_zt])
            V(I("tensor_mul", zt[4], zt[1], lim[:]), zin, [b_zt])
            V(I("tensor_sub", zt[3], zt[3], zt[4]), zin, [b_zt])
            V(I("tensor_mul", zt[3], zt[3], zt[0]), zin, [b_zt])
            for d in range(2):
                zr = zt[2][:, d * 16:(d + 1) * 16].unsqueeze(2).to_broadcast([128, 16, 16])
                zi = zt[3][:, d * 16:(d + 1) * 16].unsqueeze(2).to_broadcast([128, 16, 16])
                t1 = tA[:, 0:256].rearrange("p (a b) -> p a b", a=16)
                t2 = tB[:, 0:256].rearrange("p (a b) -> p a b", a=16)
                V(I("tensor_tensor", t1, bre, zr, ALU.mult), [b_par2, b_zt], [b_tA])
                V(I("tensor_tensor", t2, bim, zi, ALU.mult), [b_par2, b_zt], [b_tB])
                V(I("tensor_sub", Bb["r"][:, d], t1, t2), [b_tA, b_tB], [b_Bb])
                V(I("tensor_tensor", t1, bim, zr, ALU.mult), [b_par2, b_zt], [b_tA])
                V(I("tensor_tensor", t2, bre, zi, ALU.mult), [b_par2, b_zt], [b_tB])
                V(I("tensor_add", Bb["i"][:, d], t1, t2), [b_tA, b_tB], [b_Bb])

            ck("S2b")

            def cmul_tab(out_r, out_i, tr, ti, Xr_, Xi_, wbs, neg_i=False):
                for hf, (eng, btA, btB) in enumerate([("vector", b_tA, b_tB), ("vector", b_tA2, b_tB2)]):
                    gs_ = slice(hf * 8, hf * 8 + 8)
                    ta = tA[:, hf * 1024:(hf + 1) * 1024]; tb = tB[:, hf * 1024:(hf + 1) * 1024]
                    trb = tr[:, :, gs_].rearrange("p k g -> p g k").unsqueeze(3).to_broadcast([128, 8, 8, 16])
                    tib = ti[:, :, gs_].rearrange("p k g -> p g k").unsqueeze(3).to_broadcast([128, 8, 8, 16])
                    Xrb = Xr_[:, gs_, :].unsqueeze(2).to_broadcast([128, 8, 8, 16])
                    Xib = Xi_[:, gs_, :].unsqueeze(2).to_broadcast([128, 8, 8, 16])
                    v4 = lambda t: t.rearrange("p (g k h) -> p g k h", g=8, k=8)
                    o4 = lambda t: t[:, gs_, :].rearrange("p g (k h) -> p g k h", k=8)
                    rd = [b_pw, b_Bb, b_par2]
                    wb = [wbs[hf]]
                    E_ = lambda fn, r, w, eng=eng: P.op(eng, fn, r, w)
                    E_(I("tensor_tensor", v4(ta), trb, Xrb, ALU.mult), rd, [btA])
                    E_(I("tensor_tensor", v4(tb), tib, Xib, ALU.mult), rd, [btB])
                    E_(I("tensor_sub", o4(out_r), v4(ta), v4(tb)), [btA, btB], wb)
                    E_(I("tensor_tensor", v4(ta), trb, Xib, ALU.mult), rd, [btA])
                    E_(I("tensor_tensor", v4(tb), tib, Xrb, ALU.mult), rd, [btB])
                    if neg_i and eng == "vector":
                        E_(I("scalar_tensor_tensor", out=o4(out_i), in0=v4(ta), scalar=-1.0, in1=v4(tb),
                             op0=ALU.mult, op1=ALU.subtract), [btA, btB], wb)
                    elif neg_i:
                        E_(I("tensor_add", v4(ta), v4(ta), v4(tb)), [btA, btB], [btA])
                        E_(I("tensor_scalar_mul", o4(out_i), v4(ta), -1.0), [btA], wb)
                    else:
                        E_(I("tensor_add", o4(out_i), v4(ta), v4(tb)), [btA, btB], wb)

            def tab(name, lo, hi, rev, d):
                t = pw[name][:, lo:hi, d * 16:(d + 1) * 16]
                return t[:, ::-1, :] if rev else t

            for d in range(2):
                rev = (d == 1)
                cmul_tab(Qr[:, d], Qn[:, d], tab("pr", 1, 9, rev, d), tab("pi", 1, 9, rev, d), cre[:, d], cim[:, d], b_Qh, neg_i=True)
                cmul_tab(X_r[:, d], X_i[:, d], tab("mr", 1, 9, rev, d), tab("mi", 1, 9, rev, d), Bb["r"][:, d], Bb["i"][:, d], b_Xh)
            ck("S2c")
            for gq4 in range(4):
                for g2_ in range(2):
                    ps_ = slice(g2_ * 64, g2_ * 64 + 64)
                    ptf, bpf = nps(); ptb, bpb = nps()
                    for (d, pt, bp) in [(0, ptf, bpf), (1, ptb, bpb)]:
                        fns = []
                        for k4 in range(4):
                            gp = gq4 * 4 + k4
                            fns.append(mm(pt[:, k4 * 128:(k4 + 1) * 128], X_r[ps_, d, gp, :], Qr[ps_, d, gp, :], True, False))
                            fns.append(mm(pt[:, k4 * 128:(k4 + 1) * 128], X_i[ps_, d, gp, :], Qn[ps_, d, gp, :], False, True))
                        MM(fns, b_Xh + b_Qh, [bp])
                    m4 = lambda m: m.unsqueeze(1).to_broadcast([128, 4, 128])
                    v3 = lambda t: t.rearrange("p (a b) -> p a b", a=4)
                    V(I("tensor_tensor", v3(tA[:, 0:512]), v3(ptf[:, :]), m4(Mf), ALU.mult), [bpf, b_cst], [b_tA])
                    V(I("tensor_tensor", v3(tB[:, 0:512]), v3(ptb[:, :]), m4(Mb), ALU.mult), [bpb, b_cst], [b_tB])
                    V(I("tensor_add", tA[:, 0:512], tA[:, 0:512], tB[:, 0:512]), [b_tA, b_tB], [b_tA])
                    for k4 in range(4):
                        g = 2 * (gq4 * 4 + k4) + g2_
                        V(I("scalar_tensor_tensor", out=Tm[:, g, :], in0=ident, scalar=dcol[:, g:g + 1],
                            in1=tA[:, k4 * 128:(k4 + 1) * 128], op0=ALU.mult, op1=ALU.add),
                          [b_tA, b_par, b_cst], [b_T])
            ck("S2d")
            for d in range(2):
                rev = (d == 1)
                cmul_tab(X_r[:, d], X_i[:, d], tab("pr", 0, 8, not rev, d), tab("pi", 0, 8, not rev, d),
                         Bb["r"][:, d], Bb["i"][:, d], b_Xh)
            for d in range(2):
                for gq4 in range(4):
                    for g2_ in range(2):
                        ps_ = slice(g2_ * 64, g2_ * 64 + 64)
                        pt, bp = nps()
                        fns = []
                        for k4 in range(4):
                            gp = gq4 * 4 + k4
                            for ri, Pm in enumerate([X_r, X_i]):
                                c0 = (k4 * 2 + ri) * 64
                                fns.append(mm(pt[:, c0:c0 + 64], Pm[ps_, d, gp, :], identb[ps_, g2_ * 64:g2_ * 64 + 64]))
                        MM(fns, b_Xh + [b_cst], [bp])
                        g0 = 2 * gq4 * 4 + g2_
                        V(I("tensor_copy", PT[:, d, g0:g0 + 7:2, :, :].rearrange("p g r n -> p g (r n)"),
                            pt[:, :].rearrange("p (g f) -> p g f", g=4)), [bp], [b_PT])
            ck("S2e")
            V(I("tensor_scalar_mul", Thu[:], th[:], 8.0), [b_der], [b_der])
            V(I("tensor_copy", tSI[:, 0:32], Thu[:]), [b_der], [b_tSI])
            V(I("tensor_copy", tS[1][:, 0:32], tSI[:, 0:32]), [b_tSI], [b_tS[1]])
            V(I("tensor_sub", Thu[:], Thu[:], tS[1][:, 0:32]), [b_tS[1], b_der], [b_der])
            A(I("activation", R8[:], lrdt[:], AF.Exp, scale=8.0), [b_der], [b_der])
            V(I("tensor_scalar_mul", L8[:], lrdt[:], 8.0), [b_der], [b_der])
            V(I("tensor_scalar_mul", tS[2][:, 0:32], Thu[:], 256.0), [b_der], [b_tS[2]])
            sincos_turns(tS[2][:, 0:32], 32, fini[:], finr[:], [b_tS[2]], [b_der])
            A(I("activation", mag2048[:], lrdt[:], AF.Exp, scale=2048.0), [b_der], [b_der])
            V(I("tensor_mul", Acr[:], mag2048[:], finr[:]), [b_der], [b_der])
            V(I("tensor_mul", Aci[:], mag2048[:], fini[:]), [b_der], [b_der])
            gate([b_Bb, b_zt], [b_E])
            for (Ec, Es, pos) in [(E0c, E0s, cpos[:, 0:16]), (E1c, E1s, c16[:])]:
                V(I("tensor_tensor", tS[2][:, 0:512].rearrange("p (g c) -> p g c", g=32),
                    Thu[:].unsqueeze(2).to_broadcast([128, 32, 16]), pos.unsqueeze(1).to_broadcast([128, 32, 16]), ALU.mult),
                  [b_der, b_cst], [b_tS[2]])
                sincos_turns(tS[2][:, 0:512], 512, Es.rearrange("p g c -> p (g c)"), Ec.rearrange("p g c -> p (g c)"), [b_tS[2]], [b_E])
            if l == 0:
                tap("Tm", Tm.rearrange("p g f -> p (g f)"), [b_T], 2048)
                tap("Qr", Qr.rearrange("p d g f -> p (d g f)"), b_Qh, 2048)
                tap("PT", PT.rearrange("p d g r n -> p (d g r n)"), [b_PT], 2048)

            ck("S2")
            for (so, kcols, vcols, kdst, vblk) in [(4, slice(128, 256), slice(384, 512), slice(0, 128), 0),
                                                   (8, slice(0, 128), slice(256, 384), slice(T + 128, T + 256), 17)]:
                for (cols, dst_ap, wb) in [(kcols, kT[:, kdst], b_k), (vcols, vtm[:, vblk, :], b_v)]:
                    acc = tS[2][:, 0:128]
                    V(I("tensor_scalar_mul", acc, halg[:, 0, cols], sel[:, so:so + 1]),
                      [b_halg, b_sel], [b_tS[2]])
                    for r in range(1, 4):
                        V(I("scalar_tensor_tensor", out=acc, in0=halg[:, r, cols],
                                                                                  scalar=sel[:, so + r:so + r + 1], in1=acc,
                                                                                  op0=ALU.mult, op1=ALU.add),
                          [b_halg, b_sel], [b_tS[2]])
                    V(I("tensor_copy", dst_ap, acc), [b_tS[2]], [wb])

            ck("halosel")
            for i in range(8):
                for g8 in range(8):
                    P.dma("sync", I("dma_start",
                        out=U[i * 16:(i + 1) * 16, :, :].rearrange("p (a g) c -> p a g c", a=4)[:, :, g8, :],
                        in_=uT2[g8 * 16:(g8 + 1) * 16, :, i, :]), reads=[b_uT2], writes=[b_U])
            if l == 0:
                tap("U", U.rearrange("p g c -> p (g c)"), [b_U], 2048)
            ck("relayout")
            gate(a1_s2, a1_p1)
            gate(wa_s2, [b_WA, b_WB])
            P.dma("gpsimd", I("dma_start", out=wglu, in_=w_glu_d[l].rearrange("(k p) f -> p k f", p=128)), writes=[b_WA])
            P.dma("gpsimd", I("dma_start", out=wout, in_=w_out_d[l].rearrange("(k p) f -> p k f", p=128)), writes=[b_WA, b_WB])

            ssm_ctx = {}

            def ssm_batch(bt, phase):
                ssm_z(bt, phase)
                ssm_dve(bt, phase)

            def ssm_z(bt, phase):
                for d in range(2):
                    c0 = d * 16 + bt * 2
                    gsl = slice(c0, c0 + 2)
                    cosT_, sinT_, btab_ = tabsets[ssm_it[0] % 2]
                    ssm_it[0] += 1
                    if btab_ is None:
                        rd_t = [b_rstd, b_sig]; wr_c = [b_rstd]; wr_s = [b_sig]
                    else:
                        rd_t = [btab_]; wr_c = [btab_]; wr_s = [btab_]
                    e1c = E1c[:, gsl, :].unsqueeze(3).to_broadcast([128, 2, 16, 16])
                    e1s = E1s[:, gsl, :].unsqueeze(3).to_broadcast([128, 2, 16, 16])
                    e0c = E0c[:, gsl, :].unsqueeze(2).to_broadcast([128, 2, 16, 16])
                    e0s = E0s[:, gsl, :].unsqueeze(2).to_broadcast([128, 2, 16, 16])
                    q4 = lambda t: t.rearrange("p (g a b) -> p g a b", g=2, a=16)
                    q3 = lambda t: t.rearrange("p g (a b) -> p g a b", a=16)
                    pa = tS[2][:, 0:512]; pb = tSI[:, 0:512].bitcast(F32)
                    G = lambda fn, r, w: P.op("gpsimd", fn, r, w)
                    G(I("tensor_tensor", q4(pa), e1c, e0c, ALU.mult), [b_E], [b_tS[2]])
                    G(I("tensor_tensor", q4(pb), e1s, e0s, ALU.mult), [b_E], [b_tSI])
                    G(I("tensor_sub", q3(cosT_), q4(pa), q4(pb)), [b_tS[2], b_tSI], wr_c)
                    G(I("tensor_tensor", q4(pa), e1s, e0c, ALU.mult), [b_E], [b_tS[2]])
                    G(I("tensor_tensor", q4(pb), e1c, e0s, ALU.mult), [b_E], [b_tSI])
                    G(I("tensor_add", q3(sinT_), q4(pa), q4(pb)), [b_tS[2], b_tSI], wr_s)
                    zb = []
                    for ri in range(2):
                        pt, bp = nps("ssm")
                        fns = []
                        for gq in range(2):
                            gp = bt * 2 + gq
                            for g2_ in range(2):
                                g = gp * 2 + g2_
                                fns.append(mm(pt[g2_ * 64:(g2_ + 1) * 64, gq * 256:(gq + 1) * 256], PT[:, d, g, ri, :], U[:, g, :]))
                        MM(fns, [b_PT, b_U], [bp])
                        zb.append((pt, bp))
                    ssm_ctx[(bt, d)] = (zb, cosT_, sinT_, rd_t)

            def ssm_dve(bt, phase):
                for d in range(2):
                    c0 = d * 16 + bt * 2
                    gsl = slice(c0, c0 + 2)
                    zb, cosT_, sinT_, rd_t = ssm_ctx.pop((bt, d))
                    (pzr, bzr), (pzi, bzi) = zb
                    zr_ = pzr[:, :].rearrange("p (g c) -> p g c", g=2)
                    zi_ = pzi[:, :].rearrange("p (g c) -> p g c", g=2)
                    if d == 1:
                        zr_ = zr_[:, :, ::-1]; zi_ = zi_[:, :, ::-1]
                    a3 = tS[0][:, 0:512].rearrange("p (g c) -> p g c", g=2)
                    b3 = tS[1][:, 0:512].rearrange("p (g c) -> p g c", g=2)
                    V(I("tensor_tensor", a3, zr_, cosT_, ALU.mult), [bzr, *rd_t], [b_tS[0]])
                    V(I("tensor_tensor", b3, zi_, sinT_, ALU.mult), [bzi, *rd_t], [b_tS[1]])
                    V(I("tensor_add", Wr, a3, b3), [b_tS[0], b_tS[1]], [b_W])
                    V(I("tensor_tensor", a3, zi_, cosT_, ALU.mult), [bzi, *rd_t], [b_tS[0]])
                    V(I("tensor_tensor", b3, zr_, sinT_, ALU.mult), [bzr, *rd_t], [b_tS[1]])
                    V(I("tensor_sub", Wi, a3, b3), [b_tS[0], b_tS[1]], [b_W])
                    if phase == 0:
                        V(I("memset", facc[:], 0.0), [], [b_facc])
                        for gq in range(2):
                            col = c0 + gq
                            A(I("activation", Sr[:, gq, :], crev[:], AF.Exp, scale=L8[:, col:col + 1]), [b_der, b_cst], [b_S])
                        for gq in range(2):
                            for (Wt, ci) in [(Wr, 0), (Wi, 1)]:
                                V(I("scalar_tensor_tensor", out=Si[:, gq, :], in0=Wt[:, gq, :], scalar=1.0, in1=Sr[:, gq, :],
                                    op0=ALU.mult, op1=ALU.mult, accum_out=facc[:, ci * 2 + gq:ci * 2 + gq + 1]),
                                  [b_W, b_S], [b_S, b_facc])
                    else:
                        for gq in range(2):
                            col = c0 + gq
                            for (Wt, St, ci) in [(Wr, Sr, 0), (Wi, Si, 1)]:
                                cc = ci * 32 + col
                                init = carry[:, cc:cc + 1]
                                V(I("tensor_tensor_scan", St[:, gq, :], R8[:, col:col + 1].to_broadcast([128, 256]), Wt[:, gq, :], init,
                                    ALU.mult, ALU.add), [b_W, b_der, b_carry], [b_S])
                    if phase == 0:
                        fr = finr[:, gsl]; fi = fini[:, gsl]
                        s_r = facc[:, 0:2]; s_i = facc[:, 2:4]
                        o_r = Fst[:, c0:c0 + 2]; o_i = Fst[:, 32 + c0:32 + c0 + 2]
                        t0 = tS[0][:, 0:2]; t1 = tS[1][:, 0:2]
                        V(I("tensor_mul", t0, s_r, fr), [b_facc, b_der], [b_tS[0]])
                        V(I("tensor_mul", t1, s_i, fi), [b_facc, b_der], [b_tS[1]])
                        V(I("tensor_sub", o_r, t0, t1), [b_tS[0], b_tS[1]], [b_F])
                        V(I("tensor_mul", t0, s_r, fi), [b_facc, b_der], [b_tS[0]])
                        V(I("tensor_mul", t1, s_i, fr), [b_facc, b_der], [b_tS[1]])
                        V(I("tensor_add", o_i, t0, t1), [b_tS[0], b_tS[1]], [b_F])
                    else:
                        Hr_ = Hb16[(d, "r")]; Hi_ = Hb16[(d, "i")]
                        if d == 0:
                            o_r = Hr_[:, :, 1:256]; o_i = Hi_[:, :, 1:256]
                            o_r0 = Hr_[:, :, 0]; o_i0 = Hi_[:, :, 0]
                        else:
                            o_r = Hr_[:, :, 254::-1]; o_i = Hi_[:, :, 254::-1]
                            o_r0 = Hr_[:, :, 255]; o_i0 = Hi_[:, :, 255]
                        a3s = a3[:, :, 0:255]; b3s = b3[:, :, 0:255]
                        V(I("tensor_tensor", a3s, Sr[:, :, 0:255], cosT_[:, :, 0:255], ALU.mult), [b_S, *rd_t], [b_tS[0]])
                        V(I("tensor_tensor", b3s, Si[:, :, 0:255], sinT_[:, :, 0:255], ALU.mult), [b_S, *rd_t], [b_tS[1]])
                        V(I("tensor_sub", o_r, a3s, b3s), [b_tS[0], b_tS[1]], [b_H])
                        V(I("tensor_tensor", a3s, Sr[:, :, 0:255], sinT_[:, :, 0:255], ALU.mult), [b_S, *rd_t], [b_tS[0]])
                        V(I("tensor_tensor", b3s, Si[:, :, 0:255], cosT_[:, :, 0:255], ALU.mult), [b_S, *rd_t], [b_tS[1]])
                        V(I("tensor_add", o_i, a3s, b3s), [b_tS[0], b_tS[1]], [b_H])
                        V(I("tensor_copy", o_r0, carry[:, c0:c0 + 2]), [b_carry], [b_H])
                        V(I("tensor_copy", o_i0, carry[:, 32 + c0:32 + c0 + 2]), [b_carry], [b_H])

            V(I("tensor_copy", esrow[:, :, :], esink[0:1, l, :].unsqueeze(2).to_broadcast([1, 8, 128])), [b_small], [b_esr])
            gate([b_uT2], [b_att])
            att_state = {}

            def att_scores(qb):
                qs = slice(qb * 128, (qb + 1) * 128)
                for kvh in range(2):
                    hs_ = slice(kvh * 64, kvh * 64 + 64)
                    pM = pM2[kvh]; b_pM = b_pM2[kvh]
                    for kb in range(3):
                        pt, bp = nps()
                        MM([mm(pt[:, :].rearrange("p (j q) -> p j q", j=4), kT[hs_, (qb + kb) * 128:(qb + kb + 1) * 128], qT[hs_, :, qs])],
                           [b_k, b_q], [bp])
                        A(I("activation", pM[kb], pt[:, :], AF.Exp, scale=0.125), [bp], [b_pM[kb]])

            def att_mid(qb, on_dve=False):
                for kvh in range(2):
                    hs_ = slice(kvh * 64, kvh * 64 + 64)
                    pM = pM2[kvh]; b_pM = b_pM2[kvh]
                    for kb in range(3):
                        P.op("vector" if (on_dve and kvh == 0) else "gpsimd",
                             I("tensor_tensor", pM[kb], pM[kb],
                               etab[:, kb, kvh * 4:(kvh + 1) * 4, :].rearrange("p j q -> p (j q)"), ALU.mult),
                             [b_et], [b_pM[kb]])
                    pno, bpn = nps()
                    pn = pno[:, 0:256]; pd = pno[:, 256:512]
                    fn_n = []; fn_d = []
                    for j in range(4):
                        po = slice((j % 2) * 64, (j % 2) * 64 + 64)
                        cs2 = slice((j // 2) * 128, (j // 2) * 128 + 128)
                        for kb in range(3):
                            if kb == 0 and qb == 0:
                                ol = ones_pn[:, 0, :]
                            elif kb == 2 and qb == 15:
                                ol = ones_pn[:, 1, :]
                            else:
                                ol = onesb[:, 0:64]
                            fn_n.append(mm(pn[po, cs2], vtm[:, qb + kb, hs_], pM[kb][:, j * 128:(j + 1) * 128], kb == 0, kb == 2))
                            if kb == 0:
                                fn_d.append(mm(pd[po, cs2], onesb[0:1, 0:64], esrow[0:1, kvh * 4 + j, :], True, False))
                            fn_d.append(mm(pd[po, cs2], ol, pM[kb][:, j * 128:(j + 1) * 128], False, kb == 2))
                    MM(fn_n + fn_d, [b_v, b_cst, b_esr] + b_pM, [bpn])
                    att_state[(qb, kvh)] = (pn, pd, bpn)

            def att_fin(qb):
                qs = slice(qb * 128, (qb + 1) * 128)
                for kvh in range(2):
                    pn, pd, bpn = att_state.pop((qb, kvh))
                    A(I("activation", rden, pd.rearrange("p (a q) -> p a q", a=2), AF.Ln), [bpn], [b_rden])
                    A(I("activation", rden, rden, AF.Exp, scale=-1.0), [b_rden], [b_rden])
                    V(I("tensor_tensor", attT[:, kvh * 2:kvh * 2 + 2, qs],
                        pn.rearrange("p (a q) -> p a q", a=2), rden, ALU.mult),
                      [bpn, b_rden, b_uT2], [b_att])

            def att_flush():
                pass

            att_order = [1, 2, 3, 4, 5, 6, 7, 8, 9, 10, 11, 12, 13, 14, 0, 15]
            att_scores(att_order[0])
            for bt in range(8):
                ssm_z(bt, 0)
                att_mid(att_order[bt], on_dve=True)
                ssm_dve(bt, 0)
                att_fin(att_order[bt])
                att_scores(att_order[bt + 1])
            ck("ssm0")
            P.dma("gpsimd", I("dma_start", out=st_src, in_=Fst[:]), reads=[b_F], writes=[b_stg])
            P.op("gpsimd", I("collective_compute", "AllGather", ALU.bypass, replica_groups=GROUPS,
                                                         ins=[st_src.opt()], outs=[st_dst.opt()]), [b_stg], [b_stg])
            P.dma("gpsimd", I("dma_start", out=stg[:], in_=st_dst.rearrange("(r p) f -> p r f", p=128)),
                  reads=[b_stg], writes=[b_stg])

            ck("stx")
            if l == 0:
                tap("attT", attT.rearrange("p j t -> p (j t)"), [b_att], 2048)

            ck("att")
            V(I("memset", carry[:], 0.0), [], [b_carry])
            for d in range(2):
                cs_ = slice(d * 16, d * 16 + 16); ci_ = slice(32 + d * 16, 32 + d * 16 + 16)
                cr = hcr[:, 0:16]; cim_ = hcr[:, 16:32]; t0 = hcr[:, 32:48]; t1 = hcr[:, 48:64]
                V(I("memset", hcr[:], 0.0), [], [b_hcr])
                order = [0, 1, 2, 3] if d == 0 else [3, 2, 1, 0]
                hh = [b_hcr]
                for r in order:
                    V(I("scalar_tensor_tensor", out=carry[:, cs_], in0=cr, scalar=sel[:, r:r + 1], in1=carry[:, cs_],
                                                                     op0=ALU.mult, op1=ALU.add), [b_hcr, b_sel], [b_carry])
                    V(I("scalar_tensor_tensor", out=carry[:, ci_], in0=cim_, scalar=sel[:, r:r + 1], in1=carry[:, ci_],
                                                                     op0=ALU.mult, op1=ALU.add), [b_hcr, b_sel], [b_carry])
                    ar_ = Acr[:, cs_]; ai_ = Aci[:, cs_]
                    V(I("tensor_mul", t0, cr, ar_), [b_der], hh)
                    V(I("tensor_mul", t1, cim_, ai_), [b_der], hh)
                    V(I("tensor_sub", t0, t0, t1), [], hh)
                    V(I("tensor_mul", t1, cr, ai_), [b_der], hh)
                    V(I("tensor_mul", cim_, cim_, ar_), [b_der], hh)
                    V(I("tensor_add", cim_, cim_, t1), [], hh)
                    V(I("tensor_add", cr, t0, stg[:, r, cs_]), [b_stg], hh)
                    V(I("tensor_add", cim_, cim_, stg[:, r, ci_]), [b_stg], hh)

            for bt in range(8):
                ssm_z(bt, 1)
                att_mid(att_order[8 + bt])
                ssm_dve(bt, 1)
                att_fin(att_order[8 + bt])
                if bt < 7:
                    att_scores(att_order[9 + bt])
                for gq in range(2):
                    gp = bt * 2 + gq
                    pt, bp = nps("ssm")
                    fns = []
                    for g2_ in range(2):
                        g = gp * 2 + g2_
                        ps_ = slice(g2_ * 64, g2_ * 64 + 64)
                        oc = pt[:, g2_ * 256:g2_ * 256 + 256]
                        fns.append(mm(oc, Tm[:, g, :], U[:, g, :], True, False))
                        for d in range(2):
                            fns.append(mm(oc, Qr[ps_, d, gp, :], Hb16[(d, "r")][ps_, gq, :], False, False))
                            fns.append(mm(oc, Qn[ps_, d, gp, :], Hb16[(d, "i")][ps_, gq, :], False, d == 1))
                    MM(fns, [b_T, b_U, b_H] + b_Qh, [bp])
                    A(I("activation", Yall[:, gp * 2:gp * 2 + 2, :].rearrange("p g c -> p (g c)"), pt[:, :],
                                                           AF.Gelu_apprx_tanh), [bp, b_Yall], [b_Yb[bt]])
            if l == 0:
                tap("Yall", Yall.rearrange("p g c -> p (g c)"), b_Yb, 2048)
            ck("ssm1")
            gate(xr_mixer, allx)
            for k in range(8):
                P.dma("gpsimd", I("dma_start", out=xT[:, k, :], in_=xsp[k * 128:(k + 1) * 128, :]), writes=b_x[k])
            att_flush()
            b_yTok = Buf()
            gate([b_tab, b_W, b_S, b_H], [b_yTok])
            gate([b_q], [b_yT2])
            yTok = A1[:, 8192:16384].rearrange("p (k a j g h) -> p k a j g h", k=2, a=4, j=8, g=8)
            ev = [0]

            def evac(out_ap, in_ap, r, w):
                if ev[0] % 2 == 0:
                    A(I("activation", out_ap, in_ap, AF.Copy), r, w)
                else:
                    V(I("tensor_copy", out_ap, in_ap), r, w)
                ev[0] += 1
            for cblk in range(2):
                for g4 in range(8):
                    pt, bp = nps()
                    MM([mm(pt[:, k4 * 128:(k4 + 1) * 128], Yall[:, g4 * 4 + k4, cblk * 128:(cblk + 1) * 128], identb[:, :])
                        for k4 in range(4)], b_Yb + [b_cst], [bp])
                    evac(yTok[:, cblk, g4 // 2, :, (g4 % 2) * 4:(g4 % 2) * 4 + 4, :],
                         pt[:, :].rearrange("p (k j h) -> p j k h", k=4, j=8), [bp], [b_yTok])
            for a_ in range(4):
                for jp in range(4):
                    pt, bp = nps()
                    fns = []
                    for jj in range(2):
                        j = jp * 2 + jj
                        for cblk in range(2):
                            c0 = (jj * 2 + cblk) * 128
                            fns.append(mm(pt[:, c0:c0 + 128], yTok[:, cblk, a_, j, :, :].rearrange("p g h -> p (g h)"), identb[:, :]))
                    MM(fns, [b_yTok, b_cst], [bp])
                    evac(yT2[:, a_, jp * 2:jp * 2 + 2, :].rearrange("p j c -> p (j c)"), pt[:, :], [bp], [b_yT2])
            gate([b_yTok], [b_glu])
            yflat = yT2.rearrange("p a j c -> p a (j c)")
            for cb in range(4):
                cs_ = slice(cb * 512, (cb + 1) * 512)
                for oc in range(4):
                    pv, bpv = nps(); pg, bpg = nps()
                    MM([mm(pv[:, :], wglu[:, k, oc * 128:(oc + 1) * 128], yflat[:, k, cs_], k == 0, k == 3) for k in range(4)],
                       [b_WA, b_yT2], [bpv])
                    MM([mm(pg[:, :], wglu[:, k, 512 + oc * 128:512 + (oc + 1) * 128], yflat[:, k, cs_], k == 0, k == 3) for k in range(4)],
                       [b_WA, b_yT2], [bpg])
                    A(I("activation", sig[:], pg[:, :], AF.Sigmoid), [bpg], [b_sig])
                    dst = gluT[:, oc, :].rearrange("p (c j) -> p j c", j=8)[:, 2 * cb:2 * cb + 2, :]
                    V(I("tensor_tensor", dst, pv[:, :].rearrange("p (j c) -> p j c", j=2),
                                                                sig[:, :].rearrange("p (j c) -> p j c", j=2), ALU.mult),
                      [bpv, b_sig], [b_glu])
            if l == 0:
                tap("gluT", gluT.rearrange("p j t -> p (j t)"), [b_glu], 2048)
            ck("glu")
            for tt in range(NTT):
                ts = slice(tt * 512, (tt + 1) * 512)
                for m in range(8):
                    pt, bp = nps()
                    fns = [mm(pt[:, :], wout[:, k, m * 128:(m + 1) * 128], attT[:, k, ts], k == 0, False) for k in range(4)]
                    fns += [mm(pt[:, :], wout[:, 4 + k, m * 128:(m + 1) * 128], gluT[:, k, ts], False, k == 3) for k in range(4)]
                    MM(fns, [b_WA, b_WB, b_att, b_glu], [bp])
                    V(I("tensor_add", xT[:, m, ts], xT[:, m, ts], pt[:, :]), [bp], [b_x[m][tt]])
            if l == 0:
                tap("xmid", xT[:, 0, :], b_x[0], 2048)

            ck("wout")
            gate(a1_p1, b_h)
            gate([b_att, b_yT2, b_uT2, b_q], [])
            gate(misc_att, misc_ffn)
            gate([b_E], [b_WB])
            for tt in range(NTT):
                rms_tile(l, tt, g2)
            b_ws = [b_WA, b_WB]
            for gI in range(8):
                s = gI % 2
                w1, w2 = wsl[s]
                P.dma("gpsimd", I("dma_start",
                    out=w1, in_=w_ff1_d[l][:, gI * 512:(gI + 1) * 512].rearrange("(k p) f -> p k f", p=128)), writes=[b_ws[s]])
                P.dma("gpsimd", I("dma_start",
                    out=w2, in_=w_ff2_d[l][gI * 512:(gI + 1) * 512, :].rearrange("(k p) f -> p k f", p=128)), writes=[b_ws[s]])
                for tt in range(NTT):
                    ts = slice(tt * 512, (tt + 1) * 512)
                    for fc in range(4):
                        pt, bp = nps()
                        MM([mm(pt[:, :], w1[:, k, fc * 128:(fc + 1) * 128], hT[:, k, ts], k == 0, k == 7) for k in range(8)],
                           [b_ws[s], b_h[tt]], [bp])
                        ri_ = fc % 2
                        A(I("activation", rl[ri_], pt[:, :], AF.Relu), [bp], [b_rl[ri_]])
                        V(I("tensor_mul", aT[:, fc, :], rl[ri_], rl[ri_]), [b_rl[ri_]], [b_a[fc]])
                    for m in range(8):
                        pt, bp = nps()
                        MM([mm(pt[:, :], w2[:, k, m * 128:(m + 1) * 128], aT[:, k, :], k == 0, k == 3) for k in range(4)],
                           [b_ws[s]] + b_a, [bp])
                        V(I("tensor_add", xT[:, m, ts], xT[:, m, ts], pt[:, :]), [bp], [b_x[m][tt]])
            gate(misc_ffn, misc_att)

        try:
            for l in range(nlayers):
                layer(l)
        except _Stop:
            gate(xr_mixer, allx)
        emit_taps()
        for k in range(8):
            P.dma("sync", I("dma_start", out=out_d[k * 128:(k + 1) * 128, :], in_=xT[:, k, :]), reads=b_x[k])
        P.wait_all("sync", allx + b_tS)
        P.finish("sync")
        P.run()
    return nc


def _prep_inputs(x, norm1, w_in, q_gain, k_gain, sink, lam_re, lam_im, log_dt, b_re, b_im, c_re, c_im,
                 d_skip, w_glu, w_out, norm2, w_ff1, w_ff2, nlayers=DEPTH):
    f = lambda a: np.ascontiguousarray(np.asarray(a, dtype=np.float32))
    L = DEPTH
    perm = []
    for j in range(4):
        perm += list(range(j * 64, j * 64 + 64)) + list(range((4 + j) * 64, (4 + j) * 64 + 64))
    perm += list(range(512, 1280))
    shared = {
        "w_in": f(np.asarray(w_in)[:nlayers][:, :, perm]), "w_glu": f(np.asarray(w_glu)[:nlayers]),
        "w_out": f(np.asarray(w_out)[:nlayers]),
        "w_ff1": f(np.asarray(w_ff1)[:nlayers]), "w_ff2": f(np.asarray(w_ff2)[:nlayers]),
        "g1": f(np.asarray(norm1).reshape(L, 8, 128).transpose(2, 0, 1)),
        "g2": f(np.asarray(norm2).reshape(L, 8, 128).transpose(2, 0, 1)),
        "qg": f(np.tile(np.asarray(q_gain).T, (2, 1))),
        "kg": f(np.tile(np.asarray(k_gain).T, (2, 1))),
        "sinkr": f(np.broadcast_to(np.asarray(sink)[None], (128, L, 8))),
    }

    def gn(a):
        a = np.asarray(a).reshape(L, 2, 16, 2, 64)
        return f(a.transpose(3, 4, 0, 1, 2).reshape(128, L, 32))
    shared["lre"] = gn(lam_re)
    shared["lim"] = gn(lam_im)
    shared["ldt"] = gn(np.broadcast_to(np.asarray(log_dt)[:, :, :, None], (L, 2, 32, 64)))

    def bb(a):
        a = np.asarray(a).reshape(L, 16, 2, 64, 16)
        return f(a.transpose(2, 3, 0, 1, 4).reshape(128, L, 256))
    shared["bre"] = bb(b_re)
    shared["bim"] = bb(b_im)

    def cc(a):
        a = np.asarray(a).reshape(L, 2, 16, 2, 16, 64)
        return f(a.transpose(3, 5, 0, 1, 2, 4).reshape(128, L, 512))
    shared["cre"] = cc(c_re)
    shared["cim"] = cc(c_im)
    dsk = np.asarray(d_skip).reshape(L, 32, 16)
    shared["dcol"] = f(np.broadcast_to(dsk.transpose(2, 0, 1)[None], (8, 16, L, 32)).reshape(128, L, 32))
    slopes = np.exp2(-8.0 * np.arange(1, 9) / 8.0)
    ci = np.arange(128)[:, None]; qi = np.arange(128)[None, :]
    et = np.zeros((128, 3, 8, 128), np.float32)
    for kb in range(3):
        dist = np.abs(qi - ci - (kb - 1) * 128)
        valid = dist <= 128
        for h in range(8):
            et[:, kb, h, :] = np.where(valid, np.exp(-slopes[h] * dist), 0.0)
    shared["etab"] = f(et.reshape(128, 3072))
    cst = np.zeros((128, 656), np.float32)
    cst[:, 0:128] = np.eye(128)
    bi = np.arange(128)[:, None] // 16; bj = np.arange(128)[None, :] // 16
    cst[:, 128:256] = (bj >= bi)
    cst[:, 256:384] = (bi >= bj)
    cst[:, 384:393] = np.arange(9)[None, :]
    cst[:, 400:656] = np.arange(1, 257)[None, :]
    shared["cst"] = cst
    xs = np.asarray(x)
    in_maps = []
    for r in range(8):
        b, q = r // 4, r % 4
        m = dict(shared)
        m["xT"] = f(xs[b, q * T:(q + 1) * T, :].T)
        s = np.zeros((128, 16), np.float32)
        s[:, q] = 1.0
        if q > 0:
            s[:, 4 + q - 1] = 1.0; s[:, 12] = 1.0
        if q < 3:
            s[:, 8 + q + 1] = 1.0; s[:, 13] = 1.0
        m["sel"] = s
        in_maps.append(m)
    return in_maps


_NC_CACHE = {}


def kernel(**inputs):
    in_maps = _prep_inputs(**inputs)
    if "nc" not in _NC_CACHE:
        _NC_CACHE["nc"] = build_nc()
    nc = _NC_CACHE["nc"]
    res = run_bass_kernel_spmd(nc, in_maps, core_ids=list(range(8)))
    out = np.zeros((2, 4 * T, 1024), np.float32)
    for r in range(8):
        b, q = r // 4, r % 4
        out[b, q * T:(q + 1) * T, :] = np.asarray(res.results[r]["outT"]).T
    return out
```

```python
import math
import os
import numpy as np
from contextlib import ExitStack
import concourse.bass as bass
import concourse.mybir as mybir
from concourse.bass_utils import run_bass_kernel_spmd

F32 = mybir.dt.float32
BF16 = mybir.dt.bfloat16
I32 = mybir.dt.int32
AF = mybir.ActivationFunctionType
ALU = mybir.AluOpType

DEPTH = 4
T = 2048
NTT = 4
EPS = 1e-6
TWO_PI = 2.0 * math.pi


class Buf:
    def __init__(self, name=""):
        self.name = name
        self.w = None
        self.r = []


class EngQ:
    def __init__(self, name, sem):
        self.name = name
        self.sem = sem
        self.count = 0
        self.ops = []
        self.seen = {}


class Prog:
    ENGS = ["sync", "scalar", "gpsimd", "vector", "tensor"]

    def __init__(self, nc, es, ndma=16):
        self.nc = nc
        self.q = {e: EngQ(e, es.enter_context(nc.semaphore("s_" + e))) for e in self.ENGS}
        self.dq = {e: [EngQ(f"d_{e}{k}", es.enter_context(nc.semaphore(f"d_{e}{k}"))) for k in range(ndma)]
                   for e in ["sync", "gpsimd"]}
        self.rr = {e: 0 for e in self.dq}
        self.nops = 0

    def _waits(self, eng, reads, writes):
        q = self.q[eng]
        need = {}

        def add(tok):
            if tok is None:
                return
            s, v = tok
            if eng == "tensor" and s is q:
                return
            if need.get(s, 0) < v:
                need[s] = v
        for b in reads:
            add(b.w)
        for b in writes:
            add(b.w)
            for t in b.r:
                add(t)
        for s, v in need.items():
            if q.seen.get(s, 0) >= v:
                continue
            q.seen[s] = v
            q.ops.append(lambda e, s=s, v=v: e.wait_ge(s.sem, v))

    def _mark(self, tok, reads, writes):
        for b in reads:
            b.r = [t for t in b.r if t[0] is not tok[0]] + [tok]
        for b in writes:
            b.w = tok
            b.r = []

    def op(self, eng, fn, reads=(), writes=()):
        return self.group(eng, [fn], reads, writes)

    def group(self, eng, fns, reads=(), writes=()):
        q = self.q[eng]
        self._waits(eng, reads, writes)
        for fn in fns[:-1]:
            q.ops.append(lambda e, fn=fn: fn(e))
        q.count += 1
        tok = (q, q.count)
        q.ops.append(lambda e, fn=fns[-1], q=q: fn(e).then_inc(q.sem, 1))
        self._mark(tok, reads, writes)
        self.nops += len(fns)
        return tok

    def dma(self, eng, fn, reads=(), writes=()):
        self._waits(eng, reads, writes)
        k = self.rr[eng]
        self.rr[eng] = (k + 1) % len(self.dq[eng])
        d = self.dq[eng][k]
        q_ = self.q[eng]
        if d.count > 0 and q_.seen.get(d, 0) < d.count:
            q_.seen[d] = d.count
            q_.ops.append(lambda e, d=d, v=d.count: e.wait_ge(d.sem, v))
        d.count += 16
        tok = (d, d.count)
        self.q[eng].ops.append(lambda e, fn=fn, d=d: fn(e).then_inc(d.sem, 16))
        self._mark(tok, reads, writes)
        self.nops += 1
        return tok

    def wait_all(self, eng, bufs):
        self._waits(eng, [], bufs)

    def finish(self, eng="sync"):
        q = self.q[eng]
        allq = [x for x in self.q.values() if x is not q] + [d for ds in self.dq.values() for d in ds]
        for s_ in allq:
            if s_.count > 0:
                q.ops.append(lambda e, s_=s_, v=s_.count: e.wait_ge(s_.sem, v))

    def run(self):
        with self.nc.Block() as block:
            for e in self.ENGS:
                ops = self.q[e].ops

                def body(eng, ops=ops):
                    for o in ops:
                        o(eng)
                getattr(block, e)(body)


class _Stop(Exception):
    pass


def build_nc(nlayers=DEPTH, taps=None, stop_after=None):
    nc = bass.Bass("TRN2", target_bir_lowering=False)
    L = DEPTH

    def din(name, shape, dt=F32):
        return nc.dram_tensor(name, list(shape), dt, kind="ExternalInput").ap()

    xT_d = din("xT", [1024, T])
    LW = nlayers
    w_in_d = din("w_in", [LW, 1024, 1280])
    w_glu_d = din("w_glu", [LW, 512, 1024])
    w_out_d = din("w_out", [LW, 1024, 1024])
    w_ff1_d = din("w_ff1", [LW, 1024, 4096])
    w_ff2_d = din("w_ff2", [LW, 4096, 1024])
    g1_d = din("g1", [128, L, 8])
    g2_d = din("g2", [128, L, 8])
    qg_d = din("qg", [128, L])
    kg_d = din("kg", [128, L])
    sink_d = din("sinkr", [128, L, 8])
    lre_d = din("lre", [128, L, 32])
    lim_d = din("lim", [128, L, 32])
    ldt_d = din("ldt", [128, L, 32])
    bre_d = din("bre", [128, L, 256])
    bim_d = din("bim", [128, L, 256])
    cre_d = din("cre", [128, L, 512])
    cim_d = din("cim", [128, L, 512])
    dcol_d = din("dcol", [128, L, 32])
    etab_d = din("etab", [128, 3072])
    cst_d = din("cst", [128, 656])
    sel_d = din("sel", [128, 16])
    out_d = nc.dram_tensor("outT", [1024, T], F32, kind="ExternalOutput").ap()
    taps = list(taps) if taps else []
    dbg_d = nc.dram_tensor("dbg", [max(1, len(taps)), 128, 2048], F32, kind="ExternalOutput").ap() if taps else None

    xsp = nc.dram_tensor("xsp", [1024, T], F32).ap()
    hal_src = nc.dram_tensor("hal_src", [128, 512], F32).ap()
    hal_dst = nc.dram_tensor("hal_dst", [4 * 128, 512], F32).ap()
    st_src = nc.dram_tensor("st_src", [128, 64], F32).ap()
    st_dst = nc.dram_tensor("st_dst", [4 * 128, 64], F32).ap()
    GROUPS = [[0, 1, 2, 3], [4, 5, 6, 7]]

    with ExitStack() as es:
        P = Prog(nc, es)

        def sb(name, shape, dt=F32):
            return es.enter_context(nc.sbuf_tensor(name, list(shape), dt))

        XR = sb("XR", [128, 16384], F32)
        XRb = XR[:, :].bitcast(BF16)
        A1 = sb("A1", [128, 16384], BF16)
        A1f = A1[:, :].bitcast(F32)
        WA = sb("WA", [128, 16384], BF16)
        WAf = WA[:, :].bitcast(F32)
        A3 = sb("A3", [128, 8192], BF16)
        A4 = sb("A4", [128, 8192], BF16)
        MISC = sb("MISC", [128, 4096], BF16)

        xT = XR[:, :].rearrange("p (k t) -> p k t", k=8)
        b_x = [[Buf() for _ in range(NTT)] for _ in range(8)]
        allx = [b for row in b_x for b in row]
        U = XRb[:, 0:8192].rearrange("p (g c) -> p g c", g=32); b_U = Buf()
        PT = XRb[:, 8192:16384].rearrange("p (d g r n) -> p d g r n", d=2, g=32, r=2); b_PT = Buf()
        Qr = XRb[:, 16384:20480].rearrange("p (d g f) -> p d g f", d=2, g=16)
        Qn = XRb[:, 20480:24576].rearrange("p (d g f) -> p d g f", d=2, g=16); b_Qh = [Buf(), Buf()]
        Tm = XRb[:, 24576:28672].rearrange("p (g f) -> p g f", g=32); b_T = Buf()
        etab = XRb[:, 28672:31744].rearrange("p (k h q) -> p k h q", k=3, h=8); b_et = Buf()
        xr_mixer = [b_U, b_PT, b_T, b_et] + b_Qh

        hT = A1[:, :].rearrange("p (k t) -> p k t", k=8); b_h = [Buf() for _ in range(NTT)]
        bre = A1f[:, 0:256].rearrange("p (g h) -> p g h", g=16)
        bim = A1f[:, 256:512].rearrange("p (g h) -> p g h", g=16)
        cre = A1f[:, 512:1024].rearrange("p (d g h) -> p d g h", d=2, g=16)
        cim = A1f[:, 1024:1536].rearrange("p (d g h) -> p d g h", d=2, g=16)
        b_par2 = Buf()
        halg = A1f[:, 1536:3584].rearrange("p (r f) -> p r f", r=4); b_halg = Buf()
        tA = A1f[:, 4096:6144]; tB = A1f[:, 6144:8192]; b_tA = Buf(); b_tB = Buf(); b_tA2 = Buf(); b_tB2 = Buf()
        Yall = A1[:, 0:8192].rearrange("p (g c) -> p g c", g=32); b_Yall = Buf()
        yT2 = A3[:, :].rearrange("p (a j c) -> p a j c", a=4, j=8); b_yT2 = Buf()
        cosT = A1f[:, 4096:4608].rearrange("p (g c) -> p g c", g=2)
        sinT = A1f[:, 4608:5120].rearrange("p (g c) -> p g c", g=2); b_tab = Buf()
        Wr = A1f[:, 5120:5632].rearrange("p (g c) -> p g c", g=2)
        Wi = A1f[:, 5632:6144].rearrange("p (g c) -> p g c", g=2); b_W = Buf()
        Sr = A1f[:, 6144:6656].rearrange("p (g c) -> p g c", g=2)
        Si = A1f[:, 6656:7168].rearrange("p (g c) -> p g c", g=2); b_S = Buf()
        Hb16 = {}
        for d in range(2):
            for ni, n in enumerate("ri"):
                o = 14336 + (d * 2 + ni) * 512
                Hb16[(d, n)] = A1[:, o:o + 512].rearrange("p (g c) -> p g c", g=2)
        b_H = Buf()
        a1_s2 = [b_par2, b_halg, b_tA, b_tB, b_tA2, b_tB2]
        b_Yb = [Buf() for _ in range(8)]

        win = WA[:, 0:10240].rearrange("p (k f) -> p k f", k=8); b_WA = Buf(); b_WB = Buf()
        X_r = WA[:, 0:4096].rearrange("p (d g f) -> p d g f", d=2, g=16)
        X_i = WA[:, 4096:8192].rearrange("p (d g f) -> p d g f", d=2, g=16); b_Xh = [Buf(), Buf()]
        pw = {n: WAf[:, 4096 + i * 288:4096 + (i + 1) * 288].rearrange("p (k c) -> p k c", k=9)
              for i, n in enumerate(["pr", "pi", "mr", "mi"])}
        b_pw = Buf()
        cs_s = WAf[:, 5248:5536]; cs_c = WAf[:, 5536:5824]
        mg_p = WAf[:, 5824:6112]; mg_m = WAf[:, 6112:6400]
        Bb = {"r": WAf[:, 6400:6912].rearrange("p (d g h) -> p d g h", d=2, g=16),
              "i": WAf[:, 6912:7424].rearrange("p (d g h) -> p d g h", d=2, g=16)}
        b_Bb = Buf()
        zt = [WAf[:, 7424 + i * 32:7424 + (i + 1) * 32] for i in range(6)]; b_zt = Buf()
        wa_s2 = [b_pw, b_Bb, b_zt] + b_Xh
        E0c = WAf[:, 6144:6656].rearrange("p (g c) -> p g c", g=32)
        E0s = WAf[:, 6656:7168].rearrange("p (g c) -> p g c", g=32)
        E1c = WAf[:, 7168:7680].rearrange("p (g c) -> p g c", g=32)
        E1s = WAf[:, 7680:8192].rearrange("p (g c) -> p g c", g=32)
        b_E = Buf()
        wglu = WA[:, 0:4096].rearrange("p (k f) -> p k f", k=4)
        wout = WA[:, 4096:12288].rearrange("p (k f) -> p k f", k=8)
        wsl = [(WA[:, s * 8192:s * 8192 + 4096].rearrange("p (k f) -> p k f", k=8),
                WA[:, s * 8192 + 4096:s * 8192 + 8192].rearrange("p (k f) -> p k f", k=4)) for s in range(2)]

        qT = A3[:, :].rearrange("p (j t) -> p j t", j=4); b_q = Buf()
        gluT = A1[:, 8192:16384].rearrange("p (j t) -> p j t", j=4); b_glu = Buf()
        a1_p1 = b_Yb + [b_Yall, b_glu, b_tab, b_W, b_S, b_H]
        uT2 = A4[:, :].rearrange("p (a i c) -> p a i c", a=4, i=8); b_uT2 = Buf()
        attT = A4[:, :].rearrange("p (j t) -> p j t", j=4); b_att = Buf()

        pM2 = [[MISC[:, (s_ * 3 + i) * 512:(s_ * 3 + i + 1) * 512] for i in range(3)] for s_ in range(2)]
        b_pM2 = [[Buf() for _ in range(3)] for _ in range(2)]
        rden = MISC[:, 3072:3584].bitcast(F32).rearrange("p (a q) -> p a q", a=2); b_rden = Buf()
        rl = [MISC[:, i * 512:(i + 1) * 512] for i in range(2)]; b_rl = [Buf(), Buf()]
        aT = MISC[:, 1024:3072].rearrange("p (k t) -> p k t", k=4); b_a = [Buf() for _ in range(4)]
        misc_att = b_pM2[0] + b_pM2[1] + [b_rden]
        misc_ffn = b_rl + b_a

        kT = sb("kT", [128, T + 256], BF16); b_k = Buf()
        vtm = sb("vtm", [128, 18, 128], BF16); b_v = Buf()
        cst = sb("cst_sb", [128, 656]); b_cst = Buf()
        cstb = sb("cstb_sb", [128, 128], BF16)
        onesb = sb("onesb", [128, 128], BF16)
        blk2 = sb("blk2", [128, 128], BF16)
        ones_pn = sb("ones_pn", [128, 2, 64], BF16)
        sel = sb("sel_sb", [128, 16]); b_sel = Buf()
        g1 = sb("g1_sb", [128, L, 8]); g2 = sb("g2_sb", [128, L, 8])
        qg = sb("qg_sb", [128, L]); kg = sb("kg_sb", [128, L])
        sinkr = sb("sink_sb", [128, L * 8]); esink = sb("esink", [128, L, 8])
        b_small = Buf()
        epsb = sb("epsb", [128, 1])
        lre = sb("lre_sb", [128, 32]); lim = sb("lim_sb", [128, 32]); ldt = sb("ldt_sb", [128, 32])
        dcol = sb("dcol_sb", [128, 32]); b_par = Buf()
        lrdt = sb("lrdt", [128, 32]); th = sb("th", [128, 32]); Thu = sb("Thu", [128, 32])
        R8 = sb("R8", [128, 32]); Acr = sb("Acr", [128, 32]); Aci = sb("Aci", [128, 32])
        finr = sb("finr", [128, 32]); fini = sb("fini", [128, 32]); mag2048 = sb("mag2048", [128, 32])
        b_der = Buf()
        tS = [sb(f"tS{i}", [128, 512]) for i in range(3)]; b_tS = [Buf() for _ in range(3)]
        tSI = sb("tSI", [128, 512], I32); b_tSI = Buf()
        Fst = sb("Fst", [128, 64]); b_F = Buf()
        stg = sb("stg", [128, 4, 64]); b_stg = Buf()
        carry = sb("carry", [128, 64]); b_carry = Buf()
        hcr = sb("hcr", [128, 64]); b_hcr = Buf()
        halt = sb("halt", [128, 512]); b_halt = Buf()
        sq = sb("sq", [128, 2, 512], BF16); b_sq = [Buf(), Buf()]
        rstd = sb("rstd", [128, 512]); b_rstd = Buf()
        sig = sb("sig", [128, 512]); b_sig = Buf()
        gatec = sb("gatec", [128, 8]); b_gate = Buf()

        psb = [es.enter_context(nc.psum_tensor(f"ps{i}", [128, 512], F32)) for i in range(8)]
        b_ps = [Buf() for _ in range(8)]
        ps_pools = {"all": list(range(8)), "ssm": [0, 1, 2], "att_s": [5, 6, 7], "att_o": [3, 4]}
        ps_rr = {k: 0 for k in ps_pools}

        def nps(pool="all"):
            pool = "all"
            lst = ps_pools[pool]
            i = lst[ps_rr[pool] % len(lst)]
            ps_rr[pool] += 1
            return psb[i], b_ps[i]

        def V(fn, r=(), w=()):
            return P.op("vector", fn, r, w)

        def A(fn, r=(), w=()):
            return P.op("scalar", fn, r, w)

        def MM(fns, r=(), w=()):
            return P.group("tensor", fns, r, w)

        def I(method, *a, **k):
            return lambda e: getattr(e, method)(*a, **k)

        def mm(out, lhsT, rhs, start=True, stop=True):
            return I("matmul", out, lhsT=lhsT, rhs=rhs, start=start, stop=stop)

        def gate(old, new):
            V(I("memset", gatec[:, 0:1], 0.0), [], list(old) + list(new) + [b_gate])

        tap_i = [0]

        deferred_taps = []

        def tap(name, ap, bufs, n):
            if name in taps:
                deferred_taps.append((name, ap, bufs, n))

        def emit_taps():
            if not deferred_taps:
                return
            P.finish("vector")
            P.finish("sync")
            for (name, ap, bufs, n) in deferred_taps:
                idx = taps.index(name)
                for c0 in range(0, n, 512):
                    c1 = min(n, c0 + 512)
                    V(I("tensor_copy", tS[2][:, 0:c1 - c0], ap[:, c0:c1]), list(bufs), [b_tS[2]])
                    P.dma("sync", I("dma_start", out=dbg_d[idx, :, c0:c1], in_=tS[2][:, 0:c1 - c0]), reads=[b_tS[2]])

        P.dma("sync", I("dma_start", out=cst[:], in_=cst_d), writes=[b_cst])
        P.dma("sync", I("dma_start", out=sel[:], in_=sel_d), writes=[b_sel])
        for (t_sb, t_d) in [(g1, g1_d), (g2, g2_d), (qg, qg_d), (kg, kg_d)]:
            P.dma("sync", I("dma_start", out=t_sb[:], in_=t_d), writes=[b_small])
        P.dma("sync", I("dma_start", out=sinkr[:], in_=sink_d.rearrange("p l h -> p (l h)")), writes=[b_small])
        for k in range(8):
            P.dma("sync", I("dma_start", out=xT[:, k, :], in_=xT_d[k * 128:(k + 1) * 128, :]),
                  writes=b_x[k])
        ident = cst[:, 0:128]; Mf = cst[:, 128:256]; Mb = cst[:, 256:384]; kvec = cst[:, 384:393]
        cpos = cst[:, 400:656]
        V(I("tensor_copy", cstb[:], ident), [b_cst], [b_cst])
        V(I("memset", onesb[:], 1.0), [], [b_cst])
        V(I("memset", blk2[:], 0.0), [], [b_cst])
        V(I("memset", blk2[0:64, 0:64], 1.0), [], [b_cst])
        V(I("memset", blk2[64:128, 64:128], 1.0), [], [b_cst])
        V(I("memset", epsb[:], EPS), [], [b_cst])
        V(I("tensor_copy", ones_pn[:, 0, :], sel[:, 12:13].to_broadcast([128, 64])), [b_sel], [b_cst])
        V(I("tensor_copy", ones_pn[:, 1, :], sel[:, 13:14].to_broadcast([128, 64])), [b_sel], [b_cst])
        A(I("activation", esink[:, :, :].rearrange("p l h -> p (l h)"), sinkr[:], AF.Exp), [b_small], [b_small])
        identb = cstb
        c16 = sb("c16", [128, 16])
        V(I("tensor_scalar", c16[:], cpos[:, 0:16], -1.0, 16.0, ALU.add, ALU.mult), [b_cst], [b_cst])
        crev = sb("crev", [128, 256])
        V(I("tensor_scalar", crev[:], cpos, -1.0, 256.0, ALU.mult, ALU.add), [b_cst], [b_cst])
        L8 = sb("L8", [128, 32]); facc = sb("facc", [128, 4]); b_facc = Buf()
        tabsets = [(cosT, sinT, b_tab), (rstd[:, :].rearrange("p (g c) -> p g c", g=2), sig[:, :].rearrange("p (g c) -> p g c", g=2), None)]
        ssm_it = [0]
        esrow = sb("esrow", [1, 8, 128], BF16); b_esr = Buf()

        def rms_tile(l, tt, gain):
            ts = slice(tt * 512, (tt + 1) * 512)
            pt, bp = nps()
            for k in range(8):
                s = k % 2
                A(I("activation", sq[:, s, :], xT[:, k, ts], AF.Square), [b_x[k][tt]], [b_sq[s]])
                P.op("tensor", mm(pt[:, :], onesb[:, :], sq[:, s, :], start=(k == 0), stop=(k == 7)), [b_sq[s], b_cst], [bp])
            A(I("activation", rstd[:], pt[:, :], AF.Ln, bias=epsb[:], scale=1.0 / 1024.0), [bp, b_cst], [b_rstd])
            A(I("activation", rstd[:], rstd[:], AF.Exp, scale=-0.5), [b_rstd], [b_rstd])
            for k in range(8):
                V(I("scalar_tensor_tensor", out=hT[:, k, ts], in0=xT[:, k, ts], scalar=gain[:, l, k:k + 1],
                                                        in1=rstd[:], op0=ALU.mult, op1=ALU.mult),
                  [b_x[k][tt], b_rstd, b_small], [b_h[tt]])

        def sincos_turns(tin, n, out_s, out_c, rb, wb):
            a_ = tS[0][:, 0:n]; b_ = tS[1][:, 0:n]; i_ = tSI[:, 0:n]
            for (off, outp) in [(0.0, out_s), (0.25, out_c)]:
                V(I("tensor_scalar_add", a_, tin, off), rb, [b_tS[0]])
                V(I("tensor_copy", i_, a_), [b_tS[0]], [b_tSI])
                V(I("tensor_copy", b_, i_), [b_tSI], [b_tS[1]])
                V(I("tensor_sub", a_, a_, b_), [b_tS[0], b_tS[1]], [b_tS[0]])
                V(I("tensor_scalar", a_, a_, -0.4999999, 0.4999999, ALU.max, ALU.min), [b_tS[0]], [b_tS[0]])
                A(I("activation", outp, a_, AF.Sin, scale=TWO_PI), [b_tS[0]], wb)

        def ck(name):
            if stop_after == name:
                raise _Stop()

        if os.environ.get("HALO_FIRST"):
            V(I("memset", halt[:], 1.0), [], [b_halt])
            P.dma("gpsimd", I("dma_start", out=hal_src, in_=halt[:]), reads=[b_halt], writes=[b_halg])
            P.op("gpsimd", I("collective_compute", "AllGather", ALU.bypass, replica_groups=GROUPS,
                             ins=[hal_src.opt()], outs=[hal_dst.opt()]), [b_halg], [b_halg])
            P.dma("gpsimd", I("dma_start", out=halg, in_=hal_dst.rearrange("(r p) f -> p r f", p=128)),
                  reads=[b_halg], writes=[b_halg])
            if os.environ.get("HALO_FIRST") == "only":
                raise_stop = True

        def layer(l):
            P.dma("gpsimd", I("dma_start", out=win, in_=w_in_d[l].rearrange("(k p) f -> p k f", p=128)),
                  writes=[b_WA, b_WB])
            for (t_sb, t_d) in [(lre, lre_d), (lim, lim_d), (ldt, ldt_d), (dcol, dcol_d)]:
                P.dma("sync", I("dma_start", out=t_sb[:], in_=t_d[:, l]), writes=[b_par])

            rms_tile(l, 0, g1)
            for tt in range(NTT):
                ts = slice(tt * 512, (tt + 1) * 512)
                if tt + 1 < NTT:
                    rms_tile(l, tt + 1, g1)
                for k in range(8 if not os.environ.get("NOSPILL") else 0):
                    P.dma("sync", I("dma_start", out=xsp[k * 128:(k + 1) * 128, ts], in_=xT[:, k, ts]),
                          reads=[b_x[k][tt]])
                qsq = [MISC[:, 0:512], MISC[:, 512:1024]]; b_qsq = [b_pM2[0][0], b_pM2[0][1]]
                qrs = [sig, tS[2]]; b_qrs = [b_sig, b_tS[2]]

                def qk_fin(oc, pt, bp, tt=tt, ts=ts):
                    sl = oc % 2
                    p2, bp2 = nps()
                    MM([mm(p2[:, :], blk2[:, :], qsq[sl])], [b_qsq[sl], b_cst], [bp2])
                    rs = qrs[sl]; brs = b_qrs[sl]
                    A(I("activation", rs[:], p2[:, :], AF.Ln, bias=epsb[:], scale=1.0 / 64.0), [bp2, b_cst], [brs])
                    A(I("activation", rs[:], rs[:], AF.Exp, scale=-0.5), [brs], [brs])
                    if oc < 4:
                        V(I("scalar_tensor_tensor", out=qT[:, oc, ts], in0=pt[:, :], scalar=qg[:, l:l + 1],
                            in1=rs[:], op0=ALU.mult, op1=ALU.mult), [bp, brs, b_small], [b_q])
                    else:
                        V(I("scalar_tensor_tensor", out=kT[:, 128 + tt * 512:128 + (tt + 1) * 512], in0=pt[:, :],
                            scalar=kg[:, l:l + 1], in1=rs[:], op0=ALU.mult, op1=ALU.mult), [bp, brs, b_small], [b_k])
                        if tt == 0:
                            V(I("scalar_tensor_tensor", out=halt[:, 0:128], in0=pt[:, 0:128], scalar=kg[:, l:l + 1],
                                in1=rs[:, 0:128], op0=ALU.mult, op1=ALU.mult), [bp, brs, b_small], [b_halt])
                        if tt == NTT - 1:
                            V(I("scalar_tensor_tensor", out=halt[:, 128:256], in0=pt[:, 384:512], scalar=kg[:, l:l + 1],
                                in1=rs[:, 384:512], op0=ALU.mult, op1=ALU.mult), [bp, brs, b_small], [b_halt])
                pend = None
                for oc in range(5):
                    pt, bp = nps()
                    MM([mm(pt[:, :], win[:, k, oc * 128:(oc + 1) * 128], hT[:, k, ts], k == 0, k == 7) for k in range(8)],
                       [b_WA, b_h[tt]], [bp])
                    A(I("activation", qsq[oc % 2], pt[:, :], AF.Square), [bp], [b_qsq[oc % 2]])
                    if pend is not None:
                        qk_fin(*pend)
                    pend = (oc, pt, bp)
                pt, bp = nps()
                fns = []
                for b4 in range(4):
                    for k in range(8):
                        fns.append(mm(pt[:, b4 * 128:(b4 + 1) * 128], hT[:, k, tt * 512 + b4 * 128: tt * 512 + (b4 + 1) * 128],
                                      win[:, k, 640:768], k == 0, k == 7))
                MM(fns, [b_WA, b_h[tt]], [bp])
                qk_fin(*pend)
                A(I("activation", vtm[:, 1 + tt * 4:1 + (tt + 1) * 4, :].rearrange("p b f -> p (b f)"),
                                                       pt[:, :], AF.Copy), [bp], [b_v])
                if tt == 0:
                    V(I("tensor_copy", halt[:, 256:384], pt[:, 0:128]), [bp], [b_halt])
                if tt == NTT - 1:
                    V(I("tensor_copy", halt[:, 384:512], pt[:, 384:512]), [bp], [b_halt])
                for a_ in range(4):
                    oc = 6 + a_
                    pt, bp = nps()
                    MM([mm(pt[:, :], win[:, k, oc * 128:(oc + 1) * 128], hT[:, k, ts], k == 0, k == 7) for k in range(8)],
                       [b_WA, b_h[tt]], [bp])
                    V(I("tensor_copy", uT2[:, a_, :, tt * 64:(tt + 1) * 64], pt[:, :].rearrange("p (c i) -> p i c", i=8)),
                      [bp], [b_uT2])
            if l == 0:
                tap("hT", hT.rearrange("p k t -> p (k t)"), b_h, 2048)
                tap("qT", qT.rearrange("p j t -> p (j t)"), [b_q], 2048)
                tap("kT", kT[:, 128:128 + 2048], [b_k], 2048)
                tap("uT2", uT2.rearrange("p a i c -> p (a i c)"), [b_uT2], 2048)

            ck("S1")
            gate(b_h, a1_s2)
            P.dma("gpsimd", I("dma_start", out=hal_src, in_=halt[:]), reads=[b_halt], writes=[b_halg])
            if not os.environ.get("NOCC"):
                P.op("gpsimd", I("collective_compute", "AllGather", ALU.bypass, replica_groups=GROUPS,
                                 ins=[hal_src.opt()], outs=[hal_dst.opt()]), [b_halg], [b_halg])
            P.dma("gpsimd", I("dma_start", out=halg, in_=hal_dst.rearrange("(r p) f -> p r f", p=128)),
                  reads=[b_halg], writes=[b_halg])

            ck("halo")
            gate([b_WA, b_WB, b_E], wa_s2)
            gate(allx, xr_mixer)
            P.dma("gpsimd", I("dma_start", out=etab.rearrange("p k h q -> p (k h q)"), in_=etab_d), writes=[b_et])
            for (t_ap, t_d) in [(bre, bre_d), (bim, bim_d), (cre, cre_d), (cim, cim_d)]:
                P.dma("sync", I("dma_start",
                    out=t_ap.rearrange("p g h -> p (g h)") if len(t_ap.shape) == 3 else t_ap.rearrange("p d g h -> p (d g h)"),
                    in_=t_d[:, l]), writes=[b_par2])
            A(I("activation", lrdt[:], ldt[:], AF.Exp), [b_par], [b_der])
            V(I("tensor_mul", th[:], lim[:], lrdt[:]), [b_par, b_der], [b_der])
            V(I("tensor_mul", lrdt[:], lre[:], lrdt[:]), [b_par, b_der], [b_der])
            V(I("tensor_scalar_mul", th[:], th[:], 1.0 / TWO_PI), [b_der], [b_der])
            kb9 = kvec.unsqueeze(2).to_broadcast([128, 9, 32])
            ph9 = tS[2][:, 0:288]
            V(I("tensor_tensor", ph9.rearrange("p (k c) -> p k c", k=9), th[:].unsqueeze(1).to_broadcast([128, 9, 32]),
                                        kb9, ALU.mult), [b_der, b_cst], [b_tS[2]])
            sincos_turns(ph9, 288, cs_s, cs_c, [b_tS[2]], [b_pw])
            V(I("tensor_tensor", ph9.rearrange("p (k c) -> p k c", k=9), lrdt[:].unsqueeze(1).to_broadcast([128, 9, 32]),
                                        kb9, ALU.mult), [b_der, b_cst, b_pw], [b_tS[2]])
            A(I("activation", mg_p, ph9, AF.Exp), [b_tS[2]], [b_pw])
            A(I("activation", mg_m, ph9, AF.Exp, scale=-1.0), [b_tS[2]], [b_pw])
            fl = lambda t: t.rearrange("p k c -> p (k c)")
            V(I("tensor_mul", fl(pw["pr"]), mg_p, cs_c), [b_pw], [b_pw])
            V(I("tensor_mul", fl(pw["pi"]), mg_p, cs_s), [b_pw], [b_pw])
            V(I("tensor_mul", fl(pw["mr"]), mg_m, cs_c), [b_pw], [b_pw])
            V(I("scalar_tensor_tensor", out=fl(pw["mi"]), in0=mg_m, scalar=-1.0, in1=cs_s,
                                               op0=ALU.mult, op1=ALU.mult), [b_pw], [b_pw])
            ck("S2a")
            ar1, ai1 = pw["pr"][:, 1, :], pw["pi"][:, 1, :]
            zin = [b_zt, b_par, b_pw]
            V(I("tensor_mul", zt[0], lre[:], lre[:]), zin, [b_zt])
            V(I("tensor_mul", zt[1], lim[:], lim[:]), zin, [b_zt])
            V(I("tensor_add", zt[0], zt[0], zt[1]), zin, [b_zt])
            V(I("reciprocal", zt[0], zt[0]), zin, [b_zt])
            V(I("tensor_scalar_add", zt[1], ar1, -1.0), zin, [b_zt])
            V(I("tensor_mul", zt[2], zt[1], lre[:]), zin, [b_zt])
            V(I("tensor_mul", zt[3], ai1, lim[:]), zin, [b_zt])
            V(I("tensor_add", zt[2], zt[2], zt[3]), zin, [b_zt])
            V(I("tensor_mul", zt[2], zt[2], zt[0]), zin, [b_zt])
            V(I("tensor_mul", zt[3], ai1, lre[:]), zin, [b_zt])
            V(I("tensor_mul", zt[4], zt[1], lim[:]), zin, [b_zt])
            V(I("tensor_sub", zt[3], zt[3], zt[4]), zin, [b_zt])
            V(I("tensor_mul", zt[3], zt[3], zt[0]), zin, [b_zt])
            for d in range(2):
                zr = zt[2][:, d * 16:(d + 1) * 16].unsqueeze(2).to_broadcast([128, 16, 16])
                zi = zt[3][:, d * 16:(d + 1) * 16].unsqueeze(2).to_broadcast([128, 16, 16])
                t1 = tA[:, 0:256].rearrange("p (a b) -> p a b", a=16)
                t2 = tB[:, 0:256].rearrange("p (a b) -> p a b", a=16)
                V(I("tensor_tensor", t1, bre, zr, ALU.mult), [b_par2, b_zt], [b_tA])
                V(I("tensor_tensor", t2, bim, zi, ALU.mult), [b_par2, b_zt], [b_tB])
                V(I("tensor_sub", Bb["r"][:, d], t1, t2), [b_tA, b_tB], [b_Bb])
                V(I("tensor_tensor", t1, bim, zr, ALU.mult), [b_par2, b_zt], [b_tA])
                V(I("tensor_tensor", t2, bre, zi, ALU.mult), [b_par2, b_zt], [b_tB])
                V(I("tensor_add", Bb["i"][:, d], t1, t2), [b_tA, b_tB], [b_Bb])

            ck("S2b")

            def cmul_tab(out_r, out_i, tr, ti, Xr_, Xi_, wbs, neg_i=False):
                for hf, (eng, btA, btB) in enumerate([("vector", b_tA, b_tB), ("vector", b_tA2, b_tB2)]):
                    gs_ = slice(hf * 8, hf * 8 + 8)
                    ta = tA[:, hf * 1024:(hf + 1) * 1024]; tb = tB[:, hf * 1024:(hf + 1) * 1024]
                    trb = tr[:, :, gs_].rearrange("p k g -> p g k").unsqueeze(3).to_broadcast([128, 8, 8, 16])
                    tib = ti[:, :, gs_].rearrange("p k g -> p g k").unsqueeze(3).to_broadcast([128, 8, 8, 16])
                    Xrb = Xr_[:, gs_, :].unsqueeze(2).to_broadcast([128, 8, 8, 16])
                    Xib = Xi_[:, gs_, :].unsqueeze(2).to_broadcast([128, 8, 8, 16])
                    v4 = lambda t: t.rearrange("p (g k h) -> p g k h", g=8, k=8)
                    o4 = lambda t: t[:, gs_, :].rearrange("p g (k h) -> p g k h", k=8)
                    rd = [b_pw, b_Bb, b_par2]
                    wb = [wbs[hf]]
                    E_ = lambda fn, r, w, eng=eng: P.op(eng, fn, r, w)
                    E_(I("tensor_tensor", v4(ta), trb, Xrb, ALU.mult), rd, [btA])
                    E_(I("tensor_tensor", v4(tb), tib, Xib, ALU.mult), rd, [btB])
                    E_(I("tensor_sub", o4(out_r), v4(ta), v4(tb)), [btA, btB], wb)
                    E_(I("tensor_tensor", v4(ta), trb, Xib, ALU.mult), rd, [btA])
                    E_(I("tensor_tensor", v4(tb), tib, Xrb, ALU.mult), rd, [btB])
                    if neg_i and eng == "vector":
                        E_(I("scalar_tensor_tensor", out=o4(out_i), in0=v4(ta), scalar=-1.0, in1=v4(tb),
                             op0=ALU.mult, op1=ALU.subtract), [btA, btB], wb)
                    elif neg_i:
                        E_(I("tensor_add", v4(ta), v4(ta), v4(tb)), [btA, btB], [btA])
                        E_(I("tensor_scalar_mul", o4(out_i), v4(ta), -1.0), [btA], wb)
                    else:
                        E_(I("tensor_add", o4(out_i), v4(ta), v4(tb)), [btA, btB], wb)

            def tab(name, lo, hi, rev, d):
                t = pw[name][:, lo:hi, d * 16:(d + 1) * 16]
                return t[:, ::-1, :] if rev else t

            for d in range(2):
                rev = (d == 1)
                cmul_tab(Qr[:, d], Qn[:, d], tab("pr", 1, 9, rev, d), tab("pi", 1, 9, rev, d), cre[:, d], cim[:, d], b_Qh, neg_i=True)
                cmul_tab(X_r[:, d], X_i[:, d], tab("mr", 1, 9, rev, d), tab("mi", 1, 9, rev, d), Bb["r"][:, d], Bb["i"][:, d], b_Xh)
            ck("S2c")
            for gq4 in range(4):
                for g2_ in range(2):
                    ps_ = slice(g2_ * 64, g2_ * 64 + 64)
                    ptf, bpf = nps(); ptb, bpb = nps()
                    for (d, pt, bp) in [(0, ptf, bpf), (1, ptb, bpb)]:
                        fns = []
                        for k4 in range(4):
                            gp = gq4 * 4 + k4
                            fns.append(mm(pt[:, k4 * 128:(k4 + 1) * 128], X_r[ps_, d, gp, :], Qr[ps_, d, gp, :], True, False))
                            fns.append(mm(pt[:, k4 * 128:(k4 + 1) * 128], X_i[ps_, d, gp, :], Qn[ps_, d, gp, :], False, True))
                        MM(fns, b_Xh + b_Qh, [bp])
                    m4 = lambda m: m.unsqueeze(1).to_broadcast([128, 4, 128])
                    v3 = lambda t: t.rearrange("p (a b) -> p a b", a=4)
                    V(I("tensor_tensor", v3(tA[:, 0:512]), v3(ptf[:, :]), m4(Mf), ALU.mult), [bpf, b_cst], [b_tA])
                    V(I("tensor_tensor", v3(tB[:, 0:512]), v3(ptb[:, :]), m4(Mb), ALU.mult), [bpb, b_cst], [b_tB])
                    V(I("tensor_add", tA[:, 0:512], tA[:, 0:512], tB[:, 0:512]), [b_tA, b_tB], [b_tA])
                    for k4 in range(4):
                        g = 2 * (gq4 * 4 + k4) + g2_
                        V(I("scalar_tensor_tensor", out=Tm[:, g, :], in0=ident, scalar=dcol[:, g:g + 1],
                            in1=tA[:, k4 * 128:(k4 + 1) * 128], op0=ALU.mult, op1=ALU.add),
                          [b_tA, b_par, b_cst], [b_T])
            ck("S2d")
            for d in range(2):
                rev = (d == 1)
                cmul_tab(X_r[:, d], X_i[:, d], tab("pr", 0, 8, not rev, d), tab("pi", 0, 8, not rev, d),
                         Bb["r"][:, d], Bb["i"][:, d], b_Xh)
            for d in range(2):
                for gq4 in range(4):
                    for g2_ in range(2):
                        ps_ = slice(g2_ * 64, g2_ * 64 + 64)
                        pt, bp = nps()
                        fns = []
                        for k4 in range(4):
                            gp = gq4 * 4 + k4
                            for ri, Pm in enumerate([X_r, X_i]):
                                c0 = (k4 * 2 + ri) * 64
                                fns.append(mm(pt[:, c0:c0 + 64], Pm[ps_, d, gp, :], identb[ps_, g2_ * 64:g2_ * 64 + 64]))
                        MM(fns, b_Xh + [b_cst], [bp])
                        g0 = 2 * gq4 * 4 + g2_
                        V(I("tensor_copy", PT[:, d, g0:g0 + 7:2, :, :].rearrange("p g r n -> p g (r n)"),
                            pt[:, :].rearrange("p (g f) -> p g f", g=4)), [bp], [b_PT])
            ck("S2e")
            V(I("tensor_scalar_mul", Thu[:], th[:], 8.0), [b_der], [b_der])
            V(I("tensor_copy", tSI[:, 0:32], Thu[:]), [b_der], [b_tSI])
            V(I("tensor_copy", tS[1][:, 0:32], tSI[:, 0:32]), [b_tSI], [b_tS[1]])
            V(I("tensor_sub", Thu[:], Thu[:], tS[1][:, 0:32]), [b_tS[1], b_der], [b_der])
            A(I("activation", R8[:], lrdt[:], AF.Exp, scale=8.0), [b_der], [b_der])
            V(I("tensor_scalar_mul", L8[:], lrdt[:], 8.0), [b_der], [b_der])
            V(I("tensor_scalar_mul", tS[2][:, 0:32], Thu[:], 256.0), [b_der], [b_tS[2]])
            sincos_turns(tS[2][:, 0:32], 32, fini[:], finr[:], [b_tS[2]], [b_der])
            A(I("activation", mag2048[:], lrdt[:], AF.Exp, scale=2048.0), [b_der], [b_der])
            V(I("tensor_mul", Acr[:], mag2048[:], finr[:]), [b_der], [b_der])
            V(I("tensor_mul", Aci[:], mag2048[:], fini[:]), [b_der], [b_der])
            gate([b_Bb, b_zt], [b_E])
            for (Ec, Es, pos) in [(E0c, E0s, cpos[:, 0:16]), (E1c, E1s, c16[:])]:
                V(I("tensor_tensor", tS[2][:, 0:512].rearrange("p (g c) -> p g c", g=32),
                    Thu[:].unsqueeze(2).to_broadcast([128, 32, 16]), pos.unsqueeze(1).to_broadcast([128, 32, 16]), ALU.mult),
                  [b_der, b_cst], [b_tS[2]])
                sincos_turns(tS[2][:, 0:512], 512, Es.rearrange("p g c -> p (g c)"), Ec.rearrange("p g c -> p (g c)"), [b_tS[2]], [b_E])
            if l == 0:
                tap("Tm", Tm.rearrange("p g f -> p (g f)"), [b_T], 2048)
                tap("Qr", Qr.rearrange("p d g f -> p (d g f)"), b_Qh, 2048)
                tap("PT", PT.rearrange("p d g r n -> p (d g r n)"), [b_PT], 2048)

            ck("S2")
            for (so, kcols, vcols, kdst, vblk) in [(4, slice(128, 256), slice(384, 512), slice(0, 128), 0),
                                                   (8, slice(0, 128), slice(256, 384), slice(T + 128, T + 256), 17)]:
                for (cols, dst_ap, wb) in [(kcols, kT[:, kdst], b_k), (vcols, vtm[:, vblk, :], b_v)]:
                    acc = tS[2][:, 0:128]
                    V(I("tensor_scalar_mul", acc, halg[:, 0, cols], sel[:, so:so + 1]),
                      [b_halg, b_sel], [b_tS[2]])
                    for r in range(1, 4):
                        V(I("scalar_tensor_tensor", out=acc, in0=halg[:, r, cols],
                                                                                  scalar=sel[:, so + r:so + r + 1], in1=acc,
                                                                                  op0=ALU.mult, op1=ALU.add),
                          [b_halg, b_sel], [b_tS[2]])
                    V(I("tensor_copy", dst_ap, acc), [b_tS[2]], [wb])

            ck("halosel")
            for i in range(8):
                for g8 in range(8):
                    P.dma("sync", I("dma_start",
                        out=U[i * 16:(i + 1) * 16, :, :].rearrange("p (a g) c -> p a g c", a=4)[:, :, g8, :],
                        in_=uT2[g8 * 16:(g8 + 1) * 16, :, i, :]), reads=[b_uT2], writes=[b_U])
            if l == 0:
                tap("U", U.rearrange("p g c -> p (g c)"), [b_U], 2048)
            ck("relayout")
            gate(a1_s2, a1_p1)
            gate(wa_s2, [b_WA, b_WB])
            P.dma("gpsimd", I("dma_start", out=wglu, in_=w_glu_d[l].rearrange("(k p) f -> p k f", p=128)), writes=[b_WA])
            P.dma("gpsimd", I("dma_start", out=wout, in_=w_out_d[l].rearrange("(k p) f -> p k f", p=128)), writes=[b_WA, b_WB])

            ssm_ctx = {}

            def ssm_batch(bt, phase):
                ssm_z(bt, phase)
                ssm_dve(bt, phase)

            def ssm_z(bt, phase):
                for d in range(2):
                    c0 = d * 16 + bt * 2
                    gsl = slice(c0, c0 + 2)
                    cosT_, sinT_, btab_ = tabsets[ssm_it[0] % 2]
                    ssm_it[0] += 1
                    if btab_ is None:
                        rd_t = [b_rstd, b_sig]; wr_c = [b_rstd]; wr_s = [b_sig]
                    else:
                        rd_t = [btab_]; wr_c = [btab_]; wr_s = [btab_]
                    e1c = E1c[:, gsl, :].unsqueeze(3).to_broadcast([128, 2, 16, 16])
                    e1s = E1s[:, gsl, :].unsqueeze(3).to_broadcast([128, 2, 16, 16])
                    e0c = E0c[:, gsl, :].unsqueeze(2).to_broadcast([128, 2, 16, 16])
                    e0s = E0s[:, gsl, :].unsqueeze(2).to_broadcast([128, 2, 16, 16])
                    q4 = lambda t: t.rearrange("p (g a b) -> p g a b", g=2, a=16)
                    q3 = lambda t: t.rearrange("p g (a b) -> p g a b", a=16)
                    pa = tS[2][:, 0:512]; pb = tSI[:, 0:512].bitcast(F32)
                    G = lambda fn, r, w: P.op("gpsimd", fn, r, w)
                    G(I("tensor_tensor", q4(pa), e1c, e0c, ALU.mult), [b_E], [b_tS[2]])
                    G(I("tensor_tensor", q4(pb), e1s, e0s, ALU.mult), [b_E], [b_tSI])
                    G(I("tensor_sub", q3(cosT_), q4(pa), q4(pb)), [b_tS[2], b_tSI], wr_c)
                    G(I("tensor_tensor", q4(pa), e1s, e0c, ALU.mult), [b_E], [b_tS[2]])
                    G(I("tensor_tensor", q4(pb), e1c, e0s, ALU.mult), [b_E], [b_tSI])
                    G(I("tensor_add", q3(sinT_), q4(pa), q4(pb)), [b_tS[2], b_tSI], wr_s)
                    zb = []
                    for ri in range(2):
                        pt, bp = nps("ssm")
                        fns = []
                        for gq in range(2):
                            gp = bt * 2 + gq
                            for g2_ in range(2):
                                g = gp * 2 + g2_
                                fns.append(mm(pt[g2_ * 64:(g2_ + 1) * 64, gq * 256:(gq + 1) * 256], PT[:, d, g, ri, :], U[:, g, :]))
                        MM(fns, [b_PT, b_U], [bp])
                        zb.append((pt, bp))
                    ssm_ctx[(bt, d)] = (zb, cosT_, sinT_, rd_t)

            def ssm_dve(bt, phase):
                for d in range(2):
                    c0 = d * 16 + bt * 2
                    gsl = slice(c0, c0 + 2)
                    zb, cosT_, sinT_, rd_t = ssm_ctx.pop((bt, d))
                    (pzr, bzr), (pzi, bzi) = zb
                    zr_ = pzr[:, :].rearrange("p (g c) -> p g c", g=2)
                    zi_ = pzi[:, :].rearrange("p (g c) -> p g c", g=2)
                    if d == 1:
                        zr_ = zr_[:, :, ::-1]; zi_ = zi_[:, :, ::-1]
                    a3 = tS[0][:, 0:512].rearrange("p (g c) -> p g c", g=2)
                    b3 = tS[1][:, 0:512].rearrange("p (g c) -> p g c", g=2)
                    V(I("tensor_tensor", a3, zr_, cosT_, ALU.mult), [bzr, *rd_t], [b_tS[0]])
                    V(I("tensor_tensor", b3, zi_, sinT_, ALU.mult), [bzi, *rd_t], [b_tS[1]])
                    V(I("tensor_add", Wr, a3, b3), [b_tS[0], b_tS[1]], [b_W])
                    V(I("tensor_tensor", a3, zi_, cosT_, ALU.mult), [bzi, *rd_t], [b_tS[0]])
                    V(I("tensor_tensor", b3, zr_, sinT_, ALU.mult), [bzr, *rd_t], [b_tS[1]])
                    V(I("tensor_sub", Wi, a3, b3), [b_tS[0], b_tS[1]], [b_W])
                    if phase == 0:
                        V(I("memset", facc[:], 0.0), [], [b_facc])
                        for gq in range(2):
                            col = c0 + gq
                            A(I("activation", Sr[:, gq, :], crev[:], AF.Exp, scale=L8[:, col:col + 1]), [b_der, b_cst], [b_S])
                        for gq in range(2):
                            for (Wt, ci) in [(Wr, 0), (Wi, 1)]:
                                V(I("scalar_tensor_tensor", out=Si[:, gq, :], in0=Wt[:, gq, :], scalar=1.0, in1=Sr[:, gq, :],
                                    op0=ALU.mult, op1=ALU.mult, accum_out=facc[:, ci * 2 + gq:ci * 2 + gq + 1]),
                                  [b_W, b_S], [b_S, b_facc])
                    else:
                        for gq in range(2):
                            col = c0 + gq
                            for (Wt, St, ci) in [(Wr, Sr, 0), (Wi, Si, 1)]:
                                cc = ci * 32 + col
                                init = carry[:, cc:cc + 1]
                                V(I("tensor_tensor_scan", St[:, gq, :], R8[:, col:col + 1].to_broadcast([128, 256]), Wt[:, gq, :], init,
                                    ALU.mult, ALU.add), [b_W, b_der, b_carry], [b_S])
                    if phase == 0:
                        fr = finr[:, gsl]; fi = fini[:, gsl]
                        s_r = facc[:, 0:2]; s_i = facc[:, 2:4]
                        o_r = Fst[:, c0:c0 + 2]; o_i = Fst[:, 32 + c0:32 + c0 + 2]
                        t0 = tS[0][:, 0:2]; t1 = tS[1][:, 0:2]
                        V(I("tensor_mul", t0, s_r, fr), [b_facc, b_der], [b_tS[0]])
                        V(I("tensor_mul", t1, s_i, fi), [b_facc, b_der], [b_tS[1]])
                        V(I("tensor_sub", o_r, t0, t1), [b_tS[0], b_tS[1]], [b_F])
                        V(I("tensor_mul", t0, s_r, fi), [b_facc, b_der], [b_tS[0]])
                        V(I("tensor_mul", t1, s_i, fr), [b_facc, b_der], [b_tS[1]])
                        V(I("tensor_add", o_i, t0, t1), [b_tS[0], b_tS[1]], [b_F])
                    else:
                        Hr_ = Hb16[(d, "r")]; Hi_ = Hb16[(d, "i")]
                        if d == 0:
                            o_r = Hr_[:, :, 1:256]; o_i = Hi_[:, :, 1:256]
                            o_r0 = Hr_[:, :, 0]; o_i0 = Hi_[:, :, 0]
                        else:
                            o_r = Hr_[:, :, 254::-1]; o_i = Hi_[:, :, 254::-1]
                            o_r0 = Hr_[:, :, 255]; o_i0 = Hi_[:, :, 255]
                        a3s = a3[:, :, 0:255]; b3s = b3[:, :, 0:255]
                        V(I("tensor_tensor", a3s, Sr[:, :, 0:255], cosT_[:, :, 0:255], ALU.mult), [b_S, *rd_t], [b_tS[0]])
                        V(I("tensor_tensor", b3s, Si[:, :, 0:255], sinT_[:, :, 0:255], ALU.mult), [b_S, *rd_t], [b_tS[1]])
                        V(I("tensor_sub", o_r, a3s, b3s), [b_tS[0], b_tS[1]], [b_H])
                        V(I("tensor_tensor", a3s, Sr[:, :, 0:255], sinT_[:, :, 0:255], ALU.mult), [b_S, *rd_t], [b_tS[0]])
                        V(I("tensor_tensor", b3s, Si[:, :, 0:255], cosT_[:, :, 0:255], ALU.mult), [b_S, *rd_t], [b_tS[1]])
                        V(I("tensor_add", o_i, a3s, b3s), [b_tS[0], b_tS[1]], [b_H])
                        V(I("tensor_copy", o_r0, carry[:, c0:c0 + 2]), [b_carry], [b_H])
                        V(I("tensor_copy", o_i0, carry[:, 32 + c0:32 + c0 + 2]), [b_carry], [b_H])

            V(I("tensor_copy", esrow[:, :, :], esink[0:1, l, :].unsqueeze(2).to_broadcast([1, 8, 128])), [b_small], [b_esr])
            gate([b_uT2], [b_att])
            att_state = {}

            def att_scores(qb):
                qs = slice(qb * 128, (qb + 1) * 128)
                for kvh in range(2):
                    hs_ = slice(kvh * 64, kvh * 64 + 64)
                    pM = pM2[kvh]; b_pM = b_pM2[kvh]
                    for kb in range(3):
                        pt, bp = nps()
                        MM([mm(pt[:, :].rearrange("p (j q) -> p j q", j=4), kT[hs_, (qb + kb) * 128:(qb + kb + 1) * 128], qT[hs_, :, qs])],
                           [b_k, b_q], [bp])
                        A(I("activation", pM[kb], pt[:, :], AF.Exp, scale=0.125), [bp], [b_pM[kb]])

            def att_mid(qb, on_dve=False):
                for kvh in range(2):
                    hs_ = slice(kvh * 64, kvh * 64 + 64)
                    pM = pM2[kvh]; b_pM = b_pM2[kvh]
                    for kb in range(3):
                        P.op("vector" if (on_dve and kvh == 0) else "gpsimd",
                             I("tensor_tensor", pM[kb], pM[kb],
                               etab[:, kb, kvh * 4:(kvh + 1) * 4, :].rearrange("p j q -> p (j q)"), ALU.mult),
                             [b_et], [b_pM[kb]])
                    pno, bpn = nps()
                    pn = pno[:, 0:256]; pd = pno[:, 256:512]
                    fn_n = []; fn_d = []
                    for j in range(4):
                        po = slice((j % 2) * 64, (j % 2) * 64 + 64)
                        cs2 = slice((j // 2) * 128, (j // 2) * 128 + 128)
                        for kb in range(3):
                            if kb == 0 and qb == 0:
                                ol = ones_pn[:, 0, :]
                            elif kb == 2 and qb == 15:
                                ol = ones_pn[:, 1, :]
                            else:
                                ol = onesb[:, 0:64]
                            fn_n.append(mm(pn[po, cs2], vtm[:, qb + kb, hs_], pM[kb][:, j * 128:(j + 1) * 128], kb == 0, kb == 2))
                            if kb == 0:
                                fn_d.append(mm(pd[po, cs2], onesb[0:1, 0:64], esrow[0:1, kvh * 4 + j, :], True, False))
                            fn_d.append(mm(pd[po, cs2], ol, pM[kb][:, j * 128:(j + 1) * 128], False, kb == 2))
                    MM(fn_n + fn_d, [b_v, b_cst, b_esr] + b_pM, [bpn])
                    att_state[(qb, kvh)] = (pn, pd, bpn)

            def att_fin(qb):
                qs = slice(qb * 128, (qb + 1) * 128)
                for kvh in range(2):
                    pn, pd, bpn = att_state.pop((qb, kvh))
                    A(I("activation", rden, pd.rearrange("p (a q) -> p a q", a=2), AF.Ln), [bpn], [b_rden])
                    A(I("activation", rden, rden, AF.Exp, scale=-1.0), [b_rden], [b_rden])
                    V(I("tensor_tensor", attT[:, kvh * 2:kvh * 2 + 2, qs],
                        pn.rearrange("p (a q) -> p a q", a=2), rden, ALU.mult),
                      [bpn, b_rden, b_uT2], [b_att])

            def att_flush():
                pass

            att_order = [1, 2, 3, 4, 5, 6, 7, 8, 9, 10, 11, 12, 13, 14, 0, 15]
            att_scores(att_order[0])
            for bt in range(8):
                ssm_z(bt, 0)
                att_mid(att_order[bt], on_dve=True)
                ssm_dve(bt, 0)
                att_fin(att_order[bt])
                att_scores(att_order[bt + 1])
            ck("ssm0")
            P.dma("gpsimd", I("dma_start", out=st_src, in_=Fst[:]), reads=[b_F], writes=[b_stg])
            P.op("gpsimd", I("collective_compute", "AllGather", ALU.bypass, replica_groups=GROUPS,
                                                         ins=[st_src.opt()], outs=[st_dst.opt()]), [b_stg], [b_stg])
            P.dma("gpsimd", I("dma_start", out=stg[:], in_=st_dst.rearrange("(r p) f -> p r f", p=128)),
                  reads=[b_stg], writes=[b_stg])

            ck("stx")
            if l == 0:
                tap("attT", attT.rearrange("p j t -> p (j t)"), [b_att], 2048)

            ck("att")
            V(I("memset", carry[:], 0.0), [], [b_carry])
            for d in range(2):
                cs_ = slice(d * 16, d * 16 + 16); ci_ = slice(32 + d * 16, 32 + d * 16 + 16)
                cr = hcr[:, 0:16]; cim_ = hcr[:, 16:32]; t0 = hcr[:, 32:48]; t1 = hcr[:, 48:64]
                V(I("memset", hcr[:], 0.0), [], [b_hcr])
                order = [0, 1, 2, 3] if d == 0 else [3, 2, 1, 0]
                hh = [b_hcr]
                for r in order:
                    V(I("scalar_tensor_tensor", out=carry[:, cs_], in0=cr, scalar=sel[:, r:r + 1], in1=carry[:, cs_],
                                                                     op0=ALU.mult, op1=ALU.add), [b_hcr, b_sel], [b_carry])
                    V(I("scalar_tensor_tensor", out=carry[:, ci_], in0=cim_, scalar=sel[:, r:r + 1], in1=carry[:, ci_],
                                                                     op0=ALU.mult, op1=ALU.add), [b_hcr, b_sel], [b_carry])
                    ar_ = Acr[:, cs_]; ai_ = Aci[:, cs_]
                    V(I("tensor_mul", t0, cr, ar_), [b_der], hh)
                    V(I("tensor_mul", t1, cim_, ai_), [b_der], hh)
                    V(I("tensor_sub", t0, t0, t1), [], hh)
                    V(I("tensor_mul", t1, cr, ai_), [b_der], hh)
                    V(I("tensor_mul", cim_, cim_, ar_), [b_der], hh)
                    V(I("tensor_add", cim_, cim_, t1), [], hh)
                    V(I("tensor_add", cr, t0, stg[:, r, cs_]), [b_stg], hh)
                    V(I("tensor_add", cim_, cim_, stg[:, r, ci_]), [b_stg], hh)

            for bt in range(8):
                ssm_z(bt, 1)
                att_mid(att_order[8 + bt])
                ssm_dve(bt, 1)
                att_fin(att_order[8 + bt])
                if bt < 7:
                    att_scores(att_order[9 + bt])
                for gq in range(2):
                    gp = bt * 2 + gq
                    pt, bp = nps("ssm")
                    fns = []
                    for g2_ in range(2):
                        g = gp * 2 + g2_
                        ps_ = slice(g2_ * 64, g2_ * 64 + 64)
                        oc = pt[:, g2_ * 256:g2_ * 256 + 256]
                        fns.append(mm(oc, Tm[:, g, :], U[:, g, :], True, False))
                        for d in range(2):
                            fns.append(mm(oc, Qr[ps_, d, gp, :], Hb16[(d, "r")][ps_, gq, :], False, False))
                            fns.append(mm(oc, Qn[ps_, d, gp, :], Hb16[(d, "i")][ps_, gq, :], False, d == 1))
                    MM(fns, [b_T, b_U, b_H] + b_Qh, [bp])
                    A(I("activation", Yall[:, gp * 2:gp * 2 + 2, :].rearrange("p g c -> p (g c)"), pt[:, :],
                                                           AF.Gelu_apprx_tanh), [bp, b_Yall], [b_Yb[bt]])
            if l == 0:
                tap("Yall", Yall.rearrange("p g c -> p (g c)"), b_Yb, 2048)
            ck("ssm1")
            gate(xr_mixer, allx)
            for k in range(8):
                P.dma("gpsimd", I("dma_start", out=xT[:, k, :], in_=xsp[k * 128:(k + 1) * 128, :]), writes=b_x[k])
            att_flush()
            b_yTok = Buf()
            gate([b_tab, b_W, b_S, b_H], [b_yTok])
            gate([b_q], [b_yT2])
            yTok = A1[:, 8192:16384].rearrange("p (k a j g h) -> p k a j g h", k=2, a=4, j=8, g=8)
            ev = [0]

            def evac(out_ap, in_ap, r, w):
                if ev[0] % 2 == 0:
                    A(I("activation", out_ap, in_ap, AF.Copy), r, w)
                else:
                    V(I("tensor_copy", out_ap, in_ap), r, w)
                ev[0] += 1
            for cblk in range(2):
                for g4 in range(8):
                    pt, bp = nps()
                    MM([mm(pt[:, k4 * 128:(k4 + 1) * 128], Yall[:, g4 * 4 + k4, cblk * 128:(cblk + 1) * 128], identb[:, :])
                        for k4 in range(4)], b_Yb + [b_cst], [bp])
                    evac(yTok[:, cblk, g4 // 2, :, (g4 % 2) * 4:(g4 % 2) * 4 + 4, :],
                         pt[:, :].rearrange("p (k j h) -> p j k h", k=4, j=8), [bp], [b_yTok])
            for a_ in range(4):
                for jp in range(4):
                    pt, bp = nps()
                    fns = []
                    for jj in range(2):
                        j = jp * 2 + jj
                        for cblk in range(2):
                            c0 = (jj * 2 + cblk) * 128
                            fns.append(mm(pt[:, c0:c0 + 128], yTok[:, cblk, a_, j, :, :].rearrange("p g h -> p (g h)"), identb[:, :]))
                    MM(fns, [b_yTok, b_cst], [bp])
                    evac(yT2[:, a_, jp * 2:jp * 2 + 2, :].rearrange("p j c -> p (j c)"), pt[:, :], [bp], [b_yT2])
            gate([b_yTok], [b_glu])
            yflat = yT2.rearrange("p a j c -> p a (j c)")
            for cb in range(4):
                cs_ = slice(cb * 512, (cb + 1) * 512)
                for oc in range(4):
                    pv, bpv = nps(); pg, bpg = nps()
                    MM([mm(pv[:, :], wglu[:, k, oc * 128:(oc + 1) * 128], yflat[:, k, cs_], k == 0, k == 3) for k in range(4)],
                       [b_WA, b_yT2], [bpv])
                    MM([mm(pg[:, :], wglu[:, k, 512 + oc * 128:512 + (oc + 1) * 128], yflat[:, k, cs_], k == 0, k == 3) for k in range(4)],
                       [b_WA, b_yT2], [bpg])
                    A(I("activation", sig[:], pg[:, :], AF.Sigmoid), [bpg], [b_sig])
                    dst = gluT[:, oc, :].rearrange("p (c j) -> p j c", j=8)[:, 2 * cb:2 * cb + 2, :]
                    V(I("tensor_tensor", dst, pv[:, :].rearrange("p (j c) -> p j c", j=2),
                                                                sig[:, :].rearrange("p (j c) -> p j c", j=2), ALU.mult),
                      [bpv, b_sig], [b_glu])
            if l == 0:
                tap("gluT", gluT.rearrange("p j t -> p (j t)"), [b_glu], 2048)
            ck("glu")
            for tt in range(NTT):
                ts = slice(tt * 512, (tt + 1) * 512)
                for m in range(8):
                    pt, bp = nps()
                    fns = [mm(pt[:, :], wout[:, k, m * 128:(m + 1) * 128], attT[:, k, ts], k == 0, False) for k in range(4)]
                    fns += [mm(pt[:, :], wout[:, 4 + k, m * 128:(m + 1) * 128], gluT[:, k, ts], False, k == 3) for k in range(4)]
                    MM(fns, [b_WA, b_WB, b_att, b_glu], [bp])
                    V(I("tensor_add", xT[:, m, ts], xT[:, m, ts], pt[:, :]), [bp], [b_x[m][tt]])
            if l == 0:
                tap("xmid", xT[:, 0, :], b_x[0], 2048)

            ck("wout")
            gate(a1_p1, b_h)
            gate([b_att, b_yT2, b_uT2, b_q], [])
            gate(misc_att, misc_ffn)
            gate([b_E], [b_WB])
            for tt in range(NTT):
                rms_tile(l, tt, g2)
            b_ws = [b_WA, b_WB]
            for gI in range(8):
                s = gI % 2
                w1, w2 = wsl[s]
                P.dma("gpsimd", I("dma_start",
                    out=w1, in_=w_ff1_d[l][:, gI * 512:(gI + 1) * 512].rearrange("(k p) f -> p k f", p=128)), writes=[b_ws[s]])
                P.dma("gpsimd", I("dma_start",
                    out=w2, in_=w_ff2_d[l][gI * 512:(gI + 1) * 512, :].rearrange("(k p) f -> p k f", p=128)), writes=[b_ws[s]])
                for tt in range(NTT):
                    ts = slice(tt * 512, (tt + 1) * 512)
                    for fc in range(4):
                        pt, bp = nps()
                        MM([mm(pt[:, :], w1[:, k, fc * 128:(fc + 1) * 128], hT[:, k, ts], k == 0, k == 7) for k in range(8)],
                           [b_ws[s], b_h[tt]], [bp])
                        ri_ = fc % 2
                        A(I("activation", rl[ri_], pt[:, :], AF.Relu), [bp], [b_rl[ri_]])
                        V(I("tensor_mul", aT[:, fc, :], rl[ri_], rl[ri_]), [b_rl[ri_]], [b_a[fc]])
                    for m in range(8):
                        pt, bp = nps()
                        MM([mm(pt[:, :], w2[:, k, m * 128:(m + 1) * 128], aT[:, k, :], k == 0, k == 3) for k in range(4)],
                           [b_ws[s]] + b_a, [bp])
                        V(I("tensor_add", xT[:, m, ts], xT[:, m, ts], pt[:, :]), [bp], [b_x[m][tt]])
            gate(misc_ffn, misc_att)

        try:
            for l in range(nlayers):
                layer(l)
        except _Stop:
            gate(xr_mixer, allx)
        emit_taps()
        for k in range(8):
            P.dma("sync", I("dma_start", out=out_d[k * 128:(k + 1) * 128, :], in_=xT[:, k, :]), reads=b_x[k])
        P.wait_all("sync", allx + b_tS)
        P.finish("sync")
        P.run()
    return nc


def _prep_inputs(x, norm1, w_in, q_gain, k_gain, sink, lam_re, lam_im, log_dt, b_re, b_im, c_re, c_im,
                 d_skip, w_glu, w_out, norm2, w_ff1, w_ff2, nlayers=DEPTH):
    f = lambda a: np.ascontiguousarray(np.asarray(a, dtype=np.float32))
    L = DEPTH
    perm = []
    for j in range(4):
        perm += list(range(j * 64, j * 64 + 64)) + list(range((4 + j) * 64, (4 + j) * 64 + 64))
    perm += list(range(512, 1280))
    shared = {
        "w_in": f(np.asarray(w_in)[:nlayers][:, :, perm]), "w_glu": f(np.asarray(w_glu)[:nlayers]),
        "w_out": f(np.asarray(w_out)[:nlayers]),
        "w_ff1": f(np.asarray(w_ff1)[:nlayers]), "w_ff2": f(np.asarray(w_ff2)[:nlayers]),
        "g1": f(np.asarray(norm1).reshape(L, 8, 128).transpose(2, 0, 1)),
        "g2": f(np.asarray(norm2).reshape(L, 8, 128).transpose(2, 0, 1)),
        "qg": f(np.tile(np.asarray(q_gain).T, (2, 1))),
        "kg": f(np.tile(np.asarray(k_gain).T, (2, 1))),
        "sinkr": f(np.broadcast_to(np.asarray(sink)[None], (128, L, 8))),
    }

    def gn(a):
        a = np.asarray(a).reshape(L, 2, 16, 2, 64)
        return f(a.transpose(3, 4, 0, 1, 2).reshape(128, L, 32))
    shared["lre"] = gn(lam_re)
    shared["lim"] = gn(lam_im)
    shared["ldt"] = gn(np.broadcast_to(np.asarray(log_dt)[:, :, :, None], (L, 2, 32, 64)))

    def bb(a):
        a = np.asarray(a).reshape(L, 16, 2, 64, 16)
        return f(a.transpose(2, 3, 0, 1, 4).reshape(128, L, 256))
    shared["bre"] = bb(b_re)
    shared["bim"] = bb(b_im)

    def cc(a):
        a = np.asarray(a).reshape(L, 2, 16, 2, 16, 64)
        return f(a.transpose(3, 5, 0, 1, 2, 4).reshape(128, L, 512))
    shared["cre"] = cc(c_re)
    shared["cim"] = cc(c_im)
    dsk = np.asarray(d_skip).reshape(L, 32, 16)
    shared["dcol"] = f(np.broadcast_to(dsk.transpose(2, 0, 1)[None], (8, 16, L, 32)).reshape(128, L, 32))
    slopes = np.exp2(-8.0 * np.arange(1, 9) / 8.0)
    ci = np.arange(128)[:, None]; qi = np.arange(128)[None, :]
    et = np.zeros((128, 3, 8, 128), np.float32)
    for kb in range(3):
        dist = np.abs(qi - ci - (kb - 1) * 128)
        valid = dist <= 128
        for h in range(8):
            et[:, kb, h, :] = np.where(valid, np.exp(-slopes[h] * dist), 0.0)
    shared["etab"] = f(et.reshape(128, 3072))
    cst = np.zeros((128, 656), np.float32)
    cst[:, 0:128] = np.eye(128)
    bi = np.arange(128)[:, None] // 16; bj = np.arange(128)[None, :] // 16
    cst[:, 128:256] = (bj >= bi)
    cst[:, 256:384] = (bi >= bj)
    cst[:, 384:393] = np.arange(9)[None, :]
    cst[:, 400:656] = np.arange(1, 257)[None, :]
    shared["cst"] = cst
    xs = np.asarray(x)
    in_maps = []
    for r in range(8):
        b, q = r // 4, r % 4
        m = dict(shared)
        m["xT"] = f(xs[b, q * T:(q + 1) * T, :].T)
        s = np.zeros((128, 16), np.float32)
        s[:, q] = 1.0
        if q > 0:
            s[:, 4 + q - 1] = 1.0; s[:, 12] = 1.0
        if q < 3:
            s[:, 8 + q + 1] = 1.0; s[:, 13] = 1.0
        m["sel"] = s
        in_maps.append(m)
    return in_maps


_NC_CACHE = {}


def kernel(**inputs):
    in_maps = _prep_inputs(**inputs)
    if "nc" not in _NC_CACHE:
        _NC_CACHE["nc"] = build_nc()
    nc = _NC_CACHE["nc"]
    res = run_bass_kernel_spmd(nc, in_maps, core_ids=list(range(8)))
    out = np.zeros((2, 4 * T, 1024), np.float32)
    for r in range(8):
        b, q = r // 4, r % 4
        out[b, q * T:(q + 1) * T, :] = np.asarray(res.results[r]["outT"]).T
    return out
```
